# Optimizing a Trainium2 kernel written in Bass

```python
import math
import jax, jax.numpy as jnp
from jax import lax
import numpy as np

D_MODEL = 1024
BATCH = 16
SEQ = 2048
DEPTH = 1

CTX_LEN = 256
GRID_W = 64
A_HEADS = 4
A_DK = 128
A_DV = 128
A_KW = A_HEADS * A_DK
A_VW = A_HEADS * A_DV
A_CHUNK = 64
B_WIDTH = 512
HYENA_ORDER = 2
HYENA_SHORT = 3
HYENA_EMB = 33
HYENA_BANDS = (HYENA_EMB - 1) // 2
HYENA_FFN = 64
HYENA_FAST_DECAY = 0.3
HYENA_SLOW_DECAY = 1.5
HYENA_TARGET = 1e-2
HYENA_SHIFT = 0.05
HYENA_L1_EPS = 1e-6
D_FF = 2816
N_MOD = 9
RMS_EPS = 1e-6
COL_V = 0
COL_FFW = COL_V + A_VW
COL_FBW = COL_FFW + A_KW
COL_Q = COL_FBW + A_KW
COL_G = COL_Q + A_KW
COL_HY = COL_G + A_VW
COL_MERGE = COL_HY + 3 * B_WIDTH
IN_COLS = COL_MERGE + 2 * D_MODEL

kernel_name = 'hgrn2_hyena_macaron_dit_layer'


def rms_norm(x):
    xf = x.astype(jnp.float32)
    return (xf * lax.rsqrt(jnp.mean(xf * xf, axis=-1, keepdims=True) + RMS_EPS)).astype(x.dtype)


def modulate(x, shift, scale):
    return x * (1.0 + scale) + shift


def swiglu(x, wg, wu, wd):
    return (jax.nn.silu(x @ wg) * (x @ wu)) @ wd


def half_ffn(h, shift, scale, gate, wg, wu, wd):
    return h + 0.5 * gate * swiglu(modulate(rms_norm(h), shift, scale), wg, wu, wd)


def grid_pos_embed(n_tokens):
    rows = n_tokens // GRID_W
    quarter = D_MODEL // 4
    omega = 1.0 / (10000.0 ** (jnp.arange(quarter, dtype=jnp.float32) / quarter))
    ar = jnp.arange(rows, dtype=jnp.float32)[:, None] * omega
    ac = jnp.arange(GRID_W, dtype=jnp.float32)[:, None] * omega
    er = jnp.concatenate([jnp.sin(ar), jnp.cos(ar)], axis=-1)
    ec = jnp.concatenate([jnp.sin(ac), jnp.cos(ac)], axis=-1)
    emb = jnp.concatenate([jnp.broadcast_to(er[:, None, :], (rows, GRID_W, D_MODEL // 2)),
                           jnp.broadcast_to(ec[None, :, :], (rows, GRID_W, D_MODEL // 2))], axis=-1)
    return emb.reshape(rows * GRID_W, D_MODEL)


def to_heads(a):
    b, l, w = a.shape
    return a.reshape(b, l, A_HEADS, w // A_HEADS).transpose(0, 2, 1, 3)


def hgrn2_forget(z, lb):
    f = lb + (1.0 - lb) * jax.nn.sigmoid(z.astype(jnp.float32))
    return to_heads(jnp.log(f)), to_heads(1.0 - f)


def hgrn2_chunk_scan(q, k, v, logf, s0):
    b, h, l, dk = q.shape
    n_chunks = l // A_CHUNK

    def to_chunks(a):
        return a.reshape(b, h, n_chunks, A_CHUNK, a.shape[-1]).transpose(2, 0, 1, 3, 4)

    within_scan = jnp.tril(jnp.ones((A_CHUNK, A_CHUNK), dtype=bool))[:, :, None]

    def step(s, inp):
        qb, kb, vb, gb = inp
        cum = jnp.cumsum(gb, axis=2)
        diff = cum[:, :, :, None, :] - cum[:, :, None, :, :]
        decay = jnp.exp(jnp.where(within_scan, diff, -jnp.inf))
        scores = jnp.einsum('bhtsd,bhsd->bhts', qb[:, :, :, None, :] * decay, kb)
        o = scores @ vb + (qb * jnp.exp(cum)) @ s
        last = cum[:, :, -1:, :]
        s_new = jnp.exp(last[:, :, 0, :])[..., None] * s + jnp.einsum('bhsd,bhse->bhde', kb * jnp.exp(last - cum), vb)
        return s_new, o

    s_final, oc = lax.scan(step, s0, (to_chunks(q), to_chunks(k), to_chunks(v), to_chunks(logf)))
    o = oc.transpose(1, 2, 0, 3, 4).reshape(b, h, l, v.shape[-1])
    return o, s_final


def hgrn2_final_state(k, v, logf):
    cum = jnp.cumsum(logf, axis=2)
    return jnp.einsum('bhld,bhle->bhde', k * jnp.exp(cum[:, :, -1:, :] - cum), v)


def flip_seq(a):
    return jnp.flip(a, axis=2)


def hgrn2_bidir(q, v, logf_fw, k_fw, logf_bw, k_bw, s0_fw, s0_bw):
    o_fw, s_fw = hgrn2_chunk_scan(q, k_fw, v, logf_fw, s0_fw)
    o_bw, s_bw = hgrn2_chunk_scan(flip_seq(q), flip_seq(k_bw), flip_seq(v), flip_seq(logf_bw), s0_bw)
    return o_fw + flip_seq(o_bw), s_fw, s_bw


def context_states(nc, w_in, lb):
    p = nc @ w_in[:, :COL_Q]
    v = to_heads(p[..., COL_V:COL_FFW]).astype(jnp.float32)
    logf_fw, k_fw = hgrn2_forget(p[..., COL_FFW:COL_FBW], lb[0])
    logf_bw, k_bw = hgrn2_forget(p[..., COL_FBW:COL_Q], lb[1])
    s_fw = hgrn2_final_state(k_fw, v, logf_fw)
    s_bw = hgrn2_final_state(flip_seq(k_bw), flip_seq(v), flip_seq(logf_bw))
    return s_fw, s_bw


def short_conv(u, w, bias):
    l = u.shape[1]
    pad = HYENA_SHORT // 2
    up = jnp.pad(u, ((0, 0), (pad, pad), (0, 0)))
    out = bias
    for j in range(HYENA_SHORT):
        out = out + up[:, j:j + l] * w[j]
    return out


def hyena_pos_features(l):
    p = jnp.arange(l, dtype=jnp.float32)
    t = p / (l - 1)
    w = 2.0 * math.pi * p / l
    f = jnp.linspace(1e-4, HYENA_BANDS - 1, HYENA_BANDS, dtype=jnp.float32)
    ang = w[:, None] * f[None, :]
    z = jnp.concatenate([t[:, None], jnp.cos(ang), -jnp.sin(ang)], axis=-1)
    return t, z


def hyena_kernels(l, w1, b1, fr1, w2, b2, fr2, w3):
    t, z = hyena_pos_features(l)
    h = jnp.sin(fr1 * (z @ w1 + b1))
    h = jnp.sin(fr2 * (h @ w2 + b2))
    h = (h @ w3).astype(jnp.float32).reshape(l, HYENA_ORDER, 2, B_WIDTH)
    max_decay = math.log(HYENA_TARGET) / HYENA_FAST_DECAY
    min_decay = math.log(HYENA_TARGET) / HYENA_SLOW_DECAY
    deltas = jnp.abs(jnp.linspace(min_decay, max_decay, B_WIDTH, dtype=jnp.float32))
    window = jnp.exp(-t[:, None] * deltas[None, :]) + HYENA_SHIFT
    h = h * window[:, None, None, :]
    kern = jnp.concatenate([h[:, :, 0], jnp.flip(h[:, :, 1], axis=0)], axis=0)
    return kern / (jnp.sum(jnp.abs(kern), axis=0, keepdims=True) + HYENA_L1_EPS)


def fft_conv(u, kern):
    l = u.shape[1]
    uf = jnp.fft.rfft(u.astype(jnp.float32), n=2 * l, axis=1)
    kf = jnp.fft.rfft(kern, n=2 * l, axis=0)
    return jnp.fft.irfft(uf * kf[None], n=2 * l, axis=1)[:, :l]


def hyena(u3, kern, hy_bias):
    v, x1, x2 = jnp.split(u3, 3, axis=-1)
    z = v
    for o, gate in enumerate((x1, x2)):
        z = gate * (fft_conv(z, kern[:, o]).astype(z.dtype) + hy_bias[o] * z)
    return z


def token_mixers(n, s0_fw, s0_bw, kern, w_in, lb, a_norm_w, conv_w, conv_b, hy_bias, w_pa, w_pb, w_out):
    b, l, _ = n.shape
    p = n @ w_in
    v = to_heads(p[..., COL_V:COL_FFW]).astype(jnp.float32)
    logf_fw, k_fw = hgrn2_forget(p[..., COL_FFW:COL_FBW], lb[0])
    logf_bw, k_bw = hgrn2_forget(p[..., COL_FBW:COL_Q], lb[1])
    q = to_heads(jax.nn.silu(p[..., COL_Q:COL_G])).astype(jnp.float32)
    o, s_fw, s_bw = hgrn2_bidir(q, v, logf_fw, k_fw, logf_bw, k_bw, s0_fw, s0_bw)
    o = rms_norm(o.transpose(0, 2, 1, 3)) * a_norm_w
    o_a = o.reshape(b, l, A_VW).astype(n.dtype) * jax.nn.silu(p[..., COL_G:COL_HY])
    u3 = short_conv(p[..., COL_HY:COL_MERGE], conv_w, conv_b)
    o_b = hyena(u3, kern, hy_bias)
    g_a, g_b = jnp.split(p[..., COL_MERGE:], 2, axis=-1)
    y = jax.nn.sigmoid(g_a) * (o_a @ w_pa) + jax.nn.sigmoid(g_b) * (o_b @ w_pb)
    return y @ w_out, s_fw, s_bw


def setup_inputs(seed: int = 0) -> dict:
    key = jax.random.key(seed)
    ks = jax.random.split(key, 32)
    nrm = jax.random.normal
    f32 = jnp.float32
    d = D_MODEL
    return {
        'x': nrm(ks[0], (BATCH, SEQ, d), f32),
        'c': nrm(ks[1], (BATCH, d), f32),
        'ctx': nrm(ks[2], (BATCH, CTX_LEN, d), f32),
        'c_ctx': nrm(ks[3], (d,), f32),
        'mod_w': nrm(ks[4], (DEPTH, d, N_MOD * d), f32) * d ** -0.5,
        'mod_b': nrm(ks[5], (DEPTH, N_MOD * d), f32) * 0.01,
        'ffn_w_gate': nrm(ks[6], (DEPTH, 2, d, D_FF), f32) * d ** -0.5,
        'ffn_w_up': nrm(ks[7], (DEPTH, 2, d, D_FF), f32) * d ** -0.5,
        'ffn_w_down': nrm(ks[8], (DEPTH, 2, D_FF, d), f32) * D_FF ** -0.5,
        'w_in': nrm(ks[9], (DEPTH, d, IN_COLS), f32) * d ** -0.5,
        'hgrn_lb_logits': nrm(ks[10], (DEPTH + 1, 2, A_KW), f32) * 0.1,
        'hgrn_norm_w': 1.0 + 0.1 * nrm(ks[11], (DEPTH, A_DV), f32),
        'hyena_conv_w': nrm(ks[12], (DEPTH, HYENA_SHORT, 3 * B_WIDTH), f32) * HYENA_SHORT ** -0.5,
        'hyena_conv_b': nrm(ks[13], (DEPTH, 3 * B_WIDTH), f32) * 0.01,
        'hyena_w1': nrm(ks[14], (DEPTH, HYENA_EMB, HYENA_FFN), f32) * HYENA_EMB ** -0.5,
        'hyena_b1': nrm(ks[15], (DEPTH, HYENA_FFN), f32) * 0.1,
        'hyena_freq1': 1.0 + 0.1 * nrm(ks[16], (DEPTH, HYENA_FFN), f32),
        'hyena_w2': nrm(ks[17], (DEPTH, HYENA_FFN, HYENA_FFN), f32) * HYENA_FFN ** -0.5,
        'hyena_b2': nrm(ks[18], (DEPTH, HYENA_FFN), f32) * 0.1,
        'hyena_freq2': 1.0 + 0.1 * nrm(ks[19], (DEPTH, HYENA_FFN), f32),
        'hyena_w3': nrm(ks[20], (DEPTH, HYENA_FFN, HYENA_ORDER * 2 * B_WIDTH), f32) * HYENA_FFN ** -0.5,
        'hyena_bias': nrm(ks[21], (DEPTH, HYENA_ORDER, B_WIDTH), f32),
        'w_proj_a': nrm(ks[22], (DEPTH, A_VW, d), f32) * A_VW ** -0.5,
        'w_proj_b': nrm(ks[23], (DEPTH, B_WIDTH, d), f32) * B_WIDTH ** -0.5,
        'w_out': nrm(ks[24], (DEPTH, d, d), f32) * d ** -0.5,
        'final_norm_w': 1.0 + 0.1 * nrm(ks[25], (d,), f32),
    }


def reference(x, c, ctx, c_ctx, mod_w, mod_b, ffn_w_gate, ffn_w_up, ffn_w_down, w_in, hgrn_lb_logits,
              hgrn_norm_w, hyena_conv_w, hyena_conv_b, hyena_w1, hyena_b1, hyena_freq1, hyena_w2, hyena_b2,
              hyena_freq2, hyena_w3, hyena_bias, w_proj_a, w_proj_b, w_out, final_norm_w):
    n_lat = x.shape[1]
    n_ctx = ctx.shape[1]
    h = x + grid_pos_embed(n_lat).astype(x.dtype)[None]
    hc = ctx
    lb_all = jnp.cumsum(jax.nn.softmax(hgrn_lb_logits.astype(jnp.float32), axis=0), axis=0)
    for l in range(DEPTH):
        last = l == DEPTH - 1
        m = jnp.split((jax.nn.silu(c) @ mod_w[l] + mod_b[l])[:, None, :], N_MOD, axis=-1)
        mc = jnp.split((jax.nn.silu(c_ctx) @ mod_w[l] + mod_b[l])[None, None, :], N_MOD, axis=-1)
        h = half_ffn(h, m[0], m[1], m[2], ffn_w_gate[l, 0], ffn_w_up[l, 0], ffn_w_down[l, 0])
        hc = half_ffn(hc, mc[0], mc[1], mc[2], ffn_w_gate[l, 0], ffn_w_up[l, 0], ffn_w_down[l, 0])
        n = modulate(rms_norm(h), m[3], m[4])
        nc = modulate(rms_norm(hc), mc[3], mc[4])
        mix_w = (w_in[l], lb_all[l], hgrn_norm_w[l], hyena_conv_w[l], hyena_conv_b[l], hyena_bias[l],
                 w_proj_a[l], w_proj_b[l], w_out[l])
        filt_w = (hyena_w1[l], hyena_b1[l], hyena_freq1[l], hyena_w2[l], hyena_b2[l], hyena_freq2[l], hyena_w3[l])
        if last:
            s_fw, s_bw = context_states(nc, w_in[l], lb_all[l])
        else:
            zeros = jnp.zeros((hc.shape[0], A_HEADS, A_DK, A_DV), jnp.float32)
            out_c, s_fw, s_bw = token_mixers(nc, zeros, zeros, hyena_kernels(n_ctx, *filt_w), *mix_w)
            hc = hc + mc[5] * out_c
            hc = half_ffn(hc, mc[6], mc[7], mc[8], ffn_w_gate[l, 1], ffn_w_up[l, 1], ffn_w_down[l, 1])
        out, _, _ = token_mixers(n, s_fw, s_bw, hyena_kernels(n_lat, *filt_w), *mix_w)
        h = h + m[5] * out
        h = half_ffn(h, m[6], m[7], m[8], ffn_w_gate[l, 1], ffn_w_up[l, 1], ffn_w_down[l, 1])
    return rms_norm(h) * final_norm_w
```

```python
import numpy as np
from contextlib import ExitStack
import concourse.bass as bass
import concourse.mybir as mybir

F32 = mybir.dt.float32
BF16 = mybir.dt.bfloat16
AF = mybir.ActivationFunctionType
ALU = mybir.AluOpType

ENGS = ("pe", "act", "dve", "pool", "sp")
EPOCH = 12000
SAME_ENGINE_SYNC = True


class Tok:
    __slots__ = ("name", "w", "w_eng", "r", "dsem", "dcount")

    def __init__(self, name):
        self.name = name
        self.w = None
        self.w_eng = None
        self.r = []
        self.dsem = None
        self.dcount = 0


class Sched:
    def __init__(self, nc, stack):
        self.nc = nc
        self.stack = stack
        self.ops = {e: [] for e in ENGS}
        self.cnt = {e: 0 for e in ENGS}
        self.sem = {e: None for e in ENGS}
        self.nsem = 0
        self.waited = {e: {} for e in ENGS}
        self.latest = {}
        self.n_ops = 0
        self.dpool = []
        self.dtoks = []

    def new_sem(self, name):
        self.nsem += 1
        return self.stack.enter_context(self.nc.semaphore(f"{name}_{self.nsem}"))

    def tok(self, name="t"):
        return Tok(name)

    def toks(self, n, name="t"):
        return [Tok(f"{name}{i}") for i in range(n)]

    def _next_event(self, eng):
        if self.sem[eng] is None or self.cnt[eng] >= EPOCH:
            self.sem[eng] = self.new_sem(f"s_{eng}")
            self.cnt[eng] = 0
        self.cnt[eng] += 1
        return (self.sem[eng], self.cnt[eng])

    def _need(self, eng, waits, ev):
        if ev is None:
            return
        sem, val = ev[0], ev[1]
        k = id(sem)
        if self.waited[eng].get(k, 0) >= val:
            return
        cur = waits.get(k)
        if cur is None or cur[1] < val:
            waits[k] = (sem, val)

    def _collect(self, eng, reads, writes, is_dma):
        waits = {}
        for t in reads:
            if t.w is not None:
                if t.w_eng == eng and not is_dma:
                    if eng != "pe" and SAME_ENGINE_SYNC:
                        self._need(eng, waits, t.w)
                else:
                    self._need(eng, waits, t.w)
        for t in writes:
            if t.w is not None:
                if t.w_eng == eng and not is_dma:
                    if eng != "pe" and SAME_ENGINE_SYNC:
                        self._need(eng, waits, t.w)
                elif is_dma and t.w_eng == "dma":
                    pass
                else:
                    self._need(eng, waits, t.w)
            for (sem, val, reng) in t.r:
                if reng == eng and not is_dma:
                    continue
                self._need(eng, waits, (sem, val))
        wl = list(waits.values())
        for (sem, val) in wl:
            self.waited[eng][id(sem)] = val
        return wl

    def op(self, eng, fn, reads=(), writes=()):
        wl = self._collect(eng, reads, writes, False)
        ev = self._next_event(eng)
        self.ops[eng].append((wl, fn, ev[0], 1))
        self.waited[eng][id(ev[0])] = max(self.waited[eng].get(id(ev[0]), 0), 0)
        self.latest[id(ev[0])] = ev
        for t in writes:
            t.w = ev
            t.w_eng = eng
            t.r = []
        for t in reads:
            if t in writes:
                continue
            t.r = [x for x in t.r if x[2] != eng] + [(ev[0], ev[1], eng)]
        self.n_ops += 1
        return ev

    def dma(self, queue, out, in_, reads=(), writes=(), evtok=None, **kw):
        if evtok is None:
            evtok = writes[0] if len(writes) else reads[0]
        wl = self._collect(queue, reads, writes, True)
        if evtok.dsem is None:
            if self.dpool:
                evtok.dsem, evtok.dcount = self.dpool.pop()
            else:
                evtok.dsem = self.new_sem("d")
                evtok.dcount = 0
            self.dtoks.append(evtok)
        assert evtok.dcount < 60000
        evtok.dcount += 16
        ev = (evtok.dsem, evtok.dcount)
        self.latest[id(ev[0])] = ev

        def fn(e, out=out, in_=in_, kw=kw):
            return e.dma_start(out=out, in_=in_, **kw)
        self.ops[queue].append((wl, fn, ev[0], 16))
        for t in writes:
            t.w = ev
            t.w_eng = "dma"
            t.r = []
        for t in reads:
            t.r = [x for x in t.r if x[0] is not ev[0]] + [(ev[0], ev[1], "dma")]
        self.n_ops += 1
        return ev

    def barrier(self, engines=ENGS, exclude_engs=(), exclude_toks=()):
        skip = set()
        for e in exclude_engs:
            if self.sem[e] is not None:
                skip.add(id(self.sem[e]))
        for t in exclude_toks:
            if t.dsem is not None:
                skip.add(id(t.dsem))
        engines = tuple(e for e in engines if e not in exclude_engs)
        evs = [v for k, v in self.latest.items() if k not in skip]
        for e in engines:
            wl = []
            for (sem, val) in evs:
                if self.waited[e].get(id(sem), 0) >= val:
                    continue
                if sem is self.sem[e]:
                    continue
                wl.append((sem, val))
                self.waited[e][id(sem)] = val
            if wl:
                self.ops[e].append((wl, None, None, 0))
        if tuple(engines) == tuple(ENGS):
            for t in self.dtoks:
                if t.dcount < 40000:
                    self.dpool.append((t.dsem, t.dcount))
                t.dsem = None
                t.dcount = 0
            self.dtoks = []

    def emit(self):
        nc = self.nc
        with nc.Block() as block:
            def mk(engname):
                def body(e):
                    for (wl, fn, sem, inc) in self.ops[engname]:
                        for (s, v) in wl:
                            e.wait_ge(s, v)
                        if fn is not None:
                            ins = fn(e)
                            ins.then_inc(sem, inc)
                return body
            block.tensor(mk("pe"))
            block.scalar(mk("act"))
            block.vector(mk("dve"))
            block.gpsimd(mk("pool"))
            block.sync(mk("sp"))


class Arena:
    def __init__(self, nc, stack, words, name="arena"):
        self.t = stack.enter_context(nc.sbuf_tensor(name, [128, words], F32))
        self.words = words
        self.top = 0
        self.marks = []
        self.hi = words
        self.his = []

    def mark(self):
        self.marks.append(self.top)

    def release(self):
        self.top = self.marks.pop()

    def alloc_top(self, shape, dtype):
        n = int(np.prod(shape))
        w = n if dtype == F32 else (n + 1) // 2
        w = (w + 7) // 8 * 8
        self.his.append(self.hi)
        self.hi -= w
        assert self.hi >= self.top
        save = self.top
        self.top = self.hi
        hi_save = self.hi
        self.hi = self.words + 10 ** 9
        ap = self.alloc(shape, dtype)
        self.top = save
        self.hi = hi_save
        return ap

    def release_top(self):
        self.hi = self.his.pop()

    def alloc(self, shape, dtype):
        n = int(np.prod(shape))
        if dtype == F32:
            w = n
        elif dtype == BF16:
            w = (n + 1) // 2
        else:
            raise ValueError(dtype)
        w = (w + 7) // 8 * 8
        if self.top + w > min(self.words, self.hi):
            raise MemoryError(f"arena overflow: need {w} at {self.top} of {self.words}")
        ap = self.t[:, self.top:self.top + w]
        self.top += w
        if dtype == BF16:
            ap = ap.bitcast(BF16)[:, 0:n]
        else:
            ap = ap[:, 0:n]
        if len(shape) == 2:
            ap = ap.rearrange("p (a b) -> p a b", b=shape[1])
        elif len(shape) == 3:
            ap = ap.rearrange("p (a b c) -> p a b c", b=shape[1], c=shape[2])
        return ap


import math
import ml_dtypes
from concourse.bass_utils import run_bass_kernel_spmd

D = 1024
L = 2048
LC = 256
T = L + LC
DFF = 2816
NF = DFF // 128
NCORE = 8
PI = math.pi


def _wrap(S):
    def ACT(out, in_, func, reads, writes, bias=None, scale=None):
        kw = {}
        if bias is not None:
            kw["bias"] = bias
        if scale is not None:
            kw["scale"] = scale
        return S.op("act", lambda e: e.activation(out=out, in_=in_, func=func, **kw), reads, writes)

    def TT(eng, out, in0, in1, op, reads, writes):
        return S.op(eng, lambda e: e.tensor_tensor(out=out, in0=in0, in1=in1, op=op), reads, writes)

    def TS(eng, out, in0, s1, s2, op0, op1, reads, writes):
        if op1 is None:
            return S.op(eng, lambda e: e.tensor_scalar(out=out, in0=in0, scalar1=s1, scalar2=None, op0=op0), reads, writes)
        return S.op(eng, lambda e: e.tensor_scalar(out=out, in0=in0, scalar1=s1, scalar2=s2, op0=op0, op1=op1), reads, writes)

    def STT(out, in0, scalar, in1, op0, op1, reads, writes):
        return S.op("dve", lambda e: e.scalar_tensor_tensor(out=out, in0=in0, scalar=scalar, in1=in1, op0=op0, op1=op1), reads, writes)

    def MM(out, lhsT, rhs, start, stop, reads, writes):
        return S.op("pe", lambda e: e.matmul(out, lhsT=lhsT, rhs=rhs, start=start, stop=stop), reads, writes)

    def TR(out, in_, ident, reads, writes):
        return S.op("pe", lambda e: e.transpose(out, in_, ident), reads, writes)

    def CP(eng, out, in_, reads, writes):
        if eng == "act":
            return S.op("act", lambda e: e.activation(out=out, in_=in_, func=AF.Copy), reads, writes)
        return S.op(eng, lambda e: e.tensor_copy(out=out, in_=in_), reads, writes)

    def MS(eng, ap, val, writes):
        return S.op(eng, lambda e: e.memset(ap, val), (), writes)
    return ACT, TT, TS, STT, MM, TR, CP, MS


def build_program(nb=2, stop_after=None, dbg=()):
    nc = bass.Bass("TRN2", target_bir_lowering=False)
    din = {}

    def inp(name, shape, dt=F32):
        din[name] = nc.dram_tensor(name, list(shape), dt, kind="ExternalInput").ap()
        return din[name]

    x_t = inp("x_t", [nb, D, L])
    ctx_t = inp("ctx_t", [nb, D, LC])
    pos_t = inp("pos_t", [D, L])
    c_t = inp("c_t", [128, 8, 4])
    mod_w = inp("mod_w", [D, 9 * D])
    mod_b = inp("mod_b", [128, 72])
    wg = inp("wg", [2, D, DFF])
    wu = inp("wu", [2, D, DFF])
    wd = inp("wd", [2, DFF, D])
    w_in = inp("w_in", [D, 6144])
    lbl = inp("lbl", [128, 2, 8])
    normw = inp("normw", [128, 1])
    convw = inp("convw", [128, 3, 12])
    convb = inp("convb", [128, 12])
    hw1 = inp("hw1", [33, 64])
    hb1 = inp("hb1", [64, 1])
    hf1 = inp("hf1", [64, 1])
    hw2 = inp("hw2", [64, 64])
    hb2 = inp("hb2", [64, 1])
    hf2 = inp("hf2", [64, 1])
    hw3 = inp("hw3", [64, 2048])
    hbias = inp("hbias", [128, 2, 512])
    wpa = inp("wpa", [512, D])
    wpb = inp("wpb", [512, D])
    wout = inp("wout", [D, D])
    fnw = inp("fnw", [128, 8])
    zfeat = inp("zfeat", [33, L])
    win = inp("win", [128, 16, 512])
    wins = inp("wins", [128, 16, 512])
    winl = inp("winl", [1, 512])
    Fm = inp("Fm", [32, 128, 16, 128], BF16)
    Fi = inp("Fi", [16, 128, 32, 128], BF16)
    ident_d = inp("ident", [128, 128], BF16)
    masks_d = inp("masks", [64, 2, 64])

    out_t = nc.dram_tensor("out_t", [nb, D, L], F32, kind="ExternalOutput").ap()
    dbg_out = {}

    def scr(name, shape, dt):
        return nc.dram_tensor(name, list(shape), dt, kind="Internal").ap()

    wgb = scr("wgb", [2, 128, NF, 8, 128], BF16)
    wub = scr("wub", [2, 128, NF, 8, 128], BF16)
    wdb = scr("wdb", [2, 128, 8, NF, 128], BF16)
    winb = scr("winb", [128, 48, 8, 128], BF16)
    wpab = scr("wpab", [128, 8, 4, 128], BF16)
    wpbb = scr("wpbb", [128, 8, 4, 128], BF16)
    woutb = scr("woutb", [128, 8, 8, 128], BF16)
    Ksp = scr("Ksp", [2, 2, 16, 128, 512], F32)
    hs = scr("hs", [nb, 128, 8, L], F32)
    oas = scr("oas", [128, 4, L], BF16)

    with ExitStack() as st:
        S = Sched(nc, st)
        ACT, TT, TS, STT, MM, TR, CP, MS = _wrap(S)
        A = Arena(nc, st, 48000)
        pbk = [st.enter_context(nc.psum_tensor(f"pb{i}", [128, 512], F32)) for i in range(8)]
        pb = [p[:] for p in pbk]
        pbt = S.toks(8, "pb")
        pbb = [p[:].bitcast(BF16) for p in pbk]
        q_sp = "sp"

        def dump(name, ap, shape, tok, dt=F32):
            if name not in dbg:
                return
            d = nc.dram_tensor("dbg_" + name, list(shape), dt, kind="ExternalOutput").ap()
            dbg_out[name] = d
            S.dma(q_sp, d, ap, reads=[tok], evtok=tok)

        ident = A.alloc([128], BF16)
        ones_d = A.alloc([128], BF16)
        ones_v = A.alloc([128], BF16)
        ones_f = A.alloc([128], F32)
        modT = A.alloc([72, 4], F32)
        lbT = A.alloc([8], F32)
        omlT = A.alloc([8], F32)
        normw_s = A.alloc([1], F32)
        convw_s = A.alloc([3, 12], F32)
        convb_s = A.alloc([12], F32)
        fnw_s = A.alloc([8], F32)
        masks_s = A.alloc([2, 64], F32)
        epsb = A.alloc([1], F32)
        tconst = S.tok("const")
        S.dma(q_sp, ident, ident_d, writes=[tconst])
        S.dma(q_sp, normw_s, normw, writes=[tconst])
        S.dma(q_sp, convw_s, convw, writes=[tconst])
        S.dma(q_sp, convb_s, convb, writes=[tconst])
        S.dma(q_sp, fnw_s, fnw, writes=[tconst])
        S.dma(q_sp, masks_s[0:64], masks_d, writes=[tconst])
        tc2 = S.tok("const2")
        MS("pool", ones_d, 1.0 / 1024.0, [tc2])
        MS("pool", ones_v, 1.0 / 128.0, [tc2])
        MS("pool", ones_f, 1.0, [tc2])
        MS("pool", epsb, 1e-6, [tc2])
        CONST = [tconst, tc2]

        def conv_units():
            NSL = 3
            sfA = [A.alloc_top([8, 512], F32) for _ in range(NSL)]
            sbA = [A.alloc_top([4, 8, 128], BF16) for _ in range(NSL)]
            tf = S.toks(NSL, "cvf")
            tb = S.toks(NSL, "cvb")
            cvtoks.extend(tf + tb)
            it = 0
            ce = 0
            for s_ in range(2):
                for (src, dst) in ((wg[s_], wgb[s_]), (wu[s_], wub[s_])):
                    for g0 in range(0, NF, 4):
                        g = min(4, NF - g0)
                        sl = it % NSL
                        it += 1
                        S.dma("sp", sfA[sl][:, :, 0:g * 128],
                              src[:, g0 * 128:(g0 + g) * 128].rearrange("(k p) n -> p k n", p=128), writes=[tf[sl]])
                        for gi in range(g):
                            eng = "pool"
                            ce += 1
                            CP(eng, sbA[sl][:, gi, :, :], sfA[sl][:, :, gi * 128:(gi + 1) * 128], [tf[sl]], [tb[sl]])
                        S.dma("sp", dst[:, g0:g0 + g], sbA[sl][:, 0:g], reads=[tb[sl]], evtok=tb[sl])
                        yield
                for dc in range(8):
                    sl = it % NSL
                    it += 1
                    sfv = sfA[sl].rearrange("p a b -> p (a b)")[:, 0:NF * 128].rearrange("p (f c) -> p f c", c=128)
                    sbv = sbA[sl].rearrange("p a b c -> p (a b c)")[:, 0:NF * 128].rearrange("p (f c) -> p f c", c=128)
                    S.dma("sp", sfv, wd[s_][:, dc * 128:(dc + 1) * 128].rearrange("(f p) n -> p f n", p=128), writes=[tf[sl]])
                    for hf_ in range(2):
                        eng = "pool"
                        ce += 1
                        CP(eng, sbv[:, hf_ * 11:(hf_ + 1) * 11, :], sfv[:, hf_ * 11:(hf_ + 1) * 11, :], [tf[sl]], [tb[sl]])
                    S.dma("sp", wdb[s_][:, dc], sbv, reads=[tb[sl]], evtok=tb[sl])
                    yield

        cvtoks = []
        cgen = conv_units()
        for _ in cgen:
            pass

        def pbarrier():
            S.barrier(exclude_engs=("pool", "sp"), exclude_toks=cvtoks)

        def pump(n=1):
            for _ in range(n):
                if next(cgen, "done") == "done":
                    return

        A.mark()
        cts = A.alloc([8, 4], F32)
        scs = A.alloc([8, 4], F32)
        lbs = A.alloc([2, 8], F32)
        mbs = A.alloc([72], F32)
        mwb = [A.alloc([8, 1024], F32) for _ in range(2)]
        mwt = S.toks(2, "mw")
        tct, tsc, tlb, tmod = S.toks(4, "p0a")
        S.dma("act", cts, c_t, writes=[tct])
        S.dma("act", lbs, lbl, writes=[tlb])
        S.dma("act", mbs, mod_b, writes=[tlb])
        ACT(scs, cts, AF.Silu, [tct], [tsc])
        TT("dve", lbs[:, 0, :], lbs[:, 0, :], lbs[:, 1, :], ALU.subtract, [tlb], [tlb])
        ACT(lbT, lbs[:, 0, :], AF.Sigmoid, [tlb], [tmod])
        TS("dve", omlT, lbT, -1.0, 1.0, ALU.mult, ALU.add, [tmod], [tmod])
        for j in range(9):
            sl = j % 2
            S.dma("act", mwb[sl], mod_w[:, j * 1024:(j + 1) * 1024].rearrange("(kc p) n -> p kc n", p=128),
                  writes=[mwt[sl]])
            pump(2)
            for dc in range(8):
                o0 = (j * 8 + dc) * 4
                for kc in range(8):
                    MM(pb[0][:, o0:o0 + 4], mwb[sl][:, kc, dc * 128:(dc + 1) * 128], scs[:, kc, :],
                       kc == 0, kc == 7, [mwt[sl], tsc], [pbt[0]])
        psm = pb[0][:, 0:288].rearrange("p (a b) -> p a b", b=4)
        for col in range(4):
            TT("dve", modT[:, :, col], psm[:, :, col], mbs, ALU.add, [pbt[0], tlb], [tmod])
        for j in (1, 4, 7):
            TS("dve", modT[:, j * 8:(j + 1) * 8, :], modT[:, j * 8:(j + 1) * 8, :], 1.0, None, ALU.add, None, [tmod], [tmod])
        for j in (2, 8):
            TS("dve", modT[:, j * 8:(j + 1) * 8, :], modT[:, j * 8:(j + 1) * 8, :], 0.5, None, ALU.mult, None, [tmod], [tmod])
        dump("modT", modT, [128, 72, 4], tmod)
        pbarrier()
        A.release()
        CONST.append(tmod)

        def mv(j, dc, col):
            return modT[:, j * 8 + dc, col:col + 1]

        A.mark()
        w3s = A.alloc([2048], F32)
        hsm = A.alloc([8], F32)
        h2p = A.alloc([L + 8], F32)
        winl_s = A.alloc([512], F32)
        rn = A.alloc([2, 512], F32)
        hbias_s = A.alloc([2, 512], F32)
        A.mark()
        zf = A.alloc([L], F32)
        w1s = A.alloc([64], F32)
        w2s = A.alloc([64], F32)
        h1 = A.alloc([L], F32)
        arg = A.alloc([512], F32)
        wtmp = A.alloc([512], F32)
        tk0, th1, th2, targ, theo, trn = S.toks(6, "p0c")
        thf = S.toks(2, "hf"); thb = S.toks(2, "hb"); tab = S.toks(2, "ab"); twn = S.toks(2, "wn")
        S.dma("act", zf[0:33], zfeat, writes=[tk0])
        S.dma("act", w1s[0:33], hw1, writes=[tk0])
        S.dma("act", w2s[0:64], hw2, writes=[tk0])
        S.dma("act", w3s[0:64], hw3, writes=[tk0])
        S.dma("act", hsm[0:64, 0:1], hb1, writes=[tk0])
        S.dma("act", hsm[0:64, 1:2], hf1, writes=[tk0])
        S.dma("act", hsm[0:64, 2:3], hb2, writes=[tk0])
        S.dma("act", hsm[0:64, 3:4], hf2, writes=[tk0])
        S.dma("act", winl_s[0:1], winl, writes=[tk0])
        S.dma("act", hbias_s, hbias, writes=[tk0])
        TT("dve", hsm[0:64, 4:5], hsm[0:64, 0:1], hsm[0:64, 1:2], ALU.mult, [tk0], [tk0])
        TT("dve", hsm[0:64, 5:6], hsm[0:64, 2:3], hsm[0:64, 3:4], ALU.mult, [tk0], [tk0])
        MS("dve", h2p[0:64, 0:1], 0.0, [th2])

        def sin_layer(wsb, kdim, src, dst, dst_off, fcol, fbcol, tsrc, tdst):
            for ti in range(4):
                MM(pb[1][0:64, :], wsb[0:kdim, 0:64], src[0:kdim, ti * 512:(ti + 1) * 512], True, True,
                   [tk0, tsrc], [pbt[1]])
                TS("dve", arg[0:64], pb[1][0:64, :], hsm[0:64, fcol:fcol + 1], hsm[0:64, fbcol:fbcol + 1],
                   ALU.mult, ALU.add, [pbt[1], tk0], [targ])
                for _ in range(2):
                    wrap_once(arg[0:64], targ)
                TS("dve", arg[0:64], arg[0:64], 3.14159, -3.14159, ALU.min, ALU.max, [targ], [targ])
                ACT(dst[0:64, dst_off + ti * 512: dst_off + (ti + 1) * 512], arg[0:64], AF.Sin, [targ], [tdst])

        twt = S.tok("wtmp")

        def wrap_once(ap, tok):
            TS("dve", wtmp[0:64], ap, PI, -2.0 * PI, ALU.is_gt, ALU.mult, [tok], [twt])
            TT("dve", ap, ap, wtmp[0:64], ALU.add, [tok, twt], [tok])
            TS("dve", wtmp[0:64], ap, -PI, 2.0 * PI, ALU.is_lt, ALU.mult, [tok], [twt])
            TT("dve", ap, ap, wtmp[0:64], ALU.add, [tok, twt], [tok])

        sin_layer(w1s, 33, zf, h1, 0, 1, 4, tk0, th1)
        sin_layer(w2s, 64, h1, h2p, 1, 3, 5, th1, th2)
        dump("h2", h2p[0:64, 1:L + 1], [64, L], th2)
        pbarrier()
        A.release()
        heo = A.alloc([16, 2, 512], BF16)
        hfb = [A.alloc([512], F32) for _ in range(2)]
        hbb = [A.alloc([512], F32) for _ in range(2)]
        absb = [A.alloc([512], F32) for _ in range(2)]
        winb_s = [A.alloc([512], F32) for _ in range(2)]
        winsb_s = [A.alloc([512], F32) for _ in range(2)]

        fmb = [A.alloc([2, 16, 128], BF16) for _ in range(3)]
        tfm = S.toks(3, "fm")
        kst = [A.alloc([2, 512], F32) for _ in range(2)]
        tks = S.toks(2, "kst")
        it = 0
        jj = 0
        for o in range(2):
            for lt in range(16):
                sl = lt % 2
                pump(1)
                S.dma("act", winb_s[sl], win[:, lt, :], writes=[twn[sl]])
                S.dma("act", winsb_s[sl], wins[:, lt, :], writes=[twn[sl]])
                MM(pb[2], h2p[0:64, 1 + lt * 128: 1 + (lt + 1) * 128], w3s[0:64, o * 1024: o * 1024 + 512], True, True,
                   [th2, tk0], [pbt[2]])
                MM(pb[3], h2p[0:64, lt * 128:(lt + 1) * 128], w3s[0:64, o * 1024 + 512: o * 1024 + 1024], True, True,
                   [th2, tk0], [pbt[3]])
                TT("dve", hfb[sl], pb[2], winb_s[sl], ALU.mult, [pbt[2], twn[sl]], [thf[sl]])
                TT("dve", hbb[sl], pb[3], winsb_s[sl], ALU.mult, [pbt[3], twn[sl]], [thb[sl]])
                ACT(absb[0], hfb[sl], AF.Abs, [thf[sl]], [tab[0]])
                MM(pb[4 + o], ones_f, absb[0], lt == 0, False, [tab[0], tc2], [pbt[4 + o]])
                ACT(absb[1], hbb[sl], AF.Abs, [thb[sl]], [tab[1]])
                MM(pb[4 + o], ones_f, absb[1], False, False, [tab[1], tc2], [pbt[4 + o]])
                TT("dve", heo[:, lt, 0, :], hfb[sl], hbb[sl], ALU.add, [thf[sl], thb[sl]], [theo])
                TT("dve", heo[:, lt, 1, :], hfb[sl], hbb[sl], ALU.subtract, [thf[sl], thb[sl]], [theo])
            MM(pb[2][0:1, :], h2p[0:64, L:L + 1], w3s[0:64, o * 1024 + 512: o * 1024 + 1024], True, True,
               [th2, tk0], [pbt[2]])
            TT("dve", hfb[0][0:1], pb[2][0:1, :], winl_s[0:1], ALU.mult, [pbt[2], tk0], [thf[0]])
            ACT(absb[0][0:1], hfb[0][0:1], AF.Abs, [thf[0]], [tab[0]])
            MM(pb[4 + o], ones_f[0:1, :], absb[0][0:1], False, True, [tab[0], tc2], [pbt[4 + o]])
            TS("dve", rn[:, o, :], pb[4 + o], 1e-6, None, ALU.add, None, [pbt[4 + o]], [trn])
            S.op("dve", lambda e, o=o: e.reciprocal(out=rn[:, o, :], in_=rn[:, o, :]), [trn], [trn])
            for j in range(16):
                sl = jj % 3
                jj += 1
                pump(1)
                S.dma("act", fmb[sl][:, 0], Fm[j], writes=[tfm[sl]])
                S.dma("act", fmb[sl][:, 1], Fm[16 + j], writes=[tfm[sl]])
                ks = it % 2
                it += 1
                for lc in range(16):
                    MM(pb[6], fmb[sl][:, 0, lc, :], heo[:, lc, 0, :], lc == 0, lc == 15, [tfm[sl], theo], [pbt[6]])
                for lc in range(16):
                    MM(pb[7], fmb[sl][:, 1, lc, :], heo[:, lc, 1, :], lc == 0, lc == 15, [tfm[sl], theo], [pbt[7]])
                TT("dve", kst[ks][:, 0, :], pb[6], rn[:, o, :], ALU.mult, [pbt[6], trn], [tks[ks]])
                TT("dve", kst[ks][:, 0, :], kst[ks][:, 0, :], hbias_s[:, o, :], ALU.add, [tks[ks], tk0], [tks[ks]])
                TT("dve", kst[ks][:, 1, :], pb[7], rn[:, o, :], ALU.mult, [pbt[7], trn], [tks[ks]])
                S.dma("act", Ksp[o, 0, j], kst[ks][:, 0, :], reads=[tks[ks]], evtok=tks[ks])
                S.dma("act", Ksp[o, 1, j], kst[ks][:, 1, :], reads=[tks[ks]], evtok=tks[ks])
        dump("rn", rn, [128, 2, 512], trn)
        pump(1000)
        S.barrier()
        A.release()
        for _ in range(6):
            A.release_top()
        if stop_after == "p0c":
            S.emit()
            return nc, din, dbg_out

        def rstd_of(hb, off, n, sq, tsq, rst, trst, hbtok, ones_ap, nchunks=8):
            for dc in range(nchunks):
                ACT(sq[:, dc, 0:n], hb[:, dc, off:off + n], AF.Square, [hbtok], [tsq])
            for dc in range(nchunks):
                MM(pb[7][:, 0:n], ones_ap, sq[:, dc, 0:n], dc == 0, dc == nchunks - 1, [tsq, tc2], [pbt[7]])
            ACT(rst[:, 0:n], pb[7][:, 0:n], AF.Sqrt, [pbt[7]], [trst], bias=epsb[:, 0:1])
            S.op("dve", lambda e: e.reciprocal(out=rst[:, 0:n], in_=rst[:, 0:n]), [trst], [trst])

        def rstd_multi(hb, tl2, sqs, tsqs, rsts, trsts, hbtok, ones_ap):
            assert len(tl2) <= 2
            for i, (off, n) in enumerate(tl2):
                for dc in range(8):
                    ACT(sqs[i][:, dc, 0:n], hb[:, dc, off:off + n], AF.Square, [hbtok], [tsqs[i]])
            for i, (off, n) in enumerate(tl2):
                for dc in range(8):
                    MM(pb[7 - i][:, 0:n], ones_ap, sqs[i][:, dc, 0:n], dc == 0, dc == 7, [tsqs[i], tc2], [pbt[7 - i]])
            for i, (off, n) in enumerate(tl2):
                ACT(rsts[i][:, 0:n], pb[7 - i][:, 0:n], AF.Sqrt, [pbt[7 - i]], [trsts[i]], bias=epsb[:, 0:1])
            for i, (off, n) in enumerate(tl2):
                S.op("dve", lambda e, i=i, n=n: e.reciprocal(out=rsts[i][:, 0:n], in_=rsts[i][:, 0:n]),
                     [trsts[i]], [trsts[i]])

        def ffn_block(s, hb, hbtok, tiles, j0, W):
            A.mark()
            nbk = A.alloc([8, W], BF16)
            act = A.alloc([NF, W], BF16)
            actf = act.rearrange("p f w -> p (f w)")
            sqs = [actf[:, i * 4096:(i + 1) * 4096].rearrange("p (a b) -> p a b", b=512) for i in range(2)]
            rsts = [actf[:, 8192 + i * 1024: 8192 + (i + 1) * 1024].bitcast(F32) for i in range(2)]
            tmp = [A.alloc([512], F32) for _ in range(2)]
            sg = [A.alloc([512], BF16) for _ in range(2)]
            NSA, NSB = 4, 3
            wgs = [A.alloc([8, 128], BF16) for _ in range(NSA)]
            wus = [A.alloc([8, 128], BF16) for _ in range(NSA)]
            wds = [A.alloc([NF, 128], BF16) for _ in range(NSB)]
            tnb = S.toks(len(tiles), "nb")
            tact = S.toks(len(tiles), "act")
            tsqs = S.toks(2, "nrmq"); trsts = S.toks(2, "nrmr")
            ttmp = S.toks(2, "tmp"); tsg = S.toks(2, "sg")
            twg = S.toks(NSA, "wg"); twd = S.toks(NSB, "wd")
            k = 0
            assert len(tiles) == 2
            rstd_multi(hb, [(off, n) for (off, n, col) in tiles], sqs, tsqs, rsts, trsts, hbtok, ones_d)
            for ti, (off, n, col) in enumerate(tiles):
                rst, trst = rsts[ti], trsts[ti]
                for dc in range(8):
                    sl = k % 2
                    k += 1
                    TT("dve", tmp[sl][:, 0:n], hb[:, dc, off:off + n], rst[:, 0:n], ALU.mult, [hbtok, trst], [ttmp[sl]])
                    ACT(nbk[:, dc, off:off + n], tmp[sl][:, 0:n], AF.Identity, [ttmp[sl], tmod], [tnb[ti]],
                        bias=mv(j0, dc, col), scale=mv(j0 + 1, dc, col))
            k = 0
            for f in range(NF):
                sl = f % NSA
                S.dma(q_sp, wgs[sl], wgb[s, :, f], writes=[twg[sl]])
                S.dma(q_sp, wus[sl], wub[s, :, f], writes=[twg[sl]])
                for ti, (off, n, col) in enumerate(tiles):
                    pg = (2 * k) % 4
                    pu = pg + 1
                    ss = k % 2
                    k += 1
                    for kc in range(8):
                        MM(pb[pg][:, 0:n], wgs[sl][:, kc, :], nbk[:, kc, off:off + n], kc == 0, kc == 7,
                           [twg[sl], tnb[ti]], [pbt[pg]])
                    for kc in range(8):
                        MM(pb[pu][:, 0:n], wus[sl][:, kc, :], nbk[:, kc, off:off + n], kc == 0, kc == 7,
                           [twg[sl], tnb[ti]], [pbt[pu]])
                    ACT(sg[ss][:, 0:n], pb[pg][:, 0:n], AF.Silu, [pbt[pg]], [tsg[ss]])
                    TT("dve", act[:, f, off:off + n], sg[ss][:, 0:n], pb[pu][:, 0:n], ALU.mult,
                       [tsg[ss], pbt[pu]], [tact[ti]])
            k = 0
            for dc in range(8):
                sl = dc % NSB
                S.dma(q_sp, wds[sl], wdb[s, :, dc], writes=[twd[sl]])
                for ti, (off, n, col) in enumerate(tiles):
                    pp = 4 + (k % 2)
                    k += 1
                    for f in range(NF):
                        MM(pb[pp][:, 0:n], wds[sl][:, f, :], act[:, f, off:off + n], f == 0, f == NF - 1,
                           [twd[sl], tact[ti]], [pbt[pp]])
                    STT(hb[:, dc, off:off + n], pb[pp][:, 0:n], mv(j0 + 2, dc, col), hb[:, dc, off:off + n],
                        ALU.mult, ALU.add, [pbt[pp], tmod, hbtok], [hbtok])
            S.barrier()
            A.release()

        def load_w(dst, cg, tok, stg, tstg, src=None, nk=8):
            srcm = w_in if src is None else src
            S.dma(q_sp, stg[:, 0:nk, :], srcm[:, cg * 128:(cg + 1) * 128].rearrange("(kc p) n -> p kc n", p=128), writes=[tstg])
            CP("pool", dst, stg[:, 0:nk, :], [tstg], [tok])

        def proj_fm(wsb, wtok, tiles_, consume):
            for i, (off, n) in enumerate(tiles_):
                pi_ = 6 + (i % 2)
                for kc in range(8):
                    MM(pb[pi_][:, 0:n], wsb[:, kc, :], nT[:, kc, off:off + n], kc == 0, kc == 7, [wtok, tnT], [pbt[pi_]])
                consume(pi_, off, n)

        LT4 = [(0, 512), (512, 512), (1024, 512), (1536, 512)]
        LT5 = LT4 + [(2048, 256)]

        def P2(b):
            for h in range(4):
                A.mark()
                wv, wff, wfb, wq, wgt = [A.alloc([8, 128], BF16) for _ in range(5)]
                tw = S.toks(5, "hw")
                wstg = [A.alloc([8, 128], F32) for _ in range(2)]
                twstg = S.toks(2, "wstg")
                for wi, (wsb, cg, tk) in enumerate(((wv, h, tw[0]), (wq, 12 + h, tw[3]), (wgt, 16 + h, tw[4]), (wff, 4 + h, tw[1]), (wfb, 8 + h, tw[2]))):
                    load_w(wsb, cg, tk, wstg[wi % 2], twstg[wi % 2])
                vtok = A.alloc([36, 128], BF16)
                kk = A.alloc([T], F32)
                lfb = A.alloc([T], F32)
                Bb = A.alloc([T], F32)
                qf = A.alloc([L], F32)
                onesr = A.alloc([T], BF16)
                qt_ = [A.alloc([L], BF16) for _ in range(2)]
                kt_ = [A.alloc([T], BF16) for _ in range(2)]
                ktok = [A.alloc([36, 128], BF16) for _ in range(2)]
                o_ = [A.alloc([1, L], F32) for _ in range(2)]
                sgb = A.alloc([L], BF16)
                Sf = [A.alloc([128], F32) for _ in range(2)]
                Sb2 = [[A.alloc([128], BF16) for _ in range(2)] for _ in range(2)]
                tSb2 = [S.toks(2, "Sb2") for _ in range(2)]
                tmpS = [A.alloc([128], F32) for _ in range(2)]
                gcol = [A.alloc([36], F32) for _ in range(2)]
                bref = A.alloc([36], F32)
                scm = [A.alloc([64], BF16) for _ in range(2)]
                sq = A.alloc([1, 512], BF16)
                rst = A.alloc([512], F32)
                tmp = A.alloc([512], F32)
                oab = A.alloc([L], BF16)
                (tvt, tkk, tlf, tB, tq, tone, tsgb, tbref, tsq, trst, ttmp, toab) = S.toks(12, "p2")
                tqt = S.toks(2, "qt"); tkt = S.toks(2, "kt"); tktok = S.toks(2, "ktok"); to = S.toks(2, "o")
                tSf = S.toks(2, "Sf"); tSb = S.toks(2, "Sb"); ttS = S.toks(2, "tS"); tg = S.toks(2, "g"); tscm = S.toks(2, "scm")
                MS("pool", onesr, 1.0, [tone])
                for g0 in range(0, 36, 4):
                    pi_ = 6 + ((g0 // 4) % 2)
                    for ci in range(4):
                        c = g0 + ci
                        for kc in range(8):
                            MM(pb[pi_][0:64, ci * 128:(ci + 1) * 128], nT[:, kc, c * 64:(c + 1) * 64], wv[:, kc, :],
                               kc == 0, kc == 7, [tw[0], tnT], [pbt[pi_]])
                    CP("act", vtok[0:64, g0:g0 + 4, :], pb[pi_][0:64, :].rearrange("p (a b) -> p a b", b=128), [pbt[pi_]], [tvt])
                proj_fm(wq, tw[3], LT4, lambda pi_, off, n: ACT(qf[:, off:off + n], pb[pi_][:, 0:n], AF.Silu, [pbt[pi_]], [tq]))
                proj_fm(wgt, tw[4], LT4, lambda pi_, off, n: ACT(sgb[:, off:off + n], pb[pi_][:, 0:n], AF.Silu, [pbt[pi_]], [tsgb]))
                B3 = Bb.rearrange("p (c s) -> p c s", s=64)
                lf3 = lfb.rearrange("p (c s) -> p c s", s=64)
                for dr in range(2):
                    lbc = dr * 4 + h
                    wsb, wtk = (wff, tw[1]) if dr == 0 else (wfb, tw[2])
                    proj_fm(wsb, wtk, LT5, lambda pi_, off, n: ACT(kk[:, off:off + n], pb[pi_][:, 0:n], AF.Sigmoid,
                                                                    [pbt[pi_]], [tkk], scale=-1.0))
                    TS("dve", kk, kk, omlT[:, lbc:lbc + 1], None, ALU.mult, None, [tkk, tmod], [tkk])
                    ACT(lfb, kk, AF.Ln, [tkk], [tlf], bias=ones_f[:, 0:1], scale=-1.0)
                    S.op("dve", lambda e: e.tensor_tensor_scan(out=Bb, data0=onesr, data1=lfb, initial=0.0,
                                                                op0=ALU.mult, op1=ALU.add), [tone, tlf], [tB])
                    if dr == 0:
                        TT("dve", bref, B3[:, :, 0], lf3[:, :, 0], ALU.subtract, [tB, tlf], [tbref])
                        TT("dve", lf3, B3, bref.unsqueeze(2).broadcast_to([128, 36, 64]), ALU.subtract, [tB, tbref, tlf], [tlf])
                    else:
                        TT("dve", lf3, lf3, B3, ALU.subtract, [tB, tlf], [tlf])
                        TT("dve", lf3, lf3, B3[:, :, 63:64].broadcast_to([128, 36, 64]), ALU.add, [tB, tlf], [tlf])
                    ACT(Bb, lfb, AF.Exp, [tlf], [tB])
                    if dr == 0:
                        CP("dve", gcol[dr], B3[:, :, 63], [tB], [tg[dr]])
                    else:
                        CP("dve", gcol[dr], B3[:, :, 0], [tB], [tg[dr]])
                    TT("dve", qt_[dr], qf, Bb[:, 0:L], ALU.mult, [tq, tB], [tqt[dr]])
                    ACT(lfb, lfb, AF.Exp, [tlf], [tlf], scale=-1.0)
                    TT("dve", kt_[dr], kk, lfb, ALU.mult, [tkk, tlf], [tkt[dr]])
                    for g0 in range(0, 36, 4):
                        pi_ = 6 + ((g0 // 4) % 2)
                        for ci in range(4):
                            c = g0 + ci
                            TR(pbb[pi_][0:64, ci * 128:(ci + 1) * 128], kt_[dr][:, c * 64:(c + 1) * 64], ident,
                               [tkt[dr], tconst], [pbt[pi_]])
                        CP("act", ktok[dr][0:64, g0:g0 + 4, :], pbb[pi_][0:64, 0:512].rearrange("p (a b) -> p a b", b=128),
                           [pbt[pi_]], [tktok[dr]])
                    MS("pool", Sf[dr], 0.0, [tSf[dr]])
                    MS("pool", Sb2[dr][0], 0.0, [tSb2[dr][0]])
                orders = [[32, 33, 34, 35] + list(range(32)), [35, 34, 33, 32] + list(range(31, -1, -1))]
                for step in range(36):
                    par = step % 2
                    cc_ = [orders[dr][step] for dr in range(2)]
                    lat = cc_[0] < 32
                    if lat:
                        for dr in range(2):
                            c = cc_[dr]
                            MM(pb[dr][0:64, 0:64], kt_[dr][:, c * 64:(c + 1) * 64], qt_[dr][:, c * 64:(c + 1) * 64], True, True,
                               [tkt[dr], tqt[dr]], [pbt[dr]])
                    for dr in range(2):
                        c = cc_[dr]
                        MM(pb[4 + dr][:, 0:128], ktok[dr][0:64, c, :], vtok[0:64, c, :], True, True, [tktok[dr], tvt], [pbt[4 + dr]])
                    if lat:
                        for dr in range(2):
                            TT("dve", scm[dr][0:64], pb[dr][0:64, 0:64], masks_s[0:64, dr, :], ALU.mult,
                               [pbt[dr], tconst], [tscm[dr]])
                        for dr in range(2):
                            c = cc_[dr]
                            MM(pb[2 + dr][:, 0:64], vtok[0:64, c, :], scm[dr][0:64], True, False, [tvt, tscm[dr]], [pbt[2 + dr]])
                            MM(pb[2 + dr][:, 0:64], Sb2[dr][par], qt_[dr][:, c * 64:(c + 1) * 64], False, True,
                               [tSb2[dr][par], tqt[dr]], [pbt[2 + dr]])
                            CP("act", o_[dr][:, 0, c * 64:(c + 1) * 64], pb[2 + dr][:, 0:64], [pbt[2 + dr]], [to[dr]])
                    for dr in range(2):
                        c = cc_[dr]
                        TT("dve", tmpS[dr], pb[4 + dr][:, 0:128], Sf[dr], ALU.add, [pbt[4 + dr], tSf[dr]], [ttS[dr]])
                        TS("dve", Sb2[dr][1 - par], tmpS[dr], gcol[dr][:, c:c + 1], None, ALU.mult, None, [ttS[dr], tg[dr]],
                           [tSb2[dr][1 - par]])
                        ACT(Sf[dr], tmpS[dr], AF.Identity, [ttS[dr], tg[dr]], [tSf[dr]], scale=gcol[dr][:, c:c + 1])
                TT("pool", o_[0], o_[0], o_[1], ALU.add, [to[0], to[1]], [to[0]])
                for (off, n) in LT4:
                    rstd_of(o_[0], off, n, sq, tsq, rst, trst, to[0], ones_v, nchunks=1)
                    TT("dve", tmp[:, 0:n], o_[0][:, 0, off:off + n], rst[:, 0:n], ALU.mult, [to[0], trst], [ttmp])
                    STT(oab[:, off:off + n], tmp[:, 0:n], normw_s[:, 0:1], sgb[:, off:off + n], ALU.mult, ALU.mult,
                        [ttmp, tconst, tsgb], [toab])
                S.dma(q_sp, oas[:, h, :], oab, reads=[toab], evtok=toab)
                S.barrier()
                A.release()

        def P3(b, zT, tz):
            A.mark()
            gT = A.alloc([4, L], BF16)
            ztok = A.alloc([16, 512], BF16)
            pb_base = A.top
            Pbuf = A.alloc([32, 512], BF16)
            pb_end = A.top
            A.top = pb_base
            pT = A.alloc([L + 8], F32)
            uT = A.alloc([L], F32)
            wsl = [A.alloc([8, 128], BF16) for _ in range(2)]
            wstg3 = [A.alloc([8, 128], F32) for _ in range(2)]
            twstg3 = S.toks(2, "wstg3")
            assert A.top <= pb_end
            A.top = pb_end
            fib = [A.alloc([32, 128], BF16) for _ in range(2)]
            fmb = [A.alloc([2, 16, 128], BF16) for _ in range(2)]
            kb = [A.alloc([2, 512], F32) for _ in range(2)]
            tm = [A.alloc([512], F32) for _ in range(4)]
            tg_, tzt, tP, tpT, tuT = S.toks(5, "p3")
            twsl = S.toks(2, "wsl"); tfib = S.toks(2, "fib"); tfmb = S.toks(2, "fmb"); tkb = S.toks(2, "kb"); ttm = S.toks(4, "tm")

            def proj_conv(part, dst, tdst):
                S.barrier()
                MS("pool", pT[:, 0:1], 0.0, [tpT])
                MS("pool", pT[:, L + 1:L + 2], 0.0, [tpT])
                for cc in range(4):
                    sl = cc % 2
                    ci = part * 4 + cc
                    load_w(wsl[sl], 20 + ci, twsl[sl], wstg3[sl], twstg3[sl])
                    proj_fm(wsl[sl], twsl[sl], LT4,
                            lambda pi_, off, n: CP("act", pT[:, 1 + off:1 + off + n], pb[pi_][:, 0:n], [pbt[pi_]], [tpT]))
                    TS("dve", uT, pT[:, 1:L + 1], convw_s[:, 1, ci:ci + 1], convb_s[:, ci:ci + 1], ALU.mult, ALU.add,
                       [tpT, tconst], [tuT])
                    STT(uT, pT[:, 0:L], convw_s[:, 0, ci:ci + 1], uT, ALU.mult, ALU.add, [tpT, tconst, tuT], [tuT])
                    STT(dst[:, cc, :], pT[:, 2:L + 2], convw_s[:, 2, ci:ci + 1], uT, ALU.mult, ALU.add,
                        [tpT, tconst, tuT], [tdst])
                S.barrier()

            proj_conv(0, zT, tz)
            for o in range(2):
                proj_conv(1 + o, gT, tg_)
                for tt in range(16):
                    pi_ = 6 + (tt % 2)
                    for cc in range(4):
                        TR(pbb[pi_][:, cc * 128:(cc + 1) * 128], zT[:, cc, tt * 128:(tt + 1) * 128], ident,
                           [tz, tconst], [pbt[pi_]])
                    CP("act" if tt % 2 else "dve", ztok[:, tt, :], pbb[pi_][:, 0:512], [pbt[pi_]], [tzt])
                for j in range(16):
                    sl = j % 2
                    S.dma(q_sp, fmb[sl][:, 0], Fm[j], writes=[tfmb[sl]])
                    S.dma(q_sp, fmb[sl][:, 1], Fm[16 + j], writes=[tfmb[sl]])
                    S.dma(q_sp, kb[sl][:, 0, :], Ksp[o, 0, j], writes=[tkb[sl]])
                    S.dma(q_sp, kb[sl][:, 1, :], Ksp[o, 1, j], writes=[tkb[sl]])
                    pr, pim = 2 * sl, 2 * sl + 1
                    for lc in range(16):
                        MM(pb[pr], fmb[sl][:, 0, lc, :], ztok[:, lc, :], lc == 0, lc == 15, [tfmb[sl], tzt], [pbt[pr]])
                    for lc in range(16):
                        MM(pb[pim], fmb[sl][:, 1, lc, :], ztok[:, lc, :], lc == 0, lc == 15, [tfmb[sl], tzt], [pbt[pim]])
                    TT("dve", tm[0], pb[pr], kb[sl][:, 0, :], ALU.mult, [pbt[pr], tkb[sl]], [ttm[0]])
                    TT("dve", tm[1], pb[pim], kb[sl][:, 1, :], ALU.mult, [pbt[pim], tkb[sl]], [ttm[1]])
                    TT("pool", Pbuf[:, j, :], tm[0], tm[1], ALU.subtract, [ttm[0], ttm[1]], [tP])
                    TT("dve", tm[2], pb[pr], kb[sl][:, 1, :], ALU.mult, [pbt[pr], tkb[sl]], [ttm[2]])
                    TT("dve", tm[3], pb[pim], kb[sl][:, 0, :], ALU.mult, [pbt[pim], tkb[sl]], [ttm[3]])
                    TT("pool", Pbuf[:, 16 + j, :], tm[2], tm[3], ALU.add, [ttm[2], ttm[3]], [tP])
                k = 0
                for tt in range(16):
                    sl = tt % 2
                    S.dma(q_sp, fib[sl], Fi[tt], writes=[tfib[sl]])
                    for cc in range(4):
                        pi_ = 4 + (k % 2)
                        k += 1
                        for fc in range(32):
                            MM(pb[pi_][:, 0:128], Pbuf[:, fc, cc * 128:(cc + 1) * 128], fib[sl][:, fc, :], fc == 0, fc == 31,
                               [tP, tfib[sl]], [pbt[pi_]])
                        TT("dve", zT[:, cc, tt * 128:(tt + 1) * 128], gT[:, cc, tt * 128:(tt + 1) * 128], pb[pi_][:, 0:128],
                           ALU.mult, [tg_, pbt[pi_]], [tz])
            S.barrier()
            A.release()

        def P4(b, zT, tz, yT, ty):
            A.mark()
            oaT = A.alloc([4, L], BF16)
            toa = S.tok("oaT")
            S.dma(q_sp, oaT, oas, writes=[toa])
            wga = [A.alloc([8, 128], BF16) for _ in range(2)]
            wgb_ = [A.alloc([8, 128], BF16) for _ in range(2)]
            wa = [A.alloc([4, 128], BF16) for _ in range(2)]
            wb_ = [A.alloc([4, 128], BF16) for _ in range(2)]
            sga = [A.alloc([512], F32) for _ in range(2)]
            sgb2 = [A.alloc([512], F32) for _ in range(2)]
            t1 = [A.alloc([512], F32) for _ in range(2)]
            t2 = [A.alloc([512], F32) for _ in range(2)]
            tw4a = S.toks(2, "w4a"); tw4b = S.toks(2, "w4b"); tw4c = S.toks(2, "w4c"); tw4d = S.toks(2, "w4d")
            wstg4 = [A.alloc([8, 128], F32) for _ in range(2)]
            twstg4 = S.toks(2, "wstg4")
            tsa = S.toks(2, "sa"); tsb = S.toks(2, "sb"); tt1 = S.toks(2, "t1"); tt2 = S.toks(2, "t2")
            k = 0
            for dc in range(8):
                sl = dc % 2
                load_w(wga[sl], 32 + dc, tw4a[sl], wstg4[0], twstg4[0])
                load_w(wgb_[sl], 40 + dc, tw4b[sl], wstg4[1], twstg4[1])
                load_w(wa[sl], dc, tw4c[sl], wstg4[0], twstg4[0], src=wpa, nk=4)
                load_w(wb_[sl], dc, tw4d[sl], wstg4[1], twstg4[1], src=wpb, nk=4)
                for (off, n) in LT4:
                    ss = k % 2
                    k += 1
                    for kc in range(8):
                        MM(pb[0], wga[sl][:, kc, :], nT[:, kc, off:off + n], kc == 0, kc == 7, [tw4a[sl], tnT], [pbt[0]])
                    for kc in range(4):
                        MM(pb[1], wa[sl][:, kc, :], oaT[:, kc, off:off + n], kc == 0, kc == 3, [tw4c[sl], toa], [pbt[1]])
                    for kc in range(8):
                        MM(pb[2], wgb_[sl][:, kc, :], nT[:, kc, off:off + n], kc == 0, kc == 7, [tw4b[sl], tnT], [pbt[2]])
                    for kc in range(4):
                        MM(pb[3], wb_[sl][:, kc, :], zT[:, kc, off:off + n], kc == 0, kc == 3, [tw4d[sl], tz], [pbt[3]])
                    ACT(sga[ss], pb[0], AF.Sigmoid, [pbt[0]], [tsa[ss]])
                    ACT(sgb2[ss], pb[2], AF.Sigmoid, [pbt[2]], [tsb[ss]])
                    TT("dve", t1[ss], sga[ss], pb[1], ALU.mult, [tsa[ss], pbt[1]], [tt1[ss]])
                    TT("dve", t2[ss], sgb2[ss], pb[3], ALU.mult, [tsb[ss], pbt[3]], [tt2[ss]])
                    TT("pool", yT[:, dc, off:off + n], t1[ss], t2[ss], ALU.add, [tt1[ss], tt2[ss]], [ty])
            S.barrier()
            A.release()

        def P5(b, yT, ty):
            for blk in range(2):
                A.mark()
                W = 1024
                base = blk * W
                hb = A.alloc([8, W], F32)
                hbtok = S.tok("hb5")
                S.dma(q_sp, hb, hs[b, :, :, base:base + W], writes=[hbtok])
                A.mark()
                wo = [A.alloc([8, 128], BF16) for _ in range(2)]
                two = S.toks(2, "wo")
                wstg5 = [A.alloc([8, 128], F32) for _ in range(2)]
                twstg5 = S.toks(2, "wstg5")
                tiles = [(0, 512, b), (512, 512, b)]
                k = 0
                for dc in range(8):
                    sl = dc % 2
                    load_w(wo[sl], dc, two[sl], wstg5[sl], twstg5[sl], src=wout)
                    for (off, n, col) in tiles:
                        pi_ = 6 + (k % 2)
                        k += 1
                        for kc in range(8):
                            MM(pb[pi_][:, 0:n], wo[sl][:, kc, :], yT[:, kc, base + off:base + off + n], kc == 0, kc == 7,
                               [two[sl], ty], [pbt[pi_]])
                        STT(hb[:, dc, off:off + n], pb[pi_][:, 0:n], mv(5, dc, col), hb[:, dc, off:off + n],
                            ALU.mult, ALU.add, [pbt[pi_], tmod, hbtok], [hbtok])
                S.barrier()
                A.release()
                ffn_block(1, hb, hbtok, tiles, 6, W)
                A.mark()
                sqs = [A.alloc([8, 512], BF16) for _ in range(2)]
                rsts = [A.alloc([512], F32) for _ in range(2)]
                tsqs = S.toks(2, "nrm5q"); trsts = S.toks(2, "nrm5r")
                rstd_multi(hb, [(off, n) for (off, n, col) in tiles], sqs, tsqs, rsts, trsts, hbtok, ones_d)
                for tix, (off, n, col) in enumerate(tiles):
                    rst, trst = rsts[tix], trsts[tix]
                    for dc in range(8):
                        STT(hb[:, dc, off:off + n], hb[:, dc, off:off + n], fnw_s[:, dc:dc + 1], rst[:, 0:n],
                            ALU.mult, ALU.mult, [hbtok, tconst, trst], [hbtok])
                S.dma(q_sp, out_t[b][:, base:base + W].rearrange("(dc p) t -> p dc t", p=128), hb, reads=[hbtok], evtok=hbtok)
                S.barrier()
                A.release()
                A.release()

        tnT = S.tok("nT")
        nT = None
        for b in range(nb):
            A.mark()
            nT = A.alloc([8, T], BF16)
            blocks = [
                [(0, 512, b), (512, 256, b)],
                [(768, 512, b), (1280, 256, b)],
                [(1536, 512, b), (2048, 256, 2)],
            ]
            for bi, tl in enumerate(blocks):
                A.mark()
                W = 768
                base = bi * 768
                hb = A.alloc([8, W], F32)
                hbtok = S.tok("hb")
                pst = [A.alloc([8, 512], F32)]
                tps = S.toks(1, "pos")
                k = 0
                for (off, n, col) in tl:
                    lo = off - base
                    if col == 2:
                        S.dma(q_sp, hb[:, :, lo:lo + n], ctx_t[b].rearrange("(dc p) t -> p dc t", p=128), writes=[hbtok])
                    else:
                        S.dma(q_sp, hb[:, :, lo:lo + n],
                              x_t[b][:, off:off + n].rearrange("(dc p) t -> p dc t", p=128), writes=[hbtok])
                        S.dma(q_sp, pst[0][:, :, 0:n], pos_t[:, off:off + n].rearrange("(dc p) t -> p dc t", p=128),
                              writes=[tps[0]])
                        for dc in range(8):
                            k += 1
                            TT("dve", hb[:, dc, lo:lo + n], hb[:, dc, lo:lo + n],
                               pst[0][:, dc, 0:n], ALU.add, [hbtok, tps[0]], [hbtok])
                ltiles = [(off - base, n, col) for (off, n, col) in tl]
                ffn_block(0, hb, hbtok, ltiles, 0, W)
                A.mark()
                sqs = [A.alloc([8, 512], BF16) for _ in range(2)]
                rsts = [A.alloc([512], F32) for _ in range(2)]
                tmp = [A.alloc([512], F32) for _ in range(2)]
                tsqs = S.toks(2, "nrmq"); trsts = S.toks(2, "nrmr")
                ttmp = S.toks(2, "tmp")
                k = 0
                rstd_multi(hb, [(off - base, n) for (off, n, col) in tl], sqs, tsqs, rsts, trsts, hbtok, ones_d)
                for tix, (off, n, col) in enumerate(tl):
                    lo = off - base
                    rst, trst = rsts[tix], trsts[tix]
                    if col != 2:
                        S.dma(q_sp, hs[b, :, :, off:off + n], hb[:, :, lo:lo + n], reads=[hbtok], evtok=hbtok)
                    for dc in range(8):
                        sl = k % 2
                        k += 1
                        TT("dve", tmp[sl][:, 0:n], hb[:, dc, lo:lo + n], rst[:, 0:n], ALU.mult, [hbtok, trst], [ttmp[sl]])
                        ACT(nT[:, dc, off:off + n], tmp[sl][:, 0:n], AF.Identity, [ttmp[sl], tmod], [tnT],
                            bias=mv(3, dc, col), scale=mv(4, dc, col))
                S.barrier()
                A.release()
                A.release()
            if b == 0:
                dump("nT0", nT, [128, 8, T], tnT, BF16)
                dump("hs0", hs[0], [128, 8, L], tnT)
            if stop_after == "p1":
                break
            tz = S.tok("zT")
            P2(b)
            zT = A.alloc([4, L], BF16)
            if b == 0:
                dump("oas", oas, [128, 4, L], tz, BF16)
            if stop_after == "p2":
                break
            P3(b, zT, tz)
            if b == 0:
                dump("zT", zT, [128, 4, L], tz, BF16)
            if stop_after == "p3":
                break
            yT = A.alloc_top([8, L], BF16)
            ty = S.tok("yT")
            P4(b, zT, tz, yT, ty)
            if b == 0:
                dump("yT", yT, [128, 8, L], ty, BF16)
            S.barrier()
            A.release()
            if stop_after == "p4":
                break
            P5(b, yT, ty)
            A.release_top()
        S.barrier()
        S.emit()
    return nc, din, dbg_out


def _bf(a):
    return np.ascontiguousarray(a).astype(ml_dtypes.bfloat16)


_CONST_CACHE = {}


def host_consts():
    if _CONST_CACHE:
        return _CONST_CACHE
    f32 = np.float32
    quarter = D // 4
    omega = (1.0 / (10000.0 ** (np.arange(quarter, dtype=f32) / quarter))).astype(f32)
    rows = L // 64
    ar = np.arange(rows, dtype=f32)[:, None] * omega
    ac = np.arange(64, dtype=f32)[:, None] * omega
    er = np.concatenate([np.sin(ar), np.cos(ar)], axis=-1)
    ec = np.concatenate([np.sin(ac), np.cos(ac)], axis=-1)
    emb = np.concatenate([np.broadcast_to(er[:, None, :], (rows, 64, D // 2)),
                          np.broadcast_to(ec[None, :, :], (rows, 64, D // 2))], axis=-1).reshape(L, D)
    pos_t = np.ascontiguousarray(emb.T.astype(f32))
    p = np.arange(L, dtype=f32)
    t = p / (L - 1)
    w = (2.0 * math.pi * p / L).astype(f32)
    fb = np.linspace(1e-4, 15, 16, dtype=f32)
    ang = w[:, None] * fb[None, :]
    z = np.concatenate([t[:, None], np.cos(ang), -np.sin(ang)], axis=-1).astype(f32)
    zfeat = np.ascontiguousarray(z.T)
    max_decay = math.log(1e-2) / 0.3
    min_decay = math.log(1e-2) / 1.5
    deltas = np.abs(np.linspace(min_decay, max_decay, 512, dtype=f32))
    window = (np.exp(-t[:, None] * deltas[None, :]) + 0.05).astype(f32)
    win = np.ascontiguousarray(window.reshape(16, 128, 512).transpose(1, 0, 2))
    wsh = np.zeros_like(window)
    wsh[1:] = window[:-1]
    wins = np.ascontiguousarray(wsh.reshape(16, 128, 512).transpose(1, 0, 2))
    winl = np.ascontiguousarray(window[L - 1:L])
    N = 2 * L
    tt = np.arange(L, dtype=np.float64)[:, None]
    ff = (np.arange(L, dtype=np.float64) + 0.5)[None, :]
    angm = 2.0 * np.pi * tt * ff / N
    Fc = np.cos(angm)
    Fs = -np.sin(angm)
    F = np.concatenate([Fc, Fs], axis=1)
    Fm = F.reshape(16, 128, 32, 128).transpose(2, 1, 0, 3)
    Fi = (2.0 / N) * F.T
    Fi = Fi.reshape(32, 128, 16, 128).transpose(2, 1, 0, 3)
    masks = np.zeros((64, 2, 64), f32)
    si = np.arange(64)[:, None]
    ti = np.arange(64)[None, :]
    masks[:, 0, :] = (si <= ti)
    masks[:, 1, :] = (si >= ti)
    _CONST_CACHE.update(dict(pos_t=pos_t, zfeat=zfeat, win=win, wins=wins, winl=winl, Fm=_bf(Fm), Fi=_bf(Fi),
                             ident=_bf(np.eye(128, dtype=f32)), masks=masks))
    return _CONST_CACHE


def prep_core(inp, bsel):
    f32 = np.float32
    c = host_consts()
    m = dict(c)
    nbl = len(bsel)
    m["x_t"] = np.ascontiguousarray(np.stack([inp["x"][b].T for b in bsel]))
    m["ctx_t"] = np.ascontiguousarray(np.stack([inp["ctx"][b].T for b in bsel]))
    ct = np.zeros((4, D), f32)
    for i, b in enumerate(bsel):
        ct[i] = inp["c"][b]
    ct[2] = inp["c_ctx"]
    m["c_t"] = np.ascontiguousarray(ct.reshape(4, 8, 128).transpose(2, 1, 0))
    m["mod_w"] = np.ascontiguousarray(inp["mod_w"][0])
    m["mod_b"] = np.ascontiguousarray(inp["mod_b"][0].reshape(72, 128).T)
    m["wg"] = np.ascontiguousarray(inp["ffn_w_gate"][0])
    m["wu"] = np.ascontiguousarray(inp["ffn_w_up"][0])
    m["wd"] = np.ascontiguousarray(inp["ffn_w_down"][0])
    m["w_in"] = np.ascontiguousarray(inp["w_in"][0])
    m["lbl"] = np.ascontiguousarray(inp["hgrn_lb_logits"].reshape(2, 2, 4, 128).transpose(3, 0, 1, 2).reshape(128, 2, 8))
    m["normw"] = np.ascontiguousarray(inp["hgrn_norm_w"][0].reshape(128, 1))
    m["convw"] = np.ascontiguousarray(inp["hyena_conv_w"][0].reshape(3, 12, 128).transpose(2, 0, 1))
    m["convb"] = np.ascontiguousarray(inp["hyena_conv_b"][0].reshape(12, 128).T)
    m["hw1"] = np.ascontiguousarray(inp["hyena_w1"][0])
    m["hb1"] = np.ascontiguousarray(inp["hyena_b1"][0].reshape(64, 1))
    m["hf1"] = np.ascontiguousarray(inp["hyena_freq1"][0].reshape(64, 1))
    m["hw2"] = np.ascontiguousarray(inp["hyena_w2"][0])
    m["hb2"] = np.ascontiguousarray(inp["hyena_b2"][0].reshape(64, 1))
    m["hf2"] = np.ascontiguousarray(inp["hyena_freq2"][0].reshape(64, 1))
    m["hw3"] = np.ascontiguousarray(inp["hyena_w3"][0])
    m["hbias"] = np.ascontiguousarray(np.broadcast_to(inp["hyena_bias"][0][None], (128, 2, 512)))
    m["wpa"] = np.ascontiguousarray(inp["w_proj_a"][0])
    m["wpb"] = np.ascontiguousarray(inp["w_proj_b"][0])
    m["wout"] = np.ascontiguousarray(inp["w_out"][0])
    m["fnw"] = np.ascontiguousarray(inp["final_norm_w"].reshape(8, 128).T)
    return {k: (v if v.dtype == ml_dtypes.bfloat16 else v.astype(f32)) for k, v in m.items()}


_PROG = {}


def kernel(**inputs):
    inputs = {k: np.asarray(v) for k, v in inputs.items()}
    if "full" not in _PROG:
        _PROG["full"] = build_program(nb=2)
    nc, din, _ = _PROG["full"]
    in_maps = []
    for core in range(NCORE):
        m = prep_core(inputs, [2 * core, 2 * core + 1])
        in_maps.append({k: m[k] for k in din})
    res = run_bass_kernel_spmd(nc, in_maps, core_ids=list(range(NCORE)))
    out = np.empty((16, L, D), np.float32)
    for core in range(NCORE):
        o = res.results[core]["out_t"]
        for i in range(2):
            out[2 * core + i] = o[i].T
    return out
```

```python
import numpy as np
from contextlib import ExitStack
import concourse.bass as bass
import concourse.mybir as mybir

F32 = mybir.dt.float32
BF16 = mybir.dt.bfloat16
AF = mybir.ActivationFunctionType
ALU = mybir.AluOpType

ENGS = ("pe", "act", "dve", "pool", "sp")
EPOCH = 12000
SAME_ENGINE_SYNC = True


class Tok:
    __slots__ = ("name", "w", "w_eng", "r", "dsem", "dcount")

    def __init__(self, name):
        self.name = name
        self.w = None
        self.w_eng = None
        self.r = []
        self.dsem = None
        self.dcount = 0


class Sched:
    def __init__(self, nc, stack):
        self.nc = nc
        self.stack = stack
        self.ops = {e: [] for e in ENGS}
        self.cnt = {e: 0 for e in ENGS}
        self.sem = {e: None for e in ENGS}
        self.nsem = 0
        self.waited = {e: {} for e in ENGS}
        self.latest = {}
        self.n_ops = 0
        self.dpool = []
        self.dtoks = []

    def new_sem(self, name):
        self.nsem += 1
        return self.stack.enter_context(self.nc.semaphore(f"{name}_{self.nsem}"))

    def tok(self, name="t"):
        return Tok(name)

    def toks(self, n, name="t"):
        return [Tok(f"{name}{i}") for i in range(n)]

    def _next_event(self, eng):
        if self.sem[eng] is None or self.cnt[eng] >= EPOCH:
            self.sem[eng] = self.new_sem(f"s_{eng}")
            self.cnt[eng] = 0
        self.cnt[eng] += 1
        return (self.sem[eng], self.cnt[eng])

    def _need(self, eng, waits, ev):
        if ev is None:
            return
        sem, val = ev[0], ev[1]
        k = id(sem)
        if self.waited[eng].get(k, 0) >= val:
            return
        cur = waits.get(k)
        if cur is None or cur[1] < val:
            waits[k] = (sem, val)

    def _collect(self, eng, reads, writes, is_dma):
        waits = {}
        for t in reads:
            if t.w is not None:
                if t.w_eng == eng and not is_dma:
                    if eng != "pe" and SAME_ENGINE_SYNC:
                        self._need(eng, waits, t.w)
                else:
                    self._need(eng, waits, t.w)
        for t in writes:
            if t.w is not None:
                if t.w_eng == eng and not is_dma:
                    if eng != "pe" and SAME_ENGINE_SYNC:
                        self._need(eng, waits, t.w)
                elif is_dma and t.w_eng == "dma":
                    pass
                else:
                    self._need(eng, waits, t.w)
            for (sem, val, reng) in t.r:
                if reng == eng and not is_dma:
                    continue
                self._need(eng, waits, (sem, val))
        wl = list(waits.values())
        for (sem, val) in wl:
            self.waited[eng][id(sem)] = val
        return wl

    def op(self, eng, fn, reads=(), writes=()):
        wl = self._collect(eng, reads, writes, False)
        ev = self._next_event(eng)
        self.ops[eng].append((wl, fn, ev[0], 1))
        self.waited[eng][id(ev[0])] = max(self.waited[eng].get(id(ev[0]), 0), 0)
        self.latest[id(ev[0])] = ev
        for t in writes:
            t.w = ev
            t.w_eng = eng
            t.r = []
        for t in reads:
            if t in writes:
                continue
            t.r = [x for x in t.r if x[2] != eng] + [(ev[0], ev[1], eng)]
        self.n_ops += 1
        return ev

    def dma(self, queue, out, in_, reads=(), writes=(), evtok=None, **kw):
        if evtok is None:
            evtok = writes[0] if len(writes) else reads[0]
        wl = self._collect(queue, reads, writes, True)
        if evtok.dsem is None:
            if self.dpool:
                evtok.dsem, evtok.dcount = self.dpool.pop()
            else:
                evtok.dsem = self.new_sem("d")
                evtok.dcount = 0
            self.dtoks.append(evtok)
        assert evtok.dcount < 60000
        evtok.dcount += 16
        ev = (evtok.dsem, evtok.dcount)
        self.latest[id(ev[0])] = ev

        def fn(e, out=out, in_=in_, kw=kw):
            return e.dma_start(out=out, in_=in_, **kw)
        self.ops[queue].append((wl, fn, ev[0], 16))
        for t in writes:
            t.w = ev
            t.w_eng = "dma"
            t.r = []
        for t in reads:
            t.r = [x for x in t.r if x[0] is not ev[0]] + [(ev[0], ev[1], "dma")]
        self.n_ops += 1
        return ev

    def barrier(self, engines=ENGS, exclude_engs=(), exclude_toks=()):
        skip = set()
        for e in exclude_engs:
            if self.sem[e] is not None:
                skip.add(id(self.sem[e]))
        for t in exclude_toks:
            if t.dsem is not None:
                skip.add(id(t.dsem))
        engines = tuple(e for e in engines if e not in exclude_engs)
        evs = [v for k, v in self.latest.items() if k not in skip]
        for e in engines:
            wl = []
            for (sem, val) in evs:
                if self.waited[e].get(id(sem), 0) >= val:
                    continue
                if sem is self.sem[e]:
                    continue
                wl.append((sem, val))
                self.waited[e][id(sem)] = val
            if wl:
                self.ops[e].append((wl, None, None, 0))
        if tuple(engines) == tuple(ENGS):
            for t in self.dtoks:
                if t.dcount < 40000:
                    self.dpool.append((t.dsem, t.dcount))
                t.dsem = None
                t.dcount = 0
            self.dtoks = []

    def emit(self):
        nc = self.nc
        with nc.Block() as block:
            def mk(engname):
                def body(e):
                    for (wl, fn, sem, inc) in self.ops[engname]:
                        for (s, v) in wl:
                            e.wait_ge(s, v)
                        if fn is not None:
                            ins = fn(e)
                            ins.then_inc(sem, inc)
                return body
            block.tensor(mk("pe"))
            block.scalar(mk("act"))
            block.vector(mk("dve"))
            block.gpsimd(mk("pool"))
            block.sync(mk("sp"))


class Arena:
    def __init__(self, nc, stack, words, name="arena"):
        self.t = stack.enter_context(nc.sbuf_tensor(name, [128, words], F32))
        self.words = words
        self.top = 0
        self.marks = []
        self.hi = words
        self.his = []

    def mark(self):
        self.marks.append(self.top)

    def release(self):
        self.top = self.marks.pop()

    def alloc_top(self, shape, dtype):
        n = int(np.prod(shape))
        w = n if dtype == F32 else (n + 1) // 2
        w = (w + 7) // 8 * 8
        self.his.append(self.hi)
        self.hi -= w
        assert self.hi >= self.top
        save = self.top
        self.top = self.hi
        hi_save = self.hi
        self.hi = self.words + 10 ** 9
        ap = self.alloc(shape, dtype)
        self.top = save
        self.hi = hi_save
        return ap

    def release_top(self):
        self.hi = self.his.pop()

    def alloc(self, shape, dtype):
        n = int(np.prod(shape))
        if dtype == F32:
            w = n
        elif dtype == BF16:
            w = (n + 1) // 2
        else:
            raise ValueError(dtype)
        w = (w + 7) // 8 * 8
        if self.top + w > min(self.words, self.hi):
            raise MemoryError(f"arena overflow: need {w} at {self.top} of {self.words}")
        ap = self.t[:, self.top:self.top + w]
        self.top += w
        if dtype == BF16:
            ap = ap.bitcast(BF16)[:, 0:n]
        else:
            ap = ap[:, 0:n]
        if len(shape) == 2:
            ap = ap.rearrange("p (a b) -> p a b", b=shape[1])
        elif len(shape) == 3:
            ap = ap.rearrange("p (a b c) -> p a b c", b=shape[1], c=shape[2])
        return ap


import math
import ml_dtypes
from concourse.bass_utils import run_bass_kernel_spmd

D = 1024
L = 2048
LC = 256
T = L + LC
DFF = 2816
NF = DFF // 128
NCORE = 8
PI = math.pi


def _wrap(S):
    def ACT(out, in_, func, reads, writes, bias=None, scale=None):
        kw = {}
        if bias is not None:
            kw["bias"] = bias
        if scale is not None:
            kw["scale"] = scale
        return S.op("act", lambda e: e.activation(out=out, in_=in_, func=func, **kw), reads, writes)

    def TT(eng, out, in0, in1, op, reads, writes):
        return S.op(eng, lambda e: e.tensor_tensor(out=out, in0=in0, in1=in1, op=op), reads, writes)

    def TS(eng, out, in0, s1, s2, op0, op1, reads, writes):
        if op1 is None:
            return S.op(eng, lambda e: e.tensor_scalar(out=out, in0=in0, scalar1=s1, scalar2=None, op0=op0), reads, writes)
        return S.op(eng, lambda e: e.tensor_scalar(out=out, in0=in0, scalar1=s1, scalar2=s2, op0=op0, op1=op1), reads, writes)

    def STT(out, in0, scalar, in1, op0, op1, reads, writes):
        return S.op("dve", lambda e: e.scalar_tensor_tensor(out=out, in0=in0, scalar=scalar, in1=in1, op0=op0, op1=op1), reads, writes)

    def MM(out, lhsT, rhs, start, stop, reads, writes):
        return S.op("pe", lambda e: e.matmul(out, lhsT=lhsT, rhs=rhs, start=start, stop=stop), reads, writes)

    def TR(out, in_, ident, reads, writes):
        return S.op("pe", lambda e: e.transpose(out, in_, ident), reads, writes)

    def CP(eng, out, in_, reads, writes):
        if eng == "act":
            return S.op("act", lambda e: e.activation(out=out, in_=in_, func=AF.Copy), reads, writes)
        return S.op(eng, lambda e: e.tensor_copy(out=out, in_=in_), reads, writes)

    def MS(eng, ap, val, writes):
        return S.op(eng, lambda e: e.memset(ap, val), (), writes)
    return ACT, TT, TS, STT, MM, TR, CP, MS


def build_program(nb=2, stop_after=None, dbg=()):
    nc = bass.Bass("TRN2", target_bir_lowering=False)
    din = {}

    def inp(name, shape, dt=F32):
        din[name] = nc.dram_tensor(name, list(shape), dt, kind="ExternalInput").ap()
        return din[name]

    x_t = inp("x_t", [nb, D, L])
    ctx_t = inp("ctx_t", [nb, D, LC])
    pos_t = inp("pos_t", [D, L])
    c_t = inp("c_t", [128, 8, 4])
    mod_w = inp("mod_w", [D, 9 * D])
    mod_b = inp("mod_b", [128, 72])
    wg = inp("wg", [2, D, DFF])
    wu = inp("wu", [2, D, DFF])
    wd = inp("wd", [2, DFF, D])
    w_in = inp("w_in", [D, 6144])
    lbl = inp("lbl", [128, 2, 8])
    normw = inp("normw", [128, 1])
    convw = inp("convw", [128, 3, 12])
    convb = inp("convb", [128, 12])
    hw1 = inp("hw1", [33, 64])
    hb1 = inp("hb1", [64, 1])
    hf1 = inp("hf1", [64, 1])
    hw2 = inp("hw2", [64, 64])
    hb2 = inp("hb2", [64, 1])
    hf2 = inp("hf2", [64, 1])
    hw3 = inp("hw3", [64, 2048])
    hbias = inp("hbias", [128, 2, 512])
    wpa = inp("wpa", [512, D])
    wpb = inp("wpb", [512, D])
    wout = inp("wout", [D, D])
    fnw = inp("fnw", [128, 8])
    zfeat = inp("zfeat", [33, L])
    win = inp("win", [128, 16, 512])
    wins = inp("wins", [128, 16, 512])
    winl = inp("winl", [1, 512])
    Fm = inp("Fm", [32, 128, 16, 128], BF16)
    Fi = inp("Fi", [16, 128, 32, 128], BF16)
    ident_d = inp("ident", [128, 128], BF16)
    masks_d = inp("masks", [64, 2, 64])

    out_t = nc.dram_tensor("out_t", [nb, D, L], F32, kind="ExternalOutput").ap()
    dbg_out = {}

    def scr(name, shape, dt):
        return nc.dram_tensor(name, list(shape), dt, kind="Internal").ap()

    wgb = scr("wgb", [2, 128, NF, 8, 128], BF16)
    wub = scr("wub", [2, 128, NF, 8, 128], BF16)
    wdb = scr("wdb", [2, 128, 8, NF, 128], BF16)
    winb = scr("winb", [128, 48, 8, 128], BF16)
    wpab = scr("wpab", [128, 8, 4, 128], BF16)
    wpbb = scr("wpbb", [128, 8, 4, 128], BF16)
    woutb = scr("woutb", [128, 8, 8, 128], BF16)
    Ksp = scr("Ksp", [2, 2, 16, 128, 512], F32)
    hs = scr("hs", [nb, 128, 8, L], F32)
    oas = scr("oas", [128, 4, L], BF16)

    with ExitStack() as st:
        S = Sched(nc, st)
        ACT, TT, TS, STT, MM, TR, CP, MS = _wrap(S)
        A = Arena(nc, st, 48000)
        pbk = [st.enter_context(nc.psum_tensor(f"pb{i}", [128, 512], F32)) for i in range(8)]
        pb = [p[:] for p in pbk]
        pbt = S.toks(8, "pb")
        pbb = [p[:].bitcast(BF16) for p in pbk]
        q_sp = "sp"

        def dump(name, ap, shape, tok, dt=F32):
            if name not in dbg:
                return
            d = nc.dram_tensor("dbg_" + name, list(shape), dt, kind="ExternalOutput").ap()
            dbg_out[name] = d
            S.dma(q_sp, d, ap, reads=[tok], evtok=tok)

        ident = A.alloc([128], BF16)
        ones_d = A.alloc([128], BF16)
        ones_v = A.alloc([128], BF16)
        ones_f = A.alloc([128], F32)
        modT = A.alloc([72, 4], F32)
        lbT = A.alloc([8], F32)
        omlT = A.alloc([8], F32)
        normw_s = A.alloc([1], F32)
        convw_s = A.alloc([3, 12], F32)
        convb_s = A.alloc([12], F32)
        fnw_s = A.alloc([8], F32)
        masks_s = A.alloc([2, 64], F32)
        epsb = A.alloc([1], F32)
        tconst = S.tok("const")
        S.dma(q_sp, ident, ident_d, writes=[tconst])
        S.dma(q_sp, normw_s, normw, writes=[tconst])
        S.dma(q_sp, convw_s, convw, writes=[tconst])
        S.dma(q_sp, convb_s, convb, writes=[tconst])
        S.dma(q_sp, fnw_s, fnw, writes=[tconst])
        S.dma(q_sp, masks_s[0:64], masks_d, writes=[tconst])
        tc2 = S.tok("const2")
        MS("pool", ones_d, 1.0 / 1024.0, [tc2])
        MS("pool", ones_v, 1.0 / 128.0, [tc2])
        MS("pool", ones_f, 1.0, [tc2])
        MS("pool", epsb, 1e-6, [tc2])
        CONST = [tconst, tc2]

        def conv_units():
            NSL = 3
            sfA = [A.alloc_top([8, 512], F32) for _ in range(NSL)]
            sbA = [A.alloc_top([4, 8, 128], BF16) for _ in range(NSL)]
            tf = S.toks(NSL, "cvf")
            tb = S.toks(NSL, "cvb")
            cvtoks.extend(tf + tb)
            it = 0
            ce = 0
            for s_ in range(2):
                for (src, dst) in ((wg[s_], wgb[s_]), (wu[s_], wub[s_])):
                    for g0 in range(0, NF, 4):
                        g = min(4, NF - g0)
                        sl = it % NSL
                        it += 1
                        S.dma("sp", sfA[sl][:, :, 0:g * 128],
                              src[:, g0 * 128:(g0 + g) * 128].rearrange("(k p) n -> p k n", p=128), writes=[tf[sl]])
                        for gi in range(g):
                            eng = "pool"
                            ce += 1
                            CP(eng, sbA[sl][:, gi, :, :], sfA[sl][:, :, gi * 128:(gi + 1) * 128], [tf[sl]], [tb[sl]])
                        S.dma("sp", dst[:, g0:g0 + g], sbA[sl][:, 0:g], reads=[tb[sl]], evtok=tb[sl])
                        yield
                for dc in range(8):
                    sl = it % NSL
                    it += 1
                    sfv = sfA[sl].rearrange("p a b -> p (a b)")[:, 0:NF * 128].rearrange("p (f c) -> p f c", c=128)
                    sbv = sbA[sl].rearrange("p a b c -> p (a b c)")[:, 0:NF * 128].rearrange("p (f c) -> p f c", c=128)
                    S.dma("sp", sfv, wd[s_][:, dc * 128:(dc + 1) * 128].rearrange("(f p) n -> p f n", p=128), writes=[tf[sl]])
                    for hf_ in range(2):
                        eng = "pool"
                        ce += 1
                        CP(eng, sbv[:, hf_ * 11:(hf_ + 1) * 11, :], sfv[:, hf_ * 11:(hf_ + 1) * 11, :], [tf[sl]], [tb[sl]])
                    S.dma("sp", wdb[s_][:, dc], sbv, reads=[tb[sl]], evtok=tb[sl])
                    yield

        cvtoks = []
        cgen = conv_units()
        for _ in cgen:
            pass

        def pbarrier():
            S.barrier(exclude_engs=("pool", "sp"), exclude_toks=cvtoks)

        def pump(n=1):
            for _ in range(n):
                if next(cgen, "done") == "done":
                    return

        A.mark()
        cts = A.alloc([8, 4], F32)
        scs = A.alloc([8, 4], F32)
        lbs = A.alloc([2, 8], F32)
        mbs = A.alloc([72], F32)
        mwb = [A.alloc([8, 1024], F32) for _ in range(2)]
        mwt = S.toks(2, "mw")
        tct, tsc, tlb, tmod = S.toks(4, "p0a")
        S.dma("act", cts, c_t, writes=[tct])
        S.dma("act", lbs, lbl, writes=[tlb])
        S.dma("act", mbs, mod_b, writes=[tlb])
        ACT(scs, cts, AF.Silu, [tct], [tsc])
        TT("dve", lbs[:, 0, :], lbs[:, 0, :], lbs[:, 1, :], ALU.subtract, [tlb], [tlb])
        ACT(lbT, lbs[:, 0, :], AF.Sigmoid, [tlb], [tmod])
        TS("dve", omlT, lbT, -1.0, 1.0, ALU.mult, ALU.add, [tmod], [tmod])
        for j in range(9):
            sl = j % 2
            S.dma("act", mwb[sl], mod_w[:, j * 1024:(j + 1) * 1024].rearrange("(kc p) n -> p kc n", p=128),
                  writes=[mwt[sl]])
            pump(2)
            for dc in range(8):
                o0 = (j * 8 + dc) * 4
                for kc in range(8):
                    MM(pb[0][:, o0:o0 + 4], mwb[sl][:, kc, dc * 128:(dc + 1) * 128], scs[:, kc, :],
                       kc == 0, kc == 7, [mwt[sl], tsc], [pbt[0]])
        psm = pb[0][:, 0:288].rearrange("p (a b) -> p a b", b=4)
        for col in range(4):
            TT("dve", modT[:, :, col], psm[:, :, col], mbs, ALU.add, [pbt[0], tlb], [tmod])
        for j in (1, 4, 7):
            TS("dve", modT[:, j * 8:(j + 1) * 8, :], modT[:, j * 8:(j + 1) * 8, :], 1.0, None, ALU.add, None, [tmod], [tmod])
        for j in (2, 8):
            TS("dve", modT[:, j * 8:(j + 1) * 8, :], modT[:, j * 8:(j + 1) * 8, :], 0.5, None, ALU.mult, None, [tmod], [tmod])
        dump("modT", modT, [128, 72, 4], tmod)
        pbarrier()
        A.release()
        CONST.append(tmod)

        def mv(j, dc, col):
            return modT[:, j * 8 + dc, col:col + 1]

        A.mark()
        w3s = A.alloc([2048], F32)
        hsm = A.alloc([8], F32)
        h2p = A.alloc([L + 8], F32)
        winl_s = A.alloc([512], F32)
        rn = A.alloc([2, 512], F32)
        hbias_s = A.alloc([2, 512], F32)
        A.mark()
        zf = A.alloc([L], F32)
        w1s = A.alloc([64], F32)
        w2s = A.alloc([64], F32)
        h1 = A.alloc([L], F32)
        arg = A.alloc([512], F32)
        wtmp = A.alloc([512], F32)
        tk0, th1, th2, targ, theo, trn = S.toks(6, "p0c")
        thf = S.toks(2, "hf"); thb = S.toks(2, "hb"); tab = S.toks(2, "ab"); twn = S.toks(2, "wn")
        S.dma("act", zf[0:33], zfeat, writes=[tk0])
        S.dma("act", w1s[0:33], hw1, writes=[tk0])
        S.dma("act", w2s[0:64], hw2, writes=[tk0])
        S.dma("act", w3s[0:64], hw3, writes=[tk0])
        S.dma("act", hsm[0:64, 0:1], hb1, writes=[tk0])
        S.dma("act", hsm[0:64, 1:2], hf1, writes=[tk0])
        S.dma("act", hsm[0:64, 2:3], hb2, writes=[tk0])
        S.dma("act", hsm[0:64, 3:4], hf2, writes=[tk0])
        S.dma("act", winl_s[0:1], winl, writes=[tk0])
        S.dma("act", hbias_s, hbias, writes=[tk0])
        TT("dve", hsm[0:64, 4:5], hsm[0:64, 0:1], hsm[0:64, 1:2], ALU.mult, [tk0], [tk0])
        TT("dve", hsm[0:64, 5:6], hsm[0:64, 2:3], hsm[0:64, 3:4], ALU.mult, [tk0], [tk0])
        MS("dve", h2p[0:64, 0:1], 0.0, [th2])

        def sin_layer(wsb, kdim, src, dst, dst_off, fcol, fbcol, tsrc, tdst):
            for ti in range(4):
                MM(pb[1][0:64, :], wsb[0:kdim, 0:64], src[0:kdim, ti * 512:(ti + 1) * 512], True, True,
                   [tk0, tsrc], [pbt[1]])
                TS("dve", arg[0:64], pb[1][0:64, :], hsm[0:64, fcol:fcol + 1], hsm[0:64, fbcol:fbcol + 1],
                   ALU.mult, ALU.add, [pbt[1], tk0], [targ])
                for _ in range(2):
                    wrap_once(arg[0:64], targ)
                TS("dve", arg[0:64], arg[0:64], 3.14159, -3.14159, ALU.min, ALU.max, [targ], [targ])
                ACT(dst[0:64, dst_off + ti * 512: dst_off + (ti + 1) * 512], arg[0:64], AF.Sin, [targ], [tdst])

        twt = S.tok("wtmp")

        def wrap_once(ap, tok):
            TS("dve", wtmp[0:64], ap, PI, -2.0 * PI, ALU.is_gt, ALU.mult, [tok], [twt])
            TT("dve", ap, ap, wtmp[0:64], ALU.add, [tok, twt], [tok])
            TS("dve", wtmp[0:64], ap, -PI, 2.0 * PI, ALU.is_lt, ALU.mult, [tok], [twt])
            TT("dve", ap, ap, wtmp[0:64], ALU.add, [tok, twt], [tok])

        sin_layer(w1s, 33, zf, h1, 0, 1, 4, tk0, th1)
        sin_layer(w2s, 64, h1, h2p, 1, 3, 5, th1, th2)
        dump("h2", h2p[0:64, 1:L + 1], [64, L], th2)
        pbarrier()
        A.release()
        heo = A.alloc([16, 2, 512], BF16)
        hfb = [A.alloc([512], F32) for _ in range(2)]
        hbb = [A.alloc([512], F32) for _ in range(2)]
        absb = [A.alloc([512], F32) for _ in range(2)]
        winb_s = [A.alloc([512], F32) for _ in range(2)]
        winsb_s = [A.alloc([512], F32) for _ in range(2)]

        fmb = [A.alloc([2, 16, 128], BF16) for _ in range(3)]
        tfm = S.toks(3, "fm")
        kst = [A.alloc([2, 512], F32) for _ in range(2)]
        tks = S.toks(2, "kst")
        it = 0
        jj = 0
        for o in range(2):
            for lt in range(16):
                sl = lt % 2
                pump(1)
                S.dma("act", winb_s[sl], win[:, lt, :], writes=[twn[sl]])
                S.dma("act", winsb_s[sl], wins[:, lt, :], writes=[twn[sl]])
                MM(pb[2], h2p[0:64, 1 + lt * 128: 1 + (lt + 1) * 128], w3s[0:64, o * 1024: o * 1024 + 512], True, True,
                   [th2, tk0], [pbt[2]])
                MM(pb[3], h2p[0:64, lt * 128:(lt + 1) * 128], w3s[0:64, o * 1024 + 512: o * 1024 + 1024], True, True,
                   [th2, tk0], [pbt[3]])
                TT("dve", hfb[sl], pb[2], winb_s[sl], ALU.mult, [pbt[2], twn[sl]], [thf[sl]])
                TT("dve", hbb[sl], pb[3], winsb_s[sl], ALU.mult, [pbt[3], twn[sl]], [thb[sl]])
                ACT(absb[0], hfb[sl], AF.Abs, [thf[sl]], [tab[0]])
                MM(pb[4 + o], ones_f, absb[0], lt == 0, False, [tab[0], tc2], [pbt[4 + o]])
                ACT(absb[1], hbb[sl], AF.Abs, [thb[sl]], [tab[1]])
                MM(pb[4 + o], ones_f, absb[1], False, False, [tab[1], tc2], [pbt[4 + o]])
                TT("dve", heo[:, lt, 0, :], hfb[sl], hbb[sl], ALU.add, [thf[sl], thb[sl]], [theo])
                TT("dve", heo[:, lt, 1, :], hfb[sl], hbb[sl], ALU.subtract, [thf[sl], thb[sl]], [theo])
            MM(pb[2][0:1, :], h2p[0:64, L:L + 1], w3s[0:64, o * 1024 + 512: o * 1024 + 1024], True, True,
               [th2, tk0], [pbt[2]])
            TT("dve", hfb[0][0:1], pb[2][0:1, :], winl_s[0:1], ALU.mult, [pbt[2], tk0], [thf[0]])
            ACT(absb[0][0:1], hfb[0][0:1], AF.Abs, [thf[0]], [tab[0]])
            MM(pb[4 + o], ones_f[0:1, :], absb[0][0:1], False, True, [tab[0], tc2], [pbt[4 + o]])
            TS("dve", rn[:, o, :], pb[4 + o], 1e-6, None, ALU.add, None, [pbt[4 + o]], [trn])
            S.op("dve", lambda e, o=o: e.reciprocal(out=rn[:, o, :], in_=rn[:, o, :]), [trn], [trn])
            for j in range(16):
                sl = jj % 3
                jj += 1
                pump(1)
                S.dma("act", fmb[sl][:, 0], Fm[j], writes=[tfm[sl]])
                S.dma("act", fmb[sl][:, 1], Fm[16 + j], writes=[tfm[sl]])
                ks = it % 2
                it += 1
                for lc in range(16):
                    MM(pb[6], fmb[sl][:, 0, lc, :], heo[:, lc, 0, :], lc == 0, lc == 15, [tfm[sl], theo], [pbt[6]])
                for lc in range(16):
                    MM(pb[7], fmb[sl][:, 1, lc, :], heo[:, lc, 1, :], lc == 0, lc == 15, [tfm[sl], theo], [pbt[7]])
                TT("dve", kst[ks][:, 0, :], pb[6], rn[:, o, :], ALU.mult, [pbt[6], trn], [tks[ks]])
                TT("dve", kst[ks][:, 0, :], kst[ks][:, 0, :], hbias_s[:, o, :], ALU.add, [tks[ks], tk0], [tks[ks]])
                TT("dve", kst[ks][:, 1, :], pb[7], rn[:, o, :], ALU.mult, [pbt[7], trn], [tks[ks]])
                S.dma("act", Ksp[o, 0, j], kst[ks][:, 0, :], reads=[tks[ks]], evtok=tks[ks])
                S.dma("act", Ksp[o, 1, j], kst[ks][:, 1, :], reads=[tks[ks]], evtok=tks[ks])
        dump("rn", rn, [128, 2, 512], trn)
        pump(1000)
        S.barrier()
        A.release()
        for _ in range(6):
            A.release_top()
        if stop_after == "p0c":
            S.emit()
            return nc, din, dbg_out

        def rstd_of(hb, off, n, sq, tsq, rst, trst, hbtok, ones_ap, nchunks=8):
            for dc in range(nchunks):
                ACT(sq[:, dc, 0:n], hb[:, dc, off:off + n], AF.Square, [hbtok], [tsq])
            for dc in range(nchunks):
                MM(pb[7][:, 0:n], ones_ap, sq[:, dc, 0:n], dc == 0, dc == nchunks - 1, [tsq, tc2], [pbt[7]])
            ACT(rst[:, 0:n], pb[7][:, 0:n], AF.Sqrt, [pbt[7]], [trst], bias=epsb[:, 0:1])
            S.op("dve", lambda e: e.reciprocal(out=rst[:, 0:n], in_=rst[:, 0:n]), [trst], [trst])

        def rstd_multi(hb, tl2, sqs, tsqs, rsts, trsts, hbtok, ones_ap):
            assert len(tl2) <= 2
            for i, (off, n) in enumerate(tl2):
                for dc in range(8):
                    ACT(sqs[i][:, dc, 0:n], hb[:, dc, off:off + n], AF.Square, [hbtok], [tsqs[i]])
            for i, (off, n) in enumerate(tl2):
                for dc in range(8):
                    MM(pb[7 - i][:, 0:n], ones_ap, sqs[i][:, dc, 0:n], dc == 0, dc == 7, [tsqs[i], tc2], [pbt[7 - i]])
            for i, (off, n) in enumerate(tl2):
                ACT(rsts[i][:, 0:n], pb[7 - i][:, 0:n], AF.Sqrt, [pbt[7 - i]], [trsts[i]], bias=epsb[:, 0:1])
            for i, (off, n) in enumerate(tl2):
                S.op("dve", lambda e, i=i, n=n: e.reciprocal(out=rsts[i][:, 0:n], in_=rsts[i][:, 0:n]),
                     [trsts[i]], [trsts[i]])

        def ffn_block(s, hb, hbtok, tiles, j0, W):
            A.mark()
            nbk = A.alloc([8, W], BF16)
            act = A.alloc([NF, W], BF16)
            actf = act.rearrange("p f w -> p (f w)")
            sqs = [actf[:, i * 4096:(i + 1) * 4096].rearrange("p (a b) -> p a b", b=512) for i in range(2)]
            rsts = [actf[:, 8192 + i * 1024: 8192 + (i + 1) * 1024].bitcast(F32) for i in range(2)]
            tmp = [A.alloc([512], F32) for _ in range(2)]
            sg = [A.alloc([512], BF16) for _ in range(2)]
            NSA, NSB = 4, 3
            wgs = [A.alloc([8, 128], BF16) for _ in range(NSA)]
            wus = [A.alloc([8, 128], BF16) for _ in range(NSA)]
            wds = [A.alloc([NF, 128], BF16) for _ in range(NSB)]
            tnb = S.toks(len(tiles), "nb")
            tact = S.toks(len(tiles), "act")
            tsqs = S.toks(2, "nrmq"); trsts = S.toks(2, "nrmr")
            ttmp = S.toks(2, "tmp"); tsg = S.toks(2, "sg")
            twg = S.toks(NSA, "wg"); twd = S.toks(NSB, "wd")
            k = 0
            assert len(tiles) == 2
            rstd_multi(hb, [(off, n) for (off, n, col) in tiles], sqs, tsqs, rsts, trsts, hbtok, ones_d)
            for ti, (off, n, col) in enumerate(tiles):
                rst, trst = rsts[ti], trsts[ti]
                for dc in range(8):
                    sl = k % 2
                    k += 1
                    TT("dve", tmp[sl][:, 0:n], hb[:, dc, off:off + n], rst[:, 0:n], ALU.mult, [hbtok, trst], [ttmp[sl]])
                    ACT(nbk[:, dc, off:off + n], tmp[sl][:, 0:n], AF.Identity, [ttmp[sl], tmod], [tnb[ti]],
                        bias=mv(j0, dc, col), scale=mv(j0 + 1, dc, col))
            k = 0
            for f in range(NF):
                sl = f % NSA
                S.dma(q_sp, wgs[sl], wgb[s, :, f], writes=[twg[sl]])
                S.dma(q_sp, wus[sl], wub[s, :, f], writes=[twg[sl]])
                for ti, (off, n, col) in enumerate(tiles):
                    pg = (2 * k) % 4
                    pu = pg + 1
                    ss = k % 2
                    k += 1
                    for kc in range(8):
                        MM(pb[pg][:, 0:n], wgs[sl][:, kc, :], nbk[:, kc, off:off + n], kc == 0, kc == 7,
                           [twg[sl], tnb[ti]], [pbt[pg]])
                    for kc in range(8):
                        MM(pb[pu][:, 0:n], wus[sl][:, kc, :], nbk[:, kc, off:off + n], kc == 0, kc == 7,
                           [twg[sl], tnb[ti]], [pbt[pu]])
                    ACT(sg[ss][:, 0:n], pb[pg][:, 0:n], AF.Silu, [pbt[pg]], [tsg[ss]])
                    TT("dve", act[:, f, off:off + n], sg[ss][:, 0:n], pb[pu][:, 0:n], ALU.mult,
                       [tsg[ss], pbt[pu]], [tact[ti]])
            k = 0
            for dc in range(8):
                sl = dc % NSB
                S.dma(q_sp, wds[sl], wdb[s, :, dc], writes=[twd[sl]])
                for ti, (off, n, col) in enumerate(tiles):
                    pp = 4 + (k % 2)
                    k += 1
                    for f in range(NF):
                        MM(pb[pp][:, 0:n], wds[sl][:, f, :], act[:, f, off:off + n], f == 0, f == NF - 1,
                           [twd[sl], tact[ti]], [pbt[pp]])
                    STT(hb[:, dc, off:off + n], pb[pp][:, 0:n], mv(j0 + 2, dc, col), hb[:, dc, off:off + n],
                        ALU.mult, ALU.add, [pbt[pp], tmod, hbtok], [hbtok])
            S.barrier()
            A.release()

        def load_w(dst, cg, tok, stg, tstg, src=None, nk=8):
            srcm = w_in if src is None else src
            S.dma(q_sp, stg[:, 0:nk, :], srcm[:, cg * 128:(cg + 1) * 128].rearrange("(kc p) n -> p kc n", p=128), writes=[tstg])
            CP("pool", dst, stg[:, 0:nk, :], [tstg], [tok])

        def proj_fm(wsb, wtok, tiles_, consume):
            for i, (off, n) in enumerate(tiles_):
                pi_ = 6 + (i % 2)
                for kc in range(8):
                    MM(pb[pi_][:, 0:n], wsb[:, kc, :], nT[:, kc, off:off + n], kc == 0, kc == 7, [wtok, tnT], [pbt[pi_]])
                consume(pi_, off, n)

        LT4 = [(0, 512), (512, 512), (1024, 512), (1536, 512)]
        LT5 = LT4 + [(2048, 256)]

        def P2(b):
            for h in range(4):
                A.mark()
                wv, wff, wfb, wq, wgt = [A.alloc([8, 128], BF16) for _ in range(5)]
                tw = S.toks(5, "hw")
                wstg = [A.alloc([8, 128], F32) for _ in range(2)]
                twstg = S.toks(2, "wstg")
                for wi, (wsb, cg, tk) in enumerate(((wv, h, tw[0]), (wq, 12 + h, tw[3]), (wgt, 16 + h, tw[4]), (wff, 4 + h, tw[1]), (wfb, 8 + h, tw[2]))):
                    load_w(wsb, cg, tk, wstg[wi % 2], twstg[wi % 2])
                vtok = A.alloc([36, 128], BF16)
                kk = A.alloc([T], F32)
                lfb = A.alloc([T], F32)
                Bb = A.alloc([T], F32)
                qf = A.alloc([L], F32)
                onesr = A.alloc([T], BF16)
                qt_ = [A.alloc([L], BF16) for _ in range(2)]
                kt_ = [A.alloc([T], BF16) for _ in range(2)]
                ktok = [A.alloc([36, 128], BF16) for _ in range(2)]
                o_ = [A.alloc([1, L], F32) for _ in range(2)]
                sgb = A.alloc([L], BF16)
                Sf = [A.alloc([128], F32) for _ in range(2)]
                Sb2 = [[A.alloc([128], BF16) for _ in range(2)] for _ in range(2)]
                tSb2 = [S.toks(2, "Sb2") for _ in range(2)]
                tmpS = [A.alloc([128], F32) for _ in range(2)]
                gcol = [A.alloc([36], F32) for _ in range(2)]
                bref = A.alloc([36], F32)
                scm = [A.alloc([64], BF16) for _ in range(2)]
                sq = A.alloc([1, 512], BF16)
                rst = A.alloc([512], F32)
                tmp = A.alloc([512], F32)
                oab = A.alloc([L], BF16)
                (tvt, tkk, tlf, tB, tq, tone, tsgb, tbref, tsq, trst, ttmp, toab) = S.toks(12, "p2")
                tqt = S.toks(2, "qt"); tkt = S.toks(2, "kt"); tktok = S.toks(2, "ktok"); to = S.toks(2, "o")
                tSf = S.toks(2, "Sf"); tSb = S.toks(2, "Sb"); ttS = S.toks(2, "tS"); tg = S.toks(2, "g"); tscm = S.toks(2, "scm")
                MS("pool", onesr, 1.0, [tone])
                for g0 in range(0, 36, 4):
                    pi_ = 6 + ((g0 // 4) % 2)
                    for ci in range(4):
                        c = g0 + ci
                        for kc in range(8):
                            MM(pb[pi_][0:64, ci * 128:(ci + 1) * 128], nT[:, kc, c * 64:(c + 1) * 64], wv[:, kc, :],
                               kc == 0, kc == 7, [tw[0], tnT], [pbt[pi_]])
                    CP("act", vtok[0:64, g0:g0 + 4, :], pb[pi_][0:64, :].rearrange("p (a b) -> p a b", b=128), [pbt[pi_]], [tvt])
                proj_fm(wq, tw[3], LT4, lambda pi_, off, n: ACT(qf[:, off:off + n], pb[pi_][:, 0:n], AF.Silu, [pbt[pi_]], [tq]))
                proj_fm(wgt, tw[4], LT4, lambda pi_, off, n: ACT(sgb[:, off:off + n], pb[pi_][:, 0:n], AF.Silu, [pbt[pi_]], [tsgb]))
                B3 = Bb.rearrange("p (c s) -> p c s", s=64)
                lf3 = lfb.rearrange("p (c s) -> p c s", s=64)
                for dr in range(2):
                    lbc = dr * 4 + h
                    wsb, wtk = (wff, tw[1]) if dr == 0 else (wfb, tw[2])
                    proj_fm(wsb, wtk, LT5, lambda pi_, off, n: ACT(kk[:, off:off + n], pb[pi_][:, 0:n], AF.Sigmoid,
                                                                    [pbt[pi_]], [tkk], scale=-1.0))
                    TS("dve", kk, kk, omlT[:, lbc:lbc + 1], None, ALU.mult, None, [tkk, tmod], [tkk])
                    ACT(lfb, kk, AF.Ln, [tkk], [tlf], bias=ones_f[:, 0:1], scale=-1.0)
                    S.op("dve", lambda e: e.tensor_tensor_scan(out=Bb, data0=onesr, data1=lfb, initial=0.0,
                                                                op0=ALU.mult, op1=ALU.add), [tone, tlf], [tB])
                    if dr == 0:
                        TT("dve", bref, B3[:, :, 0], lf3[:, :, 0], ALU.subtract, [tB, tlf], [tbref])
                        TT("dve", lf3, B3, bref.unsqueeze(2).broadcast_to([128, 36, 64]), ALU.subtract, [tB, tbref, tlf], [tlf])
                    else:
                        TT("dve", lf3, lf3, B3, ALU.subtract, [tB, tlf], [tlf])
                        TT("dve", lf3, lf3, B3[:, :, 63:64].broadcast_to([128, 36, 64]), ALU.add, [tB, tlf], [tlf])
                    ACT(Bb, lfb, AF.Exp, [tlf], [tB])
                    if dr == 0:
                        CP("dve", gcol[dr], B3[:, :, 63], [tB], [tg[dr]])
                    else:
                        CP("dve", gcol[dr], B3[:, :, 0], [tB], [tg[dr]])
                    TT("dve", qt_[dr], qf, Bb[:, 0:L], ALU.mult, [tq, tB], [tqt[dr]])
                    ACT(lfb, lfb, AF.Exp, [tlf], [tlf], scale=-1.0)
                    TT("dve", kt_[dr], kk, lfb, ALU.mult, [tkk, tlf], [tkt[dr]])
                    for g0 in range(0, 36, 4):
                        pi_ = 6 + ((g0 // 4) % 2)
                        for ci in range(4):
                            c = g0 + ci
                            TR(pbb[pi_][0:64, ci * 128:(ci + 1) * 128], kt_[dr][:, c * 64:(c + 1) * 64], ident,
                               [tkt[dr], tconst], [pbt[pi_]])
                        CP("act", ktok[dr][0:64, g0:g0 + 4, :], pbb[pi_][0:64, 0:512].rearrange("p (a b) -> p a b", b=128),
                           [pbt[pi_]], [tktok[dr]])
                    MS("pool", tmpS[dr], 0.0, [ttS[dr]])
                    MS("pool", Sb2[dr][0], 0.0, [tSb2[dr][0]])
                orders = [[32, 33, 34, 35] + list(range(32)), [35, 34, 33, 32] + list(range(31, -1, -1))]
                for step in range(36):
                    par = step % 2
                    cc_ = [orders[dr][step] for dr in range(2)]
                    lat = cc_[0] < 32
                    if lat:
                        for dr in range(2):
                            c = cc_[dr]
                            MM(pb[dr][0:64, 0:64], kt_[dr][:, c * 64:(c + 1) * 64], qt_[dr][:, c * 64:(c + 1) * 64], True, True,
                               [tkt[dr], tqt[dr]], [pbt[dr]])
                    for dr in range(2):
                        c = cc_[dr]
                        MM(pb[4 + dr][:, 0:128], ktok[dr][0:64, c, :], vtok[0:64, c, :], True, True, [tktok[dr], tvt], [pbt[4 + dr]])
                    if lat:
                        for dr in range(2):
                            TT("dve", scm[dr][0:64], pb[dr][0:64, 0:64], masks_s[0:64, dr, :], ALU.mult,
                               [pbt[dr], tconst], [tscm[dr]])
                        for dr in range(2):
                            c = cc_[dr]
                            MM(pb[2 + dr][:, 0:64], vtok[0:64, c, :], scm[dr][0:64], True, False, [tvt, tscm[dr]], [pbt[2 + dr]])
                            MM(pb[2 + dr][:, 0:64], Sb2[dr][par], qt_[dr][:, c * 64:(c + 1) * 64], False, True,
                               [tSb2[dr][par], tqt[dr]], [pbt[2 + dr]])
                            CP("act", o_[dr][:, 0, c * 64:(c + 1) * 64], pb[2 + dr][:, 0:64], [pbt[2 + dr]], [to[dr]])
                    for dr in range(2):
                        c = cc_[dr]
                        cp_ = orders[dr][step - 1] if step > 0 else c
                        STT(tmpS[dr], tmpS[dr], gcol[dr][:, cp_:cp_ + 1], pb[4 + dr][:, 0:128], ALU.mult, ALU.add,
                            [ttS[dr], tg[dr], pbt[4 + dr]], [ttS[dr]])
                        TS("dve", Sb2[dr][1 - par], tmpS[dr], gcol[dr][:, c:c + 1], None, ALU.mult, None, [ttS[dr], tg[dr]],
                           [tSb2[dr][1 - par]])
                TT("pool", o_[0], o_[0], o_[1], ALU.add, [to[0], to[1]], [to[0]])
                for (off, n) in LT4:
                    rstd_of(o_[0], off, n, sq, tsq, rst, trst, to[0], ones_v, nchunks=1)
                    TT("dve", tmp[:, 0:n], o_[0][:, 0, off:off + n], rst[:, 0:n], ALU.mult, [to[0], trst], [ttmp])
                    STT(oab[:, off:off + n], tmp[:, 0:n], normw_s[:, 0:1], sgb[:, off:off + n], ALU.mult, ALU.mult,
                        [ttmp, tconst, tsgb], [toab])
                S.dma(q_sp, oas[:, h, :], oab, reads=[toab], evtok=toab)
                S.barrier()
                A.release()

        def P3(b, zT, tz):
            A.mark()
            gT = A.alloc([4, L], BF16)
            ztok = A.alloc([16, 512], BF16)
            pb_base = A.top
            Pbuf = A.alloc([32, 512], BF16)
            pb_end = A.top
            A.top = pb_base
            pT = A.alloc([L + 8], F32)
            uT = A.alloc([L], F32)
            wsl = [A.alloc([8, 128], BF16) for _ in range(2)]
            wstg3 = [A.alloc([8, 128], F32) for _ in range(2)]
            twstg3 = S.toks(2, "wstg3")
            assert A.top <= pb_end
            A.top = pb_end
            fib = [A.alloc([32, 128], BF16) for _ in range(2)]
            fmb = [A.alloc([2, 16, 128], BF16) for _ in range(2)]
            kb = [A.alloc([2, 512], F32) for _ in range(2)]
            tm = [A.alloc([512], F32) for _ in range(4)]
            tg_, tzt, tP, tpT, tuT = S.toks(5, "p3")
            twsl = S.toks(2, "wsl"); tfib = S.toks(2, "fib"); tfmb = S.toks(2, "fmb"); tkb = S.toks(2, "kb"); ttm = S.toks(4, "tm")

            def proj_conv(part, dst, tdst):
                S.barrier()
                MS("pool", pT[:, 0:1], 0.0, [tpT])
                MS("pool", pT[:, L + 1:L + 2], 0.0, [tpT])
                for cc in range(4):
                    sl = cc % 2
                    ci = part * 4 + cc
                    load_w(wsl[sl], 20 + ci, twsl[sl], wstg3[sl], twstg3[sl])
                    proj_fm(wsl[sl], twsl[sl], LT4,
                            lambda pi_, off, n: CP("act", pT[:, 1 + off:1 + off + n], pb[pi_][:, 0:n], [pbt[pi_]], [tpT]))
                    TS("dve", uT, pT[:, 1:L + 1], convw_s[:, 1, ci:ci + 1], convb_s[:, ci:ci + 1], ALU.mult, ALU.add,
                       [tpT, tconst], [tuT])
                    STT(uT, pT[:, 0:L], convw_s[:, 0, ci:ci + 1], uT, ALU.mult, ALU.add, [tpT, tconst, tuT], [tuT])
                    STT(dst[:, cc, :], pT[:, 2:L + 2], convw_s[:, 2, ci:ci + 1], uT, ALU.mult, ALU.add,
                        [tpT, tconst, tuT], [tdst])
                S.barrier()

            proj_conv(0, zT, tz)
            for o in range(2):
                proj_conv(1 + o, gT, tg_)
                for tt in range(16):
                    pi_ = 6 + (tt % 2)
                    for cc in range(4):
                        TR(pbb[pi_][:, cc * 128:(cc + 1) * 128], zT[:, cc, tt * 128:(tt + 1) * 128], ident,
                           [tz, tconst], [pbt[pi_]])
                    CP("act" if tt % 2 else "dve", ztok[:, tt, :], pbb[pi_][:, 0:512], [pbt[pi_]], [tzt])
                for j in range(16):
                    sl = j % 2
                    S.dma(q_sp, fmb[sl][:, 0], Fm[j], writes=[tfmb[sl]])
                    S.dma(q_sp, fmb[sl][:, 1], Fm[16 + j], writes=[tfmb[sl]])
                    S.dma(q_sp, kb[sl][:, 0, :], Ksp[o, 0, j], writes=[tkb[sl]])
                    S.dma(q_sp, kb[sl][:, 1, :], Ksp[o, 1, j], writes=[tkb[sl]])
                    pr, pim = 2 * sl, 2 * sl + 1
                    for lc in range(16):
                        MM(pb[pr], fmb[sl][:, 0, lc, :], ztok[:, lc, :], lc == 0, lc == 15, [tfmb[sl], tzt], [pbt[pr]])
                    for lc in range(16):
                        MM(pb[pim], fmb[sl][:, 1, lc, :], ztok[:, lc, :], lc == 0, lc == 15, [tfmb[sl], tzt], [pbt[pim]])
                    TT("dve", tm[0], pb[pr], kb[sl][:, 0, :], ALU.mult, [pbt[pr], tkb[sl]], [ttm[0]])
                    TT("dve", tm[1], pb[pim], kb[sl][:, 1, :], ALU.mult, [pbt[pim], tkb[sl]], [ttm[1]])
                    TT("pool", Pbuf[:, j, :], tm[0], tm[1], ALU.subtract, [ttm[0], ttm[1]], [tP])
                    TT("dve", tm[2], pb[pr], kb[sl][:, 1, :], ALU.mult, [pbt[pr], tkb[sl]], [ttm[2]])
                    TT("dve", tm[3], pb[pim], kb[sl][:, 0, :], ALU.mult, [pbt[pim], tkb[sl]], [ttm[3]])
                    TT("pool", Pbuf[:, 16 + j, :], tm[2], tm[3], ALU.add, [ttm[2], ttm[3]], [tP])
                k = 0
                for tt in range(16):
                    sl = tt % 2
                    S.dma(q_sp, fib[sl], Fi[tt], writes=[tfib[sl]])
                    for cc in range(4):
                        pi_ = 4 + (k % 2)
                        k += 1
                        for fc in range(32):
                            MM(pb[pi_][:, 0:128], Pbuf[:, fc, cc * 128:(cc + 1) * 128], fib[sl][:, fc, :], fc == 0, fc == 31,
                               [tP, tfib[sl]], [pbt[pi_]])
                        TT("dve", zT[:, cc, tt * 128:(tt + 1) * 128], gT[:, cc, tt * 128:(tt + 1) * 128], pb[pi_][:, 0:128],
                           ALU.mult, [tg_, pbt[pi_]], [tz])
            S.barrier()
            A.release()

        def P4(b, zT, tz, yT, ty):
            A.mark()
            oaT = A.alloc([4, L], BF16)
            toa = S.tok("oaT")
            S.dma(q_sp, oaT, oas, writes=[toa])
            wga = [A.alloc([8, 128], BF16) for _ in range(2)]
            wgb_ = [A.alloc([8, 128], BF16) for _ in range(2)]
            wa = [A.alloc([4, 128], BF16) for _ in range(2)]
            wb_ = [A.alloc([4, 128], BF16) for _ in range(2)]
            sga = [A.alloc([512], F32) for _ in range(2)]
            sgb2 = [A.alloc([512], F32) for _ in range(2)]
            t1 = [A.alloc([512], F32) for _ in range(2)]
            t2 = [A.alloc([512], F32) for _ in range(2)]
            tw4a = S.toks(2, "w4a"); tw4b = S.toks(2, "w4b"); tw4c = S.toks(2, "w4c"); tw4d = S.toks(2, "w4d")
            wstg4 = [A.alloc([8, 128], F32) for _ in range(2)]
            twstg4 = S.toks(2, "wstg4")
            tsa = S.toks(2, "sa"); tsb = S.toks(2, "sb"); tt1 = S.toks(2, "t1"); tt2 = S.toks(2, "t2")
            k = 0
            for dc in range(8):
                sl = dc % 2
                load_w(wga[sl], 32 + dc, tw4a[sl], wstg4[0], twstg4[0])
                load_w(wgb_[sl], 40 + dc, tw4b[sl], wstg4[1], twstg4[1])
                load_w(wa[sl], dc, tw4c[sl], wstg4[0], twstg4[0], src=wpa, nk=4)
                load_w(wb_[sl], dc, tw4d[sl], wstg4[1], twstg4[1], src=wpb, nk=4)
                for (off, n) in LT4:
                    ss = k % 2
                    k += 1
                    for kc in range(8):
                        MM(pb[0], wga[sl][:, kc, :], nT[:, kc, off:off + n], kc == 0, kc == 7, [tw4a[sl], tnT], [pbt[0]])
                    for kc in range(4):
                        MM(pb[1], wa[sl][:, kc, :], oaT[:, kc, off:off + n], kc == 0, kc == 3, [tw4c[sl], toa], [pbt[1]])
                    for kc in range(8):
                        MM(pb[2], wgb_[sl][:, kc, :], nT[:, kc, off:off + n], kc == 0, kc == 7, [tw4b[sl], tnT], [pbt[2]])
                    for kc in range(4):
                        MM(pb[3], wb_[sl][:, kc, :], zT[:, kc, off:off + n], kc == 0, kc == 3, [tw4d[sl], tz], [pbt[3]])
                    ACT(sga[ss], pb[0], AF.Sigmoid, [pbt[0]], [tsa[ss]])
                    ACT(sgb2[ss], pb[2], AF.Sigmoid, [pbt[2]], [tsb[ss]])
                    TT("dve", t1[ss], sga[ss], pb[1], ALU.mult, [tsa[ss], pbt[1]], [tt1[ss]])
                    TT("dve", t2[ss], sgb2[ss], pb[3], ALU.mult, [tsb[ss], pbt[3]], [tt2[ss]])
                    TT("pool", yT[:, dc, off:off + n], t1[ss], t2[ss], ALU.add, [tt1[ss], tt2[ss]], [ty])
            S.barrier()
            A.release()

        def P5(b, yT, ty):
            for blk in range(2):
                A.mark()
                W = 1024
                base = blk * W
                hb = A.alloc([8, W], F32)
                hbtok = S.tok("hb5")
                S.dma(q_sp, hb, hs[b, :, :, base:base + W], writes=[hbtok])
                A.mark()
                wo = [A.alloc([8, 128], BF16) for _ in range(2)]
                two = S.toks(2, "wo")
                wstg5 = [A.alloc([8, 128], F32) for _ in range(2)]
                twstg5 = S.toks(2, "wstg5")
                tiles = [(0, 512, b), (512, 512, b)]
                k = 0
                for dc in range(8):
                    sl = dc % 2
                    load_w(wo[sl], dc, two[sl], wstg5[sl], twstg5[sl], src=wout)
                    for (off, n, col) in tiles:
                        pi_ = 6 + (k % 2)
                        k += 1
                        for kc in range(8):
                            MM(pb[pi_][:, 0:n], wo[sl][:, kc, :], yT[:, kc, base + off:base + off + n], kc == 0, kc == 7,
                               [two[sl], ty], [pbt[pi_]])
                        STT(hb[:, dc, off:off + n], pb[pi_][:, 0:n], mv(5, dc, col), hb[:, dc, off:off + n],
                            ALU.mult, ALU.add, [pbt[pi_], tmod, hbtok], [hbtok])
                S.barrier()
                A.release()
                ffn_block(1, hb, hbtok, tiles, 6, W)
                A.mark()
                sqs = [A.alloc([8, 512], BF16) for _ in range(2)]
                rsts = [A.alloc([512], F32) for _ in range(2)]
                tsqs = S.toks(2, "nrm5q"); trsts = S.toks(2, "nrm5r")
                rstd_multi(hb, [(off, n) for (off, n, col) in tiles], sqs, tsqs, rsts, trsts, hbtok, ones_d)
                for tix, (off, n, col) in enumerate(tiles):
                    rst, trst = rsts[tix], trsts[tix]
                    for dc in range(8):
                        STT(hb[:, dc, off:off + n], hb[:, dc, off:off + n], fnw_s[:, dc:dc + 1], rst[:, 0:n],
                            ALU.mult, ALU.mult, [hbtok, tconst, trst], [hbtok])
                S.dma(q_sp, out_t[b][:, base:base + W].rearrange("(dc p) t -> p dc t", p=128), hb, reads=[hbtok], evtok=hbtok)
                S.barrier()
                A.release()
                A.release()

        tnT = S.tok("nT")
        nT = None
        for b in range(nb):
            A.mark()
            nT = A.alloc([8, T], BF16)
            blocks = [
                [(0, 512, b), (512, 256, b)],
                [(768, 512, b), (1280, 256, b)],
                [(1536, 512, b), (2048, 256, 2)],
            ]
            for bi, tl in enumerate(blocks):
                A.mark()
                W = 768
                base = bi * 768
                hb = A.alloc([8, W], F32)
                hbtok = S.tok("hb")
                pst = [A.alloc([8, 512], F32)]
                tps = S.toks(1, "pos")
                k = 0
                for (off, n, col) in tl:
                    lo = off - base
                    if col == 2:
                        S.dma(q_sp, hb[:, :, lo:lo + n], ctx_t[b].rearrange("(dc p) t -> p dc t", p=128), writes=[hbtok])
                    else:
                        S.dma(q_sp, hb[:, :, lo:lo + n],
                              x_t[b][:, off:off + n].rearrange("(dc p) t -> p dc t", p=128), writes=[hbtok])
                        S.dma(q_sp, pst[0][:, :, 0:n], pos_t[:, off:off + n].rearrange("(dc p) t -> p dc t", p=128),
                              writes=[tps[0]])
                        for dc in range(8):
                            k += 1
                            TT("dve", hb[:, dc, lo:lo + n], hb[:, dc, lo:lo + n],
                               pst[0][:, dc, 0:n], ALU.add, [hbtok, tps[0]], [hbtok])
                ltiles = [(off - base, n, col) for (off, n, col) in tl]
                ffn_block(0, hb, hbtok, ltiles, 0, W)
                A.mark()
                sqs = [A.alloc([8, 512], BF16) for _ in range(2)]
                rsts = [A.alloc([512], F32) for _ in range(2)]
                tmp = [A.alloc([512], F32) for _ in range(2)]
                tsqs = S.toks(2, "nrmq"); trsts = S.toks(2, "nrmr")
                ttmp = S.toks(2, "tmp")
                k = 0
                rstd_multi(hb, [(off - base, n) for (off, n, col) in tl], sqs, tsqs, rsts, trsts, hbtok, ones_d)
                for tix, (off, n, col) in enumerate(tl):
                    lo = off - base
                    rst, trst = rsts[tix], trsts[tix]
                    if col != 2:
                        S.dma(q_sp, hs[b, :, :, off:off + n], hb[:, :, lo:lo + n], reads=[hbtok], evtok=hbtok)
                    for dc in range(8):
                        sl = k % 2
                        k += 1
                        TT("dve", tmp[sl][:, 0:n], hb[:, dc, lo:lo + n], rst[:, 0:n], ALU.mult, [hbtok, trst], [ttmp[sl]])
                        ACT(nT[:, dc, off:off + n], tmp[sl][:, 0:n], AF.Identity, [ttmp[sl], tmod], [tnT],
                            bias=mv(3, dc, col), scale=mv(4, dc, col))
                S.barrier()
                A.release()
                A.release()
            if b == 0:
                dump("nT0", nT, [128, 8, T], tnT, BF16)
                dump("hs0", hs[0], [128, 8, L], tnT)
            if stop_after == "p1":
                break
            tz = S.tok("zT")
            P2(b)
            zT = A.alloc([4, L], BF16)
            if b == 0:
                dump("oas", oas, [128, 4, L], tz, BF16)
            if stop_after == "p2":
                break
            P3(b, zT, tz)
            if b == 0:
                dump("zT", zT, [128, 4, L], tz, BF16)
            if stop_after == "p3":
                break
            yT = A.alloc_top([8, L], BF16)
            ty = S.tok("yT")
            P4(b, zT, tz, yT, ty)
            if b == 0:
                dump("yT", yT, [128, 8, L], ty, BF16)
            S.barrier()
            A.release()
            if stop_after == "p4":
                break
            P5(b, yT, ty)
            A.release_top()
        S.barrier()
        S.emit()
    return nc, din, dbg_out


def _bf(a):
    return np.ascontiguousarray(a).astype(ml_dtypes.bfloat16)


_CONST_CACHE = {}


def host_consts():
    if _CONST_CACHE:
        return _CONST_CACHE
    f32 = np.float32
    quarter = D // 4
    omega = (1.0 / (10000.0 ** (np.arange(quarter, dtype=f32) / quarter))).astype(f32)
    rows = L // 64
    ar = np.arange(rows, dtype=f32)[:, None] * omega
    ac = np.arange(64, dtype=f32)[:, None] * omega
    er = np.concatenate([np.sin(ar), np.cos(ar)], axis=-1)
    ec = np.concatenate([np.sin(ac), np.cos(ac)], axis=-1)
    emb = np.concatenate([np.broadcast_to(er[:, None, :], (rows, 64, D // 2)),
                          np.broadcast_to(ec[None, :, :], (rows, 64, D // 2))], axis=-1).reshape(L, D)
    pos_t = np.ascontiguousarray(emb.T.astype(f32))
    p = np.arange(L, dtype=f32)
    t = p / (L - 1)
    w = (2.0 * math.pi * p / L).astype(f32)
    fb = np.linspace(1e-4, 15, 16, dtype=f32)
    ang = w[:, None] * fb[None, :]
    z = np.concatenate([t[:, None], np.cos(ang), -np.sin(ang)], axis=-1).astype(f32)
    zfeat = np.ascontiguousarray(z.T)
    max_decay = math.log(1e-2) / 0.3
    min_decay = math.log(1e-2) / 1.5
    deltas = np.abs(np.linspace(min_decay, max_decay, 512, dtype=f32))
    window = (np.exp(-t[:, None] * deltas[None, :]) + 0.05).astype(f32)
    win = np.ascontiguousarray(window.reshape(16, 128, 512).transpose(1, 0, 2))
    wsh = np.zeros_like(window)
    wsh[1:] = window[:-1]
    wins = np.ascontiguousarray(wsh.reshape(16, 128, 512).transpose(1, 0, 2))
    winl = np.ascontiguousarray(window[L - 1:L])
    N = 2 * L
    tt = np.arange(L, dtype=np.float64)[:, None]
    ff = (np.arange(L, dtype=np.float64) + 0.5)[None, :]
    angm = 2.0 * np.pi * tt * ff / N
    Fc = np.cos(angm)
    Fs = -np.sin(angm)
    F = np.concatenate([Fc, Fs], axis=1)
    Fm = F.reshape(16, 128, 32, 128).transpose(2, 1, 0, 3)
    Fi = (2.0 / N) * F.T
    Fi = Fi.reshape(32, 128, 16, 128).transpose(2, 1, 0, 3)
    masks = np.zeros((64, 2, 64), f32)
    si = np.arange(64)[:, None]
    ti = np.arange(64)[None, :]
    masks[:, 0, :] = (si <= ti)
    masks[:, 1, :] = (si >= ti)
    _CONST_CACHE.update(dict(pos_t=pos_t, zfeat=zfeat, win=win, wins=wins, winl=winl, Fm=_bf(Fm), Fi=_bf(Fi),
                             ident=_bf(np.eye(128, dtype=f32)), masks=masks))
    return _CONST_CACHE


def prep_core(inp, bsel):
    f32 = np.float32
    c = host_consts()
    m = dict(c)
    nbl = len(bsel)
    m["x_t"] = np.ascontiguousarray(np.stack([inp["x"][b].T for b in bsel]))
    m["ctx_t"] = np.ascontiguousarray(np.stack([inp["ctx"][b].T for b in bsel]))
    ct = np.zeros((4, D), f32)
    for i, b in enumerate(bsel):
        ct[i] = inp["c"][b]
    ct[2] = inp["c_ctx"]
    m["c_t"] = np.ascontiguousarray(ct.reshape(4, 8, 128).transpose(2, 1, 0))
    m["mod_w"] = np.ascontiguousarray(inp["mod_w"][0])
    m["mod_b"] = np.ascontiguousarray(inp["mod_b"][0].reshape(72, 128).T)
    m["wg"] = np.ascontiguousarray(inp["ffn_w_gate"][0])
    m["wu"] = np.ascontiguousarray(inp["ffn_w_up"][0])
    m["wd"] = np.ascontiguousarray(inp["ffn_w_down"][0])
    m["w_in"] = np.ascontiguousarray(inp["w_in"][0])
    m["lbl"] = np.ascontiguousarray(inp["hgrn_lb_logits"].reshape(2, 2, 4, 128).transpose(3, 0, 1, 2).reshape(128, 2, 8))
    m["normw"] = np.ascontiguousarray(inp["hgrn_norm_w"][0].reshape(128, 1))
    m["convw"] = np.ascontiguousarray(inp["hyena_conv_w"][0].reshape(3, 12, 128).transpose(2, 0, 1))
    m["convb"] = np.ascontiguousarray(inp["hyena_conv_b"][0].reshape(12, 128).T)
    m["hw1"] = np.ascontiguousarray(inp["hyena_w1"][0])
    m["hb1"] = np.ascontiguousarray(inp["hyena_b1"][0].reshape(64, 1))
    m["hf1"] = np.ascontiguousarray(inp["hyena_freq1"][0].reshape(64, 1))
    m["hw2"] = np.ascontiguousarray(inp["hyena_w2"][0])
    m["hb2"] = np.ascontiguousarray(inp["hyena_b2"][0].reshape(64, 1))
    m["hf2"] = np.ascontiguousarray(inp["hyena_freq2"][0].reshape(64, 1))
    m["hw3"] = np.ascontiguousarray(inp["hyena_w3"][0])
    m["hbias"] = np.ascontiguousarray(np.broadcast_to(inp["hyena_bias"][0][None], (128, 2, 512)))
    m["wpa"] = np.ascontiguousarray(inp["w_proj_a"][0])
    m["wpb"] = np.ascontiguousarray(inp["w_proj_b"][0])
    m["wout"] = np.ascontiguousarray(inp["w_out"][0])
    m["fnw"] = np.ascontiguousarray(inp["final_norm_w"].reshape(8, 128).T)
    return {k: (v if v.dtype == ml_dtypes.bfloat16 else v.astype(f32)) for k, v in m.items()}


_PROG = {}


def kernel(**inputs):
    inputs = {k: np.asarray(v) for k, v in inputs.items()}
    if "full" not in _PROG:
        _PROG["full"] = build_program(nb=2)
    nc, din, _ = _PROG["full"]
    in_maps = []
    for core in range(NCORE):
        m = prep_core(inputs, [2 * core, 2 * core + 1])
        in_maps.append({k: m[k] for k in din})
    res = run_bass_kernel_spmd(nc, in_maps, core_ids=list(range(NCORE)))
    out = np.empty((16, L, D), np.float32)
    for core in range(NCORE):
        o = res.results[core]["out_t"]
        for i in range(2):
            out[2 * core + i] = o[i].T
    return out
```

```python
import numpy as np
from contextlib import ExitStack
import concourse.bass as bass
import concourse.mybir as mybir

F32 = mybir.dt.float32
BF16 = mybir.dt.bfloat16
AF = mybir.ActivationFunctionType
ALU = mybir.AluOpType

ENGS = ("pe", "act", "dve", "pool", "sp")
EPOCH = 12000
SAME_ENGINE_SYNC = True


class Tok:
    __slots__ = ("name", "w", "w_eng", "r", "dsem", "dcount")

    def __init__(self, name):
        self.name = name
        self.w = None
        self.w_eng = None
        self.r = []
        self.dsem = None
        self.dcount = 0


class Sched:
    def __init__(self, nc, stack):
        self.nc = nc
        self.stack = stack
        self.ops = {e: [] for e in ENGS}
        self.cnt = {e: 0 for e in ENGS}
        self.sem = {e: None for e in ENGS}
        self.nsem = 0
        self.waited = {e: {} for e in ENGS}
        self.latest = {}
        self.n_ops = 0
        self.dpool = []
        self.dtoks = []

    def new_sem(self, name):
        self.nsem += 1
        return self.stack.enter_context(self.nc.semaphore(f"{name}_{self.nsem}"))

    def tok(self, name="t"):
        return Tok(name)

    def toks(self, n, name="t"):
        return [Tok(f"{name}{i}") for i in range(n)]

    def _next_event(self, eng):
        if self.sem[eng] is None or self.cnt[eng] >= EPOCH:
            self.sem[eng] = self.new_sem(f"s_{eng}")
            self.cnt[eng] = 0
        self.cnt[eng] += 1
        return (self.sem[eng], self.cnt[eng])

    def _need(self, eng, waits, ev):
        if ev is None:
            return
        sem, val = ev[0], ev[1]
        k = id(sem)
        if self.waited[eng].get(k, 0) >= val:
            return
        cur = waits.get(k)
        if cur is None or cur[1] < val:
            waits[k] = (sem, val)

    def _collect(self, eng, reads, writes, is_dma):
        waits = {}
        for t in reads:
            if t.w is not None:
                if t.w_eng == eng and not is_dma:
                    if eng != "pe" and SAME_ENGINE_SYNC:
                        self._need(eng, waits, t.w)
                else:
                    self._need(eng, waits, t.w)
        for t in writes:
            if t.w is not None:
                if t.w_eng == eng and not is_dma:
                    if eng != "pe" and SAME_ENGINE_SYNC:
                        self._need(eng, waits, t.w)
                elif is_dma and t.w_eng == "dma":
                    pass
                else:
                    self._need(eng, waits, t.w)
            for (sem, val, reng) in t.r:
                if reng == eng and not is_dma and (eng == "pe" or not SAME_ENGINE_SYNC):
                    continue
                self._need(eng, waits, (sem, val))
        wl = list(waits.values())
        for (sem, val) in wl:
            self.waited[eng][id(sem)] = val
        return wl

    def op(self, eng, fn, reads=(), writes=()):
        wl = self._collect(eng, reads, writes, False)
        ev = self._next_event(eng)
        self.ops[eng].append((wl, fn, ev[0], 1))
        self.waited[eng][id(ev[0])] = max(self.waited[eng].get(id(ev[0]), 0), 0)
        self.latest[id(ev[0])] = ev
        for t in writes:
            t.w = ev
            t.w_eng = eng
            t.r = []
        for t in reads:
            if t in writes:
                continue
            t.r = [x for x in t.r if x[2] != eng] + [(ev[0], ev[1], eng)]
        self.n_ops += 1
        return ev

    def dma(self, queue, out, in_, reads=(), writes=(), evtok=None, **kw):
        if evtok is None:
            evtok = writes[0] if len(writes) else reads[0]
        wl = self._collect(queue, reads, writes, True)
        if evtok.dsem is None:
            if self.dpool:
                evtok.dsem, evtok.dcount = self.dpool.pop()
            else:
                evtok.dsem = self.new_sem("d")
                evtok.dcount = 0
            self.dtoks.append(evtok)
        assert evtok.dcount < 60000
        evtok.dcount += 16
        ev = (evtok.dsem, evtok.dcount)
        self.latest[id(ev[0])] = ev

        def fn(e, out=out, in_=in_, kw=kw):
            return e.dma_start(out=out, in_=in_, **kw)
        self.ops[queue].append((wl, fn, ev[0], 16))
        for t in writes:
            t.w = ev
            t.w_eng = "dma"
            t.r = []
        for t in reads:
            t.r = [x for x in t.r if x[0] is not ev[0]] + [(ev[0], ev[1], "dma")]
        self.n_ops += 1
        return ev

    def barrier(self, engines=ENGS, exclude_engs=(), exclude_toks=()):
        skip = set()
        for e in exclude_engs:
            if self.sem[e] is not None:
                skip.add(id(self.sem[e]))
        for t in exclude_toks:
            if t.dsem is not None:
                skip.add(id(t.dsem))
        engines = tuple(e for e in engines if e not in exclude_engs)
        evs = [v for k, v in self.latest.items() if k not in skip]
        for e in engines:
            wl = []
            for (sem, val) in evs:
                if self.waited[e].get(id(sem), 0) >= val:
                    continue
                if sem is self.sem[e]:
                    continue
                wl.append((sem, val))
                self.waited[e][id(sem)] = val
            if wl:
                self.ops[e].append((wl, None, None, 0))
        if tuple(engines) == tuple(ENGS):
            for t in self.dtoks:
                if t.dcount < 40000:
                    self.dpool.append((t.dsem, t.dcount))
                t.dsem = None
                t.dcount = 0
            self.dtoks = []

    def emit(self):
        nc = self.nc
        with nc.Block() as block:
            def mk(engname):
                def body(e):
                    for (wl, fn, sem, inc) in self.ops[engname]:
                        for (s, v) in wl:
                            e.wait_ge(s, v)
                        if fn is not None:
                            ins = fn(e)
                            ins.then_inc(sem, inc)
                return body
            block.tensor(mk("pe"))
            block.scalar(mk("act"))
            block.vector(mk("dve"))
            block.gpsimd(mk("pool"))
            block.sync(mk("sp"))


class Arena:
    def __init__(self, nc, stack, words, name="arena"):
        self.t = stack.enter_context(nc.sbuf_tensor(name, [128, words], F32))
        self.words = words
        self.top = 0
        self.marks = []
        self.hi = words
        self.his = []

    def mark(self):
        self.marks.append(self.top)

    def release(self):
        self.top = self.marks.pop()

    def alloc_top(self, shape, dtype):
        n = int(np.prod(shape))
        w = n if dtype == F32 else (n + 1) // 2
        w = (w + 7) // 8 * 8
        self.his.append(self.hi)
        self.hi -= w
        assert self.hi >= self.top
        save = self.top
        self.top = self.hi
        hi_save = self.hi
        self.hi = self.words + 10 ** 9
        ap = self.alloc(shape, dtype)
        self.top = save
        self.hi = hi_save
        return ap

    def release_top(self):
        self.hi = self.his.pop()

    def alloc(self, shape, dtype):
        n = int(np.prod(shape))
        if dtype == F32:
            w = n
        elif dtype == BF16:
            w = (n + 1) // 2
        else:
            raise ValueError(dtype)
        w = (w + 7) // 8 * 8
        if self.top + w > min(self.words, self.hi):
            raise MemoryError(f"arena overflow: need {w} at {self.top} of {self.words}")
        ap = self.t[:, self.top:self.top + w]
        self.top += w
        if dtype == BF16:
            ap = ap.bitcast(BF16)[:, 0:n]
        else:
            ap = ap[:, 0:n]
        if len(shape) == 2:
            ap = ap.rearrange("p (a b) -> p a b", b=shape[1])
        elif len(shape) == 3:
            ap = ap.rearrange("p (a b c) -> p a b c", b=shape[1], c=shape[2])
        return ap


import math
import ml_dtypes
from concourse.bass_utils import run_bass_kernel_spmd

D = 1024
L = 2048
LC = 256
T = L + LC
DFF = 2816
NF = DFF // 128
NCORE = 8
PI = math.pi


def _wrap(S):
    def ACT(out, in_, func, reads, writes, bias=None, scale=None):
        kw = {}
        if bias is not None:
            kw["bias"] = bias
        if scale is not None:
            kw["scale"] = scale
        return S.op("act", lambda e: e.activation(out=out, in_=in_, func=func, **kw), reads, writes)

    def TT(eng, out, in0, in1, op, reads, writes):
        return S.op(eng, lambda e: e.tensor_tensor(out=out, in0=in0, in1=in1, op=op), reads, writes)

    def TS(eng, out, in0, s1, s2, op0, op1, reads, writes):
        if op1 is None:
            return S.op(eng, lambda e: e.tensor_scalar(out=out, in0=in0, scalar1=s1, scalar2=None, op0=op0), reads, writes)
        return S.op(eng, lambda e: e.tensor_scalar(out=out, in0=in0, scalar1=s1, scalar2=s2, op0=op0, op1=op1), reads, writes)

    def STT(out, in0, scalar, in1, op0, op1, reads, writes):
        return S.op("dve", lambda e: e.scalar_tensor_tensor(out=out, in0=in0, scalar=scalar, in1=in1, op0=op0, op1=op1), reads, writes)

    def MM(out, lhsT, rhs, start, stop, reads, writes):
        return S.op("pe", lambda e: e.matmul(out, lhsT=lhsT, rhs=rhs, start=start, stop=stop), reads, writes)

    def TR(out, in_, ident, reads, writes):
        return S.op("pe", lambda e: e.transpose(out, in_, ident), reads, writes)

    def CP(eng, out, in_, reads, writes):
        if eng == "act":
            return S.op("act", lambda e: e.activation(out=out, in_=in_, func=AF.Copy), reads, writes)
        return S.op(eng, lambda e: e.tensor_copy(out=out, in_=in_), reads, writes)

    def MS(eng, ap, val, writes):
        return S.op(eng, lambda e: e.memset(ap, val), (), writes)
    return ACT, TT, TS, STT, MM, TR, CP, MS


def build_program(nb=2, stop_after=None, dbg=()):
    nc = bass.Bass("TRN2", target_bir_lowering=False)
    din = {}

    def inp(name, shape, dt=F32):
        din[name] = nc.dram_tensor(name, list(shape), dt, kind="ExternalInput").ap()
        return din[name]

    x_t = inp("x_t", [nb, D, L])
    ctx_t = inp("ctx_t", [nb, D, LC])
    pos_t = inp("pos_t", [D, L])
    c_t = inp("c_t", [128, 8, 4])
    mod_w = inp("mod_w", [D, 9 * D])
    mod_b = inp("mod_b", [128, 72])
    wg = inp("wg", [2, D, DFF])
    wu = inp("wu", [2, D, DFF])
    wd = inp("wd", [2, DFF, D])
    w_in = inp("w_in", [D, 6144])
    lbl = inp("lbl", [128, 2, 8])
    normw = inp("normw", [128, 1])
    convw = inp("convw", [128, 3, 12])
    convb = inp("convb", [128, 12])
    hw1 = inp("hw1", [33, 64])
    hb1 = inp("hb1", [64, 1])
    hf1 = inp("hf1", [64, 1])
    hw2 = inp("hw2", [64, 64])
    hb2 = inp("hb2", [64, 1])
    hf2 = inp("hf2", [64, 1])
    hw3 = inp("hw3", [64, 2048])
    hbias = inp("hbias", [128, 2, 512])
    wpa = inp("wpa", [512, D])
    wpb = inp("wpb", [512, D])
    wout = inp("wout", [D, D])
    fnw = inp("fnw", [128, 8])
    zfeat = inp("zfeat", [33, L])
    win = inp("win", [128, 16, 512])
    wins = inp("wins", [128, 16, 512])
    winl = inp("winl", [1, 512])
    Fm = inp("Fm", [32, 128, 16, 128], BF16)
    Fi = inp("Fi", [16, 128, 32, 128], BF16)
    ident_d = inp("ident", [128, 128], BF16)
    masks_d = inp("masks", [64, 2, 64])

    out_t = nc.dram_tensor("out_t", [nb, D, L], F32, kind="ExternalOutput").ap()
    dbg_out = {}

    def scr(name, shape, dt):
        return nc.dram_tensor(name, list(shape), dt, kind="Internal").ap()

    wgb = scr("wgb", [2, 128, NF, 8, 128], BF16)
    wub = scr("wub", [2, 128, NF, 8, 128], BF16)
    wdb = scr("wdb", [2, 128, 8, NF, 128], BF16)
    winb = scr("winb", [128, 48, 8, 128], BF16)
    wpab = scr("wpab", [128, 8, 4, 128], BF16)
    wpbb = scr("wpbb", [128, 8, 4, 128], BF16)
    woutb = scr("woutb", [128, 8, 8, 128], BF16)
    Ksp = scr("Ksp", [2, 2, 16, 128, 512], F32)
    hs = scr("hs", [nb, 128, 8, L], F32)
    oas = scr("oas", [128, 4, L], BF16)

    with ExitStack() as st:
        S = Sched(nc, st)
        ACT, TT, TS, STT, MM, TR, CP, MS = _wrap(S)
        A = Arena(nc, st, 48000)
        pbk = [st.enter_context(nc.psum_tensor(f"pb{i}", [128, 512], F32)) for i in range(8)]
        pb = [p[:] for p in pbk]
        pbt = S.toks(8, "pb")
        pbb = [p[:].bitcast(BF16) for p in pbk]
        q_sp = "sp"

        def dump(name, ap, shape, tok, dt=F32):
            if name not in dbg:
                return
            d = nc.dram_tensor("dbg_" + name, list(shape), dt, kind="ExternalOutput").ap()
            dbg_out[name] = d
            S.dma(q_sp, d, ap, reads=[tok], evtok=tok)

        ident = A.alloc([128], BF16)
        ones_d = A.alloc([128], BF16)
        ones_v = A.alloc([128], BF16)
        ones_f = A.alloc([128], F32)
        modT = A.alloc([72, 4], F32)
        lbT = A.alloc([8], F32)
        omlT = A.alloc([8], F32)
        normw_s = A.alloc([1], F32)
        convw_s = A.alloc([3, 12], F32)
        convb_s = A.alloc([12], F32)
        fnw_s = A.alloc([8], F32)
        masks_s = A.alloc([2, 64], F32)
        epsb = A.alloc([1], F32)
        tconst = S.tok("const")
        S.dma(q_sp, ident, ident_d, writes=[tconst])
        S.dma(q_sp, normw_s, normw, writes=[tconst])
        S.dma(q_sp, convw_s, convw, writes=[tconst])
        S.dma(q_sp, convb_s, convb, writes=[tconst])
        S.dma(q_sp, fnw_s, fnw, writes=[tconst])
        S.dma(q_sp, masks_s[0:64], masks_d, writes=[tconst])
        tc2 = S.tok("const2")
        MS("pool", ones_d, 1.0 / 1024.0, [tc2])
        MS("pool", ones_v, 1.0 / 128.0, [tc2])
        MS("pool", ones_f, 1.0, [tc2])
        MS("pool", epsb, 1e-6, [tc2])
        CONST = [tconst, tc2]

        def conv_units():
            NSL = 3
            sfA = [A.alloc_top([8, 512], F32) for _ in range(NSL)]
            sbA = [A.alloc_top([4, 8, 128], BF16) for _ in range(NSL)]
            tf = S.toks(NSL, "cvf")
            tb = S.toks(NSL, "cvb")
            cvtoks.extend(tf + tb)
            it = 0
            ce = 0
            for s_ in range(2):
                for (src, dst) in ((wg[s_], wgb[s_]), (wu[s_], wub[s_])):
                    for g0 in range(0, NF, 4):
                        g = min(4, NF - g0)
                        sl = it % NSL
                        it += 1
                        S.dma("sp", sfA[sl][:, :, 0:g * 128],
                              src[:, g0 * 128:(g0 + g) * 128].rearrange("(k p) n -> p k n", p=128), writes=[tf[sl]])
                        for gi in range(g):
                            eng = "pool"
                            ce += 1
                            CP(eng, sbA[sl][:, gi, :, :], sfA[sl][:, :, gi * 128:(gi + 1) * 128], [tf[sl]], [tb[sl]])
                        S.dma("sp", dst[:, g0:g0 + g], sbA[sl][:, 0:g], reads=[tb[sl]], evtok=tb[sl])
                        yield
                for dc in range(8):
                    sl = it % NSL
                    it += 1
                    sfv = sfA[sl].rearrange("p a b -> p (a b)")[:, 0:NF * 128].rearrange("p (f c) -> p f c", c=128)
                    sbv = sbA[sl].rearrange("p a b c -> p (a b c)")[:, 0:NF * 128].rearrange("p (f c) -> p f c", c=128)
                    S.dma("sp", sfv, wd[s_][:, dc * 128:(dc + 1) * 128].rearrange("(f p) n -> p f n", p=128), writes=[tf[sl]])
                    for hf_ in range(2):
                        eng = "pool"
                        ce += 1
                        CP(eng, sbv[:, hf_ * 11:(hf_ + 1) * 11, :], sfv[:, hf_ * 11:(hf_ + 1) * 11, :], [tf[sl]], [tb[sl]])
                    S.dma("sp", wdb[s_][:, dc], sbv, reads=[tb[sl]], evtok=tb[sl])
                    yield

        cvtoks = []
        cgen = conv_units()
        for _ in cgen:
            pass

        def pbarrier():
            S.barrier(exclude_engs=("pool", "sp"), exclude_toks=cvtoks)

        def pump(n=1):
            for _ in range(n):
                if next(cgen, "done") == "done":
                    return

        A.mark()
        cts = A.alloc([8, 4], F32)
        scs = A.alloc([8, 4], F32)
        lbs = A.alloc([2, 8], F32)
        mbs = A.alloc([72], F32)
        mwb = [A.alloc([8, 1024], F32) for _ in range(2)]
        mwt = S.toks(2, "mw")
        tct, tsc, tlb, tmod = S.toks(4, "p0a")
        S.dma("act", cts, c_t, writes=[tct])
        S.dma("act", lbs, lbl, writes=[tlb])
        S.dma("act", mbs, mod_b, writes=[tlb])
        ACT(scs, cts, AF.Silu, [tct], [tsc])
        TT("dve", lbs[:, 0, :], lbs[:, 0, :], lbs[:, 1, :], ALU.subtract, [tlb], [tlb])
        ACT(lbT, lbs[:, 0, :], AF.Sigmoid, [tlb], [tmod])
        TS("dve", omlT, lbT, -1.0, 1.0, ALU.mult, ALU.add, [tmod], [tmod])
        for j in range(9):
            sl = j % 2
            S.dma("act", mwb[sl], mod_w[:, j * 1024:(j + 1) * 1024].rearrange("(kc p) n -> p kc n", p=128),
                  writes=[mwt[sl]])
            pump(2)
            for dc in range(8):
                o0 = (j * 8 + dc) * 4
                for kc in range(8):
                    MM(pb[0][:, o0:o0 + 4], mwb[sl][:, kc, dc * 128:(dc + 1) * 128], scs[:, kc, :],
                       kc == 0, kc == 7, [mwt[sl], tsc], [pbt[0]])
        psm = pb[0][:, 0:288].rearrange("p (a b) -> p a b", b=4)
        for col in range(4):
            TT("dve", modT[:, :, col], psm[:, :, col], mbs, ALU.add, [pbt[0], tlb], [tmod])
        for j in (1, 4, 7):
            TS("dve", modT[:, j * 8:(j + 1) * 8, :], modT[:, j * 8:(j + 1) * 8, :], 1.0, None, ALU.add, None, [tmod], [tmod])
        for j in (2, 8):
            TS("dve", modT[:, j * 8:(j + 1) * 8, :], modT[:, j * 8:(j + 1) * 8, :], 0.5, None, ALU.mult, None, [tmod], [tmod])
        dump("modT", modT, [128, 72, 4], tmod)
        pbarrier()
        A.release()
        CONST.append(tmod)

        def mv(j, dc, col):
            return modT[:, j * 8 + dc, col:col + 1]

        A.mark()
        w3s = A.alloc([2048], F32)
        hsm = A.alloc([8], F32)
        h2p = A.alloc([L + 8], F32)
        winl_s = A.alloc([512], F32)
        rn = A.alloc([2, 512], F32)
        hbias_s = A.alloc([2, 512], F32)
        A.mark()
        zf = A.alloc([L], F32)
        w1s = A.alloc([64], F32)
        w2s = A.alloc([64], F32)
        h1 = A.alloc([L], F32)
        arg = A.alloc([512], F32)
        wtmp = A.alloc([512], F32)
        tk0, th1, th2, targ, theo, trn = S.toks(6, "p0c")
        thf = S.toks(2, "hf"); thb = S.toks(2, "hb"); tab = S.toks(2, "ab"); twn = S.toks(2, "wn")
        S.dma("act", zf[0:33], zfeat, writes=[tk0])
        S.dma("act", w1s[0:33], hw1, writes=[tk0])
        S.dma("act", w2s[0:64], hw2, writes=[tk0])
        S.dma("act", w3s[0:64], hw3, writes=[tk0])
        S.dma("act", hsm[0:64, 0:1], hb1, writes=[tk0])
        S.dma("act", hsm[0:64, 1:2], hf1, writes=[tk0])
        S.dma("act", hsm[0:64, 2:3], hb2, writes=[tk0])
        S.dma("act", hsm[0:64, 3:4], hf2, writes=[tk0])
        S.dma("act", winl_s[0:1], winl, writes=[tk0])
        S.dma("act", hbias_s, hbias, writes=[tk0])
        TT("dve", hsm[0:64, 4:5], hsm[0:64, 0:1], hsm[0:64, 1:2], ALU.mult, [tk0], [tk0])
        TT("dve", hsm[0:64, 5:6], hsm[0:64, 2:3], hsm[0:64, 3:4], ALU.mult, [tk0], [tk0])
        MS("dve", h2p[0:64, 0:1], 0.0, [th2])

        def sin_layer(wsb, kdim, src, dst, dst_off, fcol, fbcol, tsrc, tdst):
            for ti in range(4):
                MM(pb[1][0:64, :], wsb[0:kdim, 0:64], src[0:kdim, ti * 512:(ti + 1) * 512], True, True,
                   [tk0, tsrc], [pbt[1]])
                TS("dve", arg[0:64], pb[1][0:64, :], hsm[0:64, fcol:fcol + 1], hsm[0:64, fbcol:fbcol + 1],
                   ALU.mult, ALU.add, [pbt[1], tk0], [targ])
                for _ in range(2):
                    wrap_once(arg[0:64], targ)
                TS("dve", arg[0:64], arg[0:64], 3.14159, -3.14159, ALU.min, ALU.max, [targ], [targ])
                ACT(dst[0:64, dst_off + ti * 512: dst_off + (ti + 1) * 512], arg[0:64], AF.Sin, [targ], [tdst])

        twt = S.tok("wtmp")

        def wrap_once(ap, tok):
            TS("dve", wtmp[0:64], ap, PI, -2.0 * PI, ALU.is_gt, ALU.mult, [tok], [twt])
            TT("dve", ap, ap, wtmp[0:64], ALU.add, [tok, twt], [tok])
            TS("dve", wtmp[0:64], ap, -PI, 2.0 * PI, ALU.is_lt, ALU.mult, [tok], [twt])
            TT("dve", ap, ap, wtmp[0:64], ALU.add, [tok, twt], [tok])

        sin_layer(w1s, 33, zf, h1, 0, 1, 4, tk0, th1)
        sin_layer(w2s, 64, h1, h2p, 1, 3, 5, th1, th2)
        dump("h2", h2p[0:64, 1:L + 1], [64, L], th2)
        pbarrier()
        A.release()
        heo = A.alloc([16, 2, 512], BF16)
        hfb = [A.alloc([512], F32) for _ in range(2)]
        hbb = [A.alloc([512], F32) for _ in range(2)]
        absb = [A.alloc([512], F32) for _ in range(2)]
        winb_s = [A.alloc([512], F32) for _ in range(2)]
        winsb_s = [A.alloc([512], F32) for _ in range(2)]

        fmb = [A.alloc([2, 16, 128], BF16) for _ in range(3)]
        tfm = S.toks(3, "fm")
        kst = [A.alloc([2, 512], F32) for _ in range(2)]
        tks = S.toks(2, "kst")
        it = 0
        jj = 0
        for o in range(2):
            for lt in range(16):
                sl = lt % 2
                pump(1)
                S.dma("act", winb_s[sl], win[:, lt, :], writes=[twn[sl]])
                S.dma("act", winsb_s[sl], wins[:, lt, :], writes=[twn[sl]])
                MM(pb[2], h2p[0:64, 1 + lt * 128: 1 + (lt + 1) * 128], w3s[0:64, o * 1024: o * 1024 + 512], True, True,
                   [th2, tk0], [pbt[2]])
                MM(pb[3], h2p[0:64, lt * 128:(lt + 1) * 128], w3s[0:64, o * 1024 + 512: o * 1024 + 1024], True, True,
                   [th2, tk0], [pbt[3]])
                TT("dve", hfb[sl], pb[2], winb_s[sl], ALU.mult, [pbt[2], twn[sl]], [thf[sl]])
                TT("dve", hbb[sl], pb[3], winsb_s[sl], ALU.mult, [pbt[3], twn[sl]], [thb[sl]])
                ACT(absb[0], hfb[sl], AF.Abs, [thf[sl]], [tab[0]])
                MM(pb[4 + o], ones_f, absb[0], lt == 0, False, [tab[0], tc2], [pbt[4 + o]])
                ACT(absb[1], hbb[sl], AF.Abs, [thb[sl]], [tab[1]])
                MM(pb[4 + o], ones_f, absb[1], False, False, [tab[1], tc2], [pbt[4 + o]])
                TT("dve", heo[:, lt, 0, :], hfb[sl], hbb[sl], ALU.add, [thf[sl], thb[sl]], [theo])
                TT("dve", heo[:, lt, 1, :], hfb[sl], hbb[sl], ALU.subtract, [thf[sl], thb[sl]], [theo])
            MM(pb[2][0:1, :], h2p[0:64, L:L + 1], w3s[0:64, o * 1024 + 512: o * 1024 + 1024], True, True,
               [th2, tk0], [pbt[2]])
            TT("dve", hfb[0][0:1], pb[2][0:1, :], winl_s[0:1], ALU.mult, [pbt[2], tk0], [thf[0]])
            ACT(absb[0][0:1], hfb[0][0:1], AF.Abs, [thf[0]], [tab[0]])
            MM(pb[4 + o], ones_f[0:1, :], absb[0][0:1], False, True, [tab[0], tc2], [pbt[4 + o]])
            TS("dve", rn[:, o, :], pb[4 + o], 1e-6, None, ALU.add, None, [pbt[4 + o]], [trn])
            S.op("dve", lambda e, o=o: e.reciprocal(out=rn[:, o, :], in_=rn[:, o, :]), [trn], [trn])
            for j in range(16):
                sl = jj % 3
                jj += 1
                pump(1)
                S.dma("act", fmb[sl][:, 0], Fm[j], writes=[tfm[sl]])
                S.dma("act", fmb[sl][:, 1], Fm[16 + j], writes=[tfm[sl]])
                ks = it % 2
                it += 1
                for lc in range(16):
                    MM(pb[6], fmb[sl][:, 0, lc, :], heo[:, lc, 0, :], lc == 0, lc == 15, [tfm[sl], theo], [pbt[6]])
                for lc in range(16):
                    MM(pb[7], fmb[sl][:, 1, lc, :], heo[:, lc, 1, :], lc == 0, lc == 15, [tfm[sl], theo], [pbt[7]])
                TT("dve", kst[ks][:, 0, :], pb[6], rn[:, o, :], ALU.mult, [pbt[6], trn], [tks[ks]])
                TT("dve", kst[ks][:, 0, :], kst[ks][:, 0, :], hbias_s[:, o, :], ALU.add, [tks[ks], tk0], [tks[ks]])
                TT("dve", kst[ks][:, 1, :], pb[7], rn[:, o, :], ALU.mult, [pbt[7], trn], [tks[ks]])
                S.dma("act", Ksp[o, 0, j], kst[ks][:, 0, :], reads=[tks[ks]], evtok=tks[ks])
                S.dma("act", Ksp[o, 1, j], kst[ks][:, 1, :], reads=[tks[ks]], evtok=tks[ks])
        dump("rn", rn, [128, 2, 512], trn)
        pump(1000)
        S.barrier()
        A.release()
        for _ in range(6):
            A.release_top()
        if stop_after == "p0c":
            S.emit()
            return nc, din, dbg_out

        def rstd_of(hb, off, n, sq, tsq, rst, trst, hbtok, ones_ap, nchunks=8):
            for dc in range(nchunks):
                ACT(sq[:, dc, 0:n], hb[:, dc, off:off + n], AF.Square, [hbtok], [tsq])
            for dc in range(nchunks):
                MM(pb[7][:, 0:n], ones_ap, sq[:, dc, 0:n], dc == 0, dc == nchunks - 1, [tsq, tc2], [pbt[7]])
            ACT(rst[:, 0:n], pb[7][:, 0:n], AF.Ln, [pbt[7]], [trst], bias=epsb[:, 0:1])
            ACT(rst[:, 0:n], rst[:, 0:n], AF.Exp, [trst], [trst], scale=-0.5)

        def rstd_multi(hb, tl2, sqs, tsqs, rsts, trsts, hbtok, ones_ap):
            assert len(tl2) <= 2
            for i, (off, n) in enumerate(tl2):
                for dc in range(8):
                    ACT(sqs[i][:, dc, 0:n], hb[:, dc, off:off + n], AF.Square, [hbtok], [tsqs[i]])
            for i, (off, n) in enumerate(tl2):
                for dc in range(8):
                    MM(pb[7 - i][:, 0:n], ones_ap, sqs[i][:, dc, 0:n], dc == 0, dc == 7, [tsqs[i], tc2], [pbt[7 - i]])
            for i, (off, n) in enumerate(tl2):
                ACT(rsts[i][:, 0:n], pb[7 - i][:, 0:n], AF.Ln, [pbt[7 - i]], [trsts[i]], bias=epsb[:, 0:1])
            for i, (off, n) in enumerate(tl2):
                ACT(rsts[i][:, 0:n], rsts[i][:, 0:n], AF.Exp, [trsts[i]], [trsts[i]], scale=-0.5)

        def ffn_block(s, hb, hbtok, tiles, j0, W):
            A.mark()
            nbk = A.alloc([8, W], BF16)
            act = A.alloc([NF, W], BF16)
            actf = act.rearrange("p f w -> p (f w)")
            sqs = [actf[:, i * 4096:(i + 1) * 4096].rearrange("p (a b) -> p a b", b=512) for i in range(2)]
            rsts = [actf[:, 8192 + i * 1024: 8192 + (i + 1) * 1024].bitcast(F32) for i in range(2)]
            tmp = [A.alloc([512], F32) for _ in range(2)]
            sg = [A.alloc([512], BF16) for _ in range(2)]
            NSA, NSB = 4, 3
            wgs = [A.alloc([8, 128], BF16) for _ in range(NSA)]
            wus = [A.alloc([8, 128], BF16) for _ in range(NSA)]
            wds = [A.alloc([NF, 128], BF16) for _ in range(NSB)]
            tnb = S.toks(len(tiles), "nb")
            tact = S.toks(len(tiles), "act")
            tsqs = S.toks(2, "nrmq"); trsts = S.toks(2, "nrmr")
            ttmp = S.toks(2, "tmp"); tsg = S.toks(2, "sg")
            twg = S.toks(NSA, "wg"); twd = S.toks(NSB, "wd")
            k = 0
            assert len(tiles) == 2
            rstd_multi(hb, [(off, n) for (off, n, col) in tiles], sqs, tsqs, rsts, trsts, hbtok, ones_d)
            for ti, (off, n, col) in enumerate(tiles):
                rst, trst = rsts[ti], trsts[ti]
                for dc in range(8):
                    sl = k % 2
                    k += 1
                    TT("dve", tmp[sl][:, 0:n], hb[:, dc, off:off + n], rst[:, 0:n], ALU.mult, [hbtok, trst], [ttmp[sl]])
                    ACT(nbk[:, dc, off:off + n], tmp[sl][:, 0:n], AF.Identity, [ttmp[sl], tmod], [tnb[ti]],
                        bias=mv(j0, dc, col), scale=mv(j0 + 1, dc, col))
            k = 0
            for f in range(NF):
                sl = f % NSA
                S.dma(q_sp, wgs[sl], wgb[s, :, f], writes=[twg[sl]])
                S.dma(q_sp, wus[sl], wub[s, :, f], writes=[twg[sl]])
                for ti, (off, n, col) in enumerate(tiles):
                    pg = (2 * k) % 4
                    pu = pg + 1
                    ss = k % 2
                    k += 1
                    for kc in range(8):
                        MM(pb[pg][:, 0:n], wgs[sl][:, kc, :], nbk[:, kc, off:off + n], kc == 0, kc == 7,
                           [twg[sl], tnb[ti]], [pbt[pg]])
                    for kc in range(8):
                        MM(pb[pu][:, 0:n], wus[sl][:, kc, :], nbk[:, kc, off:off + n], kc == 0, kc == 7,
                           [twg[sl], tnb[ti]], [pbt[pu]])
                    ACT(sg[ss][:, 0:n], pb[pg][:, 0:n], AF.Silu, [pbt[pg]], [tsg[ss]])
                    TT("dve", act[:, f, off:off + n], sg[ss][:, 0:n], pb[pu][:, 0:n], ALU.mult,
                       [tsg[ss], pbt[pu]], [tact[ti]])
            k = 0
            for dc in range(8):
                sl = dc % NSB
                S.dma(q_sp, wds[sl], wdb[s, :, dc], writes=[twd[sl]])
                for ti, (off, n, col) in enumerate(tiles):
                    pp = 4 + (k % 2)
                    k += 1
                    for f in range(NF):
                        MM(pb[pp][:, 0:n], wds[sl][:, f, :], act[:, f, off:off + n], f == 0, f == NF - 1,
                           [twd[sl], tact[ti]], [pbt[pp]])
                    STT(hb[:, dc, off:off + n], pb[pp][:, 0:n], mv(j0 + 2, dc, col), hb[:, dc, off:off + n],
                        ALU.mult, ALU.add, [pbt[pp], tmod, hbtok], [hbtok])
            S.barrier()
            A.release()

        def load_w(dst, cg, tok, stg, tstg, src=None, nk=8):
            srcm = w_in if src is None else src
            S.dma(q_sp, stg[:, 0:nk, :], srcm[:, cg * 128:(cg + 1) * 128].rearrange("(kc p) n -> p kc n", p=128), writes=[tstg])
            CP("pool", dst, stg[:, 0:nk, :], [tstg], [tok])

        def proj_fm(wsb, wtok, tiles_, consume):
            for i, (off, n) in enumerate(tiles_):
                pi_ = 6 + (i % 2)
                for kc in range(8):
                    MM(pb[pi_][:, 0:n], wsb[:, kc, :], nT[:, kc, off:off + n], kc == 0, kc == 7, [wtok, tnT], [pbt[pi_]])
                consume(pi_, off, n)

        LT4 = [(0, 512), (512, 512), (1024, 512), (1536, 512)]
        LT5 = LT4 + [(2048, 256)]

        def P2(b):
            for h in range(4):
                A.mark()
                wv, wff, wfb, wq, wgt = [A.alloc([8, 128], BF16) for _ in range(5)]
                tw = S.toks(5, "hw")
                wstg = [A.alloc([8, 128], F32) for _ in range(2)]
                twstg = S.toks(2, "wstg")
                for wi, (wsb, cg, tk) in enumerate(((wv, h, tw[0]), (wq, 12 + h, tw[3]), (wgt, 16 + h, tw[4]), (wff, 4 + h, tw[1]), (wfb, 8 + h, tw[2]))):
                    load_w(wsb, cg, tk, wstg[wi % 2], twstg[wi % 2])
                vtok = A.alloc([36, 128], BF16)
                kk = A.alloc([T], F32)
                lfb = A.alloc([T], F32)
                Bb = A.alloc([T], F32)
                qf = A.alloc([L], F32)
                onesr = A.alloc([T], BF16)
                qt_ = [A.alloc([L], BF16) for _ in range(2)]
                kt_ = [A.alloc([T], BF16) for _ in range(2)]
                ktok = [A.alloc([36, 128], BF16) for _ in range(2)]
                o_ = [A.alloc([1, L], F32) for _ in range(2)]
                sgb = A.alloc([L], BF16)
                Sf = [A.alloc([128], F32) for _ in range(2)]
                Sb2 = [[A.alloc([128], BF16) for _ in range(2)] for _ in range(2)]
                tSb2 = [S.toks(2, "Sb2") for _ in range(2)]
                tmpS = [A.alloc([128], F32) for _ in range(2)]
                gcol = [A.alloc([36], F32) for _ in range(2)]
                bref = A.alloc([36], F32)
                scm = [A.alloc([64], BF16) for _ in range(2)]
                sq = A.alloc([1, 512], BF16)
                rst = A.alloc([512], F32)
                tmp = A.alloc([512], F32)
                oab = A.alloc([L], BF16)
                (tvt, tkk, tlf, tB, tq, tone, tsgb, tbref, tsq, trst, ttmp, toab) = S.toks(12, "p2")
                tqt = S.toks(2, "qt"); tkt = S.toks(2, "kt"); tktok = S.toks(2, "ktok"); to = S.toks(2, "o")
                tSf = S.toks(2, "Sf"); tSb = S.toks(2, "Sb"); ttS = S.toks(2, "tS"); tg = S.toks(2, "g"); tscm = S.toks(2, "scm")
                MS("pool", onesr, 1.0, [tone])
                for g0 in range(0, 36, 4):
                    pi_ = 6 + ((g0 // 4) % 2)
                    for ci in range(4):
                        c = g0 + ci
                        for kc in range(8):
                            MM(pb[pi_][0:64, ci * 128:(ci + 1) * 128], nT[:, kc, c * 64:(c + 1) * 64], wv[:, kc, :],
                               kc == 0, kc == 7, [tw[0], tnT], [pbt[pi_]])
                    CP("act", vtok[0:64, g0:g0 + 4, :], pb[pi_][0:64, :].rearrange("p (a b) -> p a b", b=128), [pbt[pi_]], [tvt])
                proj_fm(wq, tw[3], LT4, lambda pi_, off, n: ACT(qf[:, off:off + n], pb[pi_][:, 0:n], AF.Silu, [pbt[pi_]], [tq]))
                proj_fm(wgt, tw[4], LT4, lambda pi_, off, n: ACT(sgb[:, off:off + n], pb[pi_][:, 0:n], AF.Silu, [pbt[pi_]], [tsgb]))
                B3 = Bb.rearrange("p (c s) -> p c s", s=64)
                lf3 = lfb.rearrange("p (c s) -> p c s", s=64)
                for dr in range(2):
                    lbc = dr * 4 + h
                    wsb, wtk = (wff, tw[1]) if dr == 0 else (wfb, tw[2])
                    proj_fm(wsb, wtk, LT5, lambda pi_, off, n: ACT(kk[:, off:off + n], pb[pi_][:, 0:n], AF.Sigmoid,
                                                                    [pbt[pi_]], [tkk], scale=-1.0))
                    TS("dve", kk, kk, omlT[:, lbc:lbc + 1], None, ALU.mult, None, [tkk, tmod], [tkk])
                    ACT(lfb, kk, AF.Ln, [tkk], [tlf], bias=ones_f[:, 0:1], scale=-1.0)
                    S.op("dve", lambda e: e.tensor_tensor_scan(out=Bb, data0=onesr, data1=lfb, initial=0.0,
                                                                op0=ALU.mult, op1=ALU.add), [tone, tlf], [tB])
                    if dr == 0:
                        TT("dve", bref, B3[:, :, 0], lf3[:, :, 0], ALU.subtract, [tB, tlf], [tbref])
                        TT("dve", lf3, B3, bref.unsqueeze(2).broadcast_to([128, 36, 64]), ALU.subtract, [tB, tbref, tlf], [tlf])
                    else:
                        TT("dve", lf3, lf3, B3, ALU.subtract, [tB, tlf], [tlf])
                        TT("dve", lf3, lf3, B3[:, :, 63:64].broadcast_to([128, 36, 64]), ALU.add, [tB, tlf], [tlf])
                    ACT(Bb, lfb, AF.Exp, [tlf], [tB])
                    if dr == 0:
                        CP("dve", gcol[dr], B3[:, :, 63], [tB], [tg[dr]])
                    else:
                        CP("dve", gcol[dr], B3[:, :, 0], [tB], [tg[dr]])
                    TT("dve", qt_[dr], qf, Bb[:, 0:L], ALU.mult, [tq, tB], [tqt[dr]])
                    ACT(lfb, lfb, AF.Exp, [tlf], [tlf], scale=-1.0)
                    TT("dve", kt_[dr], kk, lfb, ALU.mult, [tkk, tlf], [tkt[dr]])
                    for g0 in range(0, 36, 4):
                        pi_ = 6 + ((g0 // 4) % 2)
                        for ci in range(4):
                            c = g0 + ci
                            TR(pbb[pi_][0:64, ci * 128:(ci + 1) * 128], kt_[dr][:, c * 64:(c + 1) * 64], ident,
                               [tkt[dr], tconst], [pbt[pi_]])
                        CP("act", ktok[dr][0:64, g0:g0 + 4, :], pbb[pi_][0:64, 0:512].rearrange("p (a b) -> p a b", b=128),
                           [pbt[pi_]], [tktok[dr]])
                    MS("pool", tmpS[dr], 0.0, [ttS[dr]])
                    MS("pool", Sb2[dr][0], 0.0, [tSb2[dr][0]])
                orders = [[32, 33, 34, 35] + list(range(32)), [35, 34, 33, 32] + list(range(31, -1, -1))]
                for step in range(36):
                    par = step % 2
                    cc_ = [orders[dr][step] for dr in range(2)]
                    lat = cc_[0] < 32
                    if lat:
                        for dr in range(2):
                            c = cc_[dr]
                            MM(pb[dr][0:64, 0:64], kt_[dr][:, c * 64:(c + 1) * 64], qt_[dr][:, c * 64:(c + 1) * 64], True, True,
                               [tkt[dr], tqt[dr]], [pbt[dr]])
                    for dr in range(2):
                        c = cc_[dr]
                        MM(pb[4 + dr][:, 0:128], ktok[dr][0:64, c, :], vtok[0:64, c, :], True, True, [tktok[dr], tvt], [pbt[4 + dr]])
                    if lat:
                        for dr in range(2):
                            TT("dve", scm[dr][0:64], pb[dr][0:64, 0:64], masks_s[0:64, dr, :], ALU.mult,
                               [pbt[dr], tconst], [tscm[dr]])
                        for dr in range(2):
                            c = cc_[dr]
                            MM(pb[2 + dr][:, 0:64], vtok[0:64, c, :], scm[dr][0:64], True, False, [tvt, tscm[dr]], [pbt[2 + dr]])
                            MM(pb[2 + dr][:, 0:64], Sb2[dr][par], qt_[dr][:, c * 64:(c + 1) * 64], False, True,
                               [tSb2[dr][par], tqt[dr]], [pbt[2 + dr]])
                            CP("act", o_[dr][:, 0, c * 64:(c + 1) * 64], pb[2 + dr][:, 0:64], [pbt[2 + dr]], [to[dr]])
                    for dr in range(2):
                        c = cc_[dr]
                        cp_ = orders[dr][step - 1] if step > 0 else c
                        STT(tmpS[dr], tmpS[dr], gcol[dr][:, cp_:cp_ + 1], pb[4 + dr][:, 0:128], ALU.mult, ALU.add,
                            [ttS[dr], tg[dr], pbt[4 + dr]], [ttS[dr]])
                        TS("dve", Sb2[dr][1 - par], tmpS[dr], gcol[dr][:, c:c + 1], None, ALU.mult, None, [ttS[dr], tg[dr]],
                           [tSb2[dr][1 - par]])
                TT("pool", o_[0], o_[0], o_[1], ALU.add, [to[0], to[1]], [to[0]])
                for (off, n) in LT4:
                    rstd_of(o_[0], off, n, sq, tsq, rst, trst, to[0], ones_v, nchunks=1)
                    TT("dve", tmp[:, 0:n], o_[0][:, 0, off:off + n], rst[:, 0:n], ALU.mult, [to[0], trst], [ttmp])
                    STT(oab[:, off:off + n], tmp[:, 0:n], normw_s[:, 0:1], sgb[:, off:off + n], ALU.mult, ALU.mult,
                        [ttmp, tconst, tsgb], [toab])
                S.dma(q_sp, oas[:, h, :], oab, reads=[toab], evtok=toab)
                S.barrier()
                A.release()

        def P3(b, zT, tz):
            A.mark()
            gT = A.alloc([4, L], BF16)
            ztok = A.alloc([16, 512], BF16)
            pb_base = A.top
            Pbuf = A.alloc([32, 512], BF16)
            pb_end = A.top
            A.top = pb_base
            pT = A.alloc([L + 8], F32)
            uT = A.alloc([L], F32)
            wsl = [A.alloc([8, 128], BF16) for _ in range(2)]
            wstg3 = [A.alloc([8, 128], F32) for _ in range(2)]
            twstg3 = S.toks(2, "wstg3")
            assert A.top <= pb_end
            A.top = pb_end
            fib = [A.alloc([32, 128], BF16) for _ in range(2)]
            fmb = [A.alloc([2, 16, 128], BF16) for _ in range(2)]
            kb = [A.alloc([2, 512], F32) for _ in range(2)]
            tm = [A.alloc([512], F32) for _ in range(4)]
            tg_, tzt, tP, tpT, tuT = S.toks(5, "p3")
            twsl = S.toks(2, "wsl"); tfib = S.toks(2, "fib"); tfmb = S.toks(2, "fmb"); tkb = S.toks(2, "kb"); ttm = S.toks(4, "tm")

            def proj_conv(part, dst, tdst):
                S.barrier()
                MS("pool", pT[:, 0:1], 0.0, [tpT])
                MS("pool", pT[:, L + 1:L + 2], 0.0, [tpT])
                for cc in range(4):
                    sl = cc % 2
                    ci = part * 4 + cc
                    load_w(wsl[sl], 20 + ci, twsl[sl], wstg3[sl], twstg3[sl])
                    proj_fm(wsl[sl], twsl[sl], LT4,
                            lambda pi_, off, n: CP("act", pT[:, 1 + off:1 + off + n], pb[pi_][:, 0:n], [pbt[pi_]], [tpT]))
                    TS("dve", uT, pT[:, 1:L + 1], convw_s[:, 1, ci:ci + 1], convb_s[:, ci:ci + 1], ALU.mult, ALU.add,
                       [tpT, tconst], [tuT])
                    STT(uT, pT[:, 0:L], convw_s[:, 0, ci:ci + 1], uT, ALU.mult, ALU.add, [tpT, tconst, tuT], [tuT])
                    STT(dst[:, cc, :], pT[:, 2:L + 2], convw_s[:, 2, ci:ci + 1], uT, ALU.mult, ALU.add,
                        [tpT, tconst, tuT], [tdst])
                S.barrier()

            proj_conv(0, zT, tz)
            for o in range(2):
                proj_conv(1 + o, gT, tg_)
                for tt in range(16):
                    pi_ = 6 + (tt % 2)
                    for cc in range(4):
                        TR(pbb[pi_][:, cc * 128:(cc + 1) * 128], zT[:, cc, tt * 128:(tt + 1) * 128], ident,
                           [tz, tconst], [pbt[pi_]])
                    CP("act" if tt % 2 else "dve", ztok[:, tt, :], pbb[pi_][:, 0:512], [pbt[pi_]], [tzt])
                for j in range(16):
                    sl = j % 2
                    S.dma(q_sp, fmb[sl][:, 0], Fm[j], writes=[tfmb[sl]])
                    S.dma(q_sp, fmb[sl][:, 1], Fm[16 + j], writes=[tfmb[sl]])
                    S.dma(q_sp, kb[sl][:, 0, :], Ksp[o, 0, j], writes=[tkb[sl]])
                    S.dma(q_sp, kb[sl][:, 1, :], Ksp[o, 1, j], writes=[tkb[sl]])
                    pr, pim = 2 * sl, 2 * sl + 1
                    for lc in range(16):
                        MM(pb[pr], fmb[sl][:, 0, lc, :], ztok[:, lc, :], lc == 0, lc == 15, [tfmb[sl], tzt], [pbt[pr]])
                    for lc in range(16):
                        MM(pb[pim], fmb[sl][:, 1, lc, :], ztok[:, lc, :], lc == 0, lc == 15, [tfmb[sl], tzt], [pbt[pim]])
                    TT("dve", tm[0], pb[pr], kb[sl][:, 0, :], ALU.mult, [pbt[pr], tkb[sl]], [ttm[0]])
                    TT("dve", tm[1], pb[pim], kb[sl][:, 1, :], ALU.mult, [pbt[pim], tkb[sl]], [ttm[1]])
                    TT("pool", Pbuf[:, j, :], tm[0], tm[1], ALU.subtract, [ttm[0], ttm[1]], [tP])
                    TT("dve", tm[2], pb[pr], kb[sl][:, 1, :], ALU.mult, [pbt[pr], tkb[sl]], [ttm[2]])
                    TT("dve", tm[3], pb[pim], kb[sl][:, 0, :], ALU.mult, [pbt[pim], tkb[sl]], [ttm[3]])
                    TT("pool", Pbuf[:, 16 + j, :], tm[2], tm[3], ALU.add, [ttm[2], ttm[3]], [tP])
                k = 0
                for tt in range(16):
                    sl = tt % 2
                    S.dma(q_sp, fib[sl], Fi[tt], writes=[tfib[sl]])
                    for cc in range(4):
                        pi_ = 4 + (k % 2)
                        k += 1
                        for fc in range(32):
                            MM(pb[pi_][:, 0:128], Pbuf[:, fc, cc * 128:(cc + 1) * 128], fib[sl][:, fc, :], fc == 0, fc == 31,
                               [tP, tfib[sl]], [pbt[pi_]])
                        TT("dve", zT[:, cc, tt * 128:(tt + 1) * 128], gT[:, cc, tt * 128:(tt + 1) * 128], pb[pi_][:, 0:128],
                           ALU.mult, [tg_, pbt[pi_]], [tz])
            S.barrier()
            A.release()

        def P4(b, zT, tz, yT, ty):
            A.mark()
            oaT = A.alloc([4, L], BF16)
            toa = S.tok("oaT")
            S.dma(q_sp, oaT, oas, writes=[toa])
            wga = [A.alloc([8, 128], BF16) for _ in range(2)]
            wgb_ = [A.alloc([8, 128], BF16) for _ in range(2)]
            wa = [A.alloc([4, 128], BF16) for _ in range(2)]
            wb_ = [A.alloc([4, 128], BF16) for _ in range(2)]
            sga = [A.alloc([512], F32) for _ in range(2)]
            sgb2 = [A.alloc([512], F32) for _ in range(2)]
            t1 = [A.alloc([512], F32) for _ in range(2)]
            t2 = [A.alloc([512], F32) for _ in range(2)]
            tw4a = S.toks(2, "w4a"); tw4b = S.toks(2, "w4b"); tw4c = S.toks(2, "w4c"); tw4d = S.toks(2, "w4d")
            wstg4 = [A.alloc([8, 128], F32) for _ in range(2)]
            twstg4 = S.toks(2, "wstg4")
            tsa = S.toks(2, "sa"); tsb = S.toks(2, "sb"); tt1 = S.toks(2, "t1"); tt2 = S.toks(2, "t2")
            k = 0
            for dc in range(8):
                sl = dc % 2
                load_w(wga[sl], 32 + dc, tw4a[sl], wstg4[0], twstg4[0])
                load_w(wgb_[sl], 40 + dc, tw4b[sl], wstg4[1], twstg4[1])
                load_w(wa[sl], dc, tw4c[sl], wstg4[0], twstg4[0], src=wpa, nk=4)
                load_w(wb_[sl], dc, tw4d[sl], wstg4[1], twstg4[1], src=wpb, nk=4)
                for (off, n) in LT4:
                    ss = k % 2
                    k += 1
                    for kc in range(8):
                        MM(pb[0], wga[sl][:, kc, :], nT[:, kc, off:off + n], kc == 0, kc == 7, [tw4a[sl], tnT], [pbt[0]])
                    for kc in range(4):
                        MM(pb[1], wa[sl][:, kc, :], oaT[:, kc, off:off + n], kc == 0, kc == 3, [tw4c[sl], toa], [pbt[1]])
                    for kc in range(8):
                        MM(pb[2], wgb_[sl][:, kc, :], nT[:, kc, off:off + n], kc == 0, kc == 7, [tw4b[sl], tnT], [pbt[2]])
                    for kc in range(4):
                        MM(pb[3], wb_[sl][:, kc, :], zT[:, kc, off:off + n], kc == 0, kc == 3, [tw4d[sl], tz], [pbt[3]])
                    ACT(sga[ss], pb[0], AF.Sigmoid, [pbt[0]], [tsa[ss]])
                    ACT(sgb2[ss], pb[2], AF.Sigmoid, [pbt[2]], [tsb[ss]])
                    TT("dve", t1[ss], sga[ss], pb[1], ALU.mult, [tsa[ss], pbt[1]], [tt1[ss]])
                    TT("dve", t2[ss], sgb2[ss], pb[3], ALU.mult, [tsb[ss], pbt[3]], [tt2[ss]])
                    TT("pool", yT[:, dc, off:off + n], t1[ss], t2[ss], ALU.add, [tt1[ss], tt2[ss]], [ty])
            S.barrier()
            A.release()

        def P5(b, yT, ty):
            for blk in range(2):
                A.mark()
                W = 1024
                base = blk * W
                hb = A.alloc([8, W], F32)
                hbtok = S.tok("hb5")
                S.dma(q_sp, hb, hs[b, :, :, base:base + W], writes=[hbtok])
                A.mark()
                wo = [A.alloc([8, 128], BF16) for _ in range(2)]
                two = S.toks(2, "wo")
                wstg5 = [A.alloc([8, 128], F32) for _ in range(2)]
                twstg5 = S.toks(2, "wstg5")
                tiles = [(0, 512, b), (512, 512, b)]
                k = 0
                for dc in range(8):
                    sl = dc % 2
                    load_w(wo[sl], dc, two[sl], wstg5[sl], twstg5[sl], src=wout)
                    for (off, n, col) in tiles:
                        pi_ = 6 + (k % 2)
                        k += 1
                        for kc in range(8):
                            MM(pb[pi_][:, 0:n], wo[sl][:, kc, :], yT[:, kc, base + off:base + off + n], kc == 0, kc == 7,
                               [two[sl], ty], [pbt[pi_]])
                        STT(hb[:, dc, off:off + n], pb[pi_][:, 0:n], mv(5, dc, col), hb[:, dc, off:off + n],
                            ALU.mult, ALU.add, [pbt[pi_], tmod, hbtok], [hbtok])
                S.barrier()
                A.release()
                ffn_block(1, hb, hbtok, tiles, 6, W)
                A.mark()
                sqs = [A.alloc([8, 512], BF16) for _ in range(2)]
                rsts = [A.alloc([512], F32) for _ in range(2)]
                tsqs = S.toks(2, "nrm5q"); trsts = S.toks(2, "nrm5r")
                rstd_multi(hb, [(off, n) for (off, n, col) in tiles], sqs, tsqs, rsts, trsts, hbtok, ones_d)
                for tix, (off, n, col) in enumerate(tiles):
                    rst, trst = rsts[tix], trsts[tix]
                    for dc in range(8):
                        STT(hb[:, dc, off:off + n], hb[:, dc, off:off + n], fnw_s[:, dc:dc + 1], rst[:, 0:n],
                            ALU.mult, ALU.mult, [hbtok, tconst, trst], [hbtok])
                S.dma(q_sp, out_t[b][:, base:base + W].rearrange("(dc p) t -> p dc t", p=128), hb, reads=[hbtok], evtok=hbtok)
                S.barrier()
                A.release()
                A.release()

        tnT = S.tok("nT")
        nT = None
        for b in range(nb):
            A.mark()
            nT = A.alloc([8, T], BF16)
            blocks = [
                [(0, 512, b), (512, 256, b)],
                [(768, 512, b), (1280, 256, b)],
                [(1536, 512, b), (2048, 256, 2)],
            ]
            for bi, tl in enumerate(blocks):
                A.mark()
                W = 768
                base = bi * 768
                hb = A.alloc([8, W], F32)
                hbtok = S.tok("hb")
                pst = [A.alloc([8, 512], F32)]
                tps = S.toks(1, "pos")
                k = 0
                for (off, n, col) in tl:
                    lo = off - base
                    if col == 2:
                        S.dma(q_sp, hb[:, :, lo:lo + n], ctx_t[b].rearrange("(dc p) t -> p dc t", p=128), writes=[hbtok])
                    else:
                        S.dma(q_sp, hb[:, :, lo:lo + n],
                              x_t[b][:, off:off + n].rearrange("(dc p) t -> p dc t", p=128), writes=[hbtok])
                        S.dma(q_sp, pst[0][:, :, 0:n], pos_t[:, off:off + n].rearrange("(dc p) t -> p dc t", p=128),
                              writes=[tps[0]])
                        for dc in range(8):
                            k += 1
                            TT("dve", hb[:, dc, lo:lo + n], hb[:, dc, lo:lo + n],
                               pst[0][:, dc, 0:n], ALU.add, [hbtok, tps[0]], [hbtok])
                ltiles = [(off - base, n, col) for (off, n, col) in tl]
                ffn_block(0, hb, hbtok, ltiles, 0, W)
                A.mark()
                sqs = [A.alloc([8, 512], BF16) for _ in range(2)]
                rsts = [A.alloc([512], F32) for _ in range(2)]
                tmp = [A.alloc([512], F32) for _ in range(2)]
                tsqs = S.toks(2, "nrmq"); trsts = S.toks(2, "nrmr")
                ttmp = S.toks(2, "tmp")
                k = 0
                rstd_multi(hb, [(off - base, n) for (off, n, col) in tl], sqs, tsqs, rsts, trsts, hbtok, ones_d)
                for tix, (off, n, col) in enumerate(tl):
                    lo = off - base
                    rst, trst = rsts[tix], trsts[tix]
                    if col != 2:
                        S.dma(q_sp, hs[b, :, :, off:off + n], hb[:, :, lo:lo + n], reads=[hbtok], evtok=hbtok)
                    for dc in range(8):
                        sl = k % 2
                        k += 1
                        TT("dve", tmp[sl][:, 0:n], hb[:, dc, lo:lo + n], rst[:, 0:n], ALU.mult, [hbtok, trst], [ttmp[sl]])
                        ACT(nT[:, dc, off:off + n], tmp[sl][:, 0:n], AF.Identity, [ttmp[sl], tmod], [tnT],
                            bias=mv(3, dc, col), scale=mv(4, dc, col))
                S.barrier()
                A.release()
                A.release()
            if b == 0:
                dump("nT0", nT, [128, 8, T], tnT, BF16)
                dump("hs0", hs[0], [128, 8, L], tnT)
            if stop_after == "p1":
                break
            tz = S.tok("zT")
            P2(b)
            zT = A.alloc([4, L], BF16)
            if b == 0:
                dump("oas", oas, [128, 4, L], tz, BF16)
            if stop_after == "p2":
                break
            P3(b, zT, tz)
            if b == 0:
                dump("zT", zT, [128, 4, L], tz, BF16)
            if stop_after == "p3":
                break
            yT = A.alloc_top([8, L], BF16)
            ty = S.tok("yT")
            P4(b, zT, tz, yT, ty)
            if b == 0:
                dump("yT", yT, [128, 8, L], ty, BF16)
            S.barrier()
            A.release()
            if stop_after == "p4":
                break
            P5(b, yT, ty)
            A.release_top()
        S.barrier()
        S.emit()
    return nc, din, dbg_out


def _bf(a):
    return np.ascontiguousarray(a).astype(ml_dtypes.bfloat16)


_CONST_CACHE = {}


def host_consts():
    if _CONST_CACHE:
        return _CONST_CACHE
    f32 = np.float32
    quarter = D // 4
    omega = (1.0 / (10000.0 ** (np.arange(quarter, dtype=f32) / quarter))).astype(f32)
    rows = L // 64
    ar = np.arange(rows, dtype=f32)[:, None] * omega
    ac = np.arange(64, dtype=f32)[:, None] * omega
    er = np.concatenate([np.sin(ar), np.cos(ar)], axis=-1)
    ec = np.concatenate([np.sin(ac), np.cos(ac)], axis=-1)
    emb = np.concatenate([np.broadcast_to(er[:, None, :], (rows, 64, D // 2)),
                          np.broadcast_to(ec[None, :, :], (rows, 64, D // 2))], axis=-1).reshape(L, D)
    pos_t = np.ascontiguousarray(emb.T.astype(f32))
    p = np.arange(L, dtype=f32)
    t = p / (L - 1)
    w = (2.0 * math.pi * p / L).astype(f32)
    fb = np.linspace(1e-4, 15, 16, dtype=f32)
    ang = w[:, None] * fb[None, :]
    z = np.concatenate([t[:, None], np.cos(ang), -np.sin(ang)], axis=-1).astype(f32)
    zfeat = np.ascontiguousarray(z.T)
    max_decay = math.log(1e-2) / 0.3
    min_decay = math.log(1e-2) / 1.5
    deltas = np.abs(np.linspace(min_decay, max_decay, 512, dtype=f32))
    window = (np.exp(-t[:, None] * deltas[None, :]) + 0.05).astype(f32)
    win = np.ascontiguousarray(window.reshape(16, 128, 512).transpose(1, 0, 2))
    wsh = np.zeros_like(window)
    wsh[1:] = window[:-1]
    wins = np.ascontiguousarray(wsh.reshape(16, 128, 512).transpose(1, 0, 2))
    winl = np.ascontiguousarray(window[L - 1:L])
    N = 2 * L
    tt = np.arange(L, dtype=np.float64)[:, None]
    ff = (np.arange(L, dtype=np.float64) + 0.5)[None, :]
    angm = 2.0 * np.pi * tt * ff / N
    Fc = np.cos(angm)
    Fs = -np.sin(angm)
    F = np.concatenate([Fc, Fs], axis=1)
    Fm = F.reshape(16, 128, 32, 128).transpose(2, 1, 0, 3)
    Fi = (2.0 / N) * F.T
    Fi = Fi.reshape(32, 128, 16, 128).transpose(2, 1, 0, 3)
    masks = np.zeros((64, 2, 64), f32)
    si = np.arange(64)[:, None]
    ti = np.arange(64)[None, :]
    masks[:, 0, :] = (si <= ti)
    masks[:, 1, :] = (si >= ti)
    _CONST_CACHE.update(dict(pos_t=pos_t, zfeat=zfeat, win=win, wins=wins, winl=winl, Fm=_bf(Fm), Fi=_bf(Fi),
                             ident=_bf(np.eye(128, dtype=f32)), masks=masks))
    return _CONST_CACHE


def prep_core(inp, bsel):
    f32 = np.float32
    c = host_consts()
    m = dict(c)
    nbl = len(bsel)
    m["x_t"] = np.ascontiguousarray(np.stack([inp["x"][b].T for b in bsel]))
    m["ctx_t"] = np.ascontiguousarray(np.stack([inp["ctx"][b].T for b in bsel]))
    ct = np.zeros((4, D), f32)
    for i, b in enumerate(bsel):
        ct[i] = inp["c"][b]
    ct[2] = inp["c_ctx"]
    m["c_t"] = np.ascontiguousarray(ct.reshape(4, 8, 128).transpose(2, 1, 0))
    m["mod_w"] = np.ascontiguousarray(inp["mod_w"][0])
    m["mod_b"] = np.ascontiguousarray(inp["mod_b"][0].reshape(72, 128).T)
    m["wg"] = np.ascontiguousarray(inp["ffn_w_gate"][0])
    m["wu"] = np.ascontiguousarray(inp["ffn_w_up"][0])
    m["wd"] = np.ascontiguousarray(inp["ffn_w_down"][0])
    m["w_in"] = np.ascontiguousarray(inp["w_in"][0])
    m["lbl"] = np.ascontiguousarray(inp["hgrn_lb_logits"].reshape(2, 2, 4, 128).transpose(3, 0, 1, 2).reshape(128, 2, 8))
    m["normw"] = np.ascontiguousarray(inp["hgrn_norm_w"][0].reshape(128, 1))
    m["convw"] = np.ascontiguousarray(inp["hyena_conv_w"][0].reshape(3, 12, 128).transpose(2, 0, 1))
    m["convb"] = np.ascontiguousarray(inp["hyena_conv_b"][0].reshape(12, 128).T)
    m["hw1"] = np.ascontiguousarray(inp["hyena_w1"][0])
    m["hb1"] = np.ascontiguousarray(inp["hyena_b1"][0].reshape(64, 1))
    m["hf1"] = np.ascontiguousarray(inp["hyena_freq1"][0].reshape(64, 1))
    m["hw2"] = np.ascontiguousarray(inp["hyena_w2"][0])
    m["hb2"] = np.ascontiguousarray(inp["hyena_b2"][0].reshape(64, 1))
    m["hf2"] = np.ascontiguousarray(inp["hyena_freq2"][0].reshape(64, 1))
    m["hw3"] = np.ascontiguousarray(inp["hyena_w3"][0])
    m["hbias"] = np.ascontiguousarray(np.broadcast_to(inp["hyena_bias"][0][None], (128, 2, 512)))
    m["wpa"] = np.ascontiguousarray(inp["w_proj_a"][0])
    m["wpb"] = np.ascontiguousarray(inp["w_proj_b"][0])
    m["wout"] = np.ascontiguousarray(inp["w_out"][0])
    m["fnw"] = np.ascontiguousarray(inp["final_norm_w"].reshape(8, 128).T)
    return {k: (v if v.dtype == ml_dtypes.bfloat16 else v.astype(f32)) for k, v in m.items()}


_PROG = {}


def kernel(**inputs):
    inputs = {k: np.asarray(v) for k, v in inputs.items()}
    if "full" not in _PROG:
        _PROG["full"] = build_program(nb=2)
    nc, din, _ = _PROG["full"]
    in_maps = []
    for core in range(NCORE):
        m = prep_core(inputs, [2 * core, 2 * core + 1])
        in_maps.append({k: m[k] for k in din})
    res = run_bass_kernel_spmd(nc, in_maps, core_ids=list(range(NCORE)))
    out = np.empty((16, L, D), np.float32)
    for core in range(NCORE):
        o = res.results[core]["out_t"]
        for i in range(2):
            out[2 * core + i] = o[i].T
    return out
```

```python
import numpy as np
from contextlib import ExitStack
import concourse.bass as bass
import concourse.mybir as mybir

F32 = mybir.dt.float32
BF16 = mybir.dt.bfloat16
AF = mybir.ActivationFunctionType
ALU = mybir.AluOpType

ENGS = ("pe", "act", "dve", "pool", "sp")
EPOCH = 12000
SAME_ENGINE_SYNC = True


class Tok:
    __slots__ = ("name", "w", "w_eng", "r", "dsem", "dcount")

    def __init__(self, name):
        self.name = name
        self.w = None
        self.w_eng = None
        self.r = []
        self.dsem = None
        self.dcount = 0


class Sched:
    def __init__(self, nc, stack):
        self.nc = nc
        self.stack = stack
        self.ops = {e: [] for e in ENGS}
        self.cnt = {e: 0 for e in ENGS}
        self.sem = {e: None for e in ENGS}
        self.nsem = 0
        self.waited = {e: {} for e in ENGS}
        self.latest = {}
        self.n_ops = 0
        self.dpool = []
        self.dtoks = []

    def new_sem(self, name):
        self.nsem += 1
        return self.stack.enter_context(self.nc.semaphore(f"{name}_{self.nsem}"))

    def tok(self, name="t"):
        return Tok(name)

    def toks(self, n, name="t"):
        return [Tok(f"{name}{i}") for i in range(n)]

    def _next_event(self, eng):
        if self.sem[eng] is None or self.cnt[eng] >= EPOCH:
            self.sem[eng] = self.new_sem(f"s_{eng}")
            self.cnt[eng] = 0
        self.cnt[eng] += 1
        return (self.sem[eng], self.cnt[eng])

    def _need(self, eng, waits, ev):
        if ev is None:
            return
        sem, val = ev[0], ev[1]
        k = id(sem)
        if self.waited[eng].get(k, 0) >= val:
            return
        cur = waits.get(k)
        if cur is None or cur[1] < val:
            waits[k] = (sem, val)

    def _collect(self, eng, reads, writes, is_dma):
        waits = {}
        for t in reads:
            if t.w is not None:
                if t.w_eng == eng and not is_dma:
                    if eng != "pe" and SAME_ENGINE_SYNC:
                        self._need(eng, waits, t.w)
                else:
                    self._need(eng, waits, t.w)
        for t in writes:
            if t.w is not None:
                if t.w_eng == eng and not is_dma:
                    if eng != "pe" and SAME_ENGINE_SYNC:
                        self._need(eng, waits, t.w)
                elif is_dma and t.w_eng == "dma":
                    pass
                else:
                    self._need(eng, waits, t.w)
            for (sem, val, reng) in t.r:
                if reng == eng and not is_dma and (eng == "pe" or not SAME_ENGINE_SYNC):
                    continue
                self._need(eng, waits, (sem, val))
        wl = list(waits.values())
        for (sem, val) in wl:
            self.waited[eng][id(sem)] = val
        return wl

    def op(self, eng, fn, reads=(), writes=()):
        wl = self._collect(eng, reads, writes, False)
        ev = self._next_event(eng)
        self.ops[eng].append((wl, fn, ev[0], 1))
        self.waited[eng][id(ev[0])] = max(self.waited[eng].get(id(ev[0]), 0), 0)
        self.latest[id(ev[0])] = ev
        for t in writes:
            t.w = ev
            t.w_eng = eng
            t.r = []
        for t in reads:
            if t in writes:
                continue
            t.r = [x for x in t.r if x[2] != eng] + [(ev[0], ev[1], eng)]
        self.n_ops += 1
        return ev

    def dma(self, queue, out, in_, reads=(), writes=(), evtok=None, **kw):
        if evtok is None:
            evtok = writes[0] if len(writes) else reads[0]
        wl = self._collect(queue, reads, writes, True)
        if evtok.dsem is None:
            if self.dpool:
                evtok.dsem, evtok.dcount = self.dpool.pop()
            else:
                evtok.dsem = self.new_sem("d")
                evtok.dcount = 0
            self.dtoks.append(evtok)
        assert evtok.dcount < 60000
        evtok.dcount += 16
        ev = (evtok.dsem, evtok.dcount)
        self.latest[id(ev[0])] = ev

        def fn(e, out=out, in_=in_, kw=kw):
            return e.dma_start(out=out, in_=in_, **kw)
        self.ops[queue].append((wl, fn, ev[0], 16))
        for t in writes:
            t.w = ev
            t.w_eng = "dma"
            t.r = []
        for t in reads:
            t.r = [x for x in t.r if x[0] is not ev[0]] + [(ev[0], ev[1], "dma")]
        self.n_ops += 1
        return ev

    def barrier(self, engines=ENGS, exclude_engs=(), exclude_toks=()):
        skip = set()
        for e in exclude_engs:
            if self.sem[e] is not None:
                skip.add(id(self.sem[e]))
        for t in exclude_toks:
            if t.dsem is not None:
                skip.add(id(t.dsem))
        engines = tuple(e for e in engines if e not in exclude_engs)
        evs = [v for k, v in self.latest.items() if k not in skip]
        for e in engines:
            wl = []
            for (sem, val) in evs:
                if self.waited[e].get(id(sem), 0) >= val:
                    continue
                if sem is self.sem[e]:
                    continue
                wl.append((sem, val))
                self.waited[e][id(sem)] = val
            if wl:
                self.ops[e].append((wl, None, None, 0))
        if tuple(engines) == tuple(ENGS):
            for t in self.dtoks:
                if t.dcount < 40000:
                    self.dpool.append((t.dsem, t.dcount))
                t.dsem = None
                t.dcount = 0
            self.dtoks = []

    def emit(self):
        nc = self.nc
        with nc.Block() as block:
            def mk(engname):
                def body(e):
                    for (wl, fn, sem, inc) in self.ops[engname]:
                        for (s, v) in wl:
                            e.wait_ge(s, v)
                        if fn is not None:
                            ins = fn(e)
                            ins.then_inc(sem, inc)
                return body
            block.tensor(mk("pe"))
            block.scalar(mk("act"))
            block.vector(mk("dve"))
            block.gpsimd(mk("pool"))
            block.sync(mk("sp"))


class Arena:
    def __init__(self, nc, stack, words, name="arena"):
        self.t = stack.enter_context(nc.sbuf_tensor(name, [128, words], F32))
        self.words = words
        self.top = 0
        self.marks = []
        self.hi = words
        self.his = []

    def mark(self):
        self.marks.append(self.top)

    def release(self):
        self.top = self.marks.pop()

    def alloc_top(self, shape, dtype):
        n = int(np.prod(shape))
        w = n if dtype == F32 else (n + 1) // 2
        w = (w + 7) // 8 * 8
        self.his.append(self.hi)
        self.hi -= w
        assert self.hi >= self.top
        save = self.top
        self.top = self.hi
        hi_save = self.hi
        self.hi = self.words + 10 ** 9
        ap = self.alloc(shape, dtype)
        self.top = save
        self.hi = hi_save
        return ap

    def release_top(self):
        self.hi = self.his.pop()

    def alloc(self, shape, dtype):
        n = int(np.prod(shape))
        if dtype == F32:
            w = n
        elif dtype == BF16:
            w = (n + 1) // 2
        else:
            raise ValueError(dtype)
        w = (w + 7) // 8 * 8
        if self.top + w > min(self.words, self.hi):
            raise MemoryError(f"arena overflow: need {w} at {self.top} of {self.words}")
        ap = self.t[:, self.top:self.top + w]
        self.top += w
        if dtype == BF16:
            ap = ap.bitcast(BF16)[:, 0:n]
        else:
            ap = ap[:, 0:n]
        if len(shape) == 2:
            ap = ap.rearrange("p (a b) -> p a b", b=shape[1])
        elif len(shape) == 3:
            ap = ap.rearrange("p (a b c) -> p a b c", b=shape[1], c=shape[2])
        return ap


import math
import ml_dtypes
from concourse.bass_utils import run_bass_kernel_spmd

D = 1024
L = 2048
LC = 256
T = L + LC
DFF = 2816
NF = DFF // 128
NCORE = 8
PI = math.pi


def _wrap(S):
    def ACT(out, in_, func, reads, writes, bias=None, scale=None):
        kw = {}
        if bias is not None:
            kw["bias"] = bias
        if scale is not None:
            kw["scale"] = scale
        return S.op("act", lambda e: e.activation(out=out, in_=in_, func=func, **kw), reads, writes)

    def TT(eng, out, in0, in1, op, reads, writes):
        return S.op(eng, lambda e: e.tensor_tensor(out=out, in0=in0, in1=in1, op=op), reads, writes)

    def TS(eng, out, in0, s1, s2, op0, op1, reads, writes):
        if op1 is None:
            return S.op(eng, lambda e: e.tensor_scalar(out=out, in0=in0, scalar1=s1, scalar2=None, op0=op0), reads, writes)
        return S.op(eng, lambda e: e.tensor_scalar(out=out, in0=in0, scalar1=s1, scalar2=s2, op0=op0, op1=op1), reads, writes)

    def STT(out, in0, scalar, in1, op0, op1, reads, writes):
        return S.op("dve", lambda e: e.scalar_tensor_tensor(out=out, in0=in0, scalar=scalar, in1=in1, op0=op0, op1=op1), reads, writes)

    def MM(out, lhsT, rhs, start, stop, reads, writes):
        return S.op("pe", lambda e: e.matmul(out, lhsT=lhsT, rhs=rhs, start=start, stop=stop), reads, writes)

    def TR(out, in_, ident, reads, writes):
        return S.op("pe", lambda e: e.transpose(out, in_, ident), reads, writes)

    def CP(eng, out, in_, reads, writes):
        if eng == "act":
            return S.op("act", lambda e: e.activation(out=out, in_=in_, func=AF.Copy), reads, writes)
        return S.op(eng, lambda e: e.tensor_copy(out=out, in_=in_), reads, writes)

    def MS(eng, ap, val, writes):
        return S.op(eng, lambda e: e.memset(ap, val), (), writes)
    return ACT, TT, TS, STT, MM, TR, CP, MS


def build_program(nb=2, stop_after=None, dbg=()):
    nc = bass.Bass("TRN2", target_bir_lowering=False)
    din = {}

    def inp(name, shape, dt=F32):
        din[name] = nc.dram_tensor(name, list(shape), dt, kind="ExternalInput").ap()
        return din[name]

    x_t = inp("x_t", [nb, D, L])
    ctx_t = inp("ctx_t", [nb, D, LC])
    pos_t = inp("pos_t", [D, L])
    c_t = inp("c_t", [128, 8, 4])
    mod_w = inp("mod_w", [D, 9 * D])
    mod_b = inp("mod_b", [128, 72])
    wg = inp("wg", [2, D, DFF])
    wu = inp("wu", [2, D, DFF])
    wd = inp("wd", [2, DFF, D])
    w_in = inp("w_in", [D, 6144])
    lbl = inp("lbl", [128, 2, 8])
    normw = inp("normw", [128, 1])
    convw = inp("convw", [128, 3, 12])
    convb = inp("convb", [128, 12])
    hw1 = inp("hw1", [33, 64])
    hb1 = inp("hb1", [64, 1])
    hf1 = inp("hf1", [64, 1])
    hw2 = inp("hw2", [64, 64])
    hb2 = inp("hb2", [64, 1])
    hf2 = inp("hf2", [64, 1])
    hw3 = inp("hw3", [64, 2048])
    hbias = inp("hbias", [128, 2, 512])
    wpa = inp("wpa", [512, D])
    wpb = inp("wpb", [512, D])
    wout = inp("wout", [D, D])
    fnw = inp("fnw", [128, 8])
    zfeat = inp("zfeat", [33, L])
    win = inp("win", [128, 16, 512])
    wins = inp("wins", [128, 16, 512])
    winl = inp("winl", [1, 512])
    Fm = inp("Fm", [32, 128, 16, 128], BF16)
    Fi = inp("Fi", [16, 128, 32, 128], BF16)
    ident_d = inp("ident", [128, 128], BF16)
    masks_d = inp("masks", [64, 2, 64])

    out_t = nc.dram_tensor("out_t", [nb, D, L], F32, kind="ExternalOutput").ap()
    dbg_out = {}

    def scr(name, shape, dt):
        return nc.dram_tensor(name, list(shape), dt, kind="Internal").ap()

    wgb = scr("wgb", [2, 128, NF, 8, 128], BF16)
    wub = scr("wub", [2, 128, NF, 8, 128], BF16)
    wdb = scr("wdb", [2, 128, 8, NF, 128], BF16)
    winb = scr("winb", [128, 48, 8, 128], BF16)
    wpab = scr("wpab", [128, 8, 4, 128], BF16)
    wpbb = scr("wpbb", [128, 8, 4, 128], BF16)
    woutb = scr("woutb", [128, 8, 8, 128], BF16)
    Ksp = scr("Ksp", [2, 2, 16, 128, 512], F32)
    hs = scr("hs", [nb, 128, 8, L], F32)
    oas = scr("oas", [128, 4, L], BF16)

    with ExitStack() as st:
        S = Sched(nc, st)
        ACT, TT, TS, STT, MM, TR, CP, MS = _wrap(S)
        A = Arena(nc, st, 48000)
        pbk = [st.enter_context(nc.psum_tensor(f"pb{i}", [128, 512], F32)) for i in range(8)]
        pb = [p[:] for p in pbk]
        pbt = S.toks(8, "pb")
        pbb = [p[:].bitcast(BF16) for p in pbk]
        q_sp = "sp"

        def dump(name, ap, shape, tok, dt=F32):
            if name not in dbg:
                return
            d = nc.dram_tensor("dbg_" + name, list(shape), dt, kind="ExternalOutput").ap()
            dbg_out[name] = d
            S.dma(q_sp, d, ap, reads=[tok], evtok=tok)

        ident = A.alloc([128], BF16)
        ones_d = A.alloc([128], BF16)
        ones_v = A.alloc([128], BF16)
        ones_f = A.alloc([128], F32)
        modT = A.alloc([72, 4], F32)
        lbT = A.alloc([8], F32)
        omlT = A.alloc([8], F32)
        normw_s = A.alloc([1], F32)
        convw_s = A.alloc([3, 12], F32)
        convb_s = A.alloc([12], F32)
        fnw_s = A.alloc([8], F32)
        masks_s = A.alloc([2, 64], F32)
        epsb = A.alloc([1], F32)
        tconst = S.tok("const")
        S.dma(q_sp, ident, ident_d, writes=[tconst])
        S.dma(q_sp, normw_s, normw, writes=[tconst])
        S.dma(q_sp, convw_s, convw, writes=[tconst])
        S.dma(q_sp, convb_s, convb, writes=[tconst])
        S.dma(q_sp, fnw_s, fnw, writes=[tconst])
        S.dma(q_sp, masks_s[0:64], masks_d, writes=[tconst])
        tc2 = S.tok("const2")
        MS("pool", ones_d, 1.0 / 1024.0, [tc2])
        MS("pool", ones_v, 1.0 / 128.0, [tc2])
        MS("pool", ones_f, 1.0, [tc2])
        MS("pool", epsb, 1e-6, [tc2])
        CONST = [tconst, tc2]

        def conv_units():
            NSL = 3
            sfA = [A.alloc_top([8, 512], F32) for _ in range(NSL)]
            sbA = [A.alloc_top([4, 8, 128], BF16) for _ in range(NSL)]
            tf = S.toks(NSL, "cvf")
            tb = S.toks(NSL, "cvb")
            cvtoks.extend(tf + tb)
            it = 0
            ce = 0
            for s_ in range(2):
                for (src, dst) in ((wg[s_], wgb[s_]), (wu[s_], wub[s_])):
                    for g0 in range(0, NF, 4):
                        g = min(4, NF - g0)
                        sl = it % NSL
                        it += 1
                        S.dma("sp", sfA[sl][:, :, 0:g * 128],
                              src[:, g0 * 128:(g0 + g) * 128].rearrange("(k p) n -> p k n", p=128), writes=[tf[sl]])
                        for gi in range(g):
                            eng = "pool"
                            ce += 1
                            CP(eng, sbA[sl][:, gi, :, :], sfA[sl][:, :, gi * 128:(gi + 1) * 128], [tf[sl]], [tb[sl]])
                        S.dma("sp", dst[:, g0:g0 + g], sbA[sl][:, 0:g], reads=[tb[sl]], evtok=tb[sl])
                        yield
                for dc in range(8):
                    sl = it % NSL
                    it += 1
                    sfv = sfA[sl].rearrange("p a b -> p (a b)")[:, 0:NF * 128].rearrange("p (f c) -> p f c", c=128)
                    sbv = sbA[sl].rearrange("p a b c -> p (a b c)")[:, 0:NF * 128].rearrange("p (f c) -> p f c", c=128)
                    S.dma("sp", sfv, wd[s_][:, dc * 128:(dc + 1) * 128].rearrange("(f p) n -> p f n", p=128), writes=[tf[sl]])
                    for hf_ in range(2):
                        eng = "pool"
                        ce += 1
                        CP(eng, sbv[:, hf_ * 11:(hf_ + 1) * 11, :], sfv[:, hf_ * 11:(hf_ + 1) * 11, :], [tf[sl]], [tb[sl]])
                    S.dma("sp", wdb[s_][:, dc], sbv, reads=[tb[sl]], evtok=tb[sl])
                    yield

        cvtoks = []
        cgen = conv_units()
        for _ in cgen:
            pass

        def pbarrier():
            S.barrier(exclude_engs=("pool", "sp"), exclude_toks=cvtoks)

        def pump(n=1):
            for _ in range(n):
                if next(cgen, "done") == "done":
                    return

        A.mark()
        cts = A.alloc([8, 4], F32)
        scs = A.alloc([8, 4], F32)
        lbs = A.alloc([2, 8], F32)
        mbs = A.alloc([72], F32)
        mwb = [A.alloc([8, 1024], F32) for _ in range(2)]
        mwt = S.toks(2, "mw")
        tct, tsc, tlb, tmod = S.toks(4, "p0a")
        S.dma("act", cts, c_t, writes=[tct])
        S.dma("act", lbs, lbl, writes=[tlb])
        S.dma("act", mbs, mod_b, writes=[tlb])
        ACT(scs, cts, AF.Silu, [tct], [tsc])
        TT("dve", lbs[:, 0, :], lbs[:, 0, :], lbs[:, 1, :], ALU.subtract, [tlb], [tlb])
        ACT(lbT, lbs[:, 0, :], AF.Sigmoid, [tlb], [tmod])
        TS("dve", omlT, lbT, -1.0, 1.0, ALU.mult, ALU.add, [tmod], [tmod])
        for j in range(9):
            sl = j % 2
            S.dma("act", mwb[sl], mod_w[:, j * 1024:(j + 1) * 1024].rearrange("(kc p) n -> p kc n", p=128),
                  writes=[mwt[sl]])
            pump(2)
            for dc in range(8):
                o0 = (j * 8 + dc) * 4
                for kc in range(8):
                    MM(pb[0][:, o0:o0 + 4], mwb[sl][:, kc, dc * 128:(dc + 1) * 128], scs[:, kc, :],
                       kc == 0, kc == 7, [mwt[sl], tsc], [pbt[0]])
        psm = pb[0][:, 0:288].rearrange("p (a b) -> p a b", b=4)
        for col in range(4):
            TT("dve", modT[:, :, col], psm[:, :, col], mbs, ALU.add, [pbt[0], tlb], [tmod])
        for j in (1, 4, 7):
            TS("dve", modT[:, j * 8:(j + 1) * 8, :], modT[:, j * 8:(j + 1) * 8, :], 1.0, None, ALU.add, None, [tmod], [tmod])
        for j in (2, 8):
            TS("dve", modT[:, j * 8:(j + 1) * 8, :], modT[:, j * 8:(j + 1) * 8, :], 0.5, None, ALU.mult, None, [tmod], [tmod])
        dump("modT", modT, [128, 72, 4], tmod)
        pbarrier()
        A.release()
        CONST.append(tmod)

        def mv(j, dc, col):
            return modT[:, j * 8 + dc, col:col + 1]

        A.mark()
        w3s = A.alloc([2048], F32)
        hsm = A.alloc([8], F32)
        h2p = A.alloc([L + 8], F32)
        winl_s = A.alloc([512], F32)
        rn = A.alloc([2, 512], F32)
        hbias_s = A.alloc([2, 512], F32)
        A.mark()
        zf = A.alloc([L], F32)
        w1s = A.alloc([64], F32)
        w2s = A.alloc([64], F32)
        h1 = A.alloc([L], F32)
        arg = A.alloc([512], F32)
        wtmp = A.alloc([512], F32)
        tk0, th1, th2, targ, theo, trn = S.toks(6, "p0c")
        thf = S.toks(2, "hf"); thb = S.toks(2, "hb"); tab = S.toks(2, "ab"); twn = S.toks(2, "wn")
        S.dma("act", zf[0:33], zfeat, writes=[tk0])
        S.dma("act", w1s[0:33], hw1, writes=[tk0])
        S.dma("act", w2s[0:64], hw2, writes=[tk0])
        S.dma("act", w3s[0:64], hw3, writes=[tk0])
        S.dma("act", hsm[0:64, 0:1], hb1, writes=[tk0])
        S.dma("act", hsm[0:64, 1:2], hf1, writes=[tk0])
        S.dma("act", hsm[0:64, 2:3], hb2, writes=[tk0])
        S.dma("act", hsm[0:64, 3:4], hf2, writes=[tk0])
        S.dma("act", winl_s[0:1], winl, writes=[tk0])
        S.dma("act", hbias_s, hbias, writes=[tk0])
        TT("dve", hsm[0:64, 4:5], hsm[0:64, 0:1], hsm[0:64, 1:2], ALU.mult, [tk0], [tk0])
        TT("dve", hsm[0:64, 5:6], hsm[0:64, 2:3], hsm[0:64, 3:4], ALU.mult, [tk0], [tk0])
        MS("dve", h2p[0:64, 0:1], 0.0, [th2])

        def sin_layer(wsb, kdim, src, dst, dst_off, fcol, fbcol, tsrc, tdst):
            for ti in range(4):
                MM(pb[1][0:64, :], wsb[0:kdim, 0:64], src[0:kdim, ti * 512:(ti + 1) * 512], True, True,
                   [tk0, tsrc], [pbt[1]])
                TS("dve", arg[0:64], pb[1][0:64, :], hsm[0:64, fcol:fcol + 1], hsm[0:64, fbcol:fbcol + 1],
                   ALU.mult, ALU.add, [pbt[1], tk0], [targ])
                for _ in range(2):
                    wrap_once(arg[0:64], targ)
                TS("dve", arg[0:64], arg[0:64], 3.14159, -3.14159, ALU.min, ALU.max, [targ], [targ])
                ACT(dst[0:64, dst_off + ti * 512: dst_off + (ti + 1) * 512], arg[0:64], AF.Sin, [targ], [tdst])

        twt = S.tok("wtmp")

        def wrap_once(ap, tok):
            TS("dve", wtmp[0:64], ap, PI, -2.0 * PI, ALU.is_gt, ALU.mult, [tok], [twt])
            TT("dve", ap, ap, wtmp[0:64], ALU.add, [tok, twt], [tok])
            TS("dve", wtmp[0:64], ap, -PI, 2.0 * PI, ALU.is_lt, ALU.mult, [tok], [twt])
            TT("dve", ap, ap, wtmp[0:64], ALU.add, [tok, twt], [tok])

        sin_layer(w1s, 33, zf, h1, 0, 1, 4, tk0, th1)
        sin_layer(w2s, 64, h1, h2p, 1, 3, 5, th1, th2)
        dump("h2", h2p[0:64, 1:L + 1], [64, L], th2)
        pbarrier()
        A.release()
        heo = A.alloc([16, 2, 512], BF16)
        hfb = [A.alloc([512], F32) for _ in range(2)]
        hbb = [A.alloc([512], F32) for _ in range(2)]
        absb = [A.alloc([512], F32) for _ in range(2)]
        winb_s = [A.alloc([512], F32) for _ in range(2)]
        winsb_s = [A.alloc([512], F32) for _ in range(2)]

        fmb = [A.alloc([2, 16, 128], BF16) for _ in range(3)]
        tfm = S.toks(3, "fm")
        kst = [A.alloc([2, 512], F32) for _ in range(2)]
        tks = S.toks(2, "kst")
        it = 0
        jj = 0
        for o in range(2):
            for lt in range(16):
                sl = lt % 2
                pump(1)
                S.dma("act", winb_s[sl], win[:, lt, :], writes=[twn[sl]])
                S.dma("act", winsb_s[sl], wins[:, lt, :], writes=[twn[sl]])
                MM(pb[2], h2p[0:64, 1 + lt * 128: 1 + (lt + 1) * 128], w3s[0:64, o * 1024: o * 1024 + 512], True, True,
                   [th2, tk0], [pbt[2]])
                MM(pb[3], h2p[0:64, lt * 128:(lt + 1) * 128], w3s[0:64, o * 1024 + 512: o * 1024 + 1024], True, True,
                   [th2, tk0], [pbt[3]])
                TT("dve", hfb[sl], pb[2], winb_s[sl], ALU.mult, [pbt[2], twn[sl]], [thf[sl]])
                TT("dve", hbb[sl], pb[3], winsb_s[sl], ALU.mult, [pbt[3], twn[sl]], [thb[sl]])
                ACT(absb[0], hfb[sl], AF.Abs, [thf[sl]], [tab[0]])
                MM(pb[4 + o], ones_f, absb[0], lt == 0, False, [tab[0], tc2], [pbt[4 + o]])
                ACT(absb[1], hbb[sl], AF.Abs, [thb[sl]], [tab[1]])
                MM(pb[4 + o], ones_f, absb[1], False, False, [tab[1], tc2], [pbt[4 + o]])
                TT("dve", heo[:, lt, 0, :], hfb[sl], hbb[sl], ALU.add, [thf[sl], thb[sl]], [theo])
                TT("dve", heo[:, lt, 1, :], hfb[sl], hbb[sl], ALU.subtract, [thf[sl], thb[sl]], [theo])
            MM(pb[2][0:1, :], h2p[0:64, L:L + 1], w3s[0:64, o * 1024 + 512: o * 1024 + 1024], True, True,
               [th2, tk0], [pbt[2]])
            TT("dve", hfb[0][0:1], pb[2][0:1, :], winl_s[0:1], ALU.mult, [pbt[2], tk0], [thf[0]])
            ACT(absb[0][0:1], hfb[0][0:1], AF.Abs, [thf[0]], [tab[0]])
            MM(pb[4 + o], ones_f[0:1, :], absb[0][0:1], False, True, [tab[0], tc2], [pbt[4 + o]])
            TS("dve", rn[:, o, :], pb[4 + o], 1e-6, None, ALU.add, None, [pbt[4 + o]], [trn])
            S.op("dve", lambda e, o=o: e.reciprocal(out=rn[:, o, :], in_=rn[:, o, :]), [trn], [trn])
            for j in range(16):
                sl = jj % 3
                jj += 1
                pump(1)
                S.dma("act", fmb[sl][:, 0], Fm[j], writes=[tfm[sl]])
                S.dma("act", fmb[sl][:, 1], Fm[16 + j], writes=[tfm[sl]])
                ks = it % 2
                it += 1
                for lc in range(16):
                    MM(pb[6], fmb[sl][:, 0, lc, :], heo[:, lc, 0, :], lc == 0, lc == 15, [tfm[sl], theo], [pbt[6]])
                for lc in range(16):
                    MM(pb[7], fmb[sl][:, 1, lc, :], heo[:, lc, 1, :], lc == 0, lc == 15, [tfm[sl], theo], [pbt[7]])
                TT("dve", kst[ks][:, 0, :], pb[6], rn[:, o, :], ALU.mult, [pbt[6], trn], [tks[ks]])
                TT("dve", kst[ks][:, 0, :], kst[ks][:, 0, :], hbias_s[:, o, :], ALU.add, [tks[ks], tk0], [tks[ks]])
                TT("dve", kst[ks][:, 1, :], pb[7], rn[:, o, :], ALU.mult, [pbt[7], trn], [tks[ks]])
                S.dma("act", Ksp[o, 0, j], kst[ks][:, 0, :], reads=[tks[ks]], evtok=tks[ks])
                S.dma("act", Ksp[o, 1, j], kst[ks][:, 1, :], reads=[tks[ks]], evtok=tks[ks])
        dump("rn", rn, [128, 2, 512], trn)
        pump(1000)
        S.barrier()
        A.release()
        for _ in range(6):
            A.release_top()
        if stop_after == "p0c":
            S.emit()
            return nc, din, dbg_out

        def rstd_of(hb, off, n, sq, tsq, rst, trst, hbtok, ones_ap, nchunks=8):
            for dc in range(nchunks):
                ACT(sq[:, dc, 0:n], hb[:, dc, off:off + n], AF.Square, [hbtok], [tsq])
            for dc in range(nchunks):
                MM(pb[7][:, 0:n], ones_ap, sq[:, dc, 0:n], dc == 0, dc == nchunks - 1, [tsq, tc2], [pbt[7]])
            ACT(rst[:, 0:n], pb[7][:, 0:n], AF.Ln, [pbt[7]], [trst], bias=epsb[:, 0:1])
            ACT(rst[:, 0:n], rst[:, 0:n], AF.Exp, [trst], [trst], scale=-0.5)

        def rstd_multi(hb, tl2, sqs, tsqs, rsts, trsts, hbtok, ones_ap):
            assert len(tl2) <= 2
            for i, (off, n) in enumerate(tl2):
                for dc in range(8):
                    ACT(sqs[i][:, dc, 0:n], hb[:, dc, off:off + n], AF.Square, [hbtok], [tsqs[i]])
            for i, (off, n) in enumerate(tl2):
                for dc in range(8):
                    MM(pb[7 - i][:, 0:n], ones_ap, sqs[i][:, dc, 0:n], dc == 0, dc == 7, [tsqs[i], tc2], [pbt[7 - i]])
            for i, (off, n) in enumerate(tl2):
                ACT(rsts[i][:, 0:n], pb[7 - i][:, 0:n], AF.Ln, [pbt[7 - i]], [trsts[i]], bias=epsb[:, 0:1])
            for i, (off, n) in enumerate(tl2):
                ACT(rsts[i][:, 0:n], rsts[i][:, 0:n], AF.Exp, [trsts[i]], [trsts[i]], scale=-0.5)

        def ffn_block(s, hb, hbtok, tiles, j0, W):
            A.mark()
            nbk = A.alloc([8, W], BF16)
            act = A.alloc([NF, W], BF16)
            actf = act.rearrange("p f w -> p (f w)")
            sqs = [actf[:, i * 4096:(i + 1) * 4096].rearrange("p (a b) -> p a b", b=512) for i in range(2)]
            rsts = [actf[:, 8192 + i * 1024: 8192 + (i + 1) * 1024].bitcast(F32) for i in range(2)]
            tmp = [A.alloc([512], F32) for _ in range(2)]
            sg = [A.alloc([512], BF16) for _ in range(2)]
            NSA, NSB = 4, 3
            wgs = [A.alloc([8, 128], BF16) for _ in range(NSA)]
            wus = [A.alloc([8, 128], BF16) for _ in range(NSA)]
            wds = [A.alloc([NF, 128], BF16) for _ in range(NSB)]
            tnb = S.toks(len(tiles), "nb")
            tact = S.toks(len(tiles), "act")
            tsqs = S.toks(2, "nrmq"); trsts = S.toks(2, "nrmr")
            ttmp = S.toks(2, "tmp"); tsg = S.toks(2, "sg")
            twg = S.toks(NSA, "wg"); twd = S.toks(NSB, "wd")
            k = 0
            assert len(tiles) == 2
            rstd_multi(hb, [(off, n) for (off, n, col) in tiles], sqs, tsqs, rsts, trsts, hbtok, ones_d)
            for ti, (off, n, col) in enumerate(tiles):
                rst, trst = rsts[ti], trsts[ti]
                for dc in range(8):
                    sl = k % 2
                    k += 1
                    TT("dve", tmp[sl][:, 0:n], hb[:, dc, off:off + n], rst[:, 0:n], ALU.mult, [hbtok, trst], [ttmp[sl]])
                    ACT(nbk[:, dc, off:off + n], tmp[sl][:, 0:n], AF.Identity, [ttmp[sl], tmod], [tnb[ti]],
                        bias=mv(j0, dc, col), scale=mv(j0 + 1, dc, col))
            k = 0
            for f in range(NF):
                sl = f % NSA
                S.dma(q_sp, wgs[sl], wgb[s, :, f], writes=[twg[sl]])
                S.dma(q_sp, wus[sl], wub[s, :, f], writes=[twg[sl]])
                for ti, (off, n, col) in enumerate(tiles):
                    pg = (2 * k) % 4
                    pu = pg + 1
                    ss = k % 2
                    k += 1
                    for kc in range(8):
                        MM(pb[pg][:, 0:n], wgs[sl][:, kc, :], nbk[:, kc, off:off + n], kc == 0, kc == 7,
                           [twg[sl], tnb[ti]], [pbt[pg]])
                    for kc in range(8):
                        MM(pb[pu][:, 0:n], wus[sl][:, kc, :], nbk[:, kc, off:off + n], kc == 0, kc == 7,
                           [twg[sl], tnb[ti]], [pbt[pu]])
                    ACT(sg[ss][:, 0:n], pb[pg][:, 0:n], AF.Silu, [pbt[pg]], [tsg[ss]])
                    TT("dve", act[:, f, off:off + n], sg[ss][:, 0:n], pb[pu][:, 0:n], ALU.mult,
                       [tsg[ss], pbt[pu]], [tact[ti]])
            k = 0
            for dc in range(8):
                sl = dc % NSB
                S.dma(q_sp, wds[sl], wdb[s, :, dc], writes=[twd[sl]])
                for ti, (off, n, col) in enumerate(tiles):
                    pp = 4 + (k % 2)
                    k += 1
                    for f in range(NF):
                        MM(pb[pp][:, 0:n], wds[sl][:, f, :], act[:, f, off:off + n], f == 0, f == NF - 1,
                           [twd[sl], tact[ti]], [pbt[pp]])
                    STT(hb[:, dc, off:off + n], pb[pp][:, 0:n], mv(j0 + 2, dc, col), hb[:, dc, off:off + n],
                        ALU.mult, ALU.add, [pbt[pp], tmod, hbtok], [hbtok])
            S.barrier()
            A.release()

        def load_w(dst, cg, tok, stg, tstg, src=None, nk=8):
            srcm = w_in if src is None else src
            S.dma(q_sp, stg[:, 0:nk, :], srcm[:, cg * 128:(cg + 1) * 128].rearrange("(kc p) n -> p kc n", p=128), writes=[tstg])
            CP("pool", dst, stg[:, 0:nk, :], [tstg], [tok])

        def proj_fm(wsb, wtok, tiles_, consume):
            for i, (off, n) in enumerate(tiles_):
                pi_ = 6 + (i % 2)
                for kc in range(8):
                    MM(pb[pi_][:, 0:n], wsb[:, kc, :], nT[:, kc, off:off + n], kc == 0, kc == 7, [wtok, tnT], [pbt[pi_]])
                consume(pi_, off, n)

        LT4 = [(0, 512), (512, 512), (1024, 512), (1536, 512)]
        LT5 = LT4 + [(2048, 256)]

        def P2(b):
            for h in range(4):
                A.mark()
                wv, wff, wfb, wq, wgt = [A.alloc([8, 128], BF16) for _ in range(5)]
                tw = S.toks(5, "hw")
                wstg = [A.alloc([8, 128], F32) for _ in range(2)]
                twstg = S.toks(2, "wstg")
                for wi, (wsb, cg, tk) in enumerate(((wv, h, tw[0]), (wq, 12 + h, tw[3]), (wgt, 16 + h, tw[4]), (wff, 4 + h, tw[1]), (wfb, 8 + h, tw[2]))):
                    load_w(wsb, cg, tk, wstg[wi % 2], twstg[wi % 2])
                vtok = A.alloc([36, 128], BF16)
                kk = A.alloc([T], F32)
                lfb = A.alloc([T], F32)
                Bb = A.alloc([T], F32)
                qf = A.alloc([L], F32)
                onesr = A.alloc([T], BF16)
                qt_ = [A.alloc([L], BF16) for _ in range(2)]
                kt_ = [A.alloc([T], BF16) for _ in range(2)]
                ktok = [A.alloc([36, 128], BF16) for _ in range(2)]
                o_ = [A.alloc([1, L], F32) for _ in range(2)]
                sgb = A.alloc([L], BF16)
                Sf = [A.alloc([128], F32) for _ in range(2)]
                Sb2 = [[A.alloc([128], BF16) for _ in range(2)] for _ in range(2)]
                tSb2 = [S.toks(2, "Sb2") for _ in range(2)]
                tmpS = [A.alloc([128], F32) for _ in range(2)]
                gcol = [A.alloc([36], F32) for _ in range(2)]
                bref = A.alloc([36], F32)
                scm = [A.alloc([64], BF16) for _ in range(2)]
                sq = A.alloc([1, 512], BF16)
                rst = A.alloc([512], F32)
                tmp = A.alloc([512], F32)
                oab = A.alloc([L], BF16)
                (tvt, tkk, tlf, tB, tq, tone, tsgb, tbref, tsq, trst, ttmp, toab) = S.toks(12, "p2")
                tqt = S.toks(2, "qt"); tkt = S.toks(2, "kt"); tktok = S.toks(2, "ktok"); to = S.toks(2, "o")
                tSf = S.toks(2, "Sf"); tSb = S.toks(2, "Sb"); ttS = S.toks(2, "tS"); tg = S.toks(2, "g"); tscm = S.toks(2, "scm")
                MS("pool", onesr, 1.0, [tone])
                for g0 in range(0, 36, 4):
                    pi_ = 6 + ((g0 // 4) % 2)
                    for ci in range(4):
                        c = g0 + ci
                        for kc in range(8):
                            MM(pb[pi_][0:64, ci * 128:(ci + 1) * 128], nT[:, kc, c * 64:(c + 1) * 64], wv[:, kc, :],
                               kc == 0, kc == 7, [tw[0], tnT], [pbt[pi_]])
                    CP("act", vtok[0:64, g0:g0 + 4, :], pb[pi_][0:64, :].rearrange("p (a b) -> p a b", b=128), [pbt[pi_]], [tvt])
                proj_fm(wq, tw[3], LT4, lambda pi_, off, n: ACT(qf[:, off:off + n], pb[pi_][:, 0:n], AF.Silu, [pbt[pi_]], [tq]))
                proj_fm(wgt, tw[4], LT4, lambda pi_, off, n: ACT(sgb[:, off:off + n], pb[pi_][:, 0:n], AF.Silu, [pbt[pi_]], [tsgb]))
                B3 = Bb.rearrange("p (c s) -> p c s", s=64)
                lf3 = lfb.rearrange("p (c s) -> p c s", s=64)
                for dr in range(2):
                    lbc = dr * 4 + h
                    wsb, wtk = (wff, tw[1]) if dr == 0 else (wfb, tw[2])
                    proj_fm(wsb, wtk, LT5, lambda pi_, off, n: ACT(kk[:, off:off + n], pb[pi_][:, 0:n], AF.Sigmoid,
                                                                    [pbt[pi_]], [tkk], scale=-1.0))
                    TS("dve", kk, kk, omlT[:, lbc:lbc + 1], None, ALU.mult, None, [tkk, tmod], [tkk])
                    ACT(lfb, kk, AF.Ln, [tkk], [tlf], bias=ones_f[:, 0:1], scale=-1.0)
                    S.op("dve", lambda e: e.tensor_tensor_scan(out=Bb, data0=onesr, data1=lfb, initial=0.0,
                                                                op0=ALU.mult, op1=ALU.add), [tone, tlf], [tB])
                    if dr == 0:
                        TT("dve", bref, B3[:, :, 0], lf3[:, :, 0], ALU.subtract, [tB, tlf], [tbref])
                        TT("dve", lf3, B3, bref.unsqueeze(2).broadcast_to([128, 36, 64]), ALU.subtract, [tB, tbref, tlf], [tlf])
                    else:
                        TT("dve", lf3, lf3, B3, ALU.subtract, [tB, tlf], [tlf])
                        TT("dve", lf3, lf3, B3[:, :, 63:64].broadcast_to([128, 36, 64]), ALU.add, [tB, tlf], [tlf])
                    ACT(Bb, lfb, AF.Exp, [tlf], [tB])
                    if dr == 0:
                        CP("dve", gcol[dr], B3[:, :, 63], [tB], [tg[dr]])
                    else:
                        CP("dve", gcol[dr], B3[:, :, 0], [tB], [tg[dr]])
                    TT("dve", qt_[dr], qf, Bb[:, 0:L], ALU.mult, [tq, tB], [tqt[dr]])
                    ACT(lfb, lfb, AF.Exp, [tlf], [tlf], scale=-1.0)
                    TT("dve", kt_[dr], kk, lfb, ALU.mult, [tkk, tlf], [tkt[dr]])
                    for g0 in range(0, 36, 4):
                        pi_ = 6 + ((g0 // 4) % 2)
                        for ci in range(4):
                            c = g0 + ci
                            TR(pbb[pi_][0:64, ci * 128:(ci + 1) * 128], kt_[dr][:, c * 64:(c + 1) * 64], ident,
                               [tkt[dr], tconst], [pbt[pi_]])
                        CP("act", ktok[dr][0:64, g0:g0 + 4, :], pbb[pi_][0:64, 0:512].rearrange("p (a b) -> p a b", b=128),
                           [pbt[pi_]], [tktok[dr]])
                    MS("pool", tmpS[dr], 0.0, [ttS[dr]])
                    MS("pool", Sb2[dr][0], 0.0, [tSb2[dr][0]])
                orders = [[32, 33, 34, 35] + list(range(32)), [35, 34, 33, 32] + list(range(31, -1, -1))]
                for step in range(36):
                    par = step % 2
                    cc_ = [orders[dr][step] for dr in range(2)]
                    lat = cc_[0] < 32
                    if lat:
                        for dr in range(2):
                            c = cc_[dr]
                            MM(pb[dr][0:64, 0:64], kt_[dr][:, c * 64:(c + 1) * 64], qt_[dr][:, c * 64:(c + 1) * 64], True, True,
                               [tkt[dr], tqt[dr]], [pbt[dr]])
                    for dr in range(2):
                        c = cc_[dr]
                        MM(pb[4 + dr][:, 0:128], ktok[dr][0:64, c, :], vtok[0:64, c, :], True, True, [tktok[dr], tvt], [pbt[4 + dr]])
                    if lat:
                        for dr in range(2):
                            TT("dve", scm[dr][0:64], pb[dr][0:64, 0:64], masks_s[0:64, dr, :], ALU.mult,
                               [pbt[dr], tconst], [tscm[dr]])
                        for dr in range(2):
                            c = cc_[dr]
                            MM(pb[2 + dr][:, 0:64], vtok[0:64, c, :], scm[dr][0:64], True, False, [tvt, tscm[dr]], [pbt[2 + dr]])
                            MM(pb[2 + dr][:, 0:64], Sb2[dr][par], qt_[dr][:, c * 64:(c + 1) * 64], False, True,
                               [tSb2[dr][par], tqt[dr]], [pbt[2 + dr]])
                            CP("act", o_[dr][:, 0, c * 64:(c + 1) * 64], pb[2 + dr][:, 0:64], [pbt[2 + dr]], [to[dr]])
                    for dr in range(2):
                        c = cc_[dr]
                        cp_ = orders[dr][step - 1] if step > 0 else c
                        STT(tmpS[dr], tmpS[dr], gcol[dr][:, cp_:cp_ + 1], pb[4 + dr][:, 0:128], ALU.mult, ALU.add,
                            [ttS[dr], tg[dr], pbt[4 + dr]], [ttS[dr]])
                        TS("dve", Sb2[dr][1 - par], tmpS[dr], gcol[dr][:, c:c + 1], None, ALU.mult, None, [ttS[dr], tg[dr]],
                           [tSb2[dr][1 - par]])
                TT("pool", o_[0], o_[0], o_[1], ALU.add, [to[0], to[1]], [to[0]])
                for (off, n) in LT4:
                    rstd_of(o_[0], off, n, sq, tsq, rst, trst, to[0], ones_v, nchunks=1)
                    TT("dve", tmp[:, 0:n], o_[0][:, 0, off:off + n], rst[:, 0:n], ALU.mult, [to[0], trst], [ttmp])
                    STT(oab[:, off:off + n], tmp[:, 0:n], normw_s[:, 0:1], sgb[:, off:off + n], ALU.mult, ALU.mult,
                        [ttmp, tconst, tsgb], [toab])
                S.dma(q_sp, oas[:, h, :], oab, reads=[toab], evtok=toab)
                S.barrier()
                A.release()

        def P3(b, zT, tz):
            A.mark()
            gT = A.alloc([4, L], BF16)
            ztok = A.alloc([16, 512], BF16)
            pb_base = A.top
            Pbuf = A.alloc([32, 512], BF16)
            pb_end = A.top
            A.top = pb_base
            pT = A.alloc([L + 8], F32)
            uT = A.alloc([L], F32)
            wsl = [A.alloc([8, 128], BF16) for _ in range(2)]
            wstg3 = [A.alloc([8, 128], F32) for _ in range(2)]
            twstg3 = S.toks(2, "wstg3")
            assert A.top <= pb_end
            A.top = pb_end
            fib = [A.alloc([32, 128], BF16) for _ in range(2)]
            fmb = [A.alloc([2, 16, 128], BF16) for _ in range(2)]
            kb = [A.alloc([2, 512], F32) for _ in range(2)]
            tm = [A.alloc([512], F32) for _ in range(4)]
            tg_, tzt, tP, tpT, tuT = S.toks(5, "p3")
            twsl = S.toks(2, "wsl"); tfib = S.toks(2, "fib"); tfmb = S.toks(2, "fmb"); tkb = S.toks(2, "kb"); ttm = S.toks(4, "tm")

            def proj_conv(part, dst, tdst):
                S.barrier()
                MS("pool", pT[:, 0:1], 0.0, [tpT])
                MS("pool", pT[:, L + 1:L + 2], 0.0, [tpT])
                for cc in range(4):
                    sl = cc % 2
                    ci = part * 4 + cc
                    load_w(wsl[sl], 20 + ci, twsl[sl], wstg3[sl], twstg3[sl])
                    proj_fm(wsl[sl], twsl[sl], LT4,
                            lambda pi_, off, n: CP("act", pT[:, 1 + off:1 + off + n], pb[pi_][:, 0:n], [pbt[pi_]], [tpT]))
                    TS("dve", uT, pT[:, 1:L + 1], convw_s[:, 1, ci:ci + 1], convb_s[:, ci:ci + 1], ALU.mult, ALU.add,
                       [tpT, tconst], [tuT])
                    STT(uT, pT[:, 0:L], convw_s[:, 0, ci:ci + 1], uT, ALU.mult, ALU.add, [tpT, tconst, tuT], [tuT])
                    STT(dst[:, cc, :], pT[:, 2:L + 2], convw_s[:, 2, ci:ci + 1], uT, ALU.mult, ALU.add,
                        [tpT, tconst, tuT], [tdst])
                S.barrier()

            proj_conv(0, zT, tz)
            for o in range(2):
                proj_conv(1 + o, gT, tg_)
                for tt in range(16):
                    pi_ = 6 + (tt % 2)
                    for cc in range(4):
                        TR(pbb[pi_][:, cc * 128:(cc + 1) * 128], zT[:, cc, tt * 128:(tt + 1) * 128], ident,
                           [tz, tconst], [pbt[pi_]])
                    CP("act" if tt % 2 else "dve", ztok[:, tt, :], pbb[pi_][:, 0:512], [pbt[pi_]], [tzt])
                for j in range(16):
                    sl = j % 2
                    S.dma(q_sp, fmb[sl][:, 0], Fm[j], writes=[tfmb[sl]])
                    S.dma(q_sp, fmb[sl][:, 1], Fm[16 + j], writes=[tfmb[sl]])
                    S.dma(q_sp, kb[sl][:, 0, :], Ksp[o, 0, j], writes=[tkb[sl]])
                    S.dma(q_sp, kb[sl][:, 1, :], Ksp[o, 1, j], writes=[tkb[sl]])
                    pr, pim = 2 * sl, 2 * sl + 1
                    for lc in range(16):
                        MM(pb[pr], fmb[sl][:, 0, lc, :], ztok[:, lc, :], lc == 0, lc == 15, [tfmb[sl], tzt], [pbt[pr]])
                    for lc in range(16):
                        MM(pb[pim], fmb[sl][:, 1, lc, :], ztok[:, lc, :], lc == 0, lc == 15, [tfmb[sl], tzt], [pbt[pim]])
                    TT("dve", tm[0], pb[pr], kb[sl][:, 0, :], ALU.mult, [pbt[pr], tkb[sl]], [ttm[0]])
                    TT("dve", tm[1], pb[pim], kb[sl][:, 1, :], ALU.mult, [pbt[pim], tkb[sl]], [ttm[1]])
                    TT("pool", Pbuf[:, j, :], tm[0], tm[1], ALU.subtract, [ttm[0], ttm[1]], [tP])
                    TT("dve", tm[2], pb[pr], kb[sl][:, 1, :], ALU.mult, [pbt[pr], tkb[sl]], [ttm[2]])
                    TT("dve", tm[3], pb[pim], kb[sl][:, 0, :], ALU.mult, [pbt[pim], tkb[sl]], [ttm[3]])
                    TT("pool", Pbuf[:, 16 + j, :], tm[2], tm[3], ALU.add, [ttm[2], ttm[3]], [tP])
                k = 0
                for tt in range(16):
                    sl = tt % 2
                    S.dma(q_sp, fib[sl], Fi[tt], writes=[tfib[sl]])
                    for cc in range(4):
                        pi_ = 4 + (k % 2)
                        k += 1
                        for fc in range(32):
                            MM(pb[pi_][:, 0:128], Pbuf[:, fc, cc * 128:(cc + 1) * 128], fib[sl][:, fc, :], fc == 0, fc == 31,
                               [tP, tfib[sl]], [pbt[pi_]])
                        TT("dve", zT[:, cc, tt * 128:(tt + 1) * 128], gT[:, cc, tt * 128:(tt + 1) * 128], pb[pi_][:, 0:128],
                           ALU.mult, [tg_, pbt[pi_]], [tz])
            S.barrier()
            A.release()

        def P4(b, zT, tz, yT, ty):
            A.mark()
            oaT = A.alloc([4, L], BF16)
            toa = S.tok("oaT")
            S.dma(q_sp, oaT, oas, writes=[toa])
            wga = [A.alloc([8, 128], BF16) for _ in range(2)]
            wgb_ = [A.alloc([8, 128], BF16) for _ in range(2)]
            wa = [A.alloc([4, 128], BF16) for _ in range(2)]
            wb_ = [A.alloc([4, 128], BF16) for _ in range(2)]
            sga = [A.alloc([512], F32) for _ in range(2)]
            sgb2 = [A.alloc([512], F32) for _ in range(2)]
            t1 = [A.alloc([512], F32) for _ in range(2)]
            t2 = [A.alloc([512], F32) for _ in range(2)]
            tw4a = S.toks(2, "w4a"); tw4b = S.toks(2, "w4b"); tw4c = S.toks(2, "w4c"); tw4d = S.toks(2, "w4d")
            wstg4 = [A.alloc([8, 128], F32) for _ in range(2)]
            twstg4 = S.toks(2, "wstg4")
            tsa = S.toks(2, "sa"); tsb = S.toks(2, "sb"); tt1 = S.toks(2, "t1"); tt2 = S.toks(2, "t2")
            k = 0
            for dc in range(8):
                sl = dc % 2
                load_w(wga[sl], 32 + dc, tw4a[sl], wstg4[0], twstg4[0])
                load_w(wgb_[sl], 40 + dc, tw4b[sl], wstg4[1], twstg4[1])
                load_w(wa[sl], dc, tw4c[sl], wstg4[0], twstg4[0], src=wpa, nk=4)
                load_w(wb_[sl], dc, tw4d[sl], wstg4[1], twstg4[1], src=wpb, nk=4)
                for (off, n) in LT4:
                    ss = k % 2
                    k += 1
                    for kc in range(8):
                        MM(pb[4 * ss + 0], wga[sl][:, kc, :], nT[:, kc, off:off + n], kc == 0, kc == 7, [tw4a[sl], tnT], [pbt[4 * ss + 0]])
                    for kc in range(4):
                        MM(pb[4 * ss + 1], wa[sl][:, kc, :], oaT[:, kc, off:off + n], kc == 0, kc == 3, [tw4c[sl], toa], [pbt[4 * ss + 1]])
                    for kc in range(8):
                        MM(pb[4 * ss + 2], wgb_[sl][:, kc, :], nT[:, kc, off:off + n], kc == 0, kc == 7, [tw4b[sl], tnT], [pbt[4 * ss + 2]])
                    for kc in range(4):
                        MM(pb[4 * ss + 3], wb_[sl][:, kc, :], zT[:, kc, off:off + n], kc == 0, kc == 3, [tw4d[sl], tz], [pbt[4 * ss + 3]])
                    ACT(sga[ss], pb[4 * ss + 0], AF.Sigmoid, [pbt[4 * ss + 0]], [tsa[ss]])
                    ACT(sgb2[ss], pb[4 * ss + 2], AF.Sigmoid, [pbt[4 * ss + 2]], [tsb[ss]])
                    TT("dve", t1[ss], sga[ss], pb[4 * ss + 1], ALU.mult, [tsa[ss], pbt[4 * ss + 1]], [tt1[ss]])
                    TT("dve", t2[ss], sgb2[ss], pb[4 * ss + 3], ALU.mult, [tsb[ss], pbt[4 * ss + 3]], [tt2[ss]])
                    TT("pool", yT[:, dc, off:off + n], t1[ss], t2[ss], ALU.add, [tt1[ss], tt2[ss]], [ty])
            S.barrier()
            A.release()

        def P5(b, yT, ty):
            for blk in range(2):
                A.mark()
                W = 1024
                base = blk * W
                hb = A.alloc([8, W], F32)
                hbtok = S.tok("hb5")
                S.dma(q_sp, hb, hs[b, :, :, base:base + W], writes=[hbtok])
                A.mark()
                wo = [A.alloc([8, 128], BF16) for _ in range(2)]
                two = S.toks(2, "wo")
                wstg5 = [A.alloc([8, 128], F32) for _ in range(2)]
                twstg5 = S.toks(2, "wstg5")
                tiles = [(0, 512, b), (512, 512, b)]
                k = 0
                for dc in range(8):
                    sl = dc % 2
                    load_w(wo[sl], dc, two[sl], wstg5[sl], twstg5[sl], src=wout)
                    for (off, n, col) in tiles:
                        pi_ = 6 + (k % 2)
                        k += 1
                        for kc in range(8):
                            MM(pb[pi_][:, 0:n], wo[sl][:, kc, :], yT[:, kc, base + off:base + off + n], kc == 0, kc == 7,
                               [two[sl], ty], [pbt[pi_]])
                        STT(hb[:, dc, off:off + n], pb[pi_][:, 0:n], mv(5, dc, col), hb[:, dc, off:off + n],
                            ALU.mult, ALU.add, [pbt[pi_], tmod, hbtok], [hbtok])
                S.barrier()
                A.release()
                ffn_block(1, hb, hbtok, tiles, 6, W)
                A.mark()
                sqs = [A.alloc([8, 512], BF16) for _ in range(2)]
                rsts = [A.alloc([512], F32) for _ in range(2)]
                tsqs = S.toks(2, "nrm5q"); trsts = S.toks(2, "nrm5r")
                rstd_multi(hb, [(off, n) for (off, n, col) in tiles], sqs, tsqs, rsts, trsts, hbtok, ones_d)
                for tix, (off, n, col) in enumerate(tiles):
                    rst, trst = rsts[tix], trsts[tix]
                    for dc in range(8):
                        STT(hb[:, dc, off:off + n], hb[:, dc, off:off + n], fnw_s[:, dc:dc + 1], rst[:, 0:n],
                            ALU.mult, ALU.mult, [hbtok, tconst, trst], [hbtok])
                S.dma(q_sp, out_t[b][:, base:base + W].rearrange("(dc p) t -> p dc t", p=128), hb, reads=[hbtok], evtok=hbtok)
                S.barrier()
                A.release()
                A.release()

        tnT = S.tok("nT")
        nT = None
        for b in range(nb):
            A.mark()
            nT = A.alloc([8, T], BF16)
            blocks = [
                [(0, 512, b), (512, 256, b)],
                [(768, 512, b), (1280, 256, b)],
                [(1536, 512, b), (2048, 256, 2)],
            ]
            for bi, tl in enumerate(blocks):
                A.mark()
                W = 768
                base = bi * 768
                hb = A.alloc([8, W], F32)
                hbtok = S.tok("hb")
                pst = [A.alloc([8, 512], F32)]
                tps = S.toks(1, "pos")
                k = 0
                for (off, n, col) in tl:
                    lo = off - base
                    if col == 2:
                        S.dma(q_sp, hb[:, :, lo:lo + n], ctx_t[b].rearrange("(dc p) t -> p dc t", p=128), writes=[hbtok])
                    else:
                        S.dma(q_sp, hb[:, :, lo:lo + n],
                              x_t[b][:, off:off + n].rearrange("(dc p) t -> p dc t", p=128), writes=[hbtok])
                        S.dma(q_sp, pst[0][:, :, 0:n], pos_t[:, off:off + n].rearrange("(dc p) t -> p dc t", p=128),
                              writes=[tps[0]])
                        for dc in range(8):
                            k += 1
                            TT("dve", hb[:, dc, lo:lo + n], hb[:, dc, lo:lo + n],
                               pst[0][:, dc, 0:n], ALU.add, [hbtok, tps[0]], [hbtok])
                ltiles = [(off - base, n, col) for (off, n, col) in tl]
                ffn_block(0, hb, hbtok, ltiles, 0, W)
                A.mark()
                sqs = [A.alloc([8, 512], BF16) for _ in range(2)]
                rsts = [A.alloc([512], F32) for _ in range(2)]
                tmp = [A.alloc([512], F32) for _ in range(2)]
                tsqs = S.toks(2, "nrmq"); trsts = S.toks(2, "nrmr")
                ttmp = S.toks(2, "tmp")
                k = 0
                rstd_multi(hb, [(off - base, n) for (off, n, col) in tl], sqs, tsqs, rsts, trsts, hbtok, ones_d)
                for tix, (off, n, col) in enumerate(tl):
                    lo = off - base
                    rst, trst = rsts[tix], trsts[tix]
                    if col != 2:
                        S.dma(q_sp, hs[b, :, :, off:off + n], hb[:, :, lo:lo + n], reads=[hbtok], evtok=hbtok)
                    for dc in range(8):
                        sl = k % 2
                        k += 1
                        TT("dve", tmp[sl][:, 0:n], hb[:, dc, lo:lo + n], rst[:, 0:n], ALU.mult, [hbtok, trst], [ttmp[sl]])
                        ACT(nT[:, dc, off:off + n], tmp[sl][:, 0:n], AF.Identity, [ttmp[sl], tmod], [tnT],
                            bias=mv(3, dc, col), scale=mv(4, dc, col))
                S.barrier()
                A.release()
                A.release()
            if b == 0:
                dump("nT0", nT, [128, 8, T], tnT, BF16)
                dump("hs0", hs[0], [128, 8, L], tnT)
            if stop_after == "p1":
                break
            tz = S.tok("zT")
            P2(b)
            zT = A.alloc([4, L], BF16)
            if b == 0:
                dump("oas", oas, [128, 4, L], tz, BF16)
            if stop_after == "p2":
                break
            P3(b, zT, tz)
            if b == 0:
                dump("zT", zT, [128, 4, L], tz, BF16)
            if stop_after == "p3":
                break
            yT = A.alloc_top([8, L], BF16)
            ty = S.tok("yT")
            P4(b, zT, tz, yT, ty)
            if b == 0:
                dump("yT", yT, [128, 8, L], ty, BF16)
            S.barrier()
            A.release()
            if stop_after == "p4":
                break
            P5(b, yT, ty)
            A.release_top()
        S.barrier()
        S.emit()
    return nc, din, dbg_out


def _bf(a):
    return np.ascontiguousarray(a).astype(ml_dtypes.bfloat16)


_CONST_CACHE = {}


def host_consts():
    if _CONST_CACHE:
        return _CONST_CACHE
    f32 = np.float32
    quarter = D // 4
    omega = (1.0 / (10000.0 ** (np.arange(quarter, dtype=f32) / quarter))).astype(f32)
    rows = L // 64
    ar = np.arange(rows, dtype=f32)[:, None] * omega
    ac = np.arange(64, dtype=f32)[:, None] * omega
    er = np.concatenate([np.sin(ar), np.cos(ar)], axis=-1)
    ec = np.concatenate([np.sin(ac), np.cos(ac)], axis=-1)
    emb = np.concatenate([np.broadcast_to(er[:, None, :], (rows, 64, D // 2)),
                          np.broadcast_to(ec[None, :, :], (rows, 64, D // 2))], axis=-1).reshape(L, D)
    pos_t = np.ascontiguousarray(emb.T.astype(f32))
    p = np.arange(L, dtype=f32)
    t = p / (L - 1)
    w = (2.0 * math.pi * p / L).astype(f32)
    fb = np.linspace(1e-4, 15, 16, dtype=f32)
    ang = w[:, None] * fb[None, :]
    z = np.concatenate([t[:, None], np.cos(ang), -np.sin(ang)], axis=-1).astype(f32)
    zfeat = np.ascontiguousarray(z.T)
    max_decay = math.log(1e-2) / 0.3
    min_decay = math.log(1e-2) / 1.5
    deltas = np.abs(np.linspace(min_decay, max_decay, 512, dtype=f32))
    window = (np.exp(-t[:, None] * deltas[None, :]) + 0.05).astype(f32)
    win = np.ascontiguousarray(window.reshape(16, 128, 512).transpose(1, 0, 2))
    wsh = np.zeros_like(window)
    wsh[1:] = window[:-1]
    wins = np.ascontiguousarray(wsh.reshape(16, 128, 512).transpose(1, 0, 2))
    winl = np.ascontiguousarray(window[L - 1:L])
    N = 2 * L
    tt = np.arange(L, dtype=np.float64)[:, None]
    ff = (np.arange(L, dtype=np.float64) + 0.5)[None, :]
    angm = 2.0 * np.pi * tt * ff / N
    Fc = np.cos(angm)
    Fs = -np.sin(angm)
    F = np.concatenate([Fc, Fs], axis=1)
    Fm = F.reshape(16, 128, 32, 128).transpose(2, 1, 0, 3)
    Fi = (2.0 / N) * F.T
    Fi = Fi.reshape(32, 128, 16, 128).transpose(2, 1, 0, 3)
    masks = np.zeros((64, 2, 64), f32)
    si = np.arange(64)[:, None]
    ti = np.arange(64)[None, :]
    masks[:, 0, :] = (si <= ti)
    masks[:, 1, :] = (si >= ti)
    _CONST_CACHE.update(dict(pos_t=pos_t, zfeat=zfeat, win=win, wins=wins, winl=winl, Fm=_bf(Fm), Fi=_bf(Fi),
                             ident=_bf(np.eye(128, dtype=f32)), masks=masks))
    return _CONST_CACHE


def prep_core(inp, bsel):
    f32 = np.float32
    c = host_consts()
    m = dict(c)
    nbl = len(bsel)
    m["x_t"] = np.ascontiguousarray(np.stack([inp["x"][b].T for b in bsel]))
    m["ctx_t"] = np.ascontiguousarray(np.stack([inp["ctx"][b].T for b in bsel]))
    ct = np.zeros((4, D), f32)
    for i, b in enumerate(bsel):
        ct[i] = inp["c"][b]
    ct[2] = inp["c_ctx"]
    m["c_t"] = np.ascontiguousarray(ct.reshape(4, 8, 128).transpose(2, 1, 0))
    m["mod_w"] = np.ascontiguousarray(inp["mod_w"][0])
    m["mod_b"] = np.ascontiguousarray(inp["mod_b"][0].reshape(72, 128).T)
    m["wg"] = np.ascontiguousarray(inp["ffn_w_gate"][0])
    m["wu"] = np.ascontiguousarray(inp["ffn_w_up"][0])
    m["wd"] = np.ascontiguousarray(inp["ffn_w_down"][0])
    m["w_in"] = np.ascontiguousarray(inp["w_in"][0])
    m["lbl"] = np.ascontiguousarray(inp["hgrn_lb_logits"].reshape(2, 2, 4, 128).transpose(3, 0, 1, 2).reshape(128, 2, 8))
    m["normw"] = np.ascontiguousarray(inp["hgrn_norm_w"][0].reshape(128, 1))
    m["convw"] = np.ascontiguousarray(inp["hyena_conv_w"][0].reshape(3, 12, 128).transpose(2, 0, 1))
    m["convb"] = np.ascontiguousarray(inp["hyena_conv_b"][0].reshape(12, 128).T)
    m["hw1"] = np.ascontiguousarray(inp["hyena_w1"][0])
    m["hb1"] = np.ascontiguousarray(inp["hyena_b1"][0].reshape(64, 1))
    m["hf1"] = np.ascontiguousarray(inp["hyena_freq1"][0].reshape(64, 1))
    m["hw2"] = np.ascontiguousarray(inp["hyena_w2"][0])
    m["hb2"] = np.ascontiguousarray(inp["hyena_b2"][0].reshape(64, 1))
    m["hf2"] = np.ascontiguousarray(inp["hyena_freq2"][0].reshape(64, 1))
    m["hw3"] = np.ascontiguousarray(inp["hyena_w3"][0])
    m["hbias"] = np.ascontiguousarray(np.broadcast_to(inp["hyena_bias"][0][None], (128, 2, 512)))
    m["wpa"] = np.ascontiguousarray(inp["w_proj_a"][0])
    m["wpb"] = np.ascontiguousarray(inp["w_proj_b"][0])
    m["wout"] = np.ascontiguousarray(inp["w_out"][0])
    m["fnw"] = np.ascontiguousarray(inp["final_norm_w"].reshape(8, 128).T)
    return {k: (v if v.dtype == ml_dtypes.bfloat16 else v.astype(f32)) for k, v in m.items()}


_PROG = {}


def kernel(**inputs):
    inputs = {k: np.asarray(v) for k, v in inputs.items()}
    if "full" not in _PROG:
        _PROG["full"] = build_program(nb=2)
    nc, din, _ = _PROG["full"]
    in_maps = []
    for core in range(NCORE):
        m = prep_core(inputs, [2 * core, 2 * core + 1])
        in_maps.append({k: m[k] for k in din})
    res = run_bass_kernel_spmd(nc, in_maps, core_ids=list(range(NCORE)))
    out = np.empty((16, L, D), np.float32)
    for core in range(NCORE):
        o = res.results[core]["out_t"]
        for i in range(2):
            out[2 * core + i] = o[i].T
    return out
```

```python
import numpy as np
from contextlib import ExitStack
import concourse.bass as bass
import concourse.mybir as mybir

F32 = mybir.dt.float32
BF16 = mybir.dt.bfloat16
AF = mybir.ActivationFunctionType
ALU = mybir.AluOpType

ENGS = ("pe", "act", "dve", "pool", "sp")
EPOCH = 12000
SAME_ENGINE_SYNC = True


class Tok:
    __slots__ = ("name", "w", "w_eng", "r", "dsem", "dcount")

    def __init__(self, name):
        self.name = name
        self.w = None
        self.w_eng = None
        self.r = []
        self.dsem = None
        self.dcount = 0


class Sched:
    def __init__(self, nc, stack):
        self.nc = nc
        self.stack = stack
        self.ops = {e: [] for e in ENGS}
        self.cnt = {e: 0 for e in ENGS}
        self.sem = {e: None for e in ENGS}
        self.nsem = 0
        self.waited = {e: {} for e in ENGS}
        self.latest = {}
        self.n_ops = 0
        self.dpool = []
        self.dtoks = []

    def new_sem(self, name):
        self.nsem += 1
        return self.stack.enter_context(self.nc.semaphore(f"{name}_{self.nsem}"))

    def tok(self, name="t"):
        return Tok(name)

    def toks(self, n, name="t"):
        return [Tok(f"{name}{i}") for i in range(n)]

    def _next_event(self, eng):
        if self.sem[eng] is None or self.cnt[eng] >= EPOCH:
            self.sem[eng] = self.new_sem(f"s_{eng}")
            self.cnt[eng] = 0
        self.cnt[eng] += 1
        return (self.sem[eng], self.cnt[eng])

    def _need(self, eng, waits, ev):
        if ev is None:
            return
        sem, val = ev[0], ev[1]
        k = id(sem)
        if self.waited[eng].get(k, 0) >= val:
            return
        cur = waits.get(k)
        if cur is None or cur[1] < val:
            waits[k] = (sem, val)

    def _collect(self, eng, reads, writes, is_dma):
        waits = {}
        for t in reads:
            if t.w is not None:
                if t.w_eng == eng and not is_dma:
                    if eng != "pe" and SAME_ENGINE_SYNC:
                        self._need(eng, waits, t.w)
                else:
                    self._need(eng, waits, t.w)
        for t in writes:
            if t.w is not None:
                if t.w_eng == eng and not is_dma:
                    if eng != "pe" and SAME_ENGINE_SYNC:
                        self._need(eng, waits, t.w)
                elif is_dma and t.w_eng == "dma":
                    pass
                else:
                    self._need(eng, waits, t.w)
            for (sem, val, reng) in t.r:
                if reng == eng and not is_dma and (eng == "pe" or not SAME_ENGINE_SYNC):
                    continue
                self._need(eng, waits, (sem, val))
        wl = list(waits.values())
        for (sem, val) in wl:
            self.waited[eng][id(sem)] = val
        return wl

    def op(self, eng, fn, reads=(), writes=()):
        wl = self._collect(eng, reads, writes, False)
        ev = self._next_event(eng)
        self.ops[eng].append((wl, fn, ev[0], 1))
        self.waited[eng][id(ev[0])] = max(self.waited[eng].get(id(ev[0]), 0), 0)
        self.latest[id(ev[0])] = ev
        for t in writes:
            t.w = ev
            t.w_eng = eng
            t.r = []
        for t in reads:
            if t in writes:
                continue
            t.r = [x for x in t.r if x[2] != eng] + [(ev[0], ev[1], eng)]
        self.n_ops += 1
        return ev

    def dma(self, queue, out, in_, reads=(), writes=(), evtok=None, **kw):
        if evtok is None:
            evtok = writes[0] if len(writes) else reads[0]
        wl = self._collect(queue, reads, writes, True)
        if evtok.dsem is None:
            if self.dpool:
                evtok.dsem, evtok.dcount = self.dpool.pop()
            else:
                evtok.dsem = self.new_sem("d")
                evtok.dcount = 0
            self.dtoks.append(evtok)
        assert evtok.dcount < 60000
        evtok.dcount += 16
        ev = (evtok.dsem, evtok.dcount)
        self.latest[id(ev[0])] = ev

        def fn(e, out=out, in_=in_, kw=kw):
            return e.dma_start(out=out, in_=in_, **kw)
        self.ops[queue].append((wl, fn, ev[0], 16))
        for t in writes:
            t.w = ev
            t.w_eng = "dma"
            t.r = []
        for t in reads:
            t.r = [x for x in t.r if x[0] is not ev[0]] + [(ev[0], ev[1], "dma")]
        self.n_ops += 1
        return ev

    def barrier(self, engines=ENGS, exclude_engs=(), exclude_toks=()):
        skip = set()
        for e in exclude_engs:
            if self.sem[e] is not None:
                skip.add(id(self.sem[e]))
        for t in exclude_toks:
            if t.dsem is not None:
                skip.add(id(t.dsem))
        engines = tuple(e for e in engines if e not in exclude_engs)
        evs = [v for k, v in self.latest.items() if k not in skip]
        for e in engines:
            wl = []
            for (sem, val) in evs:
                if self.waited[e].get(id(sem), 0) >= val:
                    continue
                if sem is self.sem[e]:
                    continue
                wl.append((sem, val))
                self.waited[e][id(sem)] = val
            if wl:
                self.ops[e].append((wl, None, None, 0))
        if tuple(engines) == tuple(ENGS):
            for t in self.dtoks:
                if t.dcount < 40000:
                    self.dpool.append((t.dsem, t.dcount))
                t.dsem = None
                t.dcount = 0
            self.dtoks = []

    def emit(self):
        nc = self.nc
        with nc.Block() as block:
            def mk(engname):
                def body(e):
                    for (wl, fn, sem, inc) in self.ops[engname]:
                        for (s, v) in wl:
                            e.wait_ge(s, v)
                        if fn is not None:
                            ins = fn(e)
                            ins.then_inc(sem, inc)
                return body
            block.tensor(mk("pe"))
            block.scalar(mk("act"))
            block.vector(mk("dve"))
            block.gpsimd(mk("pool"))
            block.sync(mk("sp"))


class Arena:
    def __init__(self, nc, stack, words, name="arena"):
        self.t = stack.enter_context(nc.sbuf_tensor(name, [128, words], F32))
        self.words = words
        self.top = 0
        self.marks = []
        self.hi = words
        self.his = []

    def mark(self):
        self.marks.append(self.top)

    def release(self):
        self.top = self.marks.pop()

    def alloc_top(self, shape, dtype):
        n = int(np.prod(shape))
        w = n if dtype == F32 else (n + 1) // 2
        w = (w + 7) // 8 * 8
        self.his.append(self.hi)
        self.hi -= w
        assert self.hi >= self.top
        save = self.top
        self.top = self.hi
        hi_save = self.hi
        self.hi = self.words + 10 ** 9
        ap = self.alloc(shape, dtype)
        self.top = save
        self.hi = hi_save
        return ap

    def release_top(self):
        self.hi = self.his.pop()

    def alloc(self, shape, dtype):
        n = int(np.prod(shape))
        if dtype == F32:
            w = n
        elif dtype == BF16:
            w = (n + 1) // 2
        else:
            raise ValueError(dtype)
        w = (w + 7) // 8 * 8
        if self.top + w > min(self.words, self.hi):
            raise MemoryError(f"arena overflow: need {w} at {self.top} of {self.words}")
        ap = self.t[:, self.top:self.top + w]
        self.top += w
        if dtype == BF16:
            ap = ap.bitcast(BF16)[:, 0:n]
        else:
            ap = ap[:, 0:n]
        if len(shape) == 2:
            ap = ap.rearrange("p (a b) -> p a b", b=shape[1])
        elif len(shape) == 3:
            ap = ap.rearrange("p (a b c) -> p a b c", b=shape[1], c=shape[2])
        return ap


import math
import ml_dtypes
from concourse.bass_utils import run_bass_kernel_spmd

D = 1024
L = 2048
LC = 256
T = L + LC
DFF = 2816
NF = DFF // 128
NCORE = 8
PI = math.pi


def _wrap(S):
    def ACT(out, in_, func, reads, writes, bias=None, scale=None):
        kw = {}
        if bias is not None:
            kw["bias"] = bias
        if scale is not None:
            kw["scale"] = scale
        return S.op("act", lambda e: e.activation(out=out, in_=in_, func=func, **kw), reads, writes)

    def TT(eng, out, in0, in1, op, reads, writes):
        return S.op(eng, lambda e: e.tensor_tensor(out=out, in0=in0, in1=in1, op=op), reads, writes)

    def TS(eng, out, in0, s1, s2, op0, op1, reads, writes):
        if op1 is None:
            return S.op(eng, lambda e: e.tensor_scalar(out=out, in0=in0, scalar1=s1, scalar2=None, op0=op0), reads, writes)
        return S.op(eng, lambda e: e.tensor_scalar(out=out, in0=in0, scalar1=s1, scalar2=s2, op0=op0, op1=op1), reads, writes)

    def STT(out, in0, scalar, in1, op0, op1, reads, writes):
        return S.op("dve", lambda e: e.scalar_tensor_tensor(out=out, in0=in0, scalar=scalar, in1=in1, op0=op0, op1=op1), reads, writes)

    def MM(out, lhsT, rhs, start, stop, reads, writes):
        return S.op("pe", lambda e: e.matmul(out, lhsT=lhsT, rhs=rhs, start=start, stop=stop), reads, writes)

    def TR(out, in_, ident, reads, writes):
        return S.op("pe", lambda e: e.transpose(out, in_, ident), reads, writes)

    def CP(eng, out, in_, reads, writes):
        if eng == "act":
            return S.op("act", lambda e: e.activation(out=out, in_=in_, func=AF.Copy), reads, writes)
        return S.op(eng, lambda e: e.tensor_copy(out=out, in_=in_), reads, writes)

    def MS(eng, ap, val, writes):
        return S.op(eng, lambda e: e.memset(ap, val), (), writes)
    return ACT, TT, TS, STT, MM, TR, CP, MS


def build_program(nb=2, stop_after=None, dbg=()):
    nc = bass.Bass("TRN2", target_bir_lowering=False)
    din = {}

    def inp(name, shape, dt=F32):
        din[name] = nc.dram_tensor(name, list(shape), dt, kind="ExternalInput").ap()
        return din[name]

    x_t = inp("x_t", [nb, D, L])
    ctx_t = inp("ctx_t", [nb, D, LC])
    pos_t = inp("pos_t", [D, L])
    c_t = inp("c_t", [128, 8, 4])
    mod_w = inp("mod_w", [D, 9 * D])
    mod_b = inp("mod_b", [128, 72])
    wg = inp("wg", [2, D, DFF])
    wu = inp("wu", [2, D, DFF])
    wd = inp("wd", [2, DFF, D])
    w_in = inp("w_in", [D, 6144])
    lbl = inp("lbl", [128, 2, 8])
    normw = inp("normw", [128, 1])
    convw = inp("convw", [128, 3, 12])
    convb = inp("convb", [128, 12])
    hw1 = inp("hw1", [33, 64])
    hb1 = inp("hb1", [64, 1])
    hf1 = inp("hf1", [64, 1])
    hw2 = inp("hw2", [64, 64])
    hb2 = inp("hb2", [64, 1])
    hf2 = inp("hf2", [64, 1])
    hw3 = inp("hw3", [64, 2048])
    hbias = inp("hbias", [128, 2, 512])
    wpa = inp("wpa", [512, D])
    wpb = inp("wpb", [512, D])
    wout = inp("wout", [D, D])
    fnw = inp("fnw", [128, 8])
    zfeat = inp("zfeat", [33, L])
    win = inp("win", [128, 16, 512])
    wins = inp("wins", [128, 16, 512])
    winl = inp("winl", [1, 512])
    Fm = inp("Fm", [32, 128, 16, 128], BF16)
    Fi = inp("Fi", [16, 128, 32, 128], BF16)
    ident_d = inp("ident", [128, 128], BF16)
    masks_d = inp("masks", [64, 2, 64])

    out_t = nc.dram_tensor("out_t", [nb, D, L], F32, kind="ExternalOutput").ap()
    dbg_out = {}

    def scr(name, shape, dt):
        return nc.dram_tensor(name, list(shape), dt, kind="Internal").ap()

    wgb = scr("wgb", [2, 128, NF, 8, 128], BF16)
    wub = scr("wub", [2, 128, NF, 8, 128], BF16)
    wdb = scr("wdb", [2, 128, 8, NF, 128], BF16)
    winb = scr("winb", [128, 48, 8, 128], BF16)
    wpab = scr("wpab", [128, 8, 4, 128], BF16)
    wpbb = scr("wpbb", [128, 8, 4, 128], BF16)
    woutb = scr("woutb", [128, 8, 8, 128], BF16)
    Ksp = scr("Ksp", [2, 2, 16, 128, 512], F32)
    hs = scr("hs", [nb, 128, 8, L], F32)
    oas = scr("oas", [128, 4, L], BF16)

    with ExitStack() as st:
        S = Sched(nc, st)
        ACT, TT, TS, STT, MM, TR, CP, MS = _wrap(S)
        A = Arena(nc, st, 48000)
        pbk = [st.enter_context(nc.psum_tensor(f"pb{i}", [128, 512], F32)) for i in range(8)]
        pb = [p[:] for p in pbk]
        pbt = S.toks(8, "pb")
        pbb = [p[:].bitcast(BF16) for p in pbk]
        q_sp = "sp"

        def dump(name, ap, shape, tok, dt=F32):
            if name not in dbg:
                return
            d = nc.dram_tensor("dbg_" + name, list(shape), dt, kind="ExternalOutput").ap()
            dbg_out[name] = d
            S.dma(q_sp, d, ap, reads=[tok], evtok=tok)

        ident = A.alloc([128], BF16)
        ones_d = A.alloc([128], BF16)
        ones_v = A.alloc([128], BF16)
        ones_f = A.alloc([128], F32)
        modT = A.alloc([72, 4], F32)
        lbT = A.alloc([8], F32)
        omlT = A.alloc([8], F32)
        nomlT = A.alloc([8], F32)
        normw_s = A.alloc([1], F32)
        convw_s = A.alloc([3, 12], F32)
        convb_s = A.alloc([12], F32)
        fnw_s = A.alloc([8], F32)
        masks_s = A.alloc([2, 64], F32)
        epsb = A.alloc([1], F32)
        tconst = S.tok("const")
        S.dma(q_sp, ident, ident_d, writes=[tconst])
        S.dma(q_sp, normw_s, normw, writes=[tconst])
        S.dma(q_sp, convw_s, convw, writes=[tconst])
        S.dma(q_sp, convb_s, convb, writes=[tconst])
        S.dma(q_sp, fnw_s, fnw, writes=[tconst])
        S.dma(q_sp, masks_s[0:64], masks_d, writes=[tconst])
        tc2 = S.tok("const2")
        MS("pool", ones_d, 1.0 / 1024.0, [tc2])
        MS("pool", ones_v, 1.0 / 128.0, [tc2])
        MS("pool", ones_f, 1.0, [tc2])
        MS("pool", epsb, 1e-6, [tc2])
        CONST = [tconst, tc2]

        def conv_units():
            NSL = 3
            sfA = [A.alloc_top([8, 512], F32) for _ in range(NSL)]
            sbA = [A.alloc_top([4, 8, 128], BF16) for _ in range(NSL)]
            tf = S.toks(NSL, "cvf")
            tb = S.toks(NSL, "cvb")
            cvtoks.extend(tf + tb)
            it = 0
            ce = 0
            for s_ in range(2):
                for (src, dst) in ((wg[s_], wgb[s_]), (wu[s_], wub[s_])):
                    for g0 in range(0, NF, 4):
                        g = min(4, NF - g0)
                        sl = it % NSL
                        it += 1
                        S.dma("sp", sfA[sl][:, :, 0:g * 128],
                              src[:, g0 * 128:(g0 + g) * 128].rearrange("(k p) n -> p k n", p=128), writes=[tf[sl]])
                        for gi in range(g):
                            eng = "pool"
                            ce += 1
                            CP(eng, sbA[sl][:, gi, :, :], sfA[sl][:, :, gi * 128:(gi + 1) * 128], [tf[sl]], [tb[sl]])
                        S.dma("sp", dst[:, g0:g0 + g], sbA[sl][:, 0:g], reads=[tb[sl]], evtok=tb[sl])
                        yield
                for dc in range(8):
                    sl = it % NSL
                    it += 1
                    sfv = sfA[sl].rearrange("p a b -> p (a b)")[:, 0:NF * 128].rearrange("p (f c) -> p f c", c=128)
                    sbv = sbA[sl].rearrange("p a b c -> p (a b c)")[:, 0:NF * 128].rearrange("p (f c) -> p f c", c=128)
                    S.dma("sp", sfv, wd[s_][:, dc * 128:(dc + 1) * 128].rearrange("(f p) n -> p f n", p=128), writes=[tf[sl]])
                    for hf_ in range(2):
                        eng = "pool"
                        ce += 1
                        CP(eng, sbv[:, hf_ * 11:(hf_ + 1) * 11, :], sfv[:, hf_ * 11:(hf_ + 1) * 11, :], [tf[sl]], [tb[sl]])
                    S.dma("sp", wdb[s_][:, dc], sbv, reads=[tb[sl]], evtok=tb[sl])
                    yield

        cvtoks = []
        cgen = conv_units()
        for _ in cgen:
            pass

        def pbarrier():
            S.barrier(exclude_engs=("pool", "sp"), exclude_toks=cvtoks)

        def pump(n=1):
            for _ in range(n):
                if next(cgen, "done") == "done":
                    return

        A.mark()
        cts = A.alloc([8, 4], F32)
        scs = A.alloc([8, 4], F32)
        lbs = A.alloc([2, 8], F32)
        mbs = A.alloc([72], F32)
        mwb = [A.alloc([8, 1024], F32) for _ in range(2)]
        mwt = S.toks(2, "mw")
        tct, tsc, tlb, tmod = S.toks(4, "p0a")
        S.dma("act", cts, c_t, writes=[tct])
        S.dma("act", lbs, lbl, writes=[tlb])
        S.dma("act", mbs, mod_b, writes=[tlb])
        ACT(scs, cts, AF.Silu, [tct], [tsc])
        TT("dve", lbs[:, 0, :], lbs[:, 0, :], lbs[:, 1, :], ALU.subtract, [tlb], [tlb])
        ACT(lbT, lbs[:, 0, :], AF.Sigmoid, [tlb], [tmod])
        TS("dve", omlT, lbT, -1.0, 1.0, ALU.mult, ALU.add, [tmod], [tmod])
        TS("dve", nomlT, omlT, -1.0, None, ALU.mult, None, [tmod], [tmod])
        for j in range(9):
            sl = j % 2
            S.dma("act", mwb[sl], mod_w[:, j * 1024:(j + 1) * 1024].rearrange("(kc p) n -> p kc n", p=128),
                  writes=[mwt[sl]])
            pump(2)
            for dc in range(8):
                o0 = (j * 8 + dc) * 4
                for kc in range(8):
                    MM(pb[0][:, o0:o0 + 4], mwb[sl][:, kc, dc * 128:(dc + 1) * 128], scs[:, kc, :],
                       kc == 0, kc == 7, [mwt[sl], tsc], [pbt[0]])
        psm = pb[0][:, 0:288].rearrange("p (a b) -> p a b", b=4)
        for col in range(4):
            TT("dve", modT[:, :, col], psm[:, :, col], mbs, ALU.add, [pbt[0], tlb], [tmod])
        for j in (1, 4, 7):
            TS("dve", modT[:, j * 8:(j + 1) * 8, :], modT[:, j * 8:(j + 1) * 8, :], 1.0, None, ALU.add, None, [tmod], [tmod])
        for j in (2, 8):
            TS("dve", modT[:, j * 8:(j + 1) * 8, :], modT[:, j * 8:(j + 1) * 8, :], 0.5, None, ALU.mult, None, [tmod], [tmod])
        dump("modT", modT, [128, 72, 4], tmod)
        pbarrier()
        A.release()
        CONST.append(tmod)

        def mv(j, dc, col):
            return modT[:, j * 8 + dc, col:col + 1]

        A.mark()
        w3s = A.alloc([2048], F32)
        hsm = A.alloc([8], F32)
        h2p = A.alloc([L + 8], F32)
        winl_s = A.alloc([512], F32)
        rn = A.alloc([2, 512], F32)
        hbias_s = A.alloc([2, 512], F32)
        A.mark()
        zf = A.alloc([L], F32)
        w1s = A.alloc([64], F32)
        w2s = A.alloc([64], F32)
        h1 = A.alloc([L], F32)
        arg = A.alloc([512], F32)
        wtmp = A.alloc([512], F32)
        tk0, th1, th2, targ, theo, trn = S.toks(6, "p0c")
        thf = S.toks(2, "hf"); thb = S.toks(2, "hb"); tab = S.toks(2, "ab"); twn = S.toks(2, "wn")
        S.dma("act", zf[0:33], zfeat, writes=[tk0])
        S.dma("act", w1s[0:33], hw1, writes=[tk0])
        S.dma("act", w2s[0:64], hw2, writes=[tk0])
        S.dma("act", w3s[0:64], hw3, writes=[tk0])
        S.dma("act", hsm[0:64, 0:1], hb1, writes=[tk0])
        S.dma("act", hsm[0:64, 1:2], hf1, writes=[tk0])
        S.dma("act", hsm[0:64, 2:3], hb2, writes=[tk0])
        S.dma("act", hsm[0:64, 3:4], hf2, writes=[tk0])
        S.dma("act", winl_s[0:1], winl, writes=[tk0])
        S.dma("act", hbias_s, hbias, writes=[tk0])
        TT("dve", hsm[0:64, 4:5], hsm[0:64, 0:1], hsm[0:64, 1:2], ALU.mult, [tk0], [tk0])
        TT("dve", hsm[0:64, 5:6], hsm[0:64, 2:3], hsm[0:64, 3:4], ALU.mult, [tk0], [tk0])
        MS("dve", h2p[0:64, 0:1], 0.0, [th2])

        def sin_layer(wsb, kdim, src, dst, dst_off, fcol, fbcol, tsrc, tdst):
            for ti in range(4):
                MM(pb[1][0:64, :], wsb[0:kdim, 0:64], src[0:kdim, ti * 512:(ti + 1) * 512], True, True,
                   [tk0, tsrc], [pbt[1]])
                TS("dve", arg[0:64], pb[1][0:64, :], hsm[0:64, fcol:fcol + 1], hsm[0:64, fbcol:fbcol + 1],
                   ALU.mult, ALU.add, [pbt[1], tk0], [targ])
                for _ in range(2):
                    wrap_once(arg[0:64], targ)
                TS("dve", arg[0:64], arg[0:64], 3.14159, -3.14159, ALU.min, ALU.max, [targ], [targ])
                ACT(dst[0:64, dst_off + ti * 512: dst_off + (ti + 1) * 512], arg[0:64], AF.Sin, [targ], [tdst])

        twt = S.tok("wtmp")

        def wrap_once(ap, tok):
            TS("dve", wtmp[0:64], ap, PI, -2.0 * PI, ALU.is_gt, ALU.mult, [tok], [twt])
            TT("dve", ap, ap, wtmp[0:64], ALU.add, [tok, twt], [tok])
            TS("dve", wtmp[0:64], ap, -PI, 2.0 * PI, ALU.is_lt, ALU.mult, [tok], [twt])
            TT("dve", ap, ap, wtmp[0:64], ALU.add, [tok, twt], [tok])

        sin_layer(w1s, 33, zf, h1, 0, 1, 4, tk0, th1)
        sin_layer(w2s, 64, h1, h2p, 1, 3, 5, th1, th2)
        dump("h2", h2p[0:64, 1:L + 1], [64, L], th2)
        pbarrier()
        A.release()
        heo = A.alloc([16, 2, 512], BF16)
        hfb = [A.alloc([512], F32) for _ in range(2)]
        hbb = [A.alloc([512], F32) for _ in range(2)]
        absb = [A.alloc([512], F32) for _ in range(2)]
        winb_s = [A.alloc([512], F32) for _ in range(2)]
        winsb_s = [A.alloc([512], F32) for _ in range(2)]

        fmb = [A.alloc([2, 16, 128], BF16) for _ in range(3)]
        tfm = S.toks(3, "fm")
        kst = [A.alloc([2, 512], F32) for _ in range(2)]
        tks = S.toks(2, "kst")
        it = 0
        jj = 0
        for o in range(2):
            for lt in range(16):
                sl = lt % 2
                pump(1)
                S.dma("act", winb_s[sl], win[:, lt, :], writes=[twn[sl]])
                S.dma("act", winsb_s[sl], wins[:, lt, :], writes=[twn[sl]])
                MM(pb[2], h2p[0:64, 1 + lt * 128: 1 + (lt + 1) * 128], w3s[0:64, o * 1024: o * 1024 + 512], True, True,
                   [th2, tk0], [pbt[2]])
                MM(pb[3], h2p[0:64, lt * 128:(lt + 1) * 128], w3s[0:64, o * 1024 + 512: o * 1024 + 1024], True, True,
                   [th2, tk0], [pbt[3]])
                TT("dve", hfb[sl], pb[2], winb_s[sl], ALU.mult, [pbt[2], twn[sl]], [thf[sl]])
                TT("dve", hbb[sl], pb[3], winsb_s[sl], ALU.mult, [pbt[3], twn[sl]], [thb[sl]])
                ACT(absb[0], hfb[sl], AF.Abs, [thf[sl]], [tab[0]])
                MM(pb[4 + o], ones_f, absb[0], lt == 0, False, [tab[0], tc2], [pbt[4 + o]])
                ACT(absb[1], hbb[sl], AF.Abs, [thb[sl]], [tab[1]])
                MM(pb[4 + o], ones_f, absb[1], False, False, [tab[1], tc2], [pbt[4 + o]])
                TT("dve", heo[:, lt, 0, :], hfb[sl], hbb[sl], ALU.add, [thf[sl], thb[sl]], [theo])
                TT("dve", heo[:, lt, 1, :], hfb[sl], hbb[sl], ALU.subtract, [thf[sl], thb[sl]], [theo])
            MM(pb[2][0:1, :], h2p[0:64, L:L + 1], w3s[0:64, o * 1024 + 512: o * 1024 + 1024], True, True,
               [th2, tk0], [pbt[2]])
            TT("dve", hfb[0][0:1], pb[2][0:1, :], winl_s[0:1], ALU.mult, [pbt[2], tk0], [thf[0]])
            ACT(absb[0][0:1], hfb[0][0:1], AF.Abs, [thf[0]], [tab[0]])
            MM(pb[4 + o], ones_f[0:1, :], absb[0][0:1], False, True, [tab[0], tc2], [pbt[4 + o]])
            TS("dve", rn[:, o, :], pb[4 + o], 1e-6, None, ALU.add, None, [pbt[4 + o]], [trn])
            S.op("dve", lambda e, o=o: e.reciprocal(out=rn[:, o, :], in_=rn[:, o, :]), [trn], [trn])
            for j in range(16):
                sl = jj % 3
                jj += 1
                pump(1)
                S.dma("act", fmb[sl][:, 0], Fm[j], writes=[tfm[sl]])
                S.dma("act", fmb[sl][:, 1], Fm[16 + j], writes=[tfm[sl]])
                ks = it % 2
                it += 1
                for lc in range(16):
                    MM(pb[6], fmb[sl][:, 0, lc, :], heo[:, lc, 0, :], lc == 0, lc == 15, [tfm[sl], theo], [pbt[6]])
                for lc in range(16):
                    MM(pb[7], fmb[sl][:, 1, lc, :], heo[:, lc, 1, :], lc == 0, lc == 15, [tfm[sl], theo], [pbt[7]])
                TT("dve", kst[ks][:, 0, :], pb[6], rn[:, o, :], ALU.mult, [pbt[6], trn], [tks[ks]])
                TT("dve", kst[ks][:, 0, :], kst[ks][:, 0, :], hbias_s[:, o, :], ALU.add, [tks[ks], tk0], [tks[ks]])
                TT("dve", kst[ks][:, 1, :], pb[7], rn[:, o, :], ALU.mult, [pbt[7], trn], [tks[ks]])
                S.dma("act", Ksp[o, 0, j], kst[ks][:, 0, :], reads=[tks[ks]], evtok=tks[ks])
                S.dma("act", Ksp[o, 1, j], kst[ks][:, 1, :], reads=[tks[ks]], evtok=tks[ks])
        dump("rn", rn, [128, 2, 512], trn)
        pump(1000)
        S.barrier()
        A.release()
        for _ in range(6):
            A.release_top()
        if stop_after == "p0c":
            S.emit()
            return nc, din, dbg_out

        def rstd_of(hb, off, n, sq, tsq, rst, trst, hbtok, ones_ap, nchunks=8):
            for dc in range(nchunks):
                ACT(sq[:, dc, 0:n], hb[:, dc, off:off + n], AF.Square, [hbtok], [tsq])
            for dc in range(nchunks):
                MM(pb[7][:, 0:n], ones_ap, sq[:, dc, 0:n], dc == 0, dc == nchunks - 1, [tsq, tc2], [pbt[7]])
            ACT(rst[:, 0:n], pb[7][:, 0:n], AF.Ln, [pbt[7]], [trst], bias=epsb[:, 0:1])
            ACT(rst[:, 0:n], rst[:, 0:n], AF.Exp, [trst], [trst], scale=-0.5)

        def rstd_multi(hb, tl2, sqs, tsqs, rsts, trsts, hbtok, ones_ap):
            assert len(tl2) <= 2
            for i, (off, n) in enumerate(tl2):
                for dc in range(8):
                    ACT(sqs[i][:, dc, 0:n], hb[:, dc, off:off + n], AF.Square, [hbtok], [tsqs[i]])
            for i, (off, n) in enumerate(tl2):
                for dc in range(8):
                    MM(pb[7 - i][:, 0:n], ones_ap, sqs[i][:, dc, 0:n], dc == 0, dc == 7, [tsqs[i], tc2], [pbt[7 - i]])
            for i, (off, n) in enumerate(tl2):
                ACT(rsts[i][:, 0:n], pb[7 - i][:, 0:n], AF.Ln, [pbt[7 - i]], [trsts[i]], bias=epsb[:, 0:1])
            for i, (off, n) in enumerate(tl2):
                ACT(rsts[i][:, 0:n], rsts[i][:, 0:n], AF.Exp, [trsts[i]], [trsts[i]], scale=-0.5)

        def ffn_block(s, hb, hbtok, tiles, j0, W):
            A.mark()
            nbk = A.alloc([8, W], BF16)
            act = A.alloc([NF, W], BF16)
            actf = act.rearrange("p f w -> p (f w)")
            sqs = [actf[:, i * 4096:(i + 1) * 4096].rearrange("p (a b) -> p a b", b=512) for i in range(2)]
            rsts = [actf[:, 8192 + i * 1024: 8192 + (i + 1) * 1024].bitcast(F32) for i in range(2)]
            tmp = [A.alloc([512], F32) for _ in range(2)]
            sg = [A.alloc([512], BF16) for _ in range(2)]
            NSA, NSB = 4, 3
            wgs = [A.alloc([8, 128], BF16) for _ in range(NSA)]
            wus = [A.alloc([8, 128], BF16) for _ in range(NSA)]
            wds = [A.alloc([NF, 128], BF16) for _ in range(NSB)]
            tnb = S.toks(len(tiles), "nb")
            tact = S.toks(len(tiles), "act")
            tsqs = S.toks(2, "nrmq"); trsts = S.toks(2, "nrmr")
            ttmp = S.toks(2, "tmp"); tsg = S.toks(2, "sg")
            twg = S.toks(NSA, "wg"); twd = S.toks(NSB, "wd")
            k = 0
            assert len(tiles) == 2
            rstd_multi(hb, [(off, n) for (off, n, col) in tiles], sqs, tsqs, rsts, trsts, hbtok, ones_d)
            for ti, (off, n, col) in enumerate(tiles):
                rst, trst = rsts[ti], trsts[ti]
                for dc in range(8):
                    sl = k % 2
                    k += 1
                    TT("dve", tmp[sl][:, 0:n], hb[:, dc, off:off + n], rst[:, 0:n], ALU.mult, [hbtok, trst], [ttmp[sl]])
                    ACT(nbk[:, dc, off:off + n], tmp[sl][:, 0:n], AF.Identity, [ttmp[sl], tmod], [tnb[ti]],
                        bias=mv(j0, dc, col), scale=mv(j0 + 1, dc, col))
            k = 0
            for f in range(NF):
                sl = f % NSA
                S.dma(q_sp, wgs[sl], wgb[s, :, f], writes=[twg[sl]])
                S.dma(q_sp, wus[sl], wub[s, :, f], writes=[twg[sl]])
                for ti, (off, n, col) in enumerate(tiles):
                    pg = (2 * k) % 4
                    pu = pg + 1
                    ss = k % 2
                    k += 1
                    for kc in range(8):
                        MM(pb[pg][:, 0:n], wgs[sl][:, kc, :], nbk[:, kc, off:off + n], kc == 0, kc == 7,
                           [twg[sl], tnb[ti]], [pbt[pg]])
                    for kc in range(8):
                        MM(pb[pu][:, 0:n], wus[sl][:, kc, :], nbk[:, kc, off:off + n], kc == 0, kc == 7,
                           [twg[sl], tnb[ti]], [pbt[pu]])
                    ACT(sg[ss][:, 0:n], pb[pg][:, 0:n], AF.Silu, [pbt[pg]], [tsg[ss]])
                    TT("dve", act[:, f, off:off + n], sg[ss][:, 0:n], pb[pu][:, 0:n], ALU.mult,
                       [tsg[ss], pbt[pu]], [tact[ti]])
            k = 0
            for dc in range(8):
                sl = dc % NSB
                S.dma(q_sp, wds[sl], wdb[s, :, dc], writes=[twd[sl]])
                for ti, (off, n, col) in enumerate(tiles):
                    pp = 4 + (k % 2)
                    k += 1
                    for f in range(NF):
                        MM(pb[pp][:, 0:n], wds[sl][:, f, :], act[:, f, off:off + n], f == 0, f == NF - 1,
                           [twd[sl], tact[ti]], [pbt[pp]])
                    STT(hb[:, dc, off:off + n], pb[pp][:, 0:n], mv(j0 + 2, dc, col), hb[:, dc, off:off + n],
                        ALU.mult, ALU.add, [pbt[pp], tmod, hbtok], [hbtok])
            S.barrier()
            A.release()

        def load_w(dst, cg, tok, stg, tstg, src=None, nk=8):
            srcm = w_in if src is None else src
            S.dma(q_sp, stg[:, 0:nk, :], srcm[:, cg * 128:(cg + 1) * 128].rearrange("(kc p) n -> p kc n", p=128), writes=[tstg])
            CP("pool", dst, stg[:, 0:nk, :], [tstg], [tok])

        def proj_fm(wsb, wtok, tiles_, consume):
            for i, (off, n) in enumerate(tiles_):
                pi_ = 6 + (i % 2)
                for kc in range(8):
                    MM(pb[pi_][:, 0:n], wsb[:, kc, :], nT[:, kc, off:off + n], kc == 0, kc == 7, [wtok, tnT], [pbt[pi_]])
                consume(pi_, off, n)

        LT4 = [(0, 512), (512, 512), (1024, 512), (1536, 512)]
        LT5 = LT4 + [(2048, 256)]

        def P2(b):
            for h in range(4):
                A.mark()
                wv, wff, wfb, wq, wgt = [A.alloc([8, 128], BF16) for _ in range(5)]
                tw = S.toks(5, "hw")
                wstg = [A.alloc([8, 128], F32) for _ in range(2)]
                twstg = S.toks(2, "wstg")
                for wi, (wsb, cg, tk) in enumerate(((wv, h, tw[0]), (wq, 12 + h, tw[3]), (wgt, 16 + h, tw[4]), (wff, 4 + h, tw[1]), (wfb, 8 + h, tw[2]))):
                    load_w(wsb, cg, tk, wstg[wi % 2], twstg[wi % 2])
                vtok = A.alloc([36, 128], BF16)
                kk = A.alloc([T], F32)
                lfb = A.alloc([T], F32)
                Bb = A.alloc([T], F32)
                qf = A.alloc([L], F32)
                onesr = A.alloc([T], BF16)
                qt_ = [A.alloc([L], BF16) for _ in range(2)]
                kt_ = [A.alloc([T], BF16) for _ in range(2)]
                ktok = [A.alloc([36, 128], BF16) for _ in range(2)]
                o_ = [A.alloc([1, L], F32) for _ in range(2)]
                sgb = A.alloc([L], BF16)
                Sf = [A.alloc([128], F32) for _ in range(2)]
                Sb2 = [[A.alloc([128], BF16) for _ in range(2)] for _ in range(2)]
                tSb2 = [S.toks(2, "Sb2") for _ in range(2)]
                tmpS = [A.alloc([128], F32) for _ in range(2)]
                gcol = [A.alloc([36], F32) for _ in range(2)]
                bref = A.alloc([36], F32)
                scm = [A.alloc([64], BF16) for _ in range(2)]
                sq = A.alloc([1, 512], BF16)
                rst = A.alloc([512], F32)
                tmp = A.alloc([512], F32)
                oab = A.alloc([L], BF16)
                (tvt, tkk, tlf, tB, tq, tone, tsgb, tbref, tsq, trst, ttmp, toab) = S.toks(12, "p2")
                tqt = S.toks(2, "qt"); tkt = S.toks(2, "kt"); tktok = S.toks(2, "ktok"); to = S.toks(2, "o")
                tSf = S.toks(2, "Sf"); tSb = S.toks(2, "Sb"); ttS = S.toks(2, "tS"); tg = S.toks(2, "g"); tscm = S.toks(2, "scm")
                MS("pool", onesr, 1.0, [tone])
                for g0 in range(0, 36, 4):
                    pi_ = 6 + ((g0 // 4) % 2)
                    for ci in range(4):
                        c = g0 + ci
                        for kc in range(8):
                            MM(pb[pi_][0:64, ci * 128:(ci + 1) * 128], nT[:, kc, c * 64:(c + 1) * 64], wv[:, kc, :],
                               kc == 0, kc == 7, [tw[0], tnT], [pbt[pi_]])
                    CP("act", vtok[0:64, g0:g0 + 4, :], pb[pi_][0:64, :].rearrange("p (a b) -> p a b", b=128), [pbt[pi_]], [tvt])
                proj_fm(wq, tw[3], LT4, lambda pi_, off, n: ACT(qf[:, off:off + n], pb[pi_][:, 0:n], AF.Silu, [pbt[pi_]], [tq]))
                proj_fm(wgt, tw[4], LT4, lambda pi_, off, n: ACT(sgb[:, off:off + n], pb[pi_][:, 0:n], AF.Silu, [pbt[pi_]], [tsgb]))
                B3 = Bb.rearrange("p (c s) -> p c s", s=64)
                lf3 = lfb.rearrange("p (c s) -> p c s", s=64)
                for dr in range(2):
                    lbc = dr * 4 + h
                    wsb, wtk = (wff, tw[1]) if dr == 0 else (wfb, tw[2])
                    proj_fm(wsb, wtk, LT5, lambda pi_, off, n: ACT(kk[:, off:off + n], pb[pi_][:, 0:n], AF.Sigmoid,
                                                                    [pbt[pi_]], [tkk], scale=-1.0))
                    ACT(lfb, kk, AF.Ln, [tkk, tmod], [tlf], bias=ones_f[:, 0:1], scale=nomlT[:, lbc:lbc + 1])
                    S.op("dve", lambda e: e.tensor_tensor_scan(out=Bb, data0=onesr, data1=lfb, initial=0.0,
                                                                op0=ALU.mult, op1=ALU.add), [tone, tlf], [tB])
                    if dr == 0:
                        TT("dve", bref, B3[:, :, 0], lf3[:, :, 0], ALU.subtract, [tB, tlf], [tbref])
                        TT("dve", lf3, B3, bref.unsqueeze(2).broadcast_to([128, 36, 64]), ALU.subtract, [tB, tbref, tlf], [tlf])
                    else:
                        TT("dve", lf3, lf3, B3, ALU.subtract, [tB, tlf], [tlf])
                        TT("dve", lf3, lf3, B3[:, :, 63:64].broadcast_to([128, 36, 64]), ALU.add, [tB, tlf], [tlf])
                    ACT(Bb, lfb, AF.Exp, [tlf], [tB])
                    if dr == 0:
                        CP("dve", gcol[dr], B3[:, :, 63], [tB], [tg[dr]])
                    else:
                        CP("dve", gcol[dr], B3[:, :, 0], [tB], [tg[dr]])
                    TT("dve", qt_[dr], qf, Bb[:, 0:L], ALU.mult, [tq, tB], [tqt[dr]])
                    ACT(lfb, lfb, AF.Exp, [tlf], [tlf], scale=-1.0)
                    STT(kt_[dr], kk, omlT[:, lbc:lbc + 1], lfb, ALU.mult, ALU.mult, [tkk, tlf, tmod], [tkt[dr]])
                    for g0 in range(0, 36, 4):
                        pi_ = 6 + ((g0 // 4) % 2)
                        for ci in range(4):
                            c = g0 + ci
                            TR(pbb[pi_][0:64, ci * 128:(ci + 1) * 128], kt_[dr][:, c * 64:(c + 1) * 64], ident,
                               [tkt[dr], tconst], [pbt[pi_]])
                        CP("act", ktok[dr][0:64, g0:g0 + 4, :], pbb[pi_][0:64, 0:512].rearrange("p (a b) -> p a b", b=128),
                           [pbt[pi_]], [tktok[dr]])
                    MS("pool", tmpS[dr], 0.0, [ttS[dr]])
                    MS("pool", Sb2[dr][0], 0.0, [tSb2[dr][0]])
                orders = [[32, 33, 34, 35] + list(range(32)), [35, 34, 33, 32] + list(range(31, -1, -1))]
                for step in range(36):
                    par = step % 2
                    cc_ = [orders[dr][step] for dr in range(2)]
                    lat = cc_[0] < 32
                    if lat:
                        for dr in range(2):
                            c = cc_[dr]
                            MM(pb[dr][0:64, 0:64], kt_[dr][:, c * 64:(c + 1) * 64], qt_[dr][:, c * 64:(c + 1) * 64], True, True,
                               [tkt[dr], tqt[dr]], [pbt[dr]])
                    for dr in range(2):
                        c = cc_[dr]
                        MM(pb[4 + dr][:, 0:128], ktok[dr][0:64, c, :], vtok[0:64, c, :], True, True, [tktok[dr], tvt], [pbt[4 + dr]])
                    if lat:
                        for dr in range(2):
                            TT("dve", scm[dr][0:64], pb[dr][0:64, 0:64], masks_s[0:64, dr, :], ALU.mult,
                               [pbt[dr], tconst], [tscm[dr]])
                        for dr in range(2):
                            c = cc_[dr]
                            MM(pb[2 + dr][:, 0:64], vtok[0:64, c, :], scm[dr][0:64], True, False, [tvt, tscm[dr]], [pbt[2 + dr]])
                            MM(pb[2 + dr][:, 0:64], Sb2[dr][par], qt_[dr][:, c * 64:(c + 1) * 64], False, True,
                               [tSb2[dr][par], tqt[dr]], [pbt[2 + dr]])
                            CP("act", o_[dr][:, 0, c * 64:(c + 1) * 64], pb[2 + dr][:, 0:64], [pbt[2 + dr]], [to[dr]])
                    for dr in range(2):
                        c = cc_[dr]
                        cp_ = orders[dr][step - 1] if step > 0 else c
                        STT(tmpS[dr], tmpS[dr], gcol[dr][:, cp_:cp_ + 1], pb[4 + dr][:, 0:128], ALU.mult, ALU.add,
                            [ttS[dr], tg[dr], pbt[4 + dr]], [ttS[dr]])
                        TS("dve", Sb2[dr][1 - par], tmpS[dr], gcol[dr][:, c:c + 1], None, ALU.mult, None, [ttS[dr], tg[dr]],
                           [tSb2[dr][1 - par]])
                TT("pool", o_[0], o_[0], o_[1], ALU.add, [to[0], to[1]], [to[0]])
                for (off, n) in LT4:
                    rstd_of(o_[0], off, n, sq, tsq, rst, trst, to[0], ones_v, nchunks=1)
                    TT("dve", tmp[:, 0:n], o_[0][:, 0, off:off + n], rst[:, 0:n], ALU.mult, [to[0], trst], [ttmp])
                    STT(oab[:, off:off + n], tmp[:, 0:n], normw_s[:, 0:1], sgb[:, off:off + n], ALU.mult, ALU.mult,
                        [ttmp, tconst, tsgb], [toab])
                S.dma(q_sp, oas[:, h, :], oab, reads=[toab], evtok=toab)
                S.barrier()
                A.release()

        def P3(b, zT, tz):
            A.mark()
            gT = A.alloc([4, L], BF16)
            ztok = A.alloc([16, 512], BF16)
            pb_base = A.top
            Pbuf = A.alloc([32, 512], BF16)
            pb_end = A.top
            A.top = pb_base
            pT = A.alloc([L + 8], F32)
            uT = A.alloc([L], F32)
            wsl = [A.alloc([8, 128], BF16) for _ in range(2)]
            wstg3 = [A.alloc([8, 128], F32) for _ in range(2)]
            twstg3 = S.toks(2, "wstg3")
            assert A.top <= pb_end
            A.top = pb_end
            fib = [A.alloc([32, 128], BF16) for _ in range(2)]
            fmb = [A.alloc([2, 16, 128], BF16) for _ in range(2)]
            kb = [A.alloc([2, 512], F32) for _ in range(2)]
            tm = [A.alloc([512], F32) for _ in range(4)]
            tg_, tzt, tP, tpT, tuT = S.toks(5, "p3")
            twsl = S.toks(2, "wsl"); tfib = S.toks(2, "fib"); tfmb = S.toks(2, "fmb"); tkb = S.toks(2, "kb"); ttm = S.toks(4, "tm")

            def proj_conv(part, dst, tdst):
                S.barrier()
                MS("pool", pT[:, 0:1], 0.0, [tpT])
                MS("pool", pT[:, L + 1:L + 2], 0.0, [tpT])
                for cc in range(4):
                    sl = cc % 2
                    ci = part * 4 + cc
                    load_w(wsl[sl], 20 + ci, twsl[sl], wstg3[sl], twstg3[sl])
                    proj_fm(wsl[sl], twsl[sl], LT4,
                            lambda pi_, off, n: CP("act", pT[:, 1 + off:1 + off + n], pb[pi_][:, 0:n], [pbt[pi_]], [tpT]))
                    TS("dve", uT, pT[:, 1:L + 1], convw_s[:, 1, ci:ci + 1], convb_s[:, ci:ci + 1], ALU.mult, ALU.add,
                       [tpT, tconst], [tuT])
                    STT(uT, pT[:, 0:L], convw_s[:, 0, ci:ci + 1], uT, ALU.mult, ALU.add, [tpT, tconst, tuT], [tuT])
                    STT(dst[:, cc, :], pT[:, 2:L + 2], convw_s[:, 2, ci:ci + 1], uT, ALU.mult, ALU.add,
                        [tpT, tconst, tuT], [tdst])
                S.barrier()

            proj_conv(0, zT, tz)
            for o in range(2):
                proj_conv(1 + o, gT, tg_)
                for tt in range(16):
                    pi_ = 6 + (tt % 2)
                    for cc in range(4):
                        TR(pbb[pi_][:, cc * 128:(cc + 1) * 128], zT[:, cc, tt * 128:(tt + 1) * 128], ident,
                           [tz, tconst], [pbt[pi_]])
                    CP("act" if tt % 2 else "dve", ztok[:, tt, :], pbb[pi_][:, 0:512], [pbt[pi_]], [tzt])
                for j in range(16):
                    sl = j % 2
                    S.dma(q_sp, fmb[sl][:, 0], Fm[j], writes=[tfmb[sl]])
                    S.dma(q_sp, fmb[sl][:, 1], Fm[16 + j], writes=[tfmb[sl]])
                    S.dma(q_sp, kb[sl][:, 0, :], Ksp[o, 0, j], writes=[tkb[sl]])
                    S.dma(q_sp, kb[sl][:, 1, :], Ksp[o, 1, j], writes=[tkb[sl]])
                    pr, pim = 2 * sl, 2 * sl + 1
                    for lc in range(16):
                        MM(pb[pr], fmb[sl][:, 0, lc, :], ztok[:, lc, :], lc == 0, lc == 15, [tfmb[sl], tzt], [pbt[pr]])
                    for lc in range(16):
                        MM(pb[pim], fmb[sl][:, 1, lc, :], ztok[:, lc, :], lc == 0, lc == 15, [tfmb[sl], tzt], [pbt[pim]])
                    TT("dve", tm[0], pb[pr], kb[sl][:, 0, :], ALU.mult, [pbt[pr], tkb[sl]], [ttm[0]])
                    TT("dve", tm[1], pb[pim], kb[sl][:, 1, :], ALU.mult, [pbt[pim], tkb[sl]], [ttm[1]])
                    TT("pool", Pbuf[:, j, :], tm[0], tm[1], ALU.subtract, [ttm[0], ttm[1]], [tP])
                    TT("dve", tm[2], pb[pr], kb[sl][:, 1, :], ALU.mult, [pbt[pr], tkb[sl]], [ttm[2]])
                    TT("dve", tm[3], pb[pim], kb[sl][:, 0, :], ALU.mult, [pbt[pim], tkb[sl]], [ttm[3]])
                    TT("pool", Pbuf[:, 16 + j, :], tm[2], tm[3], ALU.add, [ttm[2], ttm[3]], [tP])
                k = 0
                for tt in range(16):
                    sl = tt % 2
                    S.dma(q_sp, fib[sl], Fi[tt], writes=[tfib[sl]])
                    for cc in range(4):
                        pi_ = 4 + (k % 2)
                        k += 1
                        for fc in range(32):
                            MM(pb[pi_][:, 0:128], Pbuf[:, fc, cc * 128:(cc + 1) * 128], fib[sl][:, fc, :], fc == 0, fc == 31,
                               [tP, tfib[sl]], [pbt[pi_]])
                        TT("dve", zT[:, cc, tt * 128:(tt + 1) * 128], gT[:, cc, tt * 128:(tt + 1) * 128], pb[pi_][:, 0:128],
                           ALU.mult, [tg_, pbt[pi_]], [tz])
            S.barrier()
            A.release()

        def P4(b, zT, tz, yT, ty):
            A.mark()
            oaT = A.alloc([4, L], BF16)
            toa = S.tok("oaT")
            S.dma(q_sp, oaT, oas, writes=[toa])
            wga = [A.alloc([8, 128], BF16) for _ in range(2)]
            wgb_ = [A.alloc([8, 128], BF16) for _ in range(2)]
            wa = [A.alloc([4, 128], BF16) for _ in range(2)]
            wb_ = [A.alloc([4, 128], BF16) for _ in range(2)]
            sga = [A.alloc([512], F32) for _ in range(2)]
            sgb2 = [A.alloc([512], F32) for _ in range(2)]
            t1 = [A.alloc([512], F32) for _ in range(2)]
            t2 = [A.alloc([512], F32) for _ in range(2)]
            tw4a = S.toks(2, "w4a"); tw4b = S.toks(2, "w4b"); tw4c = S.toks(2, "w4c"); tw4d = S.toks(2, "w4d")
            wstg4 = [A.alloc([8, 128], F32) for _ in range(2)]
            twstg4 = S.toks(2, "wstg4")
            tsa = S.toks(2, "sa"); tsb = S.toks(2, "sb"); tt1 = S.toks(2, "t1"); tt2 = S.toks(2, "t2")
            k = 0
            for dc in range(8):
                sl = dc % 2
                load_w(wga[sl], 32 + dc, tw4a[sl], wstg4[0], twstg4[0])
                load_w(wgb_[sl], 40 + dc, tw4b[sl], wstg4[1], twstg4[1])
                load_w(wa[sl], dc, tw4c[sl], wstg4[0], twstg4[0], src=wpa, nk=4)
                load_w(wb_[sl], dc, tw4d[sl], wstg4[1], twstg4[1], src=wpb, nk=4)
                for (off, n) in LT4:
                    ss = k % 2
                    k += 1
                    for kc in range(8):
                        MM(pb[4 * ss + 0], wga[sl][:, kc, :], nT[:, kc, off:off + n], kc == 0, kc == 7, [tw4a[sl], tnT], [pbt[4 * ss + 0]])
                    for kc in range(4):
                        MM(pb[4 * ss + 1], wa[sl][:, kc, :], oaT[:, kc, off:off + n], kc == 0, kc == 3, [tw4c[sl], toa], [pbt[4 * ss + 1]])
                    for kc in range(8):
                        MM(pb[4 * ss + 2], wgb_[sl][:, kc, :], nT[:, kc, off:off + n], kc == 0, kc == 7, [tw4b[sl], tnT], [pbt[4 * ss + 2]])
                    for kc in range(4):
                        MM(pb[4 * ss + 3], wb_[sl][:, kc, :], zT[:, kc, off:off + n], kc == 0, kc == 3, [tw4d[sl], tz], [pbt[4 * ss + 3]])
                    ACT(sga[ss], pb[4 * ss + 0], AF.Sigmoid, [pbt[4 * ss + 0]], [tsa[ss]])
                    ACT(sgb2[ss], pb[4 * ss + 2], AF.Sigmoid, [pbt[4 * ss + 2]], [tsb[ss]])
                    TT("dve", t1[ss], sga[ss], pb[4 * ss + 1], ALU.mult, [tsa[ss], pbt[4 * ss + 1]], [tt1[ss]])
                    TT("dve", t2[ss], sgb2[ss], pb[4 * ss + 3], ALU.mult, [tsb[ss], pbt[4 * ss + 3]], [tt2[ss]])
                    TT("pool", yT[:, dc, off:off + n], t1[ss], t2[ss], ALU.add, [tt1[ss], tt2[ss]], [ty])
            S.barrier()
            A.release()

        def P5(b, yT, ty):
            for blk in range(2):
                A.mark()
                W = 1024
                base = blk * W
                hb = A.alloc([8, W], F32)
                hbtok = S.tok("hb5")
                S.dma(q_sp, hb, hs[b, :, :, base:base + W], writes=[hbtok])
                A.mark()
                wo = [A.alloc([8, 128], BF16) for _ in range(2)]
                two = S.toks(2, "wo")
                wstg5 = [A.alloc([8, 128], F32) for _ in range(2)]
                twstg5 = S.toks(2, "wstg5")
                tiles = [(0, 512, b), (512, 512, b)]
                k = 0
                for dc in range(8):
                    sl = dc % 2
                    load_w(wo[sl], dc, two[sl], wstg5[sl], twstg5[sl], src=wout)
                    for (off, n, col) in tiles:
                        pi_ = 6 + (k % 2)
                        k += 1
                        for kc in range(8):
                            MM(pb[pi_][:, 0:n], wo[sl][:, kc, :], yT[:, kc, base + off:base + off + n], kc == 0, kc == 7,
                               [two[sl], ty], [pbt[pi_]])
                        STT(hb[:, dc, off:off + n], pb[pi_][:, 0:n], mv(5, dc, col), hb[:, dc, off:off + n],
                            ALU.mult, ALU.add, [pbt[pi_], tmod, hbtok], [hbtok])
                S.barrier()
                A.release()
                ffn_block(1, hb, hbtok, tiles, 6, W)
                A.mark()
                sqs = [A.alloc([8, 512], BF16) for _ in range(2)]
                rsts = [A.alloc([512], F32) for _ in range(2)]
                tsqs = S.toks(2, "nrm5q"); trsts = S.toks(2, "nrm5r")
                rstd_multi(hb, [(off, n) for (off, n, col) in tiles], sqs, tsqs, rsts, trsts, hbtok, ones_d)
                for tix, (off, n, col) in enumerate(tiles):
                    rst, trst = rsts[tix], trsts[tix]
                    for dc in range(8):
                        STT(hb[:, dc, off:off + n], hb[:, dc, off:off + n], fnw_s[:, dc:dc + 1], rst[:, 0:n],
                            ALU.mult, ALU.mult, [hbtok, tconst, trst], [hbtok])
                S.dma(q_sp, out_t[b][:, base:base + W].rearrange("(dc p) t -> p dc t", p=128), hb, reads=[hbtok], evtok=hbtok)
                S.barrier()
                A.release()
                A.release()

        tnT = S.tok("nT")
        nT = None
        for b in range(nb):
            A.mark()
            nT = A.alloc([8, T], BF16)
            blocks = [
                [(0, 512, b), (512, 256, b)],
                [(768, 512, b), (1280, 256, b)],
                [(1536, 512, b), (2048, 256, 2)],
            ]
            for bi, tl in enumerate(blocks):
                A.mark()
                W = 768
                base = bi * 768
                hb = A.alloc([8, W], F32)
                hbtok = S.tok("hb")
                pst = [A.alloc([8, 512], F32)]
                tps = S.toks(1, "pos")
                k = 0
                for (off, n, col) in tl:
                    lo = off - base
                    if col == 2:
                        S.dma(q_sp, hb[:, :, lo:lo + n], ctx_t[b].rearrange("(dc p) t -> p dc t", p=128), writes=[hbtok])
                    else:
                        S.dma(q_sp, hb[:, :, lo:lo + n],
                              x_t[b][:, off:off + n].rearrange("(dc p) t -> p dc t", p=128), writes=[hbtok])
                        S.dma(q_sp, pst[0][:, :, 0:n], pos_t[:, off:off + n].rearrange("(dc p) t -> p dc t", p=128),
                              writes=[tps[0]])
                        for dc in range(8):
                            k += 1
                            TT("dve", hb[:, dc, lo:lo + n], hb[:, dc, lo:lo + n],
                               pst[0][:, dc, 0:n], ALU.add, [hbtok, tps[0]], [hbtok])
                ltiles = [(off - base, n, col) for (off, n, col) in tl]
                ffn_block(0, hb, hbtok, ltiles, 0, W)
                A.mark()
                sqs = [A.alloc([8, 512], BF16) for _ in range(2)]
                rsts = [A.alloc([512], F32) for _ in range(2)]
                tmp = [A.alloc([512], F32) for _ in range(2)]
                tsqs = S.toks(2, "nrmq"); trsts = S.toks(2, "nrmr")
                ttmp = S.toks(2, "tmp")
                k = 0
                rstd_multi(hb, [(off - base, n) for (off, n, col) in tl], sqs, tsqs, rsts, trsts, hbtok, ones_d)
                for tix, (off, n, col) in enumerate(tl):
                    lo = off - base
                    rst, trst = rsts[tix], trsts[tix]
                    if col != 2:
                        S.dma(q_sp, hs[b, :, :, off:off + n], hb[:, :, lo:lo + n], reads=[hbtok], evtok=hbtok)
                    for dc in range(8):
                        sl = k % 2
                        k += 1
                        TT("dve", tmp[sl][:, 0:n], hb[:, dc, lo:lo + n], rst[:, 0:n], ALU.mult, [hbtok, trst], [ttmp[sl]])
                        ACT(nT[:, dc, off:off + n], tmp[sl][:, 0:n], AF.Identity, [ttmp[sl], tmod], [tnT],
                            bias=mv(3, dc, col), scale=mv(4, dc, col))
                S.barrier()
                A.release()
                A.release()
            if b == 0:
                dump("nT0", nT, [128, 8, T], tnT, BF16)
                dump("hs0", hs[0], [128, 8, L], tnT)
            if stop_after == "p1":
                break
            tz = S.tok("zT")
            P2(b)
            zT = A.alloc([4, L], BF16)
            if b == 0:
                dump("oas", oas, [128, 4, L], tz, BF16)
            if stop_after == "p2":
                break
            P3(b, zT, tz)
            if b == 0:
                dump("zT", zT, [128, 4, L], tz, BF16)
            if stop_after == "p3":
                break
            yT = A.alloc_top([8, L], BF16)
            ty = S.tok("yT")
            P4(b, zT, tz, yT, ty)
            if b == 0:
                dump("yT", yT, [128, 8, L], ty, BF16)
            S.barrier()
            A.release()
            if stop_after == "p4":
                break
            P5(b, yT, ty)
            A.release_top()
        S.barrier()
        S.emit()
    return nc, din, dbg_out


def _bf(a):
    return np.ascontiguousarray(a).astype(ml_dtypes.bfloat16)


_CONST_CACHE = {}


def host_consts():
    if _CONST_CACHE:
        return _CONST_CACHE
    f32 = np.float32
    quarter = D // 4
    omega = (1.0 / (10000.0 ** (np.arange(quarter, dtype=f32) / quarter))).astype(f32)
    rows = L // 64
    ar = np.arange(rows, dtype=f32)[:, None] * omega
    ac = np.arange(64, dtype=f32)[:, None] * omega
    er = np.concatenate([np.sin(ar), np.cos(ar)], axis=-1)
    ec = np.concatenate([np.sin(ac), np.cos(ac)], axis=-1)
    emb = np.concatenate([np.broadcast_to(er[:, None, :], (rows, 64, D // 2)),
                          np.broadcast_to(ec[None, :, :], (rows, 64, D // 2))], axis=-1).reshape(L, D)
    pos_t = np.ascontiguousarray(emb.T.astype(f32))
    p = np.arange(L, dtype=f32)
    t = p / (L - 1)
    w = (2.0 * math.pi * p / L).astype(f32)
    fb = np.linspace(1e-4, 15, 16, dtype=f32)
    ang = w[:, None] * fb[None, :]
    z = np.concatenate([t[:, None], np.cos(ang), -np.sin(ang)], axis=-1).astype(f32)
    zfeat = np.ascontiguousarray(z.T)
    max_decay = math.log(1e-2) / 0.3
    min_decay = math.log(1e-2) / 1.5
    deltas = np.abs(np.linspace(min_decay, max_decay, 512, dtype=f32))
    window = (np.exp(-t[:, None] * deltas[None, :]) + 0.05).astype(f32)
    win = np.ascontiguousarray(window.reshape(16, 128, 512).transpose(1, 0, 2))
    wsh = np.zeros_like(window)
    wsh[1:] = window[:-1]
    wins = np.ascontiguousarray(wsh.reshape(16, 128, 512).transpose(1, 0, 2))
    winl = np.ascontiguousarray(window[L - 1:L])
    N = 2 * L
    tt = np.arange(L, dtype=np.float64)[:, None]
    ff = (np.arange(L, dtype=np.float64) + 0.5)[None, :]
    angm = 2.0 * np.pi * tt * ff / N
    Fc = np.cos(angm)
    Fs = -np.sin(angm)
    F = np.concatenate([Fc, Fs], axis=1)
    Fm = F.reshape(16, 128, 32, 128).transpose(2, 1, 0, 3)
    Fi = (2.0 / N) * F.T
    Fi = Fi.reshape(32, 128, 16, 128).transpose(2, 1, 0, 3)
    masks = np.zeros((64, 2, 64), f32)
    si = np.arange(64)[:, None]
    ti = np.arange(64)[None, :]
    masks[:, 0, :] = (si <= ti)
    masks[:, 1, :] = (si >= ti)
    _CONST_CACHE.update(dict(pos_t=pos_t, zfeat=zfeat, win=win, wins=wins, winl=winl, Fm=_bf(Fm), Fi=_bf(Fi),
                             ident=_bf(np.eye(128, dtype=f32)), masks=masks))
    return _CONST_CACHE


def prep_core(inp, bsel):
    f32 = np.float32
    c = host_consts()
    m = dict(c)
    nbl = len(bsel)
    m["x_t"] = np.ascontiguousarray(np.stack([inp["x"][b].T for b in bsel]))
    m["ctx_t"] = np.ascontiguousarray(np.stack([inp["ctx"][b].T for b in bsel]))
    ct = np.zeros((4, D), f32)
    for i, b in enumerate(bsel):
        ct[i] = inp["c"][b]
    ct[2] = inp["c_ctx"]
    m["c_t"] = np.ascontiguousarray(ct.reshape(4, 8, 128).transpose(2, 1, 0))
    m["mod_w"] = np.ascontiguousarray(inp["mod_w"][0])
    m["mod_b"] = np.ascontiguousarray(inp["mod_b"][0].reshape(72, 128).T)
    m["wg"] = np.ascontiguousarray(inp["ffn_w_gate"][0])
    m["wu"] = np.ascontiguousarray(inp["ffn_w_up"][0])
    m["wd"] = np.ascontiguousarray(inp["ffn_w_down"][0])
    m["w_in"] = np.ascontiguousarray(inp["w_in"][0])
    m["lbl"] = np.ascontiguousarray(inp["hgrn_lb_logits"].reshape(2, 2, 4, 128).transpose(3, 0, 1, 2).reshape(128, 2, 8))
    m["normw"] = np.ascontiguousarray(inp["hgrn_norm_w"][0].reshape(128, 1))
    m["convw"] = np.ascontiguousarray(inp["hyena_conv_w"][0].reshape(3, 12, 128).transpose(2, 0, 1))
    m["convb"] = np.ascontiguousarray(inp["hyena_conv_b"][0].reshape(12, 128).T)
    m["hw1"] = np.ascontiguousarray(inp["hyena_w1"][0])
    m["hb1"] = np.ascontiguousarray(inp["hyena_b1"][0].reshape(64, 1))
    m["hf1"] = np.ascontiguousarray(inp["hyena_freq1"][0].reshape(64, 1))
    m["hw2"] = np.ascontiguousarray(inp["hyena_w2"][0])
    m["hb2"] = np.ascontiguousarray(inp["hyena_b2"][0].reshape(64, 1))
    m["hf2"] = np.ascontiguousarray(inp["hyena_freq2"][0].reshape(64, 1))
    m["hw3"] = np.ascontiguousarray(inp["hyena_w3"][0])
    m["hbias"] = np.ascontiguousarray(np.broadcast_to(inp["hyena_bias"][0][None], (128, 2, 512)))
    m["wpa"] = np.ascontiguousarray(inp["w_proj_a"][0])
    m["wpb"] = np.ascontiguousarray(inp["w_proj_b"][0])
    m["wout"] = np.ascontiguousarray(inp["w_out"][0])
    m["fnw"] = np.ascontiguousarray(inp["final_norm_w"].reshape(8, 128).T)
    return {k: (v if v.dtype == ml_dtypes.bfloat16 else v.astype(f32)) for k, v in m.items()}


_PROG = {}


def kernel(**inputs):
    inputs = {k: np.asarray(v) for k, v in inputs.items()}
    if "full" not in _PROG:
        _PROG["full"] = build_program(nb=2)
    nc, din, _ = _PROG["full"]
    in_maps = []
    for core in range(NCORE):
        m = prep_core(inputs, [2 * core, 2 * core + 1])
        in_maps.append({k: m[k] for k in din})
    res = run_bass_kernel_spmd(nc, in_maps, core_ids=list(range(NCORE)))
    out = np.empty((16, L, D), np.float32)
    for core in range(NCORE):
        o = res.results[core]["out_t"]
        for i in range(2):
            out[2 * core + i] = o[i].T
    return out
```

```python
import numpy as np
from contextlib import ExitStack
import concourse.bass as bass
import concourse.mybir as mybir

F32 = mybir.dt.float32
BF16 = mybir.dt.bfloat16
AF = mybir.ActivationFunctionType
ALU = mybir.AluOpType

ENGS = ("pe", "act", "dve", "pool", "sp")
EPOCH = 12000
SAME_ENGINE_SYNC = True


class Tok:
    __slots__ = ("name", "w", "w_eng", "r", "dsem", "dcount")

    def __init__(self, name):
        self.name = name
        self.w = None
        self.w_eng = None
        self.r = []
        self.dsem = None
        self.dcount = 0


class Sched:
    def __init__(self, nc, stack):
        self.nc = nc
        self.stack = stack
        self.ops = {e: [] for e in ENGS}
        self.cnt = {e: 0 for e in ENGS}
        self.sem = {e: None for e in ENGS}
        self.nsem = 0
        self.waited = {e: {} for e in ENGS}
        self.latest = {}
        self.n_ops = 0
        self.dpool = []
        self.dtoks = []

    def new_sem(self, name):
        self.nsem += 1
        return self.stack.enter_context(self.nc.semaphore(f"{name}_{self.nsem}"))

    def tok(self, name="t"):
        return Tok(name)

    def toks(self, n, name="t"):
        return [Tok(f"{name}{i}") for i in range(n)]

    def _next_event(self, eng):
        if self.sem[eng] is None or self.cnt[eng] >= EPOCH:
            self.sem[eng] = self.new_sem(f"s_{eng}")
            self.cnt[eng] = 0
        self.cnt[eng] += 1
        return (self.sem[eng], self.cnt[eng])

    def _need(self, eng, waits, ev):
        if ev is None:
            return
        sem, val = ev[0], ev[1]
        k = id(sem)
        if self.waited[eng].get(k, 0) >= val:
            return
        cur = waits.get(k)
        if cur is None or cur[1] < val:
            waits[k] = (sem, val)

    def _collect(self, eng, reads, writes, is_dma):
        waits = {}
        for t in reads:
            if t.w is not None:
                if t.w_eng == eng and not is_dma:
                    if eng != "pe" and SAME_ENGINE_SYNC:
                        self._need(eng, waits, t.w)
                else:
                    self._need(eng, waits, t.w)
        for t in writes:
            if t.w is not None:
                if t.w_eng == eng and not is_dma:
                    if eng != "pe" and SAME_ENGINE_SYNC:
                        self._need(eng, waits, t.w)
                elif is_dma and t.w_eng == "dma":
                    pass
                else:
                    self._need(eng, waits, t.w)
            for (sem, val, reng) in t.r:
                if reng == eng and not is_dma and (eng == "pe" or not SAME_ENGINE_SYNC):
                    continue
                self._need(eng, waits, (sem, val))
        wl = list(waits.values())
        for (sem, val) in wl:
            self.waited[eng][id(sem)] = val
        return wl

    def op(self, eng, fn, reads=(), writes=()):
        wl = self._collect(eng, reads, writes, False)
        ev = self._next_event(eng)
        self.ops[eng].append((wl, fn, ev[0], 1))
        self.waited[eng][id(ev[0])] = max(self.waited[eng].get(id(ev[0]), 0), 0)
        self.latest[id(ev[0])] = ev
        for t in writes:
            t.w = ev
            t.w_eng = eng
            t.r = []
        for t in reads:
            if t in writes:
                continue
            t.r = [x for x in t.r if x[2] != eng] + [(ev[0], ev[1], eng)]
        self.n_ops += 1
        return ev

    def dma(self, queue, out, in_, reads=(), writes=(), evtok=None, **kw):
        if evtok is None:
            evtok = writes[0] if len(writes) else reads[0]
        wl = self._collect(queue, reads, writes, True)
        if evtok.dsem is None:
            if self.dpool:
                evtok.dsem, evtok.dcount = self.dpool.pop()
            else:
                evtok.dsem = self.new_sem("d")
                evtok.dcount = 0
            self.dtoks.append(evtok)
        assert evtok.dcount < 60000
        evtok.dcount += 16
        ev = (evtok.dsem, evtok.dcount)
        self.latest[id(ev[0])] = ev

        def fn(e, out=out, in_=in_, kw=kw):
            return e.dma_start(out=out, in_=in_, **kw)
        self.ops[queue].append((wl, fn, ev[0], 16))
        for t in writes:
            t.w = ev
            t.w_eng = "dma"
            t.r = []
        for t in reads:
            t.r = [x for x in t.r if x[0] is not ev[0]] + [(ev[0], ev[1], "dma")]
        self.n_ops += 1
        return ev

    def barrier(self, engines=ENGS, exclude_engs=(), exclude_toks=()):
        skip = set()
        for e in exclude_engs:
            if self.sem[e] is not None:
                skip.add(id(self.sem[e]))
        for t in exclude_toks:
            if t.dsem is not None:
                skip.add(id(t.dsem))
        engines = tuple(e for e in engines if e not in exclude_engs)
        evs = [v for k, v in self.latest.items() if k not in skip]
        for e in engines:
            wl = []
            for (sem, val) in evs:
                if self.waited[e].get(id(sem), 0) >= val:
                    continue
                if sem is self.sem[e]:
                    continue
                wl.append((sem, val))
                self.waited[e][id(sem)] = val
            if wl:
                self.ops[e].append((wl, None, None, 0))
        if tuple(engines) == tuple(ENGS):
            for t in self.dtoks:
                if t.dcount < 40000:
                    self.dpool.append((t.dsem, t.dcount))
                t.dsem = None
                t.dcount = 0
            self.dtoks = []

    def emit(self):
        nc = self.nc
        with nc.Block() as block:
            def mk(engname):
                def body(e):
                    for (wl, fn, sem, inc) in self.ops[engname]:
                        for (s, v) in wl:
                            e.wait_ge(s, v)
                        if fn is not None:
                            ins = fn(e)
                            ins.then_inc(sem, inc)
                return body
            block.tensor(mk("pe"))
            block.scalar(mk("act"))
            block.vector(mk("dve"))
            block.gpsimd(mk("pool"))
            block.sync(mk("sp"))


class Arena:
    def __init__(self, nc, stack, words, name="arena"):
        self.t = stack.enter_context(nc.sbuf_tensor(name, [128, words], F32))
        self.words = words
        self.top = 0
        self.marks = []
        self.hi = words
        self.his = []

    def mark(self):
        self.marks.append(self.top)

    def release(self):
        self.top = self.marks.pop()

    def alloc_top(self, shape, dtype):
        n = int(np.prod(shape))
        w = n if dtype == F32 else (n + 1) // 2
        w = (w + 7) // 8 * 8
        self.his.append(self.hi)
        self.hi -= w
        assert self.hi >= self.top
        save = self.top
        self.top = self.hi
        hi_save = self.hi
        self.hi = self.words + 10 ** 9
        ap = self.alloc(shape, dtype)
        self.top = save
        self.hi = hi_save
        return ap

    def release_top(self):
        self.hi = self.his.pop()

    def alloc(self, shape, dtype):
        n = int(np.prod(shape))
        if dtype == F32:
            w = n
        elif dtype == BF16:
            w = (n + 1) // 2
        else:
            raise ValueError(dtype)
        w = (w + 7) // 8 * 8
        if self.top + w > min(self.words, self.hi):
            raise MemoryError(f"arena overflow: need {w} at {self.top} of {self.words}")
        ap = self.t[:, self.top:self.top + w]
        self.top += w
        if dtype == BF16:
            ap = ap.bitcast(BF16)[:, 0:n]
        else:
            ap = ap[:, 0:n]
        if len(shape) == 2:
            ap = ap.rearrange("p (a b) -> p a b", b=shape[1])
        elif len(shape) == 3:
            ap = ap.rearrange("p (a b c) -> p a b c", b=shape[1], c=shape[2])
        return ap


import math
import ml_dtypes
from concourse.bass_utils import run_bass_kernel_spmd

D = 1024
L = 2048
LC = 256
T = L + LC
DFF = 2816
NF = DFF // 128
NCORE = 8
PI = math.pi


def _wrap(S):
    def ACT(out, in_, func, reads, writes, bias=None, scale=None):
        kw = {}
        if bias is not None:
            kw["bias"] = bias
        if scale is not None:
            kw["scale"] = scale
        return S.op("act", lambda e: e.activation(out=out, in_=in_, func=func, **kw), reads, writes)

    def TT(eng, out, in0, in1, op, reads, writes):
        return S.op(eng, lambda e: e.tensor_tensor(out=out, in0=in0, in1=in1, op=op), reads, writes)

    def TS(eng, out, in0, s1, s2, op0, op1, reads, writes):
        if op1 is None:
            return S.op(eng, lambda e: e.tensor_scalar(out=out, in0=in0, scalar1=s1, scalar2=None, op0=op0), reads, writes)
        return S.op(eng, lambda e: e.tensor_scalar(out=out, in0=in0, scalar1=s1, scalar2=s2, op0=op0, op1=op1), reads, writes)

    def STT(out, in0, scalar, in1, op0, op1, reads, writes):
        return S.op("dve", lambda e: e.scalar_tensor_tensor(out=out, in0=in0, scalar=scalar, in1=in1, op0=op0, op1=op1), reads, writes)

    def MM(out, lhsT, rhs, start, stop, reads, writes):
        return S.op("pe", lambda e: e.matmul(out, lhsT=lhsT, rhs=rhs, start=start, stop=stop), reads, writes)

    def TR(out, in_, ident, reads, writes):
        return S.op("pe", lambda e: e.transpose(out, in_, ident), reads, writes)

    def CP(eng, out, in_, reads, writes):
        if eng == "act":
            return S.op("act", lambda e: e.activation(out=out, in_=in_, func=AF.Copy), reads, writes)
        return S.op(eng, lambda e: e.tensor_copy(out=out, in_=in_), reads, writes)

    def MS(eng, ap, val, writes):
        return S.op(eng, lambda e: e.memset(ap, val), (), writes)
    return ACT, TT, TS, STT, MM, TR, CP, MS


def build_program(nb=2, stop_after=None, dbg=()):
    nc = bass.Bass("TRN2", target_bir_lowering=False)
    din = {}

    def inp(name, shape, dt=F32):
        din[name] = nc.dram_tensor(name, list(shape), dt, kind="ExternalInput").ap()
        return din[name]

    x_t = inp("x_t", [nb, D, L])
    ctx_t = inp("ctx_t", [nb, D, LC])
    pos_t = inp("pos_t", [D, L])
    c_t = inp("c_t", [128, 8, 4])
    mod_w = inp("mod_w", [D, 9 * D])
    mod_b = inp("mod_b", [128, 72])
    wg = inp("wg", [2, D, DFF])
    wu = inp("wu", [2, D, DFF])
    wd = inp("wd", [2, DFF, D])
    w_in = inp("w_in", [D, 6144])
    lbl = inp("lbl", [128, 2, 8])
    normw = inp("normw", [128, 1])
    convw = inp("convw", [128, 3, 12])
    convb = inp("convb", [128, 12])
    hw1 = inp("hw1", [33, 64])
    hb1 = inp("hb1", [64, 1])
    hf1 = inp("hf1", [64, 1])
    hw2 = inp("hw2", [64, 64])
    hb2 = inp("hb2", [64, 1])
    hf2 = inp("hf2", [64, 1])
    hw3 = inp("hw3", [64, 2048])
    hbias = inp("hbias", [128, 2, 512])
    wpa = inp("wpa", [512, D])
    wpb = inp("wpb", [512, D])
    wout = inp("wout", [D, D])
    fnw = inp("fnw", [128, 8])
    zfeat = inp("zfeat", [33, L])
    win = inp("win", [128, 16, 512])
    wins = inp("wins", [128, 16, 512])
    winl = inp("winl", [1, 512])
    Fm = inp("Fm", [32, 128, 16, 128], BF16)
    Fi = inp("Fi", [16, 128, 32, 128], BF16)
    ident_d = inp("ident", [128, 128], BF16)
    masks_d = inp("masks", [64, 2, 64])

    out_t = nc.dram_tensor("out_t", [nb, D, L], F32, kind="ExternalOutput").ap()
    dbg_out = {}

    def scr(name, shape, dt):
        return nc.dram_tensor(name, list(shape), dt, kind="Internal").ap()

    wgb = scr("wgb", [2, 128, NF, 8, 128], BF16)
    wub = scr("wub", [2, 128, NF, 8, 128], BF16)
    wdb = scr("wdb", [2, 128, 8, NF, 128], BF16)
    winb = scr("winb", [128, 48, 8, 128], BF16)
    wpab = scr("wpab", [128, 8, 4, 128], BF16)
    wpbb = scr("wpbb", [128, 8, 4, 128], BF16)
    woutb = scr("woutb", [128, 8, 8, 128], BF16)
    Ksp = scr("Ksp", [2, 2, 16, 128, 512], F32)
    hs = scr("hs", [nb, 128, 8, L], F32)
    oas = scr("oas", [128, 4, L], BF16)

    with ExitStack() as st:
        S = Sched(nc, st)
        ACT, TT, TS, STT, MM, TR, CP, MS = _wrap(S)
        A = Arena(nc, st, 48000)
        pbk = [st.enter_context(nc.psum_tensor(f"pb{i}", [128, 512], F32)) for i in range(8)]
        pb = [p[:] for p in pbk]
        pbt = S.toks(8, "pb")
        pbb = [p[:].bitcast(BF16) for p in pbk]
        q_sp = "sp"

        def dump(name, ap, shape, tok, dt=F32):
            if name not in dbg:
                return
            d = nc.dram_tensor("dbg_" + name, list(shape), dt, kind="ExternalOutput").ap()
            dbg_out[name] = d
            S.dma(q_sp, d, ap, reads=[tok], evtok=tok)

        ident = A.alloc([128], BF16)
        ones_d = A.alloc([128], BF16)
        ones_v = A.alloc([128], BF16)
        ones_f = A.alloc([128], F32)
        modT = A.alloc([72, 4], F32)
        lbT = A.alloc([8], F32)
        omlT = A.alloc([8], F32)
        nomlT = A.alloc([8], F32)
        normw_s = A.alloc([1], F32)
        convw_s = A.alloc([3, 12], F32)
        convb_s = A.alloc([12], F32)
        fnw_s = A.alloc([8], F32)
        masks_s = A.alloc([2, 64], F32)
        epsb = A.alloc([1], F32)
        tconst = S.tok("const")
        S.dma(q_sp, ident, ident_d, writes=[tconst])
        S.dma(q_sp, normw_s, normw, writes=[tconst])
        S.dma(q_sp, convw_s, convw, writes=[tconst])
        S.dma(q_sp, convb_s, convb, writes=[tconst])
        S.dma(q_sp, fnw_s, fnw, writes=[tconst])
        S.dma(q_sp, masks_s[0:64], masks_d, writes=[tconst])
        tc2 = S.tok("const2")
        MS("pool", ones_d, 1.0 / 1024.0, [tc2])
        MS("pool", ones_v, 1.0 / 128.0, [tc2])
        MS("pool", ones_f, 1.0, [tc2])
        MS("pool", epsb, 1e-6, [tc2])
        CONST = [tconst, tc2]

        def conv_units():
            NSL = 3
            sfA = [A.alloc_top([8, 512], F32) for _ in range(NSL)]
            sbA = [A.alloc_top([4, 8, 128], BF16) for _ in range(NSL)]
            tf = S.toks(NSL, "cvf")
            tb = S.toks(NSL, "cvb")
            cvtoks.extend(tf + tb)
            it = 0
            ce = 0
            for s_ in range(2):
                for (src, dst) in ((wg[s_], wgb[s_]), (wu[s_], wub[s_])):
                    for g0 in range(0, NF, 4):
                        g = min(4, NF - g0)
                        sl = it % NSL
                        it += 1
                        S.dma("sp", sfA[sl][:, :, 0:g * 128],
                              src[:, g0 * 128:(g0 + g) * 128].rearrange("(k p) n -> p k n", p=128), writes=[tf[sl]])
                        for gi in range(g):
                            eng = "pool"
                            ce += 1
                            CP(eng, sbA[sl][:, gi, :, :], sfA[sl][:, :, gi * 128:(gi + 1) * 128], [tf[sl]], [tb[sl]])
                        S.dma("sp", dst[:, g0:g0 + g], sbA[sl][:, 0:g], reads=[tb[sl]], evtok=tb[sl])
                        yield
                for dc in range(8):
                    sl = it % NSL
                    it += 1
                    sfv = sfA[sl].rearrange("p a b -> p (a b)")[:, 0:NF * 128].rearrange("p (f c) -> p f c", c=128)
                    sbv = sbA[sl].rearrange("p a b c -> p (a b c)")[:, 0:NF * 128].rearrange("p (f c) -> p f c", c=128)
                    S.dma("sp", sfv, wd[s_][:, dc * 128:(dc + 1) * 128].rearrange("(f p) n -> p f n", p=128), writes=[tf[sl]])
                    for hf_ in range(2):
                        eng = "pool"
                        ce += 1
                        CP(eng, sbv[:, hf_ * 11:(hf_ + 1) * 11, :], sfv[:, hf_ * 11:(hf_ + 1) * 11, :], [tf[sl]], [tb[sl]])
                    S.dma("sp", wdb[s_][:, dc], sbv, reads=[tb[sl]], evtok=tb[sl])
                    yield

        cvtoks = []
        cgen = conv_units()
        for _ in cgen:
            pass

        def pbarrier():
            S.barrier(exclude_engs=("pool", "sp"), exclude_toks=cvtoks)

        def pump(n=1):
            for _ in range(n):
                if next(cgen, "done") == "done":
                    return

        A.mark()
        cts = A.alloc([8, 4], F32)
        scs = A.alloc([8, 4], F32)
        lbs = A.alloc([2, 8], F32)
        mbs = A.alloc([72], F32)
        mwb = [A.alloc([8, 1024], F32) for _ in range(2)]
        mwt = S.toks(2, "mw")
        tct, tsc, tlb, tmod = S.toks(4, "p0a")
        S.dma("act", cts, c_t, writes=[tct])
        S.dma("act", lbs, lbl, writes=[tlb])
        S.dma("act", mbs, mod_b, writes=[tlb])
        ACT(scs, cts, AF.Silu, [tct], [tsc])
        TT("dve", lbs[:, 0, :], lbs[:, 0, :], lbs[:, 1, :], ALU.subtract, [tlb], [tlb])
        ACT(lbT, lbs[:, 0, :], AF.Sigmoid, [tlb], [tmod])
        TS("dve", omlT, lbT, -1.0, 1.0, ALU.mult, ALU.add, [tmod], [tmod])
        TS("dve", nomlT, omlT, -1.0, None, ALU.mult, None, [tmod], [tmod])
        for j in range(9):
            sl = j % 2
            S.dma("act", mwb[sl], mod_w[:, j * 1024:(j + 1) * 1024].rearrange("(kc p) n -> p kc n", p=128),
                  writes=[mwt[sl]])
            pump(2)
            for dc in range(8):
                o0 = (j * 8 + dc) * 4
                for kc in range(8):
                    MM(pb[0][:, o0:o0 + 4], mwb[sl][:, kc, dc * 128:(dc + 1) * 128], scs[:, kc, :],
                       kc == 0, kc == 7, [mwt[sl], tsc], [pbt[0]])
        psm = pb[0][:, 0:288].rearrange("p (a b) -> p a b", b=4)
        for col in range(4):
            TT("dve", modT[:, :, col], psm[:, :, col], mbs, ALU.add, [pbt[0], tlb], [tmod])
        for j in (1, 4, 7):
            TS("dve", modT[:, j * 8:(j + 1) * 8, :], modT[:, j * 8:(j + 1) * 8, :], 1.0, None, ALU.add, None, [tmod], [tmod])
        for j in (2, 8):
            TS("dve", modT[:, j * 8:(j + 1) * 8, :], modT[:, j * 8:(j + 1) * 8, :], 0.5, None, ALU.mult, None, [tmod], [tmod])
        dump("modT", modT, [128, 72, 4], tmod)
        pbarrier()
        A.release()
        CONST.append(tmod)

        def mv(j, dc, col):
            return modT[:, j * 8 + dc, col:col + 1]

        A.mark()
        w3s = A.alloc([2048], F32)
        hsm = A.alloc([8], F32)
        h2p = A.alloc([L + 8], F32)
        winl_s = A.alloc([512], F32)
        rn = A.alloc([2, 512], F32)
        hbias_s = A.alloc([2, 512], F32)
        A.mark()
        zf = A.alloc([L], F32)
        w1s = A.alloc([64], F32)
        w2s = A.alloc([64], F32)
        h1 = A.alloc([L], F32)
        arg = A.alloc([512], F32)
        wtmp = A.alloc([512], F32)
        tk0, th1, th2, targ, theo, trn = S.toks(6, "p0c")
        thf = S.toks(2, "hf"); thb = S.toks(2, "hb"); tab = S.toks(2, "ab"); twn = S.toks(2, "wn")
        S.dma("act", zf[0:33], zfeat, writes=[tk0])
        S.dma("act", w1s[0:33], hw1, writes=[tk0])
        S.dma("act", w2s[0:64], hw2, writes=[tk0])
        S.dma("act", w3s[0:64], hw3, writes=[tk0])
        S.dma("act", hsm[0:64, 0:1], hb1, writes=[tk0])
        S.dma("act", hsm[0:64, 1:2], hf1, writes=[tk0])
        S.dma("act", hsm[0:64, 2:3], hb2, writes=[tk0])
        S.dma("act", hsm[0:64, 3:4], hf2, writes=[tk0])
        S.dma("act", winl_s[0:1], winl, writes=[tk0])
        S.dma("act", hbias_s, hbias, writes=[tk0])
        TT("dve", hsm[0:64, 4:5], hsm[0:64, 0:1], hsm[0:64, 1:2], ALU.mult, [tk0], [tk0])
        TT("dve", hsm[0:64, 5:6], hsm[0:64, 2:3], hsm[0:64, 3:4], ALU.mult, [tk0], [tk0])
        MS("dve", h2p[0:64, 0:1], 0.0, [th2])

        def sin_layer(wsb, kdim, src, dst, dst_off, fcol, fbcol, tsrc, tdst):
            for ti in range(4):
                MM(pb[1][0:64, :], wsb[0:kdim, 0:64], src[0:kdim, ti * 512:(ti + 1) * 512], True, True,
                   [tk0, tsrc], [pbt[1]])
                TS("dve", arg[0:64], pb[1][0:64, :], hsm[0:64, fcol:fcol + 1], hsm[0:64, fbcol:fbcol + 1],
                   ALU.mult, ALU.add, [pbt[1], tk0], [targ])
                for _ in range(2):
                    wrap_once(arg[0:64], targ)
                TS("dve", arg[0:64], arg[0:64], 3.14159, -3.14159, ALU.min, ALU.max, [targ], [targ])
                ACT(dst[0:64, dst_off + ti * 512: dst_off + (ti + 1) * 512], arg[0:64], AF.Sin, [targ], [tdst])

        twt = S.tok("wtmp")

        def wrap_once(ap, tok):
            TS("dve", wtmp[0:64], ap, PI, -2.0 * PI, ALU.is_gt, ALU.mult, [tok], [twt])
            TT("dve", ap, ap, wtmp[0:64], ALU.add, [tok, twt], [tok])
            TS("dve", wtmp[0:64], ap, -PI, 2.0 * PI, ALU.is_lt, ALU.mult, [tok], [twt])
            TT("dve", ap, ap, wtmp[0:64], ALU.add, [tok, twt], [tok])

        sin_layer(w1s, 33, zf, h1, 0, 1, 4, tk0, th1)
        sin_layer(w2s, 64, h1, h2p, 1, 3, 5, th1, th2)
        dump("h2", h2p[0:64, 1:L + 1], [64, L], th2)
        pbarrier()
        A.release()
        heo = A.alloc([16, 2, 512], BF16)
        hfb = [A.alloc([512], F32) for _ in range(2)]
        hbb = [A.alloc([512], F32) for _ in range(2)]
        absb = [A.alloc([512], F32) for _ in range(2)]
        winb_s = [A.alloc([512], F32) for _ in range(2)]
        winsb_s = [A.alloc([512], F32) for _ in range(2)]

        fmb = [A.alloc([2, 16, 128], BF16) for _ in range(3)]
        tfm = S.toks(3, "fm")
        kst = [A.alloc([2, 512], F32) for _ in range(2)]
        tks = S.toks(2, "kst")
        it = 0
        jj = 0
        for o in range(2):
            for lt in range(16):
                sl = lt % 2
                pump(1)
                S.dma("act", winb_s[sl], win[:, lt, :], writes=[twn[sl]])
                S.dma("act", winsb_s[sl], wins[:, lt, :], writes=[twn[sl]])
                MM(pb[2], h2p[0:64, 1 + lt * 128: 1 + (lt + 1) * 128], w3s[0:64, o * 1024: o * 1024 + 512], True, True,
                   [th2, tk0], [pbt[2]])
                MM(pb[3], h2p[0:64, lt * 128:(lt + 1) * 128], w3s[0:64, o * 1024 + 512: o * 1024 + 1024], True, True,
                   [th2, tk0], [pbt[3]])
                TT("dve", hfb[sl], pb[2], winb_s[sl], ALU.mult, [pbt[2], twn[sl]], [thf[sl]])
                TT("dve", hbb[sl], pb[3], winsb_s[sl], ALU.mult, [pbt[3], twn[sl]], [thb[sl]])
                ACT(absb[0], hfb[sl], AF.Abs, [thf[sl]], [tab[0]])
                MM(pb[4 + o], ones_f, absb[0], lt == 0, False, [tab[0], tc2], [pbt[4 + o]])
                ACT(absb[1], hbb[sl], AF.Abs, [thb[sl]], [tab[1]])
                MM(pb[4 + o], ones_f, absb[1], False, False, [tab[1], tc2], [pbt[4 + o]])
                TT("dve", heo[:, lt, 0, :], hfb[sl], hbb[sl], ALU.add, [thf[sl], thb[sl]], [theo])
                TT("dve", heo[:, lt, 1, :], hfb[sl], hbb[sl], ALU.subtract, [thf[sl], thb[sl]], [theo])
            MM(pb[2][0:1, :], h2p[0:64, L:L + 1], w3s[0:64, o * 1024 + 512: o * 1024 + 1024], True, True,
               [th2, tk0], [pbt[2]])
            TT("dve", hfb[0][0:1], pb[2][0:1, :], winl_s[0:1], ALU.mult, [pbt[2], tk0], [thf[0]])
            ACT(absb[0][0:1], hfb[0][0:1], AF.Abs, [thf[0]], [tab[0]])
            MM(pb[4 + o], ones_f[0:1, :], absb[0][0:1], False, True, [tab[0], tc2], [pbt[4 + o]])
            TS("dve", rn[:, o, :], pb[4 + o], 1e-6, None, ALU.add, None, [pbt[4 + o]], [trn])
            S.op("dve", lambda e, o=o: e.reciprocal(out=rn[:, o, :], in_=rn[:, o, :]), [trn], [trn])
            for j in range(16):
                sl = jj % 3
                jj += 1
                pump(1)
                S.dma("act", fmb[sl][:, 0], Fm[j], writes=[tfm[sl]])
                S.dma("act", fmb[sl][:, 1], Fm[16 + j], writes=[tfm[sl]])
                ks = it % 2
                it += 1
                for lc in range(16):
                    MM(pb[6], fmb[sl][:, 0, lc, :], heo[:, lc, 0, :], lc == 0, lc == 15, [tfm[sl], theo], [pbt[6]])
                for lc in range(16):
                    MM(pb[7], fmb[sl][:, 1, lc, :], heo[:, lc, 1, :], lc == 0, lc == 15, [tfm[sl], theo], [pbt[7]])
                TT("dve", kst[ks][:, 0, :], pb[6], rn[:, o, :], ALU.mult, [pbt[6], trn], [tks[ks]])
                TT("dve", kst[ks][:, 0, :], kst[ks][:, 0, :], hbias_s[:, o, :], ALU.add, [tks[ks], tk0], [tks[ks]])
                TT("dve", kst[ks][:, 1, :], pb[7], rn[:, o, :], ALU.mult, [pbt[7], trn], [tks[ks]])
                S.dma("act", Ksp[o, 0, j], kst[ks][:, 0, :], reads=[tks[ks]], evtok=tks[ks])
                S.dma("act", Ksp[o, 1, j], kst[ks][:, 1, :], reads=[tks[ks]], evtok=tks[ks])
        dump("rn", rn, [128, 2, 512], trn)
        pump(1000)
        S.barrier()
        A.release()
        for _ in range(6):
            A.release_top()
        if stop_after == "p0c":
            S.emit()
            return nc, din, dbg_out

        def rstd_of(hb, off, n, sq, tsq, rst, trst, hbtok, ones_ap, nchunks=8):
            for dc in range(nchunks):
                ACT(sq[:, dc, 0:n], hb[:, dc, off:off + n], AF.Square, [hbtok], [tsq])
            for dc in range(nchunks):
                MM(pb[7][:, 0:n], ones_ap, sq[:, dc, 0:n], dc == 0, dc == nchunks - 1, [tsq, tc2], [pbt[7]])
            ACT(rst[:, 0:n], pb[7][:, 0:n], AF.Ln, [pbt[7]], [trst], bias=epsb[:, 0:1])
            ACT(rst[:, 0:n], rst[:, 0:n], AF.Exp, [trst], [trst], scale=-0.5)

        def rstd_multi(hb, tl2, sqs, tsqs, rsts, trsts, hbtok, ones_ap):
            assert len(tl2) <= 2
            for i, (off, n) in enumerate(tl2):
                for dc in range(8):
                    ACT(sqs[i][:, dc, 0:n], hb[:, dc, off:off + n], AF.Square, [hbtok], [tsqs[i]])
            for i, (off, n) in enumerate(tl2):
                for dc in range(8):
                    MM(pb[7 - i][:, 0:n], ones_ap, sqs[i][:, dc, 0:n], dc == 0, dc == 7, [tsqs[i], tc2], [pbt[7 - i]])
            for i, (off, n) in enumerate(tl2):
                ACT(rsts[i][:, 0:n], pb[7 - i][:, 0:n], AF.Ln, [pbt[7 - i]], [trsts[i]], bias=epsb[:, 0:1])
            for i, (off, n) in enumerate(tl2):
                ACT(rsts[i][:, 0:n], rsts[i][:, 0:n], AF.Exp, [trsts[i]], [trsts[i]], scale=-0.5)

        def ffn_block(s, hb, hbtok, tiles, j0, W):
            A.mark()
            nbk = A.alloc([8, W], BF16)
            act = A.alloc([NF, W], BF16)
            actf = act.rearrange("p f w -> p (f w)")
            sqs = [actf[:, i * 4096:(i + 1) * 4096].rearrange("p (a b) -> p a b", b=512) for i in range(2)]
            rsts = [actf[:, 8192 + i * 1024: 8192 + (i + 1) * 1024].bitcast(F32) for i in range(2)]
            tmp = [A.alloc([512], F32) for _ in range(2)]
            sg = [A.alloc([512], BF16) for _ in range(2)]
            NSA, NSB = 4, 3
            wgs = [A.alloc([8, 128], BF16) for _ in range(NSA)]
            wus = [A.alloc([8, 128], BF16) for _ in range(NSA)]
            wds = [A.alloc([NF, 128], BF16) for _ in range(NSB)]
            tnb = S.toks(len(tiles), "nb")
            tact = S.toks(len(tiles), "act")
            tsqs = S.toks(2, "nrmq"); trsts = S.toks(2, "nrmr")
            ttmp = S.toks(2, "tmp"); tsg = S.toks(2, "sg")
            twg = S.toks(NSA, "wg"); twd = S.toks(NSB, "wd")
            k = 0
            assert len(tiles) == 2
            rstd_multi(hb, [(off, n) for (off, n, col) in tiles], sqs, tsqs, rsts, trsts, hbtok, ones_d)
            for ti, (off, n, col) in enumerate(tiles):
                rst, trst = rsts[ti], trsts[ti]
                for dc in range(8):
                    sl = k % 2
                    k += 1
                    TT("dve", tmp[sl][:, 0:n], hb[:, dc, off:off + n], rst[:, 0:n], ALU.mult, [hbtok, trst], [ttmp[sl]])
                    ACT(nbk[:, dc, off:off + n], tmp[sl][:, 0:n], AF.Identity, [ttmp[sl], tmod], [tnb[ti]],
                        bias=mv(j0, dc, col), scale=mv(j0 + 1, dc, col))
            k = 0
            for f in range(NF):
                sl = f % NSA
                S.dma(q_sp, wgs[sl], wgb[s, :, f], writes=[twg[sl]])
                S.dma(q_sp, wus[sl], wub[s, :, f], writes=[twg[sl]])
                for ti, (off, n, col) in enumerate(tiles):
                    pg = (2 * k) % 4
                    pu = pg + 1
                    ss = k % 2
                    k += 1
                    for kc in range(8):
                        MM(pb[pg][:, 0:n], wgs[sl][:, kc, :], nbk[:, kc, off:off + n], kc == 0, kc == 7,
                           [twg[sl], tnb[ti]], [pbt[pg]])
                    for kc in range(8):
                        MM(pb[pu][:, 0:n], wus[sl][:, kc, :], nbk[:, kc, off:off + n], kc == 0, kc == 7,
                           [twg[sl], tnb[ti]], [pbt[pu]])
                    ACT(sg[ss][:, 0:n], pb[pg][:, 0:n], AF.Silu, [pbt[pg]], [tsg[ss]])
                    TT("dve", act[:, f, off:off + n], sg[ss][:, 0:n], pb[pu][:, 0:n], ALU.mult,
                       [tsg[ss], pbt[pu]], [tact[ti]])
            k = 0
            for dc in range(8):
                sl = dc % NSB
                S.dma(q_sp, wds[sl], wdb[s, :, dc], writes=[twd[sl]])
                for ti, (off, n, col) in enumerate(tiles):
                    pp = 4 + (k % 2)
                    k += 1
                    for f in range(NF):
                        MM(pb[pp][:, 0:n], wds[sl][:, f, :], act[:, f, off:off + n], f == 0, f == NF - 1,
                           [twd[sl], tact[ti]], [pbt[pp]])
                    STT(hb[:, dc, off:off + n], pb[pp][:, 0:n], mv(j0 + 2, dc, col), hb[:, dc, off:off + n],
                        ALU.mult, ALU.add, [pbt[pp], tmod, hbtok], [hbtok])
            S.barrier()
            A.release()

        def load_w(dst, cg, tok, stg, tstg, src=None, nk=8):
            srcm = w_in if src is None else src
            S.dma(q_sp, stg[:, 0:nk, :], srcm[:, cg * 128:(cg + 1) * 128].rearrange("(kc p) n -> p kc n", p=128), writes=[tstg])
            CP("pool", dst, stg[:, 0:nk, :], [tstg], [tok])

        def proj_fm(wsb, wtok, tiles_, consume):
            for i, (off, n) in enumerate(tiles_):
                pi_ = 6 + (i % 2)
                for kc in range(8):
                    MM(pb[pi_][:, 0:n], wsb[:, kc, :], nT[:, kc, off:off + n], kc == 0, kc == 7, [wtok, tnT], [pbt[pi_]])
                consume(pi_, off, n)

        LT4 = [(0, 512), (512, 512), (1024, 512), (1536, 512)]
        LT5 = LT4 + [(2048, 256)]

        def P2(b):
            for h in range(4):
                A.mark()
                wv, wff, wfb, wq, wgt = [A.alloc([8, 128], BF16) for _ in range(5)]
                tw = S.toks(5, "hw")
                wstg = [A.alloc([8, 128], F32) for _ in range(2)]
                twstg = S.toks(2, "wstg")
                for wi, (wsb, cg, tk) in enumerate(((wv, h, tw[0]), (wq, 12 + h, tw[3]), (wgt, 16 + h, tw[4]), (wff, 4 + h, tw[1]), (wfb, 8 + h, tw[2]))):
                    load_w(wsb, cg, tk, wstg[wi % 2], twstg[wi % 2])
                vtok = A.alloc([36, 128], BF16)
                kk = A.alloc([T], F32)
                lfb = A.alloc([T], F32)
                Bb = A.alloc([T], F32)
                qf = A.alloc([L], F32)
                onesr = A.alloc([T], BF16)
                qt_ = [A.alloc([L], BF16) for _ in range(2)]
                kt_ = [A.alloc([T], BF16) for _ in range(2)]
                ktok = [A.alloc([36, 128], BF16) for _ in range(2)]
                o_ = [A.alloc([1, L], F32) for _ in range(2)]
                sgb = A.alloc([L], BF16)
                Sf = [A.alloc([128], F32) for _ in range(2)]
                Sb2 = [[A.alloc([128], BF16) for _ in range(2)] for _ in range(2)]
                tSb2 = [S.toks(2, "Sb2") for _ in range(2)]
                tmpS = [A.alloc([128], F32) for _ in range(2)]
                gcol = [A.alloc([36], F32) for _ in range(2)]
                bref = A.alloc([36], F32)
                scm = [A.alloc([64], BF16) for _ in range(2)]
                sq = A.alloc([1, 512], BF16)
                rst = A.alloc([512], F32)
                tmp = A.alloc([512], F32)
                oab = A.alloc([L], BF16)
                (tvt, tkk, tlf, tB, tq, tone, tsgb, tbref, tsq, trst, ttmp, toab) = S.toks(12, "p2")
                tqt = S.toks(2, "qt"); tkt = S.toks(2, "kt"); tktok = S.toks(2, "ktok"); to = S.toks(2, "o")
                tSf = S.toks(2, "Sf"); tSb = S.toks(2, "Sb"); ttS = S.toks(2, "tS"); tg = S.toks(2, "g"); tscm = S.toks(2, "scm")
                MS("pool", onesr, 1.0, [tone])
                for g0 in range(0, 36, 4):
                    pi_ = 6 + ((g0 // 4) % 2)
                    for ci in range(4):
                        c = g0 + ci
                        for kc in range(8):
                            MM(pb[pi_][0:64, ci * 128:(ci + 1) * 128], nT[:, kc, c * 64:(c + 1) * 64], wv[:, kc, :],
                               kc == 0, kc == 7, [tw[0], tnT], [pbt[pi_]])
                    CP("act", vtok[0:64, g0:g0 + 4, :], pb[pi_][0:64, :].rearrange("p (a b) -> p a b", b=128), [pbt[pi_]], [tvt])
                proj_fm(wq, tw[3], LT4, lambda pi_, off, n: ACT(qf[:, off:off + n], pb[pi_][:, 0:n], AF.Silu, [pbt[pi_]], [tq]))
                proj_fm(wgt, tw[4], LT4, lambda pi_, off, n: ACT(sgb[:, off:off + n], pb[pi_][:, 0:n], AF.Silu, [pbt[pi_]], [tsgb]))
                B3 = Bb.rearrange("p (c s) -> p c s", s=64)
                lf3 = lfb.rearrange("p (c s) -> p c s", s=64)
                for dr in range(2):
                    lbc = dr * 4 + h
                    wsb, wtk = (wff, tw[1]) if dr == 0 else (wfb, tw[2])
                    proj_fm(wsb, wtk, LT5, lambda pi_, off, n: ACT(kk[:, off:off + n], pb[pi_][:, 0:n], AF.Sigmoid,
                                                                    [pbt[pi_]], [tkk], scale=-1.0))
                    ACT(lfb, kk, AF.Ln, [tkk, tmod], [tlf], bias=ones_f[:, 0:1], scale=nomlT[:, lbc:lbc + 1])
                    S.op("dve", lambda e: e.tensor_tensor_scan(out=Bb, data0=onesr, data1=lfb, initial=0.0,
                                                                op0=ALU.mult, op1=ALU.add), [tone, tlf], [tB])
                    if dr == 0:
                        TT("dve", bref, B3[:, :, 0], lf3[:, :, 0], ALU.subtract, [tB, tlf], [tbref])
                        TT("dve", lf3, B3, bref.unsqueeze(2).broadcast_to([128, 36, 64]), ALU.subtract, [tB, tbref, tlf], [tlf])
                    else:
                        TT("dve", lf3, lf3, B3, ALU.subtract, [tB, tlf], [tlf])
                        TT("dve", lf3, lf3, B3[:, :, 63:64].broadcast_to([128, 36, 64]), ALU.add, [tB, tlf], [tlf])
                    ACT(Bb, lfb, AF.Exp, [tlf], [tB])
                    if dr == 0:
                        CP("dve", gcol[dr], B3[:, :, 63], [tB], [tg[dr]])
                    else:
                        CP("dve", gcol[dr], B3[:, :, 0], [tB], [tg[dr]])
                    TT("dve", qt_[dr], qf, Bb[:, 0:L], ALU.mult, [tq, tB], [tqt[dr]])
                    ACT(lfb, lfb, AF.Exp, [tlf], [tlf], scale=-1.0)
                    STT(kt_[dr], kk, omlT[:, lbc:lbc + 1], lfb, ALU.mult, ALU.mult, [tkk, tlf, tmod], [tkt[dr]])
                    for g0 in range(0, 36, 4):
                        pi_ = 6 + ((g0 // 4) % 2)
                        for ci in range(4):
                            c = g0 + ci
                            TR(pbb[pi_][0:64, ci * 128:(ci + 1) * 128], kt_[dr][:, c * 64:(c + 1) * 64], ident,
                               [tkt[dr], tconst], [pbt[pi_]])
                        CP("act", ktok[dr][0:64, g0:g0 + 4, :], pbb[pi_][0:64, 0:512].rearrange("p (a b) -> p a b", b=128),
                           [pbt[pi_]], [tktok[dr]])
                    MS("pool", tmpS[dr], 0.0, [ttS[dr]])
                    MS("pool", Sb2[dr][0], 0.0, [tSb2[dr][0]])
                orders = [[32, 33, 34, 35] + list(range(32)), [35, 34, 33, 32] + list(range(31, -1, -1))]
                for step in range(36):
                    par = step % 2
                    cc_ = [orders[dr][step] for dr in range(2)]
                    lat = cc_[0] < 32
                    if lat:
                        for dr in range(2):
                            c = cc_[dr]
                            MM(pb[dr][0:64, 0:64], kt_[dr][:, c * 64:(c + 1) * 64], qt_[dr][:, c * 64:(c + 1) * 64], True, True,
                               [tkt[dr], tqt[dr]], [pbt[dr]])
                    for dr in range(2):
                        c = cc_[dr]
                        MM(pb[4 + dr][:, 0:128], ktok[dr][0:64, c, :], vtok[0:64, c, :], True, True, [tktok[dr], tvt], [pbt[4 + dr]])
                    if lat:
                        for dr in range(2):
                            TT("dve", scm[dr][0:64], pb[dr][0:64, 0:64], masks_s[0:64, dr, :], ALU.mult,
                               [pbt[dr], tconst], [tscm[dr]])
                        for dr in range(2):
                            c = cc_[dr]
                            MM(pb[2 + dr][:, 0:64], vtok[0:64, c, :], scm[dr][0:64], True, False, [tvt, tscm[dr]], [pbt[2 + dr]])
                            MM(pb[2 + dr][:, 0:64], Sb2[dr][par], qt_[dr][:, c * 64:(c + 1) * 64], False, True,
                               [tSb2[dr][par], tqt[dr]], [pbt[2 + dr]])
                            CP("act", o_[dr][:, 0, c * 64:(c + 1) * 64], pb[2 + dr][:, 0:64], [pbt[2 + dr]], [to[dr]])
                    for dr in range(2):
                        c = cc_[dr]
                        cp_ = orders[dr][step - 1] if step > 0 else c
                        STT(tmpS[dr], tmpS[dr], gcol[dr][:, cp_:cp_ + 1], pb[4 + dr][:, 0:128], ALU.mult, ALU.add,
                            [ttS[dr], tg[dr], pbt[4 + dr]], [ttS[dr]])
                        TS("dve", Sb2[dr][1 - par], tmpS[dr], gcol[dr][:, c:c + 1], None, ALU.mult, None, [ttS[dr], tg[dr]],
                           [tSb2[dr][1 - par]])
                TT("pool", o_[0], o_[0], o_[1], ALU.add, [to[0], to[1]], [to[0]])
                for (off, n) in LT4:
                    rstd_of(o_[0], off, n, sq, tsq, rst, trst, to[0], ones_v, nchunks=1)
                    TT("dve", tmp[:, 0:n], o_[0][:, 0, off:off + n], rst[:, 0:n], ALU.mult, [to[0], trst], [ttmp])
                    STT(oab[:, off:off + n], tmp[:, 0:n], normw_s[:, 0:1], sgb[:, off:off + n], ALU.mult, ALU.mult,
                        [ttmp, tconst, tsgb], [toab])
                S.dma(q_sp, oas[:, h, :], oab, reads=[toab], evtok=toab)
                S.barrier()
                A.release()

        def P3(b, zT, tz):
            A.mark()
            gT = A.alloc([4, L], BF16)
            ztok = A.alloc([16, 512], BF16)
            pb_base = A.top
            Pbuf = A.alloc([32, 512], BF16)
            pb_end = A.top
            A.top = pb_base
            pT = A.alloc([L + 8], F32)
            uT = A.alloc([L], F32)
            wsl = [A.alloc([8, 128], BF16) for _ in range(2)]
            wstg3 = [A.alloc([8, 128], F32) for _ in range(2)]
            twstg3 = S.toks(2, "wstg3")
            assert A.top <= pb_end
            A.top = pb_end
            fib = [A.alloc([32, 128], BF16) for _ in range(2)]
            fmb = [A.alloc([2, 16, 128], BF16) for _ in range(2)]
            kb = [A.alloc([2, 512], F32) for _ in range(2)]
            tm = [A.alloc([512], F32) for _ in range(4)]
            tg_, tzt, tP, tpT, tuT = S.toks(5, "p3")
            twsl = S.toks(2, "wsl"); tfib = S.toks(2, "fib"); tfmb = S.toks(2, "fmb"); tkb = S.toks(2, "kb"); ttm = S.toks(4, "tm")

            def proj_conv(part, dst, tdst):
                S.barrier()
                MS("pool", pT[:, 0:1], 0.0, [tpT])
                MS("pool", pT[:, L + 1:L + 2], 0.0, [tpT])
                for cc in range(4):
                    sl = cc % 2
                    ci = part * 4 + cc
                    load_w(wsl[sl], 20 + ci, twsl[sl], wstg3[sl], twstg3[sl])
                    proj_fm(wsl[sl], twsl[sl], LT4,
                            lambda pi_, off, n: CP("act", pT[:, 1 + off:1 + off + n], pb[pi_][:, 0:n], [pbt[pi_]], [tpT]))
                    TS("dve", uT, pT[:, 1:L + 1], convw_s[:, 1, ci:ci + 1], convb_s[:, ci:ci + 1], ALU.mult, ALU.add,
                       [tpT, tconst], [tuT])
                    STT(uT, pT[:, 0:L], convw_s[:, 0, ci:ci + 1], uT, ALU.mult, ALU.add, [tpT, tconst, tuT], [tuT])
                    STT(dst[:, cc, :], pT[:, 2:L + 2], convw_s[:, 2, ci:ci + 1], uT, ALU.mult, ALU.add,
                        [tpT, tconst, tuT], [tdst])
                S.barrier()

            proj_conv(0, zT, tz)
            for o in range(2):
                proj_conv(1 + o, gT, tg_)
                for tt in range(16):
                    pi_ = 6 + (tt % 2)
                    for cc in range(4):
                        TR(pbb[pi_][:, cc * 128:(cc + 1) * 128], zT[:, cc, tt * 128:(tt + 1) * 128], ident,
                           [tz, tconst], [pbt[pi_]])
                    CP("act" if tt % 2 else "dve", ztok[:, tt, :], pbb[pi_][:, 0:512], [pbt[pi_]], [tzt])
                for j in range(16):
                    sl = j % 2
                    S.dma(q_sp, fmb[sl][:, 0], Fm[j], writes=[tfmb[sl]])
                    S.dma(q_sp, fmb[sl][:, 1], Fm[16 + j], writes=[tfmb[sl]])
                    S.dma(q_sp, kb[sl][:, 0, :], Ksp[o, 0, j], writes=[tkb[sl]])
                    S.dma(q_sp, kb[sl][:, 1, :], Ksp[o, 1, j], writes=[tkb[sl]])
                    pr, pim = 2 * sl, 2 * sl + 1
                    for lc in range(16):
                        MM(pb[pr], fmb[sl][:, 0, lc, :], ztok[:, lc, :], lc == 0, lc == 15, [tfmb[sl], tzt], [pbt[pr]])
                    for lc in range(16):
                        MM(pb[pim], fmb[sl][:, 1, lc, :], ztok[:, lc, :], lc == 0, lc == 15, [tfmb[sl], tzt], [pbt[pim]])
                    TT("dve", tm[0], pb[pr], kb[sl][:, 0, :], ALU.mult, [pbt[pr], tkb[sl]], [ttm[0]])
                    TT("dve", tm[1], pb[pim], kb[sl][:, 1, :], ALU.mult, [pbt[pim], tkb[sl]], [ttm[1]])
                    TT("pool", Pbuf[:, j, :], tm[0], tm[1], ALU.subtract, [ttm[0], ttm[1]], [tP])
                    TT("dve", tm[2], pb[pr], kb[sl][:, 1, :], ALU.mult, [pbt[pr], tkb[sl]], [ttm[2]])
                    TT("dve", tm[3], pb[pim], kb[sl][:, 0, :], ALU.mult, [pbt[pim], tkb[sl]], [ttm[3]])
                    TT("pool", Pbuf[:, 16 + j, :], tm[2], tm[3], ALU.add, [ttm[2], ttm[3]], [tP])
                k = 0
                for tt in range(16):
                    sl = tt % 2
                    S.dma(q_sp, fib[sl], Fi[tt], writes=[tfib[sl]])
                    for cc in range(4):
                        pi_ = 4 + (k % 2)
                        k += 1
                        for fc in range(32):
                            MM(pb[pi_][:, 0:128], Pbuf[:, fc, cc * 128:(cc + 1) * 128], fib[sl][:, fc, :], fc == 0, fc == 31,
                               [tP, tfib[sl]], [pbt[pi_]])
                        TT("dve", zT[:, cc, tt * 128:(tt + 1) * 128], gT[:, cc, tt * 128:(tt + 1) * 128], pb[pi_][:, 0:128],
                           ALU.mult, [tg_, pbt[pi_]], [tz])
            S.barrier()
            A.release()

        def P4(b, zT, tz, yT, ty):
            A.mark()
            oaT = A.alloc([4, L], BF16)
            toa = S.tok("oaT")
            S.dma(q_sp, oaT, oas, writes=[toa])
            wga = [A.alloc([8, 128], BF16) for _ in range(2)]
            wgb_ = [A.alloc([8, 128], BF16) for _ in range(2)]
            wa = [A.alloc([4, 128], BF16) for _ in range(2)]
            wb_ = [A.alloc([4, 128], BF16) for _ in range(2)]
            sga = [A.alloc([512], F32) for _ in range(2)]
            sgb2 = [A.alloc([512], F32) for _ in range(2)]
            t1 = [A.alloc([512], F32) for _ in range(2)]
            t2 = [A.alloc([512], F32) for _ in range(2)]
            tw4a = S.toks(2, "w4a"); tw4b = S.toks(2, "w4b"); tw4c = S.toks(2, "w4c"); tw4d = S.toks(2, "w4d")
            wstg4 = [A.alloc([8, 128], F32) for _ in range(2)]
            twstg4 = S.toks(2, "wstg4")
            tsa = S.toks(2, "sa"); tsb = S.toks(2, "sb"); tt1 = S.toks(2, "t1"); tt2 = S.toks(2, "t2")
            k = 0
            for dc in range(8):
                sl = dc % 2
                load_w(wga[sl], 32 + dc, tw4a[sl], wstg4[0], twstg4[0])
                load_w(wgb_[sl], 40 + dc, tw4b[sl], wstg4[1], twstg4[1])
                load_w(wa[sl], dc, tw4c[sl], wstg4[0], twstg4[0], src=wpa, nk=4)
                load_w(wb_[sl], dc, tw4d[sl], wstg4[1], twstg4[1], src=wpb, nk=4)
                for (off, n) in LT4:
                    ss = k % 2
                    k += 1
                    for kc in range(8):
                        MM(pb[4 * ss + 0], wga[sl][:, kc, :], nT[:, kc, off:off + n], kc == 0, kc == 7, [tw4a[sl], tnT], [pbt[4 * ss + 0]])
                    for kc in range(4):
                        MM(pb[4 * ss + 1], wa[sl][:, kc, :], oaT[:, kc, off:off + n], kc == 0, kc == 3, [tw4c[sl], toa], [pbt[4 * ss + 1]])
                    for kc in range(8):
                        MM(pb[4 * ss + 2], wgb_[sl][:, kc, :], nT[:, kc, off:off + n], kc == 0, kc == 7, [tw4b[sl], tnT], [pbt[4 * ss + 2]])
                    for kc in range(4):
                        MM(pb[4 * ss + 3], wb_[sl][:, kc, :], zT[:, kc, off:off + n], kc == 0, kc == 3, [tw4d[sl], tz], [pbt[4 * ss + 3]])
                    ACT(sga[ss], pb[4 * ss + 0], AF.Sigmoid, [pbt[4 * ss + 0]], [tsa[ss]])
                    ACT(sgb2[ss], pb[4 * ss + 2], AF.Sigmoid, [pbt[4 * ss + 2]], [tsb[ss]])
                    TT("dve", t1[ss], sga[ss], pb[4 * ss + 1], ALU.mult, [tsa[ss], pbt[4 * ss + 1]], [tt1[ss]])
                    TT("dve", t2[ss], sgb2[ss], pb[4 * ss + 3], ALU.mult, [tsb[ss], pbt[4 * ss + 3]], [tt2[ss]])
                    TT("pool", yT[:, dc, off:off + n], t1[ss], t2[ss], ALU.add, [tt1[ss], tt2[ss]], [ty])
            S.barrier()
            A.release()

        def P5(b, yT, ty):
            for blk in range(2):
                A.mark()
                W = 1024
                base = blk * W
                hb = A.alloc([8, W], F32)
                hbtok = S.tok("hb5")
                tld = S.toks(8, "hld")
                for dc in range(8):
                    S.dma(q_sp, hb[:, dc, :], hs[b, :, dc, base:base + W], writes=[tld[dc]])
                A.mark()
                wo = [A.alloc([8, 128], BF16) for _ in range(2)]
                two = S.toks(2, "wo")
                wstg5 = [A.alloc([8, 128], F32) for _ in range(2)]
                twstg5 = S.toks(2, "wstg5")
                tiles = [(0, 512, b), (512, 512, b)]
                k = 0
                for dc in range(8):
                    sl = dc % 2
                    load_w(wo[sl], dc, two[sl], wstg5[sl], twstg5[sl], src=wout)
                    for (off, n, col) in tiles:
                        pi_ = 6 + (k % 2)
                        k += 1
                        for kc in range(8):
                            MM(pb[pi_][:, 0:n], wo[sl][:, kc, :], yT[:, kc, base + off:base + off + n], kc == 0, kc == 7,
                               [two[sl], ty], [pbt[pi_]])
                        STT(hb[:, dc, off:off + n], pb[pi_][:, 0:n], mv(5, dc, col), hb[:, dc, off:off + n],
                            ALU.mult, ALU.add, [pbt[pi_], tmod, tld[dc]], [hbtok])
                S.barrier()
                A.release()
                ffn_block(1, hb, hbtok, tiles, 6, W)
                A.mark()
                sqs = [A.alloc([8, 512], BF16) for _ in range(2)]
                rsts = [A.alloc([512], F32) for _ in range(2)]
                tsqs = S.toks(2, "nrm5q"); trsts = S.toks(2, "nrm5r")
                rstd_multi(hb, [(off, n) for (off, n, col) in tiles], sqs, tsqs, rsts, trsts, hbtok, ones_d)
                for tix, (off, n, col) in enumerate(tiles):
                    rst, trst = rsts[tix], trsts[tix]
                    for dc in range(8):
                        STT(hb[:, dc, off:off + n], hb[:, dc, off:off + n], fnw_s[:, dc:dc + 1], rst[:, 0:n],
                            ALU.mult, ALU.mult, [hbtok, tconst, trst], [hbtok])
                S.dma(q_sp, out_t[b][:, base:base + W].rearrange("(dc p) t -> p dc t", p=128), hb, reads=[hbtok], evtok=hbtok)
                S.barrier()
                A.release()
                A.release()

        tnT = S.tok("nT")
        nT = None
        for b in range(nb):
            A.mark()
            nT = A.alloc([8, T], BF16)
            blocks = [
                [(0, 512, b), (512, 256, b)],
                [(768, 512, b), (1280, 256, b)],
                [(1536, 512, b), (2048, 256, 2)],
            ]
            for bi, tl in enumerate(blocks):
                A.mark()
                W = 768
                base = bi * 768
                hb = A.alloc([8, W], F32)
                hbtok = S.tok("hb")
                pst = [A.alloc([8, 512], F32)]
                tps = S.toks(1, "pos")
                k = 0
                for (off, n, col) in tl:
                    lo = off - base
                    if col == 2:
                        S.dma(q_sp, hb[:, :, lo:lo + n], ctx_t[b].rearrange("(dc p) t -> p dc t", p=128), writes=[hbtok])
                    else:
                        S.dma(q_sp, hb[:, :, lo:lo + n],
                              x_t[b][:, off:off + n].rearrange("(dc p) t -> p dc t", p=128), writes=[hbtok])
                        S.dma(q_sp, pst[0][:, :, 0:n], pos_t[:, off:off + n].rearrange("(dc p) t -> p dc t", p=128),
                              writes=[tps[0]])
                        for dc in range(8):
                            k += 1
                            TT("dve", hb[:, dc, lo:lo + n], hb[:, dc, lo:lo + n],
                               pst[0][:, dc, 0:n], ALU.add, [hbtok, tps[0]], [hbtok])
                ltiles = [(off - base, n, col) for (off, n, col) in tl]
                ffn_block(0, hb, hbtok, ltiles, 0, W)
                A.mark()
                sqs = [A.alloc([8, 512], BF16) for _ in range(2)]
                rsts = [A.alloc([512], F32) for _ in range(2)]
                tmp = [A.alloc([512], F32) for _ in range(2)]
                tsqs = S.toks(2, "nrmq"); trsts = S.toks(2, "nrmr")
                ttmp = S.toks(2, "tmp")
                k = 0
                rstd_multi(hb, [(off - base, n) for (off, n, col) in tl], sqs, tsqs, rsts, trsts, hbtok, ones_d)
                for tix, (off, n, col) in enumerate(tl):
                    lo = off - base
                    rst, trst = rsts[tix], trsts[tix]
                    if col != 2:
                        S.dma(q_sp, hs[b, :, :, off:off + n], hb[:, :, lo:lo + n], reads=[hbtok], evtok=hbtok)
                    for dc in range(8):
                        sl = k % 2
                        k += 1
                        TT("dve", tmp[sl][:, 0:n], hb[:, dc, lo:lo + n], rst[:, 0:n], ALU.mult, [hbtok, trst], [ttmp[sl]])
                        ACT(nT[:, dc, off:off + n], tmp[sl][:, 0:n], AF.Identity, [ttmp[sl], tmod], [tnT],
                            bias=mv(3, dc, col), scale=mv(4, dc, col))
                S.barrier()
                A.release()
                A.release()
            if b == 0:
                dump("nT0", nT, [128, 8, T], tnT, BF16)
                dump("hs0", hs[0], [128, 8, L], tnT)
            if stop_after == "p1":
                break
            tz = S.tok("zT")
            P2(b)
            zT = A.alloc([4, L], BF16)
            if b == 0:
                dump("oas", oas, [128, 4, L], tz, BF16)
            if stop_after == "p2":
                break
            P3(b, zT, tz)
            if b == 0:
                dump("zT", zT, [128, 4, L], tz, BF16)
            if stop_after == "p3":
                break
            yT = A.alloc_top([8, L], BF16)
            ty = S.tok("yT")
            P4(b, zT, tz, yT, ty)
            if b == 0:
                dump("yT", yT, [128, 8, L], ty, BF16)
            S.barrier()
            A.release()
            if stop_after == "p4":
                break
            P5(b, yT, ty)
            A.release_top()
        S.barrier()
        S.emit()
    return nc, din, dbg_out


def _bf(a):
    return np.ascontiguousarray(a).astype(ml_dtypes.bfloat16)


_CONST_CACHE = {}


def host_consts():
    if _CONST_CACHE:
        return _CONST_CACHE
    f32 = np.float32
    quarter = D // 4
    omega = (1.0 / (10000.0 ** (np.arange(quarter, dtype=f32) / quarter))).astype(f32)
    rows = L // 64
    ar = np.arange(rows, dtype=f32)[:, None] * omega
    ac = np.arange(64, dtype=f32)[:, None] * omega
    er = np.concatenate([np.sin(ar), np.cos(ar)], axis=-1)
    ec = np.concatenate([np.sin(ac), np.cos(ac)], axis=-1)
    emb = np.concatenate([np.broadcast_to(er[:, None, :], (rows, 64, D // 2)),
                          np.broadcast_to(ec[None, :, :], (rows, 64, D // 2))], axis=-1).reshape(L, D)
    pos_t = np.ascontiguousarray(emb.T.astype(f32))
    p = np.arange(L, dtype=f32)
    t = p / (L - 1)
    w = (2.0 * math.pi * p / L).astype(f32)
    fb = np.linspace(1e-4, 15, 16, dtype=f32)
    ang = w[:, None] * fb[None, :]
    z = np.concatenate([t[:, None], np.cos(ang), -np.sin(ang)], axis=-1).astype(f32)
    zfeat = np.ascontiguousarray(z.T)
    max_decay = math.log(1e-2) / 0.3
    min_decay = math.log(1e-2) / 1.5
    deltas = np.abs(np.linspace(min_decay, max_decay, 512, dtype=f32))
    window = (np.exp(-t[:, None] * deltas[None, :]) + 0.05).astype(f32)
    win = np.ascontiguousarray(window.reshape(16, 128, 512).transpose(1, 0, 2))
    wsh = np.zeros_like(window)
    wsh[1:] = window[:-1]
    wins = np.ascontiguousarray(wsh.reshape(16, 128, 512).transpose(1, 0, 2))
    winl = np.ascontiguousarray(window[L - 1:L])
    N = 2 * L
    tt = np.arange(L, dtype=np.float64)[:, None]
    ff = (np.arange(L, dtype=np.float64) + 0.5)[None, :]
    angm = 2.0 * np.pi * tt * ff / N
    Fc = np.cos(angm)
    Fs = -np.sin(angm)
    F = np.concatenate([Fc, Fs], axis=1)
    Fm = F.reshape(16, 128, 32, 128).transpose(2, 1, 0, 3)
    Fi = (2.0 / N) * F.T
    Fi = Fi.reshape(32, 128, 16, 128).transpose(2, 1, 0, 3)
    masks = np.zeros((64, 2, 64), f32)
    si = np.arange(64)[:, None]
    ti = np.arange(64)[None, :]
    masks[:, 0, :] = (si <= ti)
    masks[:, 1, :] = (si >= ti)
    _CONST_CACHE.update(dict(pos_t=pos_t, zfeat=zfeat, win=win, wins=wins, winl=winl, Fm=_bf(Fm), Fi=_bf(Fi),
                             ident=_bf(np.eye(128, dtype=f32)), masks=masks))
    return _CONST_CACHE


def prep_core(inp, bsel):
    f32 = np.float32
    c = host_consts()
    m = dict(c)
    nbl = len(bsel)
    m["x_t"] = np.ascontiguousarray(np.stack([inp["x"][b].T for b in bsel]))
    m["ctx_t"] = np.ascontiguousarray(np.stack([inp["ctx"][b].T for b in bsel]))
    ct = np.zeros((4, D), f32)
    for i, b in enumerate(bsel):
        ct[i] = inp["c"][b]
    ct[2] = inp["c_ctx"]
    m["c_t"] = np.ascontiguousarray(ct.reshape(4, 8, 128).transpose(2, 1, 0))
    m["mod_w"] = np.ascontiguousarray(inp["mod_w"][0])
    m["mod_b"] = np.ascontiguousarray(inp["mod_b"][0].reshape(72, 128).T)
    m["wg"] = np.ascontiguousarray(inp["ffn_w_gate"][0])
    m["wu"] = np.ascontiguousarray(inp["ffn_w_up"][0])
    m["wd"] = np.ascontiguousarray(inp["ffn_w_down"][0])
    m["w_in"] = np.ascontiguousarray(inp["w_in"][0])
    m["lbl"] = np.ascontiguousarray(inp["hgrn_lb_logits"].reshape(2, 2, 4, 128).transpose(3, 0, 1, 2).reshape(128, 2, 8))
    m["normw"] = np.ascontiguousarray(inp["hgrn_norm_w"][0].reshape(128, 1))
    m["convw"] = np.ascontiguousarray(inp["hyena_conv_w"][0].reshape(3, 12, 128).transpose(2, 0, 1))
    m["convb"] = np.ascontiguousarray(inp["hyena_conv_b"][0].reshape(12, 128).T)
    m["hw1"] = np.ascontiguousarray(inp["hyena_w1"][0])
    m["hb1"] = np.ascontiguousarray(inp["hyena_b1"][0].reshape(64, 1))
    m["hf1"] = np.ascontiguousarray(inp["hyena_freq1"][0].reshape(64, 1))
    m["hw2"] = np.ascontiguousarray(inp["hyena_w2"][0])
    m["hb2"] = np.ascontiguousarray(inp["hyena_b2"][0].reshape(64, 1))
    m["hf2"] = np.ascontiguousarray(inp["hyena_freq2"][0].reshape(64, 1))
    m["hw3"] = np.ascontiguousarray(inp["hyena_w3"][0])
    m["hbias"] = np.ascontiguousarray(np.broadcast_to(inp["hyena_bias"][0][None], (128, 2, 512)))
    m["wpa"] = np.ascontiguousarray(inp["w_proj_a"][0])
    m["wpb"] = np.ascontiguousarray(inp["w_proj_b"][0])
    m["wout"] = np.ascontiguousarray(inp["w_out"][0])
    m["fnw"] = np.ascontiguousarray(inp["final_norm_w"].reshape(8, 128).T)
    return {k: (v if v.dtype == ml_dtypes.bfloat16 else v.astype(f32)) for k, v in m.items()}


_PROG = {}


def kernel(**inputs):
    inputs = {k: np.asarray(v) for k, v in inputs.items()}
    if "full" not in _PROG:
        _PROG["full"] = build_program(nb=2)
    nc, din, _ = _PROG["full"]
    in_maps = []
    for core in range(NCORE):
        m = prep_core(inputs, [2 * core, 2 * core + 1])
        in_maps.append({k: m[k] for k in din})
    res = run_bass_kernel_spmd(nc, in_maps, core_ids=list(range(NCORE)))
    out = np.empty((16, L, D), np.float32)
    for core in range(NCORE):
        o = res.results[core]["out_t"]
        for i in range(2):
            out[2 * core + i] = o[i].T
    return out
```

```python
import numpy as np
from contextlib import ExitStack
import concourse.bass as bass
import concourse.mybir as mybir

F32 = mybir.dt.float32
BF16 = mybir.dt.bfloat16
AF = mybir.ActivationFunctionType
ALU = mybir.AluOpType

ENGS = ("pe", "act", "dve", "pool", "sp")
EPOCH = 12000
SAME_ENGINE_SYNC = True


class Tok:
    __slots__ = ("name", "w", "w_eng", "r", "dsem", "dcount")

    def __init__(self, name):
        self.name = name
        self.w = None
        self.w_eng = None
        self.r = []
        self.dsem = None
        self.dcount = 0


class Sched:
    def __init__(self, nc, stack):
        self.nc = nc
        self.stack = stack
        self.ops = {e: [] for e in ENGS}
        self.cnt = {e: 0 for e in ENGS}
        self.sem = {e: None for e in ENGS}
        self.nsem = 0
        self.waited = {e: {} for e in ENGS}
        self.latest = {}
        self.n_ops = 0
        self.dpool = []
        self.dtoks = []

    def new_sem(self, name):
        self.nsem += 1
        return self.stack.enter_context(self.nc.semaphore(f"{name}_{self.nsem}"))

    def tok(self, name="t"):
        return Tok(name)

    def toks(self, n, name="t"):
        return [Tok(f"{name}{i}") for i in range(n)]

    def _next_event(self, eng):
        if self.sem[eng] is None or self.cnt[eng] >= EPOCH:
            self.sem[eng] = self.new_sem(f"s_{eng}")
            self.cnt[eng] = 0
        self.cnt[eng] += 1
        return (self.sem[eng], self.cnt[eng])

    def _need(self, eng, waits, ev):
        if ev is None:
            return
        sem, val = ev[0], ev[1]
        k = id(sem)
        if self.waited[eng].get(k, 0) >= val:
            return
        cur = waits.get(k)
        if cur is None or cur[1] < val:
            waits[k] = (sem, val)

    def _collect(self, eng, reads, writes, is_dma):
        waits = {}
        for t in reads:
            if t.w is not None:
                if t.w_eng == eng and not is_dma:
                    if eng != "pe" and SAME_ENGINE_SYNC:
                        self._need(eng, waits, t.w)
                else:
                    self._need(eng, waits, t.w)
        for t in writes:
            if t.w is not None:
                if t.w_eng == eng and not is_dma:
                    if eng != "pe" and SAME_ENGINE_SYNC:
                        self._need(eng, waits, t.w)
                elif is_dma and t.w_eng == "dma":
                    pass
                else:
                    self._need(eng, waits, t.w)
            for (sem, val, reng) in t.r:
                if reng == eng and not is_dma and (eng == "pe" or not SAME_ENGINE_SYNC):
                    continue
                self._need(eng, waits, (sem, val))
        wl = list(waits.values())
        for (sem, val) in wl:
            self.waited[eng][id(sem)] = val
        return wl

    def op(self, eng, fn, reads=(), writes=()):
        wl = self._collect(eng, reads, writes, False)
        ev = self._next_event(eng)
        self.ops[eng].append((wl, fn, ev[0], 1))
        self.waited[eng][id(ev[0])] = max(self.waited[eng].get(id(ev[0]), 0), 0)
        self.latest[id(ev[0])] = ev
        for t in writes:
            t.w = ev
            t.w_eng = eng
            t.r = []
        for t in reads:
            if t in writes:
                continue
            t.r = [x for x in t.r if x[2] != eng] + [(ev[0], ev[1], eng)]
        self.n_ops += 1
        return ev

    def dma(self, queue, out, in_, reads=(), writes=(), evtok=None, **kw):
        if evtok is None:
            evtok = writes[0] if len(writes) else reads[0]
        wl = self._collect(queue, reads, writes, True)
        if evtok.dsem is None:
            if self.dpool:
                evtok.dsem, evtok.dcount = self.dpool.pop()
            else:
                evtok.dsem = self.new_sem("d")
                evtok.dcount = 0
            self.dtoks.append(evtok)
        assert evtok.dcount < 60000
        evtok.dcount += 16
        ev = (evtok.dsem, evtok.dcount)
        self.latest[id(ev[0])] = ev

        def fn(e, out=out, in_=in_, kw=kw):
            return e.dma_start(out=out, in_=in_, **kw)
        self.ops[queue].append((wl, fn, ev[0], 16))
        for t in writes:
            t.w = ev
            t.w_eng = "dma"
            t.r = []
        for t in reads:
            t.r = [x for x in t.r if x[0] is not ev[0]] + [(ev[0], ev[1], "dma")]
        self.n_ops += 1
        return ev

    def barrier(self, engines=ENGS, exclude_engs=(), exclude_toks=()):
        skip = set()
        for e in exclude_engs:
            if self.sem[e] is not None:
                skip.add(id(self.sem[e]))
        for t in exclude_toks:
            if t.dsem is not None:
                skip.add(id(t.dsem))
        engines = tuple(e for e in engines if e not in exclude_engs)
        evs = [v for k, v in self.latest.items() if k not in skip]
        for e in engines:
            wl = []
            for (sem, val) in evs:
                if self.waited[e].get(id(sem), 0) >= val:
                    continue
                if sem is self.sem[e]:
                    continue
                wl.append((sem, val))
                self.waited[e][id(sem)] = val
            if wl:
                self.ops[e].append((wl, None, None, 0))
        if tuple(engines) == tuple(ENGS):
            for t in self.dtoks:
                if t.dcount < 40000:
                    self.dpool.append((t.dsem, t.dcount))
                t.dsem = None
                t.dcount = 0
            self.dtoks = []

    def emit(self):
        nc = self.nc
        with nc.Block() as block:
            def mk(engname):
                def body(e):
                    for (wl, fn, sem, inc) in self.ops[engname]:
                        for (s, v) in wl:
                            e.wait_ge(s, v)
                        if fn is not None:
                            ins = fn(e)
                            ins.then_inc(sem, inc)
                return body
            block.tensor(mk("pe"))
            block.scalar(mk("act"))
            block.vector(mk("dve"))
            block.gpsimd(mk("pool"))
            block.sync(mk("sp"))


class Arena:
    def __init__(self, nc, stack, words, name="arena"):
        self.t = stack.enter_context(nc.sbuf_tensor(name, [128, words], F32))
        self.words = words
        self.top = 0
        self.marks = []
        self.hi = words
        self.his = []

    def mark(self):
        self.marks.append(self.top)

    def release(self):
        self.top = self.marks.pop()

    def alloc_top(self, shape, dtype):
        n = int(np.prod(shape))
        w = n if dtype == F32 else (n + 1) // 2
        w = (w + 7) // 8 * 8
        self.his.append(self.hi)
        self.hi -= w
        assert self.hi >= self.top
        save = self.top
        self.top = self.hi
        hi_save = self.hi
        self.hi = self.words + 10 ** 9
        ap = self.alloc(shape, dtype)
        self.top = save
        self.hi = hi_save
        return ap

    def release_top(self):
        self.hi = self.his.pop()

    def alloc(self, shape, dtype):
        n = int(np.prod(shape))
        if dtype == F32:
            w = n
        elif dtype == BF16:
            w = (n + 1) // 2
        else:
            raise ValueError(dtype)
        w = (w + 7) // 8 * 8
        if self.top + w > min(self.words, self.hi):
            raise MemoryError(f"arena overflow: need {w} at {self.top} of {self.words}")
        ap = self.t[:, self.top:self.top + w]
        self.top += w
        if dtype == BF16:
            ap = ap.bitcast(BF16)[:, 0:n]
        else:
            ap = ap[:, 0:n]
        if len(shape) == 2:
            ap = ap.rearrange("p (a b) -> p a b", b=shape[1])
        elif len(shape) == 3:
            ap = ap.rearrange("p (a b c) -> p a b c", b=shape[1], c=shape[2])
        return ap


import math
import ml_dtypes
from concourse.bass_utils import run_bass_kernel_spmd

D = 1024
L = 2048
LC = 256
T = L + LC
DFF = 2816
NF = DFF // 128
NCORE = 8
PI = math.pi


def _wrap(S):
    def ACT(out, in_, func, reads, writes, bias=None, scale=None):
        kw = {}
        if bias is not None:
            kw["bias"] = bias
        if scale is not None:
            kw["scale"] = scale
        return S.op("act", lambda e: e.activation(out=out, in_=in_, func=func, **kw), reads, writes)

    def TT(eng, out, in0, in1, op, reads, writes):
        return S.op(eng, lambda e: e.tensor_tensor(out=out, in0=in0, in1=in1, op=op), reads, writes)

    def TS(eng, out, in0, s1, s2, op0, op1, reads, writes):
        if op1 is None:
            return S.op(eng, lambda e: e.tensor_scalar(out=out, in0=in0, scalar1=s1, scalar2=None, op0=op0), reads, writes)
        return S.op(eng, lambda e: e.tensor_scalar(out=out, in0=in0, scalar1=s1, scalar2=s2, op0=op0, op1=op1), reads, writes)

    def STT(out, in0, scalar, in1, op0, op1, reads, writes):
        return S.op("dve", lambda e: e.scalar_tensor_tensor(out=out, in0=in0, scalar=scalar, in1=in1, op0=op0, op1=op1), reads, writes)

    def MM(out, lhsT, rhs, start, stop, reads, writes):
        return S.op("pe", lambda e: e.matmul(out, lhsT=lhsT, rhs=rhs, start=start, stop=stop), reads, writes)

    def TR(out, in_, ident, reads, writes):
        return S.op("pe", lambda e: e.transpose(out, in_, ident), reads, writes)

    def CP(eng, out, in_, reads, writes):
        if eng == "act":
            return S.op("act", lambda e: e.activation(out=out, in_=in_, func=AF.Copy), reads, writes)
        return S.op(eng, lambda e: e.tensor_copy(out=out, in_=in_), reads, writes)

    def MS(eng, ap, val, writes):
        return S.op(eng, lambda e: e.memset(ap, val), (), writes)
    return ACT, TT, TS, STT, MM, TR, CP, MS


def build_program(nb=2, stop_after=None, dbg=()):
    nc = bass.Bass("TRN2", target_bir_lowering=False)
    din = {}

    def inp(name, shape, dt=F32):
        din[name] = nc.dram_tensor(name, list(shape), dt, kind="ExternalInput").ap()
        return din[name]

    x_t = inp("x_t", [nb, D, L])
    ctx_t = inp("ctx_t", [nb, D, LC])
    pos_t = inp("pos_t", [D, L])
    c_t = inp("c_t", [128, 8, 4])
    mod_w = inp("mod_w", [D, 9 * D])
    mod_b = inp("mod_b", [128, 72])
    wg = inp("wg", [2, D, DFF])
    wu = inp("wu", [2, D, DFF])
    wd = inp("wd", [2, DFF, D])
    w_in = inp("w_in", [D, 6144])
    lbl = inp("lbl", [128, 2, 8])
    normw = inp("normw", [128, 1])
    convw = inp("convw", [128, 3, 12])
    convb = inp("convb", [128, 12])
    hw1 = inp("hw1", [33, 64])
    hb1 = inp("hb1", [64, 1])
    hf1 = inp("hf1", [64, 1])
    hw2 = inp("hw2", [64, 64])
    hb2 = inp("hb2", [64, 1])
    hf2 = inp("hf2", [64, 1])
    hw3 = inp("hw3", [64, 2048])
    hbias = inp("hbias", [128, 2, 512])
    wpa = inp("wpa", [512, D])
    wpb = inp("wpb", [512, D])
    wout = inp("wout", [D, D])
    fnw = inp("fnw", [128, 8])
    zfeat = inp("zfeat", [33, L])
    win = inp("win", [128, 16, 512])
    wins = inp("wins", [128, 16, 512])
    winl = inp("winl", [1, 512])
    Fm = inp("Fm", [32, 128, 16, 128], BF16)
    Fi = inp("Fi", [16, 128, 32, 128], BF16)
    ident_d = inp("ident", [128, 128], BF16)
    masks_d = inp("masks", [64, 2, 64])

    out_t = nc.dram_tensor("out_t", [nb, D, L], F32, kind="ExternalOutput").ap()
    dbg_out = {}

    def scr(name, shape, dt):
        return nc.dram_tensor(name, list(shape), dt, kind="Internal").ap()

    wgb = scr("wgb", [2, 128, NF, 8, 128], BF16)
    wub = scr("wub", [2, 128, NF, 8, 128], BF16)
    wdb = scr("wdb", [2, 128, 8, NF, 128], BF16)
    winb = scr("winb", [128, 48, 8, 128], BF16)
    wpab = scr("wpab", [128, 8, 4, 128], BF16)
    wpbb = scr("wpbb", [128, 8, 4, 128], BF16)
    woutb = scr("woutb", [128, 8, 8, 128], BF16)
    Ksp = scr("Ksp", [2, 2, 16, 128, 512], F32)
    hs = scr("hs", [nb, 128, 8, L], F32)
    oas = scr("oas", [128, 4, L], BF16)

    with ExitStack() as st:
        S = Sched(nc, st)
        ACT, TT, TS, STT, MM, TR, CP, MS = _wrap(S)
        A = Arena(nc, st, 48000)
        pbk = [st.enter_context(nc.psum_tensor(f"pb{i}", [128, 512], F32)) for i in range(8)]
        pb = [p[:] for p in pbk]
        pbt = S.toks(8, "pb")
        pbb = [p[:].bitcast(BF16) for p in pbk]
        q_sp = "sp"

        def dump(name, ap, shape, tok, dt=F32):
            if name not in dbg:
                return
            d = nc.dram_tensor("dbg_" + name, list(shape), dt, kind="ExternalOutput").ap()
            dbg_out[name] = d
            S.dma(q_sp, d, ap, reads=[tok], evtok=tok)

        ident = A.alloc([128], BF16)
        ones_d = A.alloc([128], BF16)
        ones_v = A.alloc([128], BF16)
        ones_f = A.alloc([128], F32)
        modT = A.alloc([72, 4], F32)
        lbT = A.alloc([8], F32)
        omlT = A.alloc([8], F32)
        nomlT = A.alloc([8], F32)
        normw_s = A.alloc([1], F32)
        convw_s = A.alloc([3, 12], F32)
        convb_s = A.alloc([12], F32)
        fnw_s = A.alloc([8], F32)
        masks_s = A.alloc([2, 64], F32)
        epsb = A.alloc([1], F32)
        tconst = S.tok("const")
        S.dma(q_sp, ident, ident_d, writes=[tconst])
        S.dma(q_sp, normw_s, normw, writes=[tconst])
        S.dma(q_sp, convw_s, convw, writes=[tconst])
        S.dma(q_sp, convb_s, convb, writes=[tconst])
        S.dma(q_sp, fnw_s, fnw, writes=[tconst])
        S.dma(q_sp, masks_s[0:64], masks_d, writes=[tconst])
        tc2 = S.tok("const2")
        MS("pool", ones_d, 1.0 / 1024.0, [tc2])
        MS("pool", ones_v, 1.0 / 128.0, [tc2])
        MS("pool", ones_f, 1.0, [tc2])
        MS("pool", epsb, 1e-6, [tc2])
        CONST = [tconst, tc2]

        def conv_units():
            NSL = 3
            sfA = [A.alloc_top([8, 512], F32) for _ in range(NSL)]
            sbA = [A.alloc_top([4, 8, 128], BF16) for _ in range(NSL)]
            tf = S.toks(NSL, "cvf")
            tb = S.toks(NSL, "cvb")
            cvtoks.extend(tf + tb)
            it = 0
            ce = 0
            for s_ in range(2):
                for (src, dst) in ((wg[s_], wgb[s_]), (wu[s_], wub[s_])):
                    for g0 in range(0, NF, 4):
                        g = min(4, NF - g0)
                        sl = it % NSL
                        it += 1
                        S.dma("sp", sfA[sl][:, :, 0:g * 128],
                              src[:, g0 * 128:(g0 + g) * 128].rearrange("(k p) n -> p k n", p=128), writes=[tf[sl]])
                        for gi in range(g):
                            eng = "pool"
                            ce += 1
                            CP(eng, sbA[sl][:, gi, :, :], sfA[sl][:, :, gi * 128:(gi + 1) * 128], [tf[sl]], [tb[sl]])
                        S.dma("sp", dst[:, g0:g0 + g], sbA[sl][:, 0:g], reads=[tb[sl]], evtok=tb[sl])
                        yield
                for dc in range(8):
                    sl = it % NSL
                    it += 1
                    sfv = sfA[sl].rearrange("p a b -> p (a b)")[:, 0:NF * 128].rearrange("p (f c) -> p f c", c=128)
                    sbv = sbA[sl].rearrange("p a b c -> p (a b c)")[:, 0:NF * 128].rearrange("p (f c) -> p f c", c=128)
                    S.dma("sp", sfv, wd[s_][:, dc * 128:(dc + 1) * 128].rearrange("(f p) n -> p f n", p=128), writes=[tf[sl]])
                    for hf_ in range(2):
                        eng = "pool"
                        ce += 1
                        CP(eng, sbv[:, hf_ * 11:(hf_ + 1) * 11, :], sfv[:, hf_ * 11:(hf_ + 1) * 11, :], [tf[sl]], [tb[sl]])
                    S.dma("sp", wdb[s_][:, dc], sbv, reads=[tb[sl]], evtok=tb[sl])
                    yield

        cvtoks = []
        cgen = conv_units()
        for _ in cgen:
            pass

        def pbarrier():
            S.barrier(exclude_engs=("pool", "sp"), exclude_toks=cvtoks)

        def pump(n=1):
            for _ in range(n):
                if next(cgen, "done") == "done":
                    return

        A.mark()
        cts = A.alloc([8, 4], F32)
        scs = A.alloc([8, 4], F32)
        lbs = A.alloc([2, 8], F32)
        mbs = A.alloc([72], F32)
        mwb = [A.alloc([8, 1024], F32) for _ in range(2)]
        mwt = S.toks(2, "mw")
        tct, tsc, tlb, tmod = S.toks(4, "p0a")
        S.dma("act", cts, c_t, writes=[tct])
        S.dma("act", lbs, lbl, writes=[tlb])
        S.dma("act", mbs, mod_b, writes=[tlb])
        ACT(scs, cts, AF.Silu, [tct], [tsc])
        TT("dve", lbs[:, 0, :], lbs[:, 0, :], lbs[:, 1, :], ALU.subtract, [tlb], [tlb])
        ACT(lbT, lbs[:, 0, :], AF.Sigmoid, [tlb], [tmod])
        TS("dve", omlT, lbT, -1.0, 1.0, ALU.mult, ALU.add, [tmod], [tmod])
        TS("dve", nomlT, omlT, -1.0, None, ALU.mult, None, [tmod], [tmod])
        for j in range(9):
            sl = j % 2
            S.dma("act", mwb[sl], mod_w[:, j * 1024:(j + 1) * 1024].rearrange("(kc p) n -> p kc n", p=128),
                  writes=[mwt[sl]])
            pump(2)
            for dc in range(8):
                o0 = (j * 8 + dc) * 4
                for kc in range(8):
                    MM(pb[0][:, o0:o0 + 4], mwb[sl][:, kc, dc * 128:(dc + 1) * 128], scs[:, kc, :],
                       kc == 0, kc == 7, [mwt[sl], tsc], [pbt[0]])
        psm = pb[0][:, 0:288].rearrange("p (a b) -> p a b", b=4)
        for col in range(4):
            TT("dve", modT[:, :, col], psm[:, :, col], mbs, ALU.add, [pbt[0], tlb], [tmod])
        for j in (1, 4, 7):
            TS("dve", modT[:, j * 8:(j + 1) * 8, :], modT[:, j * 8:(j + 1) * 8, :], 1.0, None, ALU.add, None, [tmod], [tmod])
        for j in (2, 8):
            TS("dve", modT[:, j * 8:(j + 1) * 8, :], modT[:, j * 8:(j + 1) * 8, :], 0.5, None, ALU.mult, None, [tmod], [tmod])
        dump("modT", modT, [128, 72, 4], tmod)
        pbarrier()
        A.release()
        CONST.append(tmod)

        def mv(j, dc, col):
            return modT[:, j * 8 + dc, col:col + 1]

        A.mark()
        w3s = A.alloc([2048], F32)
        hsm = A.alloc([8], F32)
        h2p = A.alloc([L + 8], F32)
        winl_s = A.alloc([512], F32)
        rn = A.alloc([2, 512], F32)
        hbias_s = A.alloc([2, 512], F32)
        A.mark()
        zf = A.alloc([L], F32)
        w1s = A.alloc([64], F32)
        w2s = A.alloc([64], F32)
        h1 = A.alloc([L], F32)
        arg = A.alloc([512], F32)
        wtmp = A.alloc([512], F32)
        tk0, th1, th2, targ, theo, trn = S.toks(6, "p0c")
        thf = S.toks(2, "hf"); thb = S.toks(2, "hb"); tab = S.toks(2, "ab"); twn = S.toks(2, "wn")
        S.dma("act", zf[0:33], zfeat, writes=[tk0])
        S.dma("act", w1s[0:33], hw1, writes=[tk0])
        S.dma("act", w2s[0:64], hw2, writes=[tk0])
        S.dma("act", w3s[0:64], hw3, writes=[tk0])
        S.dma("act", hsm[0:64, 0:1], hb1, writes=[tk0])
        S.dma("act", hsm[0:64, 1:2], hf1, writes=[tk0])
        S.dma("act", hsm[0:64, 2:3], hb2, writes=[tk0])
        S.dma("act", hsm[0:64, 3:4], hf2, writes=[tk0])
        S.dma("act", winl_s[0:1], winl, writes=[tk0])
        S.dma("act", hbias_s, hbias, writes=[tk0])
        TT("dve", hsm[0:64, 4:5], hsm[0:64, 0:1], hsm[0:64, 1:2], ALU.mult, [tk0], [tk0])
        TT("dve", hsm[0:64, 5:6], hsm[0:64, 2:3], hsm[0:64, 3:4], ALU.mult, [tk0], [tk0])
        MS("dve", h2p[0:64, 0:1], 0.0, [th2])

        def sin_layer(wsb, kdim, src, dst, dst_off, fcol, fbcol, tsrc, tdst):
            for ti in range(4):
                MM(pb[1][0:64, :], wsb[0:kdim, 0:64], src[0:kdim, ti * 512:(ti + 1) * 512], True, True,
                   [tk0, tsrc], [pbt[1]])
                TS("dve", arg[0:64], pb[1][0:64, :], hsm[0:64, fcol:fcol + 1], hsm[0:64, fbcol:fbcol + 1],
                   ALU.mult, ALU.add, [pbt[1], tk0], [targ])
                for _ in range(2):
                    wrap_once(arg[0:64], targ)
                TS("dve", arg[0:64], arg[0:64], 3.14159, -3.14159, ALU.min, ALU.max, [targ], [targ])
                ACT(dst[0:64, dst_off + ti * 512: dst_off + (ti + 1) * 512], arg[0:64], AF.Sin, [targ], [tdst])

        twt = S.tok("wtmp")

        def wrap_once(ap, tok):
            TS("dve", wtmp[0:64], ap, PI, -2.0 * PI, ALU.is_gt, ALU.mult, [tok], [twt])
            TT("dve", ap, ap, wtmp[0:64], ALU.add, [tok, twt], [tok])
            TS("dve", wtmp[0:64], ap, -PI, 2.0 * PI, ALU.is_lt, ALU.mult, [tok], [twt])
            TT("dve", ap, ap, wtmp[0:64], ALU.add, [tok, twt], [tok])

        sin_layer(w1s, 33, zf, h1, 0, 1, 4, tk0, th1)
        sin_layer(w2s, 64, h1, h2p, 1, 3, 5, th1, th2)
        dump("h2", h2p[0:64, 1:L + 1], [64, L], th2)
        pbarrier()
        A.release()
        heo = A.alloc([16, 2, 512], BF16)
        hfb = [A.alloc([512], F32) for _ in range(2)]
        hbb = [A.alloc([512], F32) for _ in range(2)]
        absb = [A.alloc([512], F32) for _ in range(2)]
        winb_s = [A.alloc([512], F32) for _ in range(2)]
        winsb_s = [A.alloc([512], F32) for _ in range(2)]

        fmb = [A.alloc([2, 16, 128], BF16) for _ in range(3)]
        tfm = S.toks(3, "fm")
        kst = [A.alloc([2, 512], F32) for _ in range(2)]
        tks = S.toks(2, "kst")
        it = 0
        jj = 0
        for o in range(2):
            for lt in range(16):
                sl = lt % 2
                pump(1)
                S.dma("act", winb_s[sl], win[:, lt, :], writes=[twn[sl]])
                S.dma("act", winsb_s[sl], wins[:, lt, :], writes=[twn[sl]])
                MM(pb[2], h2p[0:64, 1 + lt * 128: 1 + (lt + 1) * 128], w3s[0:64, o * 1024: o * 1024 + 512], True, True,
                   [th2, tk0], [pbt[2]])
                MM(pb[3], h2p[0:64, lt * 128:(lt + 1) * 128], w3s[0:64, o * 1024 + 512: o * 1024 + 1024], True, True,
                   [th2, tk0], [pbt[3]])
                TT("dve", hfb[sl], pb[2], winb_s[sl], ALU.mult, [pbt[2], twn[sl]], [thf[sl]])
                TT("dve", hbb[sl], pb[3], winsb_s[sl], ALU.mult, [pbt[3], twn[sl]], [thb[sl]])
                ACT(absb[0], hfb[sl], AF.Abs, [thf[sl]], [tab[0]])
                MM(pb[4 + o], ones_f, absb[0], lt == 0, False, [tab[0], tc2], [pbt[4 + o]])
                ACT(absb[1], hbb[sl], AF.Abs, [thb[sl]], [tab[1]])
                MM(pb[4 + o], ones_f, absb[1], False, False, [tab[1], tc2], [pbt[4 + o]])
                TT("dve", heo[:, lt, 0, :], hfb[sl], hbb[sl], ALU.add, [thf[sl], thb[sl]], [theo])
                TT("dve", heo[:, lt, 1, :], hfb[sl], hbb[sl], ALU.subtract, [thf[sl], thb[sl]], [theo])
            MM(pb[2][0:1, :], h2p[0:64, L:L + 1], w3s[0:64, o * 1024 + 512: o * 1024 + 1024], True, True,
               [th2, tk0], [pbt[2]])
            TT("dve", hfb[0][0:1], pb[2][0:1, :], winl_s[0:1], ALU.mult, [pbt[2], tk0], [thf[0]])
            ACT(absb[0][0:1], hfb[0][0:1], AF.Abs, [thf[0]], [tab[0]])
            MM(pb[4 + o], ones_f[0:1, :], absb[0][0:1], False, True, [tab[0], tc2], [pbt[4 + o]])
            TS("dve", rn[:, o, :], pb[4 + o], 1e-6, None, ALU.add, None, [pbt[4 + o]], [trn])
            S.op("dve", lambda e, o=o: e.reciprocal(out=rn[:, o, :], in_=rn[:, o, :]), [trn], [trn])
            for j in range(16):
                sl = jj % 3
                jj += 1
                pump(1)
                S.dma("act", fmb[sl][:, 0], Fm[j], writes=[tfm[sl]])
                S.dma("act", fmb[sl][:, 1], Fm[16 + j], writes=[tfm[sl]])
                ks = it % 2
                it += 1
                for lc in range(16):
                    MM(pb[6], fmb[sl][:, 0, lc, :], heo[:, lc, 0, :], lc == 0, lc == 15, [tfm[sl], theo], [pbt[6]])
                for lc in range(16):
                    MM(pb[7], fmb[sl][:, 1, lc, :], heo[:, lc, 1, :], lc == 0, lc == 15, [tfm[sl], theo], [pbt[7]])
                TT("dve", kst[ks][:, 0, :], pb[6], rn[:, o, :], ALU.mult, [pbt[6], trn], [tks[ks]])
                TT("dve", kst[ks][:, 0, :], kst[ks][:, 0, :], hbias_s[:, o, :], ALU.add, [tks[ks], tk0], [tks[ks]])
                TT("dve", kst[ks][:, 1, :], pb[7], rn[:, o, :], ALU.mult, [pbt[7], trn], [tks[ks]])
                S.dma("act", Ksp[o, 0, j], kst[ks][:, 0, :], reads=[tks[ks]], evtok=tks[ks])
                S.dma("act", Ksp[o, 1, j], kst[ks][:, 1, :], reads=[tks[ks]], evtok=tks[ks])
        dump("rn", rn, [128, 2, 512], trn)
        pump(1000)
        S.barrier()
        A.release()
        for _ in range(6):
            A.release_top()
        if stop_after == "p0c":
            S.emit()
            return nc, din, dbg_out

        def rstd_of(hb, off, n, sq, tsq, rst, trst, hbtok, ones_ap, nchunks=8):
            for dc in range(nchunks):
                ACT(sq[:, dc, 0:n], hb[:, dc, off:off + n], AF.Square, [hbtok], [tsq])
            for dc in range(nchunks):
                MM(pb[7][:, 0:n], ones_ap, sq[:, dc, 0:n], dc == 0, dc == nchunks - 1, [tsq, tc2], [pbt[7]])
            ACT(rst[:, 0:n], pb[7][:, 0:n], AF.Ln, [pbt[7]], [trst], bias=epsb[:, 0:1])
            ACT(rst[:, 0:n], rst[:, 0:n], AF.Exp, [trst], [trst], scale=-0.5)

        def rstd_multi(hb, tl2, sqs, tsqs, rsts, trsts, hbtok, ones_ap):
            assert len(tl2) <= 2
            tsqd = S.toks(2, "sqd")
            for i, (off, n) in enumerate(tl2):
                for dc in range(5, 8):
                    TT("dve", sqs[i][:, dc, 0:n], hb[:, dc, off:off + n], hb[:, dc, off:off + n], ALU.mult, [hbtok], [tsqd[i]])
            for i, (off, n) in enumerate(tl2):
                for dc in range(5):
                    ACT(sqs[i][:, dc, 0:n], hb[:, dc, off:off + n], AF.Square, [hbtok], [tsqs[i]])
            for i, (off, n) in enumerate(tl2):
                for dc in range(8):
                    MM(pb[7 - i][:, 0:n], ones_ap, sqs[i][:, dc, 0:n], dc == 0, dc == 7,
                       [tsqs[i] if dc < 5 else tsqd[i], tc2], [pbt[7 - i]])
            for i, (off, n) in enumerate(tl2):
                ACT(rsts[i][:, 0:n], pb[7 - i][:, 0:n], AF.Ln, [pbt[7 - i]], [trsts[i]], bias=epsb[:, 0:1])
            for i, (off, n) in enumerate(tl2):
                ACT(rsts[i][:, 0:n], rsts[i][:, 0:n], AF.Exp, [trsts[i]], [trsts[i]], scale=-0.5)

        def ffn_block(s, hb, hbtok, tiles, j0, W):
            A.mark()
            nbk = A.alloc([8, W], BF16)
            act = A.alloc([NF, W], BF16)
            actf = act.rearrange("p f w -> p (f w)")
            sqs = [actf[:, i * 4096:(i + 1) * 4096].rearrange("p (a b) -> p a b", b=512) for i in range(2)]
            rsts = [actf[:, 8192 + i * 1024: 8192 + (i + 1) * 1024].bitcast(F32) for i in range(2)]
            tmp = [A.alloc([512], F32) for _ in range(2)]
            sg = [A.alloc([512], BF16) for _ in range(2)]
            NSA, NSB = 4, 3
            wgs = [A.alloc([8, 128], BF16) for _ in range(NSA)]
            wus = [A.alloc([8, 128], BF16) for _ in range(NSA)]
            wds = [A.alloc([NF, 128], BF16) for _ in range(NSB)]
            tnb = S.toks(len(tiles), "nb")
            tact = S.toks(len(tiles), "act")
            tsqs = S.toks(2, "nrmq"); trsts = S.toks(2, "nrmr")
            ttmp = S.toks(2, "tmp"); tsg = S.toks(2, "sg")
            twg = S.toks(NSA, "wg"); twd = S.toks(NSB, "wd")
            k = 0
            assert len(tiles) == 2
            rstd_multi(hb, [(off, n) for (off, n, col) in tiles], sqs, tsqs, rsts, trsts, hbtok, ones_d)
            for ti, (off, n, col) in enumerate(tiles):
                rst, trst = rsts[ti], trsts[ti]
                for dc in range(8):
                    sl = k % 2
                    k += 1
                    TT("dve", tmp[sl][:, 0:n], hb[:, dc, off:off + n], rst[:, 0:n], ALU.mult, [hbtok, trst], [ttmp[sl]])
                    ACT(nbk[:, dc, off:off + n], tmp[sl][:, 0:n], AF.Identity, [ttmp[sl], tmod], [tnb[ti]],
                        bias=mv(j0, dc, col), scale=mv(j0 + 1, dc, col))
            k = 0
            for f in range(NF):
                sl = f % NSA
                S.dma(q_sp, wgs[sl], wgb[s, :, f], writes=[twg[sl]])
                S.dma(q_sp, wus[sl], wub[s, :, f], writes=[twg[sl]])
                for ti, (off, n, col) in enumerate(tiles):
                    pg = (2 * k) % 4
                    pu = pg + 1
                    ss = k % 2
                    k += 1
                    for kc in range(8):
                        MM(pb[pg][:, 0:n], wgs[sl][:, kc, :], nbk[:, kc, off:off + n], kc == 0, kc == 7,
                           [twg[sl], tnb[ti]], [pbt[pg]])
                    for kc in range(8):
                        MM(pb[pu][:, 0:n], wus[sl][:, kc, :], nbk[:, kc, off:off + n], kc == 0, kc == 7,
                           [twg[sl], tnb[ti]], [pbt[pu]])
                    ACT(sg[ss][:, 0:n], pb[pg][:, 0:n], AF.Silu, [pbt[pg]], [tsg[ss]])
                    TT("dve", act[:, f, off:off + n], sg[ss][:, 0:n], pb[pu][:, 0:n], ALU.mult,
                       [tsg[ss], pbt[pu]], [tact[ti]])
            k = 0
            for dc in range(8):
                sl = dc % NSB
                S.dma(q_sp, wds[sl], wdb[s, :, dc], writes=[twd[sl]])
                for ti, (off, n, col) in enumerate(tiles):
                    pp = 4 + (k % 2)
                    k += 1
                    for f in range(NF):
                        MM(pb[pp][:, 0:n], wds[sl][:, f, :], act[:, f, off:off + n], f == 0, f == NF - 1,
                           [twd[sl], tact[ti]], [pbt[pp]])
                    STT(hb[:, dc, off:off + n], pb[pp][:, 0:n], mv(j0 + 2, dc, col), hb[:, dc, off:off + n],
                        ALU.mult, ALU.add, [pbt[pp], tmod, hbtok], [hbtok])
            S.barrier()
            A.release()

        def load_w(dst, cg, tok, stg, tstg, src=None, nk=8):
            srcm = w_in if src is None else src
            S.dma(q_sp, stg[:, 0:nk, :], srcm[:, cg * 128:(cg + 1) * 128].rearrange("(kc p) n -> p kc n", p=128), writes=[tstg])
            CP("pool", dst, stg[:, 0:nk, :], [tstg], [tok])

        def proj_fm(wsb, wtok, tiles_, consume):
            for i, (off, n) in enumerate(tiles_):
                pi_ = 6 + (i % 2)
                for kc in range(8):
                    MM(pb[pi_][:, 0:n], wsb[:, kc, :], nT[:, kc, off:off + n], kc == 0, kc == 7, [wtok, tnT], [pbt[pi_]])
                consume(pi_, off, n)

        LT4 = [(0, 512), (512, 512), (1024, 512), (1536, 512)]
        LT5 = LT4 + [(2048, 256)]

        def P2(b):
            for h in range(4):
                A.mark()
                wv, wff, wfb, wq, wgt = [A.alloc([8, 128], BF16) for _ in range(5)]
                tw = S.toks(5, "hw")
                wstg = [A.alloc([8, 128], F32) for _ in range(2)]
                twstg = S.toks(2, "wstg")
                for wi, (wsb, cg, tk) in enumerate(((wv, h, tw[0]), (wq, 12 + h, tw[3]), (wgt, 16 + h, tw[4]), (wff, 4 + h, tw[1]), (wfb, 8 + h, tw[2]))):
                    load_w(wsb, cg, tk, wstg[wi % 2], twstg[wi % 2])
                vtok = A.alloc([36, 128], BF16)
                kk = A.alloc([T], F32)
                lfb = A.alloc([T], F32)
                Bb = A.alloc([T], F32)
                qf = A.alloc([L], F32)
                onesr = A.alloc([T], BF16)
                qt_ = [A.alloc([L], BF16) for _ in range(2)]
                kt_ = [A.alloc([T], BF16) for _ in range(2)]
                ktok = [A.alloc([36, 128], BF16) for _ in range(2)]
                o_ = [A.alloc([1, L], F32) for _ in range(2)]
                sgb = A.alloc([L], BF16)
                Sf = [A.alloc([128], F32) for _ in range(2)]
                Sb2 = [[A.alloc([128], BF16) for _ in range(2)] for _ in range(2)]
                tSb2 = [S.toks(2, "Sb2") for _ in range(2)]
                tmpS = [A.alloc([128], F32) for _ in range(2)]
                gcol = [A.alloc([36], F32) for _ in range(2)]
                bref = A.alloc([36], F32)
                scm = [A.alloc([64], BF16) for _ in range(2)]
                sq = A.alloc([1, 512], BF16)
                rst = A.alloc([512], F32)
                tmp = A.alloc([512], F32)
                oab = A.alloc([L], BF16)
                (tvt, tkk, tlf, tB, tq, tone, tsgb, tbref, tsq, trst, ttmp, toab) = S.toks(12, "p2")
                tqt = S.toks(2, "qt"); tkt = S.toks(2, "kt"); tktok = S.toks(2, "ktok"); to = S.toks(2, "o")
                tSf = S.toks(2, "Sf"); tSb = S.toks(2, "Sb"); ttS = S.toks(2, "tS"); tg = S.toks(2, "g"); tscm = S.toks(2, "scm")
                MS("pool", onesr, 1.0, [tone])
                for g0 in range(0, 36, 4):
                    pi_ = 6 + ((g0 // 4) % 2)
                    for ci in range(4):
                        c = g0 + ci
                        for kc in range(8):
                            MM(pb[pi_][0:64, ci * 128:(ci + 1) * 128], nT[:, kc, c * 64:(c + 1) * 64], wv[:, kc, :],
                               kc == 0, kc == 7, [tw[0], tnT], [pbt[pi_]])
                    CP("act", vtok[0:64, g0:g0 + 4, :], pb[pi_][0:64, :].rearrange("p (a b) -> p a b", b=128), [pbt[pi_]], [tvt])
                proj_fm(wq, tw[3], LT4, lambda pi_, off, n: ACT(qf[:, off:off + n], pb[pi_][:, 0:n], AF.Silu, [pbt[pi_]], [tq]))
                proj_fm(wgt, tw[4], LT4, lambda pi_, off, n: ACT(sgb[:, off:off + n], pb[pi_][:, 0:n], AF.Silu, [pbt[pi_]], [tsgb]))
                B3 = Bb.rearrange("p (c s) -> p c s", s=64)
                lf3 = lfb.rearrange("p (c s) -> p c s", s=64)
                for dr in range(2):
                    lbc = dr * 4 + h
                    wsb, wtk = (wff, tw[1]) if dr == 0 else (wfb, tw[2])
                    proj_fm(wsb, wtk, LT5, lambda pi_, off, n: ACT(kk[:, off:off + n], pb[pi_][:, 0:n], AF.Sigmoid,
                                                                    [pbt[pi_]], [tkk], scale=-1.0))
                    ACT(lfb, kk, AF.Ln, [tkk, tmod], [tlf], bias=ones_f[:, 0:1], scale=nomlT[:, lbc:lbc + 1])
                    S.op("dve", lambda e: e.tensor_tensor_scan(out=Bb, data0=onesr, data1=lfb, initial=0.0,
                                                                op0=ALU.mult, op1=ALU.add), [tone, tlf], [tB])
                    if dr == 0:
                        TT("dve", bref, B3[:, :, 0], lf3[:, :, 0], ALU.subtract, [tB, tlf], [tbref])
                        TT("dve", lf3, B3, bref.unsqueeze(2).broadcast_to([128, 36, 64]), ALU.subtract, [tB, tbref, tlf], [tlf])
                    else:
                        TT("dve", lf3, lf3, B3, ALU.subtract, [tB, tlf], [tlf])
                        TT("dve", lf3, lf3, B3[:, :, 63:64].broadcast_to([128, 36, 64]), ALU.add, [tB, tlf], [tlf])
                    ACT(Bb, lfb, AF.Exp, [tlf], [tB])
                    if dr == 0:
                        CP("dve", gcol[dr], B3[:, :, 63], [tB], [tg[dr]])
                    else:
                        CP("dve", gcol[dr], B3[:, :, 0], [tB], [tg[dr]])
                    TT("dve", qt_[dr], qf, Bb[:, 0:L], ALU.mult, [tq, tB], [tqt[dr]])
                    ACT(lfb, lfb, AF.Exp, [tlf], [tlf], scale=-1.0)
                    STT(kt_[dr], kk, omlT[:, lbc:lbc + 1], lfb, ALU.mult, ALU.mult, [tkk, tlf, tmod], [tkt[dr]])
                    for g0 in range(0, 36, 4):
                        pi_ = 6 + ((g0 // 4) % 2)
                        for ci in range(4):
                            c = g0 + ci
                            TR(pbb[pi_][0:64, ci * 128:(ci + 1) * 128], kt_[dr][:, c * 64:(c + 1) * 64], ident,
                               [tkt[dr], tconst], [pbt[pi_]])
                        CP("act", ktok[dr][0:64, g0:g0 + 4, :], pbb[pi_][0:64, 0:512].rearrange("p (a b) -> p a b", b=128),
                           [pbt[pi_]], [tktok[dr]])
                    MS("pool", tmpS[dr], 0.0, [ttS[dr]])
                    MS("pool", Sb2[dr][0], 0.0, [tSb2[dr][0]])
                orders = [[32, 33, 34, 35] + list(range(32)), [35, 34, 33, 32] + list(range(31, -1, -1))]
                for step in range(36):
                    par = step % 2
                    cc_ = [orders[dr][step] for dr in range(2)]
                    lat = cc_[0] < 32
                    if lat:
                        for dr in range(2):
                            c = cc_[dr]
                            MM(pb[dr][0:64, 0:64], kt_[dr][:, c * 64:(c + 1) * 64], qt_[dr][:, c * 64:(c + 1) * 64], True, True,
                               [tkt[dr], tqt[dr]], [pbt[dr]])
                    for dr in range(2):
                        c = cc_[dr]
                        MM(pb[4 + dr][:, 0:128], ktok[dr][0:64, c, :], vtok[0:64, c, :], True, True, [tktok[dr], tvt], [pbt[4 + dr]])
                    if lat:
                        for dr in range(2):
                            TT("dve", scm[dr][0:64], pb[dr][0:64, 0:64], masks_s[0:64, dr, :], ALU.mult,
                               [pbt[dr], tconst], [tscm[dr]])
                        for dr in range(2):
                            c = cc_[dr]
                            MM(pb[2 + dr][:, 0:64], vtok[0:64, c, :], scm[dr][0:64], True, False, [tvt, tscm[dr]], [pbt[2 + dr]])
                            MM(pb[2 + dr][:, 0:64], Sb2[dr][par], qt_[dr][:, c * 64:(c + 1) * 64], False, True,
                               [tSb2[dr][par], tqt[dr]], [pbt[2 + dr]])
                            CP("act", o_[dr][:, 0, c * 64:(c + 1) * 64], pb[2 + dr][:, 0:64], [pbt[2 + dr]], [to[dr]])
                    for dr in range(2):
                        c = cc_[dr]
                        cp_ = orders[dr][step - 1] if step > 0 else c
                        STT(tmpS[dr], tmpS[dr], gcol[dr][:, cp_:cp_ + 1], pb[4 + dr][:, 0:128], ALU.mult, ALU.add,
                            [ttS[dr], tg[dr], pbt[4 + dr]], [ttS[dr]])
                        TS("dve", Sb2[dr][1 - par], tmpS[dr], gcol[dr][:, c:c + 1], None, ALU.mult, None, [ttS[dr], tg[dr]],
                           [tSb2[dr][1 - par]])
                TT("pool", o_[0], o_[0], o_[1], ALU.add, [to[0], to[1]], [to[0]])
                for (off, n) in LT4:
                    rstd_of(o_[0], off, n, sq, tsq, rst, trst, to[0], ones_v, nchunks=1)
                    TT("dve", tmp[:, 0:n], o_[0][:, 0, off:off + n], rst[:, 0:n], ALU.mult, [to[0], trst], [ttmp])
                    STT(oab[:, off:off + n], tmp[:, 0:n], normw_s[:, 0:1], sgb[:, off:off + n], ALU.mult, ALU.mult,
                        [ttmp, tconst, tsgb], [toab])
                S.dma(q_sp, oas[:, h, :], oab, reads=[toab], evtok=toab)
                S.barrier()
                A.release()

        def P3(b, zT, tz):
            A.mark()
            gT = A.alloc([4, L], BF16)
            ztok = A.alloc([16, 512], BF16)
            pb_base = A.top
            Pbuf = A.alloc([32, 512], BF16)
            pb_end = A.top
            A.top = pb_base
            pT = A.alloc([L + 8], F32)
            uT = A.alloc([L], F32)
            wsl = [A.alloc([8, 128], BF16) for _ in range(2)]
            wstg3 = [A.alloc([8, 128], F32) for _ in range(2)]
            twstg3 = S.toks(2, "wstg3")
            assert A.top <= pb_end
            A.top = pb_end
            fib = [A.alloc([32, 128], BF16) for _ in range(2)]
            fmb = [A.alloc([2, 16, 128], BF16) for _ in range(2)]
            kb = [A.alloc([2, 512], F32) for _ in range(2)]
            tm = [A.alloc([512], F32) for _ in range(4)]
            tg_, tzt, tP, tpT, tuT = S.toks(5, "p3")
            twsl = S.toks(2, "wsl"); tfib = S.toks(2, "fib"); tfmb = S.toks(2, "fmb"); tkb = S.toks(2, "kb"); ttm = S.toks(4, "tm")

            def proj_conv(part, dst, tdst):
                S.barrier()
                MS("pool", pT[:, 0:1], 0.0, [tpT])
                MS("pool", pT[:, L + 1:L + 2], 0.0, [tpT])
                for cc in range(4):
                    sl = cc % 2
                    ci = part * 4 + cc
                    load_w(wsl[sl], 20 + ci, twsl[sl], wstg3[sl], twstg3[sl])
                    proj_fm(wsl[sl], twsl[sl], LT4,
                            lambda pi_, off, n: CP("act", pT[:, 1 + off:1 + off + n], pb[pi_][:, 0:n], [pbt[pi_]], [tpT]))
                    TS("dve", uT, pT[:, 1:L + 1], convw_s[:, 1, ci:ci + 1], convb_s[:, ci:ci + 1], ALU.mult, ALU.add,
                       [tpT, tconst], [tuT])
                    STT(uT, pT[:, 0:L], convw_s[:, 0, ci:ci + 1], uT, ALU.mult, ALU.add, [tpT, tconst, tuT], [tuT])
                    STT(dst[:, cc, :], pT[:, 2:L + 2], convw_s[:, 2, ci:ci + 1], uT, ALU.mult, ALU.add,
                        [tpT, tconst, tuT], [tdst])
                S.barrier()

            proj_conv(0, zT, tz)
            for o in range(2):
                proj_conv(1 + o, gT, tg_)
                for tt in range(16):
                    pi_ = 6 + (tt % 2)
                    for cc in range(4):
                        TR(pbb[pi_][:, cc * 128:(cc + 1) * 128], zT[:, cc, tt * 128:(tt + 1) * 128], ident,
                           [tz, tconst], [pbt[pi_]])
                    CP("act" if tt % 2 else "dve", ztok[:, tt, :], pbb[pi_][:, 0:512], [pbt[pi_]], [tzt])
                for j in range(16):
                    sl = j % 2
                    S.dma(q_sp, fmb[sl][:, 0], Fm[j], writes=[tfmb[sl]])
                    S.dma(q_sp, fmb[sl][:, 1], Fm[16 + j], writes=[tfmb[sl]])
                    S.dma(q_sp, kb[sl][:, 0, :], Ksp[o, 0, j], writes=[tkb[sl]])
                    S.dma(q_sp, kb[sl][:, 1, :], Ksp[o, 1, j], writes=[tkb[sl]])
                    pr, pim = 2 * sl, 2 * sl + 1
                    for lc in range(16):
                        MM(pb[pr], fmb[sl][:, 0, lc, :], ztok[:, lc, :], lc == 0, lc == 15, [tfmb[sl], tzt], [pbt[pr]])
                    for lc in range(16):
                        MM(pb[pim], fmb[sl][:, 1, lc, :], ztok[:, lc, :], lc == 0, lc == 15, [tfmb[sl], tzt], [pbt[pim]])
                    TT("dve", tm[0], pb[pr], kb[sl][:, 0, :], ALU.mult, [pbt[pr], tkb[sl]], [ttm[0]])
                    TT("dve", tm[1], pb[pim], kb[sl][:, 1, :], ALU.mult, [pbt[pim], tkb[sl]], [ttm[1]])
                    TT("pool", Pbuf[:, j, :], tm[0], tm[1], ALU.subtract, [ttm[0], ttm[1]], [tP])
                    TT("dve", tm[2], pb[pr], kb[sl][:, 1, :], ALU.mult, [pbt[pr], tkb[sl]], [ttm[2]])
                    TT("dve", tm[3], pb[pim], kb[sl][:, 0, :], ALU.mult, [pbt[pim], tkb[sl]], [ttm[3]])
                    TT("pool", Pbuf[:, 16 + j, :], tm[2], tm[3], ALU.add, [ttm[2], ttm[3]], [tP])
                k = 0
                for tt in range(16):
                    sl = tt % 2
                    S.dma(q_sp, fib[sl], Fi[tt], writes=[tfib[sl]])
                    for cc in range(4):
                        pi_ = 4 + (k % 2)
                        k += 1
                        for fc in range(32):
                            MM(pb[pi_][:, 0:128], Pbuf[:, fc, cc * 128:(cc + 1) * 128], fib[sl][:, fc, :], fc == 0, fc == 31,
                               [tP, tfib[sl]], [pbt[pi_]])
                        TT("dve", zT[:, cc, tt * 128:(tt + 1) * 128], gT[:, cc, tt * 128:(tt + 1) * 128], pb[pi_][:, 0:128],
                           ALU.mult, [tg_, pbt[pi_]], [tz])
            S.barrier()
            A.release()

        def P4(b, zT, tz, yT, ty):
            A.mark()
            oaT = A.alloc([4, L], BF16)
            toa = S.tok("oaT")
            S.dma(q_sp, oaT, oas, writes=[toa])
            wga = [A.alloc([8, 128], BF16) for _ in range(2)]
            wgb_ = [A.alloc([8, 128], BF16) for _ in range(2)]
            wa = [A.alloc([4, 128], BF16) for _ in range(2)]
            wb_ = [A.alloc([4, 128], BF16) for _ in range(2)]
            sga = [A.alloc([512], F32) for _ in range(2)]
            sgb2 = [A.alloc([512], F32) for _ in range(2)]
            t1 = [A.alloc([512], F32) for _ in range(2)]
            t2 = [A.alloc([512], F32) for _ in range(2)]
            tw4a = S.toks(2, "w4a"); tw4b = S.toks(2, "w4b"); tw4c = S.toks(2, "w4c"); tw4d = S.toks(2, "w4d")
            wstg4 = [A.alloc([8, 128], F32) for _ in range(2)]
            twstg4 = S.toks(2, "wstg4")
            tsa = S.toks(2, "sa"); tsb = S.toks(2, "sb"); tt1 = S.toks(2, "t1"); tt2 = S.toks(2, "t2")
            k = 0
            for dc in range(8):
                sl = dc % 2
                load_w(wga[sl], 32 + dc, tw4a[sl], wstg4[0], twstg4[0])
                load_w(wgb_[sl], 40 + dc, tw4b[sl], wstg4[1], twstg4[1])
                load_w(wa[sl], dc, tw4c[sl], wstg4[0], twstg4[0], src=wpa, nk=4)
                load_w(wb_[sl], dc, tw4d[sl], wstg4[1], twstg4[1], src=wpb, nk=4)
                for (off, n) in LT4:
                    ss = k % 2
                    k += 1
                    for kc in range(8):
                        MM(pb[4 * ss + 0], wga[sl][:, kc, :], nT[:, kc, off:off + n], kc == 0, kc == 7, [tw4a[sl], tnT], [pbt[4 * ss + 0]])
                    for kc in range(4):
                        MM(pb[4 * ss + 1], wa[sl][:, kc, :], oaT[:, kc, off:off + n], kc == 0, kc == 3, [tw4c[sl], toa], [pbt[4 * ss + 1]])
                    for kc in range(8):
                        MM(pb[4 * ss + 2], wgb_[sl][:, kc, :], nT[:, kc, off:off + n], kc == 0, kc == 7, [tw4b[sl], tnT], [pbt[4 * ss + 2]])
                    for kc in range(4):
                        MM(pb[4 * ss + 3], wb_[sl][:, kc, :], zT[:, kc, off:off + n], kc == 0, kc == 3, [tw4d[sl], tz], [pbt[4 * ss + 3]])
                    ACT(sga[ss], pb[4 * ss + 0], AF.Sigmoid, [pbt[4 * ss + 0]], [tsa[ss]])
                    ACT(sgb2[ss], pb[4 * ss + 2], AF.Sigmoid, [pbt[4 * ss + 2]], [tsb[ss]])
                    TT("dve", t1[ss], sga[ss], pb[4 * ss + 1], ALU.mult, [tsa[ss], pbt[4 * ss + 1]], [tt1[ss]])
                    TT("dve", t2[ss], sgb2[ss], pb[4 * ss + 3], ALU.mult, [tsb[ss], pbt[4 * ss + 3]], [tt2[ss]])
                    TT("pool", yT[:, dc, off:off + n], t1[ss], t2[ss], ALU.add, [tt1[ss], tt2[ss]], [ty])
            S.barrier()
            A.release()

        def P5(b, yT, ty):
            for blk in range(2):
                A.mark()
                W = 1024
                base = blk * W
                hb = A.alloc([8, W], F32)
                hbtok = S.tok("hb5")
                tld = S.toks(8, "hld")
                for dc in range(8):
                    S.dma(q_sp, hb[:, dc, :], hs[b, :, dc, base:base + W], writes=[tld[dc]])
                A.mark()
                wo = [A.alloc([8, 128], BF16) for _ in range(2)]
                two = S.toks(2, "wo")
                wstg5 = [A.alloc([8, 128], F32) for _ in range(2)]
                twstg5 = S.toks(2, "wstg5")
                tiles = [(0, 512, b), (512, 512, b)]
                k = 0
                for dc in range(8):
                    sl = dc % 2
                    load_w(wo[sl], dc, two[sl], wstg5[sl], twstg5[sl], src=wout)
                    for (off, n, col) in tiles:
                        pi_ = 6 + (k % 2)
                        k += 1
                        for kc in range(8):
                            MM(pb[pi_][:, 0:n], wo[sl][:, kc, :], yT[:, kc, base + off:base + off + n], kc == 0, kc == 7,
                               [two[sl], ty], [pbt[pi_]])
                        STT(hb[:, dc, off:off + n], pb[pi_][:, 0:n], mv(5, dc, col), hb[:, dc, off:off + n],
                            ALU.mult, ALU.add, [pbt[pi_], tmod, tld[dc]], [hbtok])
                S.barrier()
                A.release()
                ffn_block(1, hb, hbtok, tiles, 6, W)
                A.mark()
                sqs = [A.alloc([8, 512], BF16) for _ in range(2)]
                rsts = [A.alloc([512], F32) for _ in range(2)]
                tsqs = S.toks(2, "nrm5q"); trsts = S.toks(2, "nrm5r")
                rstd_multi(hb, [(off, n) for (off, n, col) in tiles], sqs, tsqs, rsts, trsts, hbtok, ones_d)
                for tix, (off, n, col) in enumerate(tiles):
                    rst, trst = rsts[tix], trsts[tix]
                    for dc in range(8):
                        STT(hb[:, dc, off:off + n], hb[:, dc, off:off + n], fnw_s[:, dc:dc + 1], rst[:, 0:n],
                            ALU.mult, ALU.mult, [hbtok, tconst, trst], [hbtok])
                S.dma(q_sp, out_t[b][:, base:base + W].rearrange("(dc p) t -> p dc t", p=128), hb, reads=[hbtok], evtok=hbtok)
                S.barrier()
                A.release()
                A.release()

        tnT = S.tok("nT")
        nT = None
        for b in range(nb):
            A.mark()
            nT = A.alloc([8, T], BF16)
            blocks = [
                [(0, 512, b), (512, 256, b)],
                [(768, 512, b), (1280, 256, b)],
                [(1536, 512, b), (2048, 256, 2)],
            ]
            for bi, tl in enumerate(blocks):
                A.mark()
                W = 768
                base = bi * 768
                hb = A.alloc([8, W], F32)
                hbtok = S.tok("hb")
                pst = [A.alloc([8, 512], F32)]
                tps = S.toks(1, "pos")
                k = 0
                for (off, n, col) in tl:
                    lo = off - base
                    if col == 2:
                        S.dma(q_sp, hb[:, :, lo:lo + n], ctx_t[b].rearrange("(dc p) t -> p dc t", p=128), writes=[hbtok])
                    else:
                        S.dma(q_sp, hb[:, :, lo:lo + n],
                              x_t[b][:, off:off + n].rearrange("(dc p) t -> p dc t", p=128), writes=[hbtok])
                        S.dma(q_sp, pst[0][:, :, 0:n], pos_t[:, off:off + n].rearrange("(dc p) t -> p dc t", p=128),
                              writes=[tps[0]])
                        for dc in range(8):
                            k += 1
                            TT("dve", hb[:, dc, lo:lo + n], hb[:, dc, lo:lo + n],
                               pst[0][:, dc, 0:n], ALU.add, [hbtok, tps[0]], [hbtok])
                ltiles = [(off - base, n, col) for (off, n, col) in tl]
                ffn_block(0, hb, hbtok, ltiles, 0, W)
                A.mark()
                sqs = [A.alloc([8, 512], BF16) for _ in range(2)]
                rsts = [A.alloc([512], F32) for _ in range(2)]
                tmp = [A.alloc([512], F32) for _ in range(2)]
                tsqs = S.toks(2, "nrmq"); trsts = S.toks(2, "nrmr")
                ttmp = S.toks(2, "tmp")
                k = 0
                rstd_multi(hb, [(off - base, n) for (off, n, col) in tl], sqs, tsqs, rsts, trsts, hbtok, ones_d)
                for tix, (off, n, col) in enumerate(tl):
                    lo = off - base
                    rst, trst = rsts[tix], trsts[tix]
                    if col != 2:
                        S.dma(q_sp, hs[b, :, :, off:off + n], hb[:, :, lo:lo + n], reads=[hbtok], evtok=hbtok)
                    for dc in range(8):
                        sl = k % 2
                        k += 1
                        TT("dve", tmp[sl][:, 0:n], hb[:, dc, lo:lo + n], rst[:, 0:n], ALU.mult, [hbtok, trst], [ttmp[sl]])
                        ACT(nT[:, dc, off:off + n], tmp[sl][:, 0:n], AF.Identity, [ttmp[sl], tmod], [tnT],
                            bias=mv(3, dc, col), scale=mv(4, dc, col))
                S.barrier()
                A.release()
                A.release()
            if b == 0:
                dump("nT0", nT, [128, 8, T], tnT, BF16)
                dump("hs0", hs[0], [128, 8, L], tnT)
            if stop_after == "p1":
                break
            tz = S.tok("zT")
            P2(b)
            zT = A.alloc([4, L], BF16)
            if b == 0:
                dump("oas", oas, [128, 4, L], tz, BF16)
            if stop_after == "p2":
                break
            P3(b, zT, tz)
            if b == 0:
                dump("zT", zT, [128, 4, L], tz, BF16)
            if stop_after == "p3":
                break
            yT = A.alloc_top([8, L], BF16)
            ty = S.tok("yT")
            P4(b, zT, tz, yT, ty)
            if b == 0:
                dump("yT", yT, [128, 8, L], ty, BF16)
            S.barrier()
            A.release()
            if stop_after == "p4":
                break
            P5(b, yT, ty)
            A.release_top()
        S.barrier()
        S.emit()
    return nc, din, dbg_out


def _bf(a):
    return np.ascontiguousarray(a).astype(ml_dtypes.bfloat16)


_CONST_CACHE = {}


def host_consts():
    if _CONST_CACHE:
        return _CONST_CACHE
    f32 = np.float32
    quarter = D // 4
    omega = (1.0 / (10000.0 ** (np.arange(quarter, dtype=f32) / quarter))).astype(f32)
    rows = L // 64
    ar = np.arange(rows, dtype=f32)[:, None] * omega
    ac = np.arange(64, dtype=f32)[:, None] * omega
    er = np.concatenate([np.sin(ar), np.cos(ar)], axis=-1)
    ec = np.concatenate([np.sin(ac), np.cos(ac)], axis=-1)
    emb = np.concatenate([np.broadcast_to(er[:, None, :], (rows, 64, D // 2)),
                          np.broadcast_to(ec[None, :, :], (rows, 64, D // 2))], axis=-1).reshape(L, D)
    pos_t = np.ascontiguousarray(emb.T.astype(f32))
    p = np.arange(L, dtype=f32)
    t = p / (L - 1)
    w = (2.0 * math.pi * p / L).astype(f32)
    fb = np.linspace(1e-4, 15, 16, dtype=f32)
    ang = w[:, None] * fb[None, :]
    z = np.concatenate([t[:, None], np.cos(ang), -np.sin(ang)], axis=-1).astype(f32)
    zfeat = np.ascontiguousarray(z.T)
    max_decay = math.log(1e-2) / 0.3
    min_decay = math.log(1e-2) / 1.5
    deltas = np.abs(np.linspace(min_decay, max_decay, 512, dtype=f32))
    window = (np.exp(-t[:, None] * deltas[None, :]) + 0.05).astype(f32)
    win = np.ascontiguousarray(window.reshape(16, 128, 512).transpose(1, 0, 2))
    wsh = np.zeros_like(window)
    wsh[1:] = window[:-1]
    wins = np.ascontiguousarray(wsh.reshape(16, 128, 512).transpose(1, 0, 2))
    winl = np.ascontiguousarray(window[L - 1:L])
    N = 2 * L
    tt = np.arange(L, dtype=np.float64)[:, None]
    ff = (np.arange(L, dtype=np.float64) + 0.5)[None, :]
    angm = 2.0 * np.pi * tt * ff / N
    Fc = np.cos(angm)
    Fs = -np.sin(angm)
    F = np.concatenate([Fc, Fs], axis=1)
    Fm = F.reshape(16, 128, 32, 128).transpose(2, 1, 0, 3)
    Fi = (2.0 / N) * F.T
    Fi = Fi.reshape(32, 128, 16, 128).transpose(2, 1, 0, 3)
    masks = np.zeros((64, 2, 64), f32)
    si = np.arange(64)[:, None]
    ti = np.arange(64)[None, :]
    masks[:, 0, :] = (si <= ti)
    masks[:, 1, :] = (si >= ti)
    _CONST_CACHE.update(dict(pos_t=pos_t, zfeat=zfeat, win=win, wins=wins, winl=winl, Fm=_bf(Fm), Fi=_bf(Fi),
                             ident=_bf(np.eye(128, dtype=f32)), masks=masks))
    return _CONST_CACHE


def prep_core(inp, bsel):
    f32 = np.float32
    c = host_consts()
    m = dict(c)
    nbl = len(bsel)
    m["x_t"] = np.ascontiguousarray(np.stack([inp["x"][b].T for b in bsel]))
    m["ctx_t"] = np.ascontiguousarray(np.stack([inp["ctx"][b].T for b in bsel]))
    ct = np.zeros((4, D), f32)
    for i, b in enumerate(bsel):
        ct[i] = inp["c"][b]
    ct[2] = inp["c_ctx"]
    m["c_t"] = np.ascontiguousarray(ct.reshape(4, 8, 128).transpose(2, 1, 0))
    m["mod_w"] = np.ascontiguousarray(inp["mod_w"][0])
    m["mod_b"] = np.ascontiguousarray(inp["mod_b"][0].reshape(72, 128).T)
    m["wg"] = np.ascontiguousarray(inp["ffn_w_gate"][0])
    m["wu"] = np.ascontiguousarray(inp["ffn_w_up"][0])
    m["wd"] = np.ascontiguousarray(inp["ffn_w_down"][0])
    m["w_in"] = np.ascontiguousarray(inp["w_in"][0])
    m["lbl"] = np.ascontiguousarray(inp["hgrn_lb_logits"].reshape(2, 2, 4, 128).transpose(3, 0, 1, 2).reshape(128, 2, 8))
    m["normw"] = np.ascontiguousarray(inp["hgrn_norm_w"][0].reshape(128, 1))
    m["convw"] = np.ascontiguousarray(inp["hyena_conv_w"][0].reshape(3, 12, 128).transpose(2, 0, 1))
    m["convb"] = np.ascontiguousarray(inp["hyena_conv_b"][0].reshape(12, 128).T)
    m["hw1"] = np.ascontiguousarray(inp["hyena_w1"][0])
    m["hb1"] = np.ascontiguousarray(inp["hyena_b1"][0].reshape(64, 1))
    m["hf1"] = np.ascontiguousarray(inp["hyena_freq1"][0].reshape(64, 1))
    m["hw2"] = np.ascontiguousarray(inp["hyena_w2"][0])
    m["hb2"] = np.ascontiguousarray(inp["hyena_b2"][0].reshape(64, 1))
    m["hf2"] = np.ascontiguousarray(inp["hyena_freq2"][0].reshape(64, 1))
    m["hw3"] = np.ascontiguousarray(inp["hyena_w3"][0])
    m["hbias"] = np.ascontiguousarray(np.broadcast_to(inp["hyena_bias"][0][None], (128, 2, 512)))
    m["wpa"] = np.ascontiguousarray(inp["w_proj_a"][0])
    m["wpb"] = np.ascontiguousarray(inp["w_proj_b"][0])
    m["wout"] = np.ascontiguousarray(inp["w_out"][0])
    m["fnw"] = np.ascontiguousarray(inp["final_norm_w"].reshape(8, 128).T)
    return {k: (v if v.dtype == ml_dtypes.bfloat16 else v.astype(f32)) for k, v in m.items()}


_PROG = {}


def kernel(**inputs):
    inputs = {k: np.asarray(v) for k, v in inputs.items()}
    if "full" not in _PROG:
        _PROG["full"] = build_program(nb=2)
    nc, din, _ = _PROG["full"]
    in_maps = []
    for core in range(NCORE):
        m = prep_core(inputs, [2 * core, 2 * core + 1])
        in_maps.append({k: m[k] for k in din})
    res = run_bass_kernel_spmd(nc, in_maps, core_ids=list(range(NCORE)))
    out = np.empty((16, L, D), np.float32)
    for core in range(NCORE):
        o = res.results[core]["out_t"]
        for i in range(2):
            out[2 * core + i] = o[i].T
    return out
```

```python
import numpy as np
from contextlib import ExitStack
import concourse.bass as bass
import concourse.mybir as mybir

F32 = mybir.dt.float32
BF16 = mybir.dt.bfloat16
AF = mybir.ActivationFunctionType
ALU = mybir.AluOpType

ENGS = ("pe", "act", "dve", "pool", "sp")
EPOCH = 12000
SAME_ENGINE_SYNC = True


class Tok:
    __slots__ = ("name", "w", "w_eng", "r", "dsem", "dcount")

    def __init__(self, name):
        self.name = name
        self.w = None
        self.w_eng = None
        self.r = []
        self.dsem = None
        self.dcount = 0


class Sched:
    def __init__(self, nc, stack):
        self.nc = nc
        self.stack = stack
        self.ops = {e: [] for e in ENGS}
        self.cnt = {e: 0 for e in ENGS}
        self.sem = {e: None for e in ENGS}
        self.nsem = 0
        self.waited = {e: {} for e in ENGS}
        self.latest = {}
        self.n_ops = 0
        self.dpool = []
        self.dtoks = []

    def new_sem(self, name):
        self.nsem += 1
        return self.stack.enter_context(self.nc.semaphore(f"{name}_{self.nsem}"))

    def tok(self, name="t"):
        return Tok(name)

    def toks(self, n, name="t"):
        return [Tok(f"{name}{i}") for i in range(n)]

    def _next_event(self, eng):
        if self.sem[eng] is None or self.cnt[eng] >= EPOCH:
            self.sem[eng] = self.new_sem(f"s_{eng}")
            self.cnt[eng] = 0
        self.cnt[eng] += 1
        return (self.sem[eng], self.cnt[eng])

    def _need(self, eng, waits, ev):
        if ev is None:
            return
        sem, val = ev[0], ev[1]
        k = id(sem)
        if self.waited[eng].get(k, 0) >= val:
            return
        cur = waits.get(k)
        if cur is None or cur[1] < val:
            waits[k] = (sem, val)

    def _collect(self, eng, reads, writes, is_dma):
        waits = {}
        for t in reads:
            if t.w is not None:
                if t.w_eng == eng and not is_dma:
                    if eng != "pe" and SAME_ENGINE_SYNC:
                        self._need(eng, waits, t.w)
                else:
                    self._need(eng, waits, t.w)
        for t in writes:
            if t.w is not None:
                if t.w_eng == eng and not is_dma:
                    if eng != "pe" and SAME_ENGINE_SYNC:
                        self._need(eng, waits, t.w)
                elif is_dma and t.w_eng == "dma":
                    pass
                else:
                    self._need(eng, waits, t.w)
            for (sem, val, reng) in t.r:
                if reng == eng and not is_dma and (eng == "pe" or not SAME_ENGINE_SYNC):
                    continue
                self._need(eng, waits, (sem, val))
        wl = list(waits.values())
        for (sem, val) in wl:
            self.waited[eng][id(sem)] = val
        return wl

    def op(self, eng, fn, reads=(), writes=()):
        wl = self._collect(eng, reads, writes, False)
        ev = self._next_event(eng)
        self.ops[eng].append((wl, fn, ev[0], 1))
        self.waited[eng][id(ev[0])] = max(self.waited[eng].get(id(ev[0]), 0), 0)
        self.latest[id(ev[0])] = ev
        for t in writes:
            t.w = ev
            t.w_eng = eng
            t.r = []
        for t in reads:
            if t in writes:
                continue
            t.r = [x for x in t.r if x[2] != eng] + [(ev[0], ev[1], eng)]
        self.n_ops += 1
        return ev

    def dma(self, queue, out, in_, reads=(), writes=(), evtok=None, **kw):
        if evtok is None:
            evtok = writes[0] if len(writes) else reads[0]
        wl = self._collect(queue, reads, writes, True)
        if evtok.dsem is None:
            if self.dpool:
                evtok.dsem, evtok.dcount = self.dpool.pop()
            else:
                evtok.dsem = self.new_sem("d")
                evtok.dcount = 0
            self.dtoks.append(evtok)
        assert evtok.dcount < 60000
        evtok.dcount += 16
        ev = (evtok.dsem, evtok.dcount)
        self.latest[id(ev[0])] = ev

        def fn(e, out=out, in_=in_, kw=kw):
            return e.dma_start(out=out, in_=in_, **kw)
        self.ops[queue].append((wl, fn, ev[0], 16))
        for t in writes:
            t.w = ev
            t.w_eng = "dma"
            t.r = []
        for t in reads:
            t.r = [x for x in t.r if x[0] is not ev[0]] + [(ev[0], ev[1], "dma")]
        self.n_ops += 1
        return ev

    def barrier(self, engines=ENGS, exclude_engs=(), exclude_toks=()):
        skip = set()
        for e in exclude_engs:
            if self.sem[e] is not None:
                skip.add(id(self.sem[e]))
        for t in exclude_toks:
            if t.dsem is not None:
                skip.add(id(t.dsem))
        engines = tuple(e for e in engines if e not in exclude_engs)
        evs = [v for k, v in self.latest.items() if k not in skip]
        for e in engines:
            wl = []
            for (sem, val) in evs:
                if self.waited[e].get(id(sem), 0) >= val:
                    continue
                if sem is self.sem[e]:
                    continue
                wl.append((sem, val))
                self.waited[e][id(sem)] = val
            if wl:
                self.ops[e].append((wl, None, None, 0))
        if tuple(engines) == tuple(ENGS):
            for t in self.dtoks:
                if t.dcount < 40000:
                    self.dpool.append((t.dsem, t.dcount))
                t.dsem = None
                t.dcount = 0
            self.dtoks = []

    def emit(self):
        nc = self.nc
        with nc.Block() as block:
            def mk(engname):
                def body(e):
                    for (wl, fn, sem, inc) in self.ops[engname]:
                        for (s, v) in wl:
                            e.wait_ge(s, v)
                        if fn is not None:
                            ins = fn(e)
                            ins.then_inc(sem, inc)
                return body
            block.tensor(mk("pe"))
            block.scalar(mk("act"))
            block.vector(mk("dve"))
            block.gpsimd(mk("pool"))
            block.sync(mk("sp"))


class Arena:
    def __init__(self, nc, stack, words, name="arena"):
        self.t = stack.enter_context(nc.sbuf_tensor(name, [128, words], F32))
        self.words = words
        self.top = 0
        self.marks = []
        self.hi = words
        self.his = []

    def mark(self):
        self.marks.append(self.top)

    def release(self):
        self.top = self.marks.pop()

    def alloc_top(self, shape, dtype):
        n = int(np.prod(shape))
        w = n if dtype == F32 else (n + 1) // 2
        w = (w + 7) // 8 * 8
        self.his.append(self.hi)
        self.hi -= w
        assert self.hi >= self.top
        save = self.top
        self.top = self.hi
        hi_save = self.hi
        self.hi = self.words + 10 ** 9
        ap = self.alloc(shape, dtype)
        self.top = save
        self.hi = hi_save
        return ap

    def release_top(self):
        self.hi = self.his.pop()

    def alloc(self, shape, dtype):
        n = int(np.prod(shape))
        if dtype == F32:
            w = n
        elif dtype == BF16:
            w = (n + 1) // 2
        else:
            raise ValueError(dtype)
        w = (w + 7) // 8 * 8
        if self.top + w > min(self.words, self.hi):
            raise MemoryError(f"arena overflow: need {w} at {self.top} of {self.words}")
        ap = self.t[:, self.top:self.top + w]
        self.top += w
        if dtype == BF16:
            ap = ap.bitcast(BF16)[:, 0:n]
        else:
            ap = ap[:, 0:n]
        if len(shape) == 2:
            ap = ap.rearrange("p (a b) -> p a b", b=shape[1])
        elif len(shape) == 3:
            ap = ap.rearrange("p (a b c) -> p a b c", b=shape[1], c=shape[2])
        return ap


import math
import ml_dtypes
from concourse.bass_utils import run_bass_kernel_spmd

D = 1024
L = 2048
LC = 256
T = L + LC
DFF = 2816
NF = DFF // 128
NCORE = 8
PI = math.pi


def _wrap(S):
    def ACT(out, in_, func, reads, writes, bias=None, scale=None):
        kw = {}
        if bias is not None:
            kw["bias"] = bias
        if scale is not None:
            kw["scale"] = scale
        return S.op("act", lambda e: e.activation(out=out, in_=in_, func=func, **kw), reads, writes)

    def TT(eng, out, in0, in1, op, reads, writes):
        return S.op(eng, lambda e: e.tensor_tensor(out=out, in0=in0, in1=in1, op=op), reads, writes)

    def TS(eng, out, in0, s1, s2, op0, op1, reads, writes):
        if op1 is None:
            return S.op(eng, lambda e: e.tensor_scalar(out=out, in0=in0, scalar1=s1, scalar2=None, op0=op0), reads, writes)
        return S.op(eng, lambda e: e.tensor_scalar(out=out, in0=in0, scalar1=s1, scalar2=s2, op0=op0, op1=op1), reads, writes)

    def STT(out, in0, scalar, in1, op0, op1, reads, writes):
        return S.op("dve", lambda e: e.scalar_tensor_tensor(out=out, in0=in0, scalar=scalar, in1=in1, op0=op0, op1=op1), reads, writes)

    def MM(out, lhsT, rhs, start, stop, reads, writes):
        return S.op("pe", lambda e: e.matmul(out, lhsT=lhsT, rhs=rhs, start=start, stop=stop), reads, writes)

    def TR(out, in_, ident, reads, writes):
        return S.op("pe", lambda e: e.transpose(out, in_, ident), reads, writes)

    def CP(eng, out, in_, reads, writes):
        if eng == "act":
            return S.op("act", lambda e: e.activation(out=out, in_=in_, func=AF.Copy), reads, writes)
        return S.op(eng, lambda e: e.tensor_copy(out=out, in_=in_), reads, writes)

    def MS(eng, ap, val, writes):
        return S.op(eng, lambda e: e.memset(ap, val), (), writes)
    return ACT, TT, TS, STT, MM, TR, CP, MS


def build_program(nb=2, stop_after=None, dbg=()):
    nc = bass.Bass("TRN2", target_bir_lowering=False)
    din = {}

    def inp(name, shape, dt=F32):
        din[name] = nc.dram_tensor(name, list(shape), dt, kind="ExternalInput").ap()
        return din[name]

    x_t = inp("x_t", [nb, D, L])
    ctx_t = inp("ctx_t", [nb, D, LC])
    pos_t = inp("pos_t", [D, L])
    c_t = inp("c_t", [128, 8, 4])
    mod_w = inp("mod_w", [D, 9 * D])
    mod_b = inp("mod_b", [128, 72])
    wg = inp("wg", [2, D, DFF])
    wu = inp("wu", [2, D, DFF])
    wd = inp("wd", [2, DFF, D])
    w_in = inp("w_in", [D, 6144])
    lbl = inp("lbl", [128, 2, 8])
    normw = inp("normw", [128, 1])
    convw = inp("convw", [128, 3, 12])
    convb = inp("convb", [128, 12])
    hw1 = inp("hw1", [33, 64])
    hb1 = inp("hb1", [64, 1])
    hf1 = inp("hf1", [64, 1])
    hw2 = inp("hw2", [64, 64])
    hb2 = inp("hb2", [64, 1])
    hf2 = inp("hf2", [64, 1])
    hw3 = inp("hw3", [64, 2048])
    hbias = inp("hbias", [128, 2, 512])
    wpa = inp("wpa", [512, D])
    wpb = inp("wpb", [512, D])
    wout = inp("wout", [D, D])
    fnw = inp("fnw", [128, 8])
    zfeat = inp("zfeat", [33, L])
    win = inp("win", [128, 16, 512])
    wins = inp("wins", [128, 16, 512])
    winl = inp("winl", [1, 512])
    Fm = inp("Fm", [32, 128, 16, 128], BF16)
    Fi = inp("Fi", [16, 128, 32, 128], BF16)
    ident_d = inp("ident", [128, 128], BF16)
    masks_d = inp("masks", [64, 2, 64])

    out_t = nc.dram_tensor("out_t", [nb, D, L], F32, kind="ExternalOutput").ap()
    dbg_out = {}

    def scr(name, shape, dt):
        return nc.dram_tensor(name, list(shape), dt, kind="Internal").ap()

    wgb = scr("wgb", [2, 128, NF, 8, 128], BF16)
    wub = scr("wub", [2, 128, NF, 8, 128], BF16)
    wdb = scr("wdb", [2, 128, 8, NF, 128], BF16)
    winb = scr("winb", [128, 48, 8, 128], BF16)
    wpab = scr("wpab", [128, 8, 4, 128], BF16)
    wpbb = scr("wpbb", [128, 8, 4, 128], BF16)
    woutb = scr("woutb", [128, 8, 8, 128], BF16)
    Ksp = scr("Ksp", [2, 2, 16, 128, 512], F32)
    hs = scr("hs", [nb, 128, 8, L], F32)
    oas = scr("oas", [128, 4, L], BF16)

    with ExitStack() as st:
        S = Sched(nc, st)
        ACT, TT, TS, STT, MM, TR, CP, MS = _wrap(S)
        A = Arena(nc, st, 48000)
        pbk = [st.enter_context(nc.psum_tensor(f"pb{i}", [128, 512], F32)) for i in range(8)]
        pb = [p[:] for p in pbk]
        pbt = S.toks(8, "pb")
        pbb = [p[:].bitcast(BF16) for p in pbk]
        q_sp = "sp"

        def dump(name, ap, shape, tok, dt=F32):
            if name not in dbg:
                return
            d = nc.dram_tensor("dbg_" + name, list(shape), dt, kind="ExternalOutput").ap()
            dbg_out[name] = d
            S.dma(q_sp, d, ap, reads=[tok], evtok=tok)

        ident = A.alloc([128], BF16)
        ones_d = A.alloc([128], BF16)
        ones_v = A.alloc([128], BF16)
        ones_f = A.alloc([128], F32)
        modT = A.alloc([72, 4], F32)
        lbT = A.alloc([8], F32)
        omlT = A.alloc([8], F32)
        nomlT = A.alloc([8], F32)
        normw_s = A.alloc([1], F32)
        convw_s = A.alloc([3, 12], F32)
        convb_s = A.alloc([12], F32)
        fnw_s = A.alloc([8], F32)
        masks_s = A.alloc([2, 64], F32)
        epsb = A.alloc([1], F32)
        tconst = S.tok("const")
        S.dma(q_sp, ident, ident_d, writes=[tconst])
        S.dma(q_sp, normw_s, normw, writes=[tconst])
        S.dma(q_sp, convw_s, convw, writes=[tconst])
        S.dma(q_sp, convb_s, convb, writes=[tconst])
        S.dma(q_sp, fnw_s, fnw, writes=[tconst])
        S.dma(q_sp, masks_s[0:64], masks_d, writes=[tconst])
        tc2 = S.tok("const2")
        MS("pool", ones_d, 1.0 / 1024.0, [tc2])
        MS("pool", ones_v, 1.0 / 128.0, [tc2])
        MS("pool", ones_f, 1.0, [tc2])
        MS("pool", epsb, 1e-6, [tc2])
        CONST = [tconst, tc2]

        def conv_units():
            NSL = 3
            sfA = [A.alloc_top([8, 512], F32) for _ in range(NSL)]
            sbA = [A.alloc_top([4, 8, 128], BF16) for _ in range(NSL)]
            tf = S.toks(NSL, "cvf")
            tb = S.toks(NSL, "cvb")
            cvtoks.extend(tf + tb)
            it = 0
            ce = 0
            for s_ in range(2):
                for (src, dst) in ((wg[s_], wgb[s_]), (wu[s_], wub[s_])):
                    for g0 in range(0, NF, 4):
                        g = min(4, NF - g0)
                        sl = it % NSL
                        it += 1
                        S.dma("sp", sfA[sl][:, :, 0:g * 128],
                              src[:, g0 * 128:(g0 + g) * 128].rearrange("(k p) n -> p k n", p=128), writes=[tf[sl]])
                        for gi in range(g):
                            eng = "pool"
                            ce += 1
                            CP(eng, sbA[sl][:, gi, :, :], sfA[sl][:, :, gi * 128:(gi + 1) * 128], [tf[sl]], [tb[sl]])
                        S.dma("sp", dst[:, g0:g0 + g], sbA[sl][:, 0:g], reads=[tb[sl]], evtok=tb[sl])
                        yield
                for dc in range(8):
                    sl = it % NSL
                    it += 1
                    sfv = sfA[sl].rearrange("p a b -> p (a b)")[:, 0:NF * 128].rearrange("p (f c) -> p f c", c=128)
                    sbv = sbA[sl].rearrange("p a b c -> p (a b c)")[:, 0:NF * 128].rearrange("p (f c) -> p f c", c=128)
                    S.dma("sp", sfv, wd[s_][:, dc * 128:(dc + 1) * 128].rearrange("(f p) n -> p f n", p=128), writes=[tf[sl]])
                    for hf_ in range(2):
                        eng = "pool"
                        ce += 1
                        CP(eng, sbv[:, hf_ * 11:(hf_ + 1) * 11, :], sfv[:, hf_ * 11:(hf_ + 1) * 11, :], [tf[sl]], [tb[sl]])
                    S.dma("sp", wdb[s_][:, dc], sbv, reads=[tb[sl]], evtok=tb[sl])
                    yield

        cvtoks = []
        cgen = conv_units()
        for _ in cgen:
            pass

        def pbarrier():
            S.barrier(exclude_engs=("pool", "sp"), exclude_toks=cvtoks)

        def pump(n=1):
            for _ in range(n):
                if next(cgen, "done") == "done":
                    return

        A.mark()
        cts = A.alloc([8, 4], F32)
        scs = A.alloc([8, 4], F32)
        lbs = A.alloc([2, 8], F32)
        mbs = A.alloc([72], F32)
        mwb = [A.alloc([8, 1024], F32) for _ in range(2)]
        mwt = S.toks(2, "mw")
        tct, tsc, tlb, tmod = S.toks(4, "p0a")
        S.dma("act", cts, c_t, writes=[tct])
        S.dma("act", lbs, lbl, writes=[tlb])
        S.dma("act", mbs, mod_b, writes=[tlb])
        ACT(scs, cts, AF.Silu, [tct], [tsc])
        TT("dve", lbs[:, 0, :], lbs[:, 0, :], lbs[:, 1, :], ALU.subtract, [tlb], [tlb])
        ACT(lbT, lbs[:, 0, :], AF.Sigmoid, [tlb], [tmod])
        TS("dve", omlT, lbT, -1.0, 1.0, ALU.mult, ALU.add, [tmod], [tmod])
        TS("dve", nomlT, omlT, -1.0, None, ALU.mult, None, [tmod], [tmod])
        for j in range(9):
            sl = j % 2
            S.dma("act", mwb[sl], mod_w[:, j * 1024:(j + 1) * 1024].rearrange("(kc p) n -> p kc n", p=128),
                  writes=[mwt[sl]])
            pump(2)
            for dc in range(8):
                o0 = (j * 8 + dc) * 4
                for kc in range(8):
                    MM(pb[0][:, o0:o0 + 4], mwb[sl][:, kc, dc * 128:(dc + 1) * 128], scs[:, kc, :],
                       kc == 0, kc == 7, [mwt[sl], tsc], [pbt[0]])
        psm = pb[0][:, 0:288].rearrange("p (a b) -> p a b", b=4)
        for col in range(4):
            TT("dve", modT[:, :, col], psm[:, :, col], mbs, ALU.add, [pbt[0], tlb], [tmod])
        for j in (1, 4, 7):
            TS("dve", modT[:, j * 8:(j + 1) * 8, :], modT[:, j * 8:(j + 1) * 8, :], 1.0, None, ALU.add, None, [tmod], [tmod])
        for j in (2, 8):
            TS("dve", modT[:, j * 8:(j + 1) * 8, :], modT[:, j * 8:(j + 1) * 8, :], 0.5, None, ALU.mult, None, [tmod], [tmod])
        dump("modT", modT, [128, 72, 4], tmod)
        pbarrier()
        A.release()
        CONST.append(tmod)

        def mv(j, dc, col):
            return modT[:, j * 8 + dc, col:col + 1]

        A.mark()
        w3s = A.alloc([2048], F32)
        hsm = A.alloc([8], F32)
        h2p = A.alloc([L + 8], F32)
        winl_s = A.alloc([512], F32)
        rn = A.alloc([2, 512], F32)
        hbias_s = A.alloc([2, 512], F32)
        A.mark()
        zf = A.alloc([L], F32)
        w1s = A.alloc([64], F32)
        w2s = A.alloc([64], F32)
        h1 = A.alloc([L], F32)
        arg = A.alloc([512], F32)
        wtmp = A.alloc([512], F32)
        tk0, th1, th2, targ, theo, trn = S.toks(6, "p0c")
        thf = S.toks(2, "hf"); thb = S.toks(2, "hb"); tab = S.toks(2, "ab"); twn = S.toks(2, "wn")
        S.dma("act", zf[0:33], zfeat, writes=[tk0])
        S.dma("act", w1s[0:33], hw1, writes=[tk0])
        S.dma("act", w2s[0:64], hw2, writes=[tk0])
        S.dma("act", w3s[0:64], hw3, writes=[tk0])
        S.dma("act", hsm[0:64, 0:1], hb1, writes=[tk0])
        S.dma("act", hsm[0:64, 1:2], hf1, writes=[tk0])
        S.dma("act", hsm[0:64, 2:3], hb2, writes=[tk0])
        S.dma("act", hsm[0:64, 3:4], hf2, writes=[tk0])
        S.dma("act", winl_s[0:1], winl, writes=[tk0])
        S.dma("act", hbias_s, hbias, writes=[tk0])
        TT("dve", hsm[0:64, 4:5], hsm[0:64, 0:1], hsm[0:64, 1:2], ALU.mult, [tk0], [tk0])
        TT("dve", hsm[0:64, 5:6], hsm[0:64, 2:3], hsm[0:64, 3:4], ALU.mult, [tk0], [tk0])
        MS("dve", h2p[0:64, 0:1], 0.0, [th2])

        def sin_layer(wsb, kdim, src, dst, dst_off, fcol, fbcol, tsrc, tdst):
            for ti in range(4):
                MM(pb[1][0:64, :], wsb[0:kdim, 0:64], src[0:kdim, ti * 512:(ti + 1) * 512], True, True,
                   [tk0, tsrc], [pbt[1]])
                TS("dve", arg[0:64], pb[1][0:64, :], hsm[0:64, fcol:fcol + 1], hsm[0:64, fbcol:fbcol + 1],
                   ALU.mult, ALU.add, [pbt[1], tk0], [targ])
                for _ in range(2):
                    wrap_once(arg[0:64], targ)
                TS("dve", arg[0:64], arg[0:64], 3.14159, -3.14159, ALU.min, ALU.max, [targ], [targ])
                ACT(dst[0:64, dst_off + ti * 512: dst_off + (ti + 1) * 512], arg[0:64], AF.Sin, [targ], [tdst])

        twt = S.tok("wtmp")

        def wrap_once(ap, tok):
            TS("dve", wtmp[0:64], ap, PI, -2.0 * PI, ALU.is_gt, ALU.mult, [tok], [twt])
            TT("dve", ap, ap, wtmp[0:64], ALU.add, [tok, twt], [tok])
            TS("dve", wtmp[0:64], ap, -PI, 2.0 * PI, ALU.is_lt, ALU.mult, [tok], [twt])
            TT("dve", ap, ap, wtmp[0:64], ALU.add, [tok, twt], [tok])

        sin_layer(w1s, 33, zf, h1, 0, 1, 4, tk0, th1)
        sin_layer(w2s, 64, h1, h2p, 1, 3, 5, th1, th2)
        dump("h2", h2p[0:64, 1:L + 1], [64, L], th2)
        pbarrier()
        A.release()
        heo = A.alloc([16, 2, 512], BF16)
        hfb = [A.alloc([512], F32) for _ in range(2)]
        hbb = [A.alloc([512], F32) for _ in range(2)]
        absb = [A.alloc([512], F32) for _ in range(2)]
        winb_s = [A.alloc([512], F32) for _ in range(2)]
        winsb_s = [A.alloc([512], F32) for _ in range(2)]

        fmb = [A.alloc([2, 16, 128], BF16) for _ in range(3)]
        tfm = S.toks(3, "fm")
        kst = [A.alloc([2, 512], F32) for _ in range(2)]
        tks = S.toks(2, "kst")
        it = 0
        jj = 0
        for o in range(2):
            for lt in range(16):
                sl = lt % 2
                pump(1)
                S.dma("act", winb_s[sl], win[:, lt, :], writes=[twn[sl]])
                S.dma("act", winsb_s[sl], wins[:, lt, :], writes=[twn[sl]])
                MM(pb[2], h2p[0:64, 1 + lt * 128: 1 + (lt + 1) * 128], w3s[0:64, o * 1024: o * 1024 + 512], True, True,
                   [th2, tk0], [pbt[2]])
                MM(pb[3], h2p[0:64, lt * 128:(lt + 1) * 128], w3s[0:64, o * 1024 + 512: o * 1024 + 1024], True, True,
                   [th2, tk0], [pbt[3]])
                TT("dve", hfb[sl], pb[2], winb_s[sl], ALU.mult, [pbt[2], twn[sl]], [thf[sl]])
                TT("dve", hbb[sl], pb[3], winsb_s[sl], ALU.mult, [pbt[3], twn[sl]], [thb[sl]])
                ACT(absb[0], hfb[sl], AF.Abs, [thf[sl]], [tab[0]])
                MM(pb[4 + o], ones_f, absb[0], lt == 0, False, [tab[0], tc2], [pbt[4 + o]])
                ACT(absb[1], hbb[sl], AF.Abs, [thb[sl]], [tab[1]])
                MM(pb[4 + o], ones_f, absb[1], False, False, [tab[1], tc2], [pbt[4 + o]])
                TT("dve", heo[:, lt, 0, :], hfb[sl], hbb[sl], ALU.add, [thf[sl], thb[sl]], [theo])
                TT("dve", heo[:, lt, 1, :], hfb[sl], hbb[sl], ALU.subtract, [thf[sl], thb[sl]], [theo])
            MM(pb[2][0:1, :], h2p[0:64, L:L + 1], w3s[0:64, o * 1024 + 512: o * 1024 + 1024], True, True,
               [th2, tk0], [pbt[2]])
            TT("dve", hfb[0][0:1], pb[2][0:1, :], winl_s[0:1], ALU.mult, [pbt[2], tk0], [thf[0]])
            ACT(absb[0][0:1], hfb[0][0:1], AF.Abs, [thf[0]], [tab[0]])
            MM(pb[4 + o], ones_f[0:1, :], absb[0][0:1], False, True, [tab[0], tc2], [pbt[4 + o]])
            TS("dve", rn[:, o, :], pb[4 + o], 1e-6, None, ALU.add, None, [pbt[4 + o]], [trn])
            S.op("dve", lambda e, o=o: e.reciprocal(out=rn[:, o, :], in_=rn[:, o, :]), [trn], [trn])
            for j in range(16):
                sl = jj % 3
                jj += 1
                pump(1)
                S.dma("act", fmb[sl][:, 0], Fm[j], writes=[tfm[sl]])
                S.dma("act", fmb[sl][:, 1], Fm[16 + j], writes=[tfm[sl]])
                ks = it % 2
                it += 1
                for lc in range(16):
                    MM(pb[6], fmb[sl][:, 0, lc, :], heo[:, lc, 0, :], lc == 0, lc == 15, [tfm[sl], theo], [pbt[6]])
                for lc in range(16):
                    MM(pb[7], fmb[sl][:, 1, lc, :], heo[:, lc, 1, :], lc == 0, lc == 15, [tfm[sl], theo], [pbt[7]])
                TT("dve", kst[ks][:, 0, :], pb[6], rn[:, o, :], ALU.mult, [pbt[6], trn], [tks[ks]])
                TT("dve", kst[ks][:, 0, :], kst[ks][:, 0, :], hbias_s[:, o, :], ALU.add, [tks[ks], tk0], [tks[ks]])
                TT("dve", kst[ks][:, 1, :], pb[7], rn[:, o, :], ALU.mult, [pbt[7], trn], [tks[ks]])
                S.dma("act", Ksp[o, 0, j], kst[ks][:, 0, :], reads=[tks[ks]], evtok=tks[ks])
                S.dma("act", Ksp[o, 1, j], kst[ks][:, 1, :], reads=[tks[ks]], evtok=tks[ks])
        dump("rn", rn, [128, 2, 512], trn)
        pump(1000)
        S.barrier()
        A.release()
        for _ in range(6):
            A.release_top()
        if stop_after == "p0c":
            S.emit()
            return nc, din, dbg_out

        def rstd_of(hb, off, n, sq, tsq, rst, trst, hbtok, ones_ap, nchunks=8):
            for dc in range(nchunks):
                ACT(sq[:, dc, 0:n], hb[:, dc, off:off + n], AF.Square, [hbtok], [tsq])
            for dc in range(nchunks):
                MM(pb[7][:, 0:n], ones_ap, sq[:, dc, 0:n], dc == 0, dc == nchunks - 1, [tsq, tc2], [pbt[7]])
            ACT(rst[:, 0:n], pb[7][:, 0:n], AF.Ln, [pbt[7]], [trst], bias=epsb[:, 0:1])
            ACT(rst[:, 0:n], rst[:, 0:n], AF.Exp, [trst], [trst], scale=-0.5)

        def rstd_multi(hb, tl2, sqs, tsqs, rsts, trsts, hbtok, ones_ap):
            assert len(tl2) <= 2
            tsqd = S.toks(2, "sqd")
            for i, (off, n) in enumerate(tl2):
                for dc in range(4, 8):
                    TT("dve", sqs[i][:, dc, 0:n], hb[:, dc, off:off + n], hb[:, dc, off:off + n], ALU.mult, [hbtok], [tsqd[i]])
            for i, (off, n) in enumerate(tl2):
                for dc in range(4):
                    ACT(sqs[i][:, dc, 0:n], hb[:, dc, off:off + n], AF.Square, [hbtok], [tsqs[i]])
            for i, (off, n) in enumerate(tl2):
                for dc in range(8):
                    MM(pb[7 - i][:, 0:n], ones_ap, sqs[i][:, dc, 0:n], dc == 0, dc == 7,
                       [tsqs[i] if dc < 4 else tsqd[i], tc2], [pbt[7 - i]])
            for i, (off, n) in enumerate(tl2):
                ACT(rsts[i][:, 0:n], pb[7 - i][:, 0:n], AF.Ln, [pbt[7 - i]], [trsts[i]], bias=epsb[:, 0:1])
            for i, (off, n) in enumerate(tl2):
                ACT(rsts[i][:, 0:n], rsts[i][:, 0:n], AF.Exp, [trsts[i]], [trsts[i]], scale=-0.5)

        def ffn_block(s, hb, hbtok, tiles, j0, W):
            A.mark()
            nbk = A.alloc([8, W], BF16)
            act = A.alloc([NF, W], BF16)
            actf = act.rearrange("p f w -> p (f w)")
            sqs = [actf[:, i * 4096:(i + 1) * 4096].rearrange("p (a b) -> p a b", b=512) for i in range(2)]
            rsts = [actf[:, 8192 + i * 1024: 8192 + (i + 1) * 1024].bitcast(F32) for i in range(2)]
            tmp = [A.alloc([512], F32) for _ in range(2)]
            sg = [A.alloc([512], BF16) for _ in range(2)]
            NSA, NSB = 4, 3
            wgs = [A.alloc([8, 128], BF16) for _ in range(NSA)]
            wus = [A.alloc([8, 128], BF16) for _ in range(NSA)]
            wds = [A.alloc([NF, 128], BF16) for _ in range(NSB)]
            tnb = S.toks(len(tiles), "nb")
            tact = S.toks(len(tiles), "act")
            tsqs = S.toks(2, "nrmq"); trsts = S.toks(2, "nrmr")
            ttmp = S.toks(2, "tmp"); tsg = S.toks(2, "sg")
            twg = S.toks(NSA, "wg"); twd = S.toks(NSB, "wd")
            k = 0
            assert len(tiles) == 2
            rstd_multi(hb, [(off, n) for (off, n, col) in tiles], sqs, tsqs, rsts, trsts, hbtok, ones_d)
            for ti, (off, n, col) in enumerate(tiles):
                rst, trst = rsts[ti], trsts[ti]
                for dc in range(8):
                    sl = k % 2
                    k += 1
                    TT("dve", tmp[sl][:, 0:n], hb[:, dc, off:off + n], rst[:, 0:n], ALU.mult, [hbtok, trst], [ttmp[sl]])
                    ACT(nbk[:, dc, off:off + n], tmp[sl][:, 0:n], AF.Identity, [ttmp[sl], tmod], [tnb[ti]],
                        bias=mv(j0, dc, col), scale=mv(j0 + 1, dc, col))
            k = 0
            for f in range(NF):
                sl = f % NSA
                S.dma(q_sp, wgs[sl], wgb[s, :, f], writes=[twg[sl]])
                S.dma(q_sp, wus[sl], wub[s, :, f], writes=[twg[sl]])
                for ti, (off, n, col) in enumerate(tiles):
                    pg = (2 * k) % 4
                    pu = pg + 1
                    ss = k % 2
                    k += 1
                    for kc in range(8):
                        MM(pb[pg][:, 0:n], wgs[sl][:, kc, :], nbk[:, kc, off:off + n], kc == 0, kc == 7,
                           [twg[sl], tnb[ti]], [pbt[pg]])
                    for kc in range(8):
                        MM(pb[pu][:, 0:n], wus[sl][:, kc, :], nbk[:, kc, off:off + n], kc == 0, kc == 7,
                           [twg[sl], tnb[ti]], [pbt[pu]])
                    ACT(sg[ss][:, 0:n], pb[pg][:, 0:n], AF.Silu, [pbt[pg]], [tsg[ss]])
                    TT("dve", act[:, f, off:off + n], sg[ss][:, 0:n], pb[pu][:, 0:n], ALU.mult,
                       [tsg[ss], pbt[pu]], [tact[ti]])
            k = 0
            for dc in range(8):
                sl = dc % NSB
                S.dma(q_sp, wds[sl], wdb[s, :, dc], writes=[twd[sl]])
                for ti, (off, n, col) in enumerate(tiles):
                    pp = 4 + (k % 2)
                    k += 1
                    for f in range(NF):
                        MM(pb[pp][:, 0:n], wds[sl][:, f, :], act[:, f, off:off + n], f == 0, f == NF - 1,
                           [twd[sl], tact[ti]], [pbt[pp]])
                    STT(hb[:, dc, off:off + n], pb[pp][:, 0:n], mv(j0 + 2, dc, col), hb[:, dc, off:off + n],
                        ALU.mult, ALU.add, [pbt[pp], tmod, hbtok], [hbtok])
            S.barrier()
            A.release()

        def load_w(dst, cg, tok, stg, tstg, src=None, nk=8):
            srcm = w_in if src is None else src
            S.dma(q_sp, stg[:, 0:nk, :], srcm[:, cg * 128:(cg + 1) * 128].rearrange("(kc p) n -> p kc n", p=128), writes=[tstg])
            CP("pool", dst, stg[:, 0:nk, :], [tstg], [tok])

        def proj_fm(wsb, wtok, tiles_, consume):
            for i, (off, n) in enumerate(tiles_):
                pi_ = 6 + (i % 2)
                for kc in range(8):
                    MM(pb[pi_][:, 0:n], wsb[:, kc, :], nT[:, kc, off:off + n], kc == 0, kc == 7, [wtok, tnT], [pbt[pi_]])
                consume(pi_, off, n)

        LT4 = [(0, 512), (512, 512), (1024, 512), (1536, 512)]
        LT5 = LT4 + [(2048, 256)]

        def P2(b):
            for h in range(4):
                A.mark()
                wv, wff, wfb, wq, wgt = [A.alloc([8, 128], BF16) for _ in range(5)]
                tw = S.toks(5, "hw")
                wstg = [A.alloc([8, 128], F32) for _ in range(2)]
                twstg = S.toks(2, "wstg")
                for wi, (wsb, cg, tk) in enumerate(((wv, h, tw[0]), (wq, 12 + h, tw[3]), (wgt, 16 + h, tw[4]), (wff, 4 + h, tw[1]), (wfb, 8 + h, tw[2]))):
                    load_w(wsb, cg, tk, wstg[wi % 2], twstg[wi % 2])
                vtok = A.alloc([36, 128], BF16)
                kk = A.alloc([T], F32)
                lfb = A.alloc([T], F32)
                Bb = A.alloc([T], F32)
                qf = A.alloc([L], F32)
                onesr = A.alloc([T], BF16)
                qt_ = [A.alloc([L], BF16) for _ in range(2)]
                kt_ = [A.alloc([T], BF16) for _ in range(2)]
                ktok = [A.alloc([36, 128], BF16) for _ in range(2)]
                o_ = [A.alloc([1, L], F32) for _ in range(2)]
                sgb = A.alloc([L], BF16)
                Sf = [A.alloc([128], F32) for _ in range(2)]
                Sb2 = [[A.alloc([128], BF16) for _ in range(2)] for _ in range(2)]
                tSb2 = [S.toks(2, "Sb2") for _ in range(2)]
                tmpS = [A.alloc([128], F32) for _ in range(2)]
                gcol = [A.alloc([36], F32) for _ in range(2)]
                bref = A.alloc([36], F32)
                scm = [A.alloc([64], BF16) for _ in range(2)]
                sq = A.alloc([1, 512], BF16)
                rst = A.alloc([512], F32)
                tmp = A.alloc([512], F32)
                oab = A.alloc([L], BF16)
                (tvt, tkk, tlf, tB, tq, tone, tsgb, tbref, tsq, trst, ttmp, toab) = S.toks(12, "p2")
                tqt = S.toks(2, "qt"); tkt = S.toks(2, "kt"); tktok = S.toks(2, "ktok"); to = S.toks(2, "o")
                tSf = S.toks(2, "Sf"); tSb = S.toks(2, "Sb"); ttS = S.toks(2, "tS"); tg = S.toks(2, "g"); tscm = S.toks(2, "scm")
                MS("pool", onesr, 1.0, [tone])
                for g0 in range(0, 36, 4):
                    pi_ = 6 + ((g0 // 4) % 2)
                    for ci in range(4):
                        c = g0 + ci
                        for kc in range(8):
                            MM(pb[pi_][0:64, ci * 128:(ci + 1) * 128], nT[:, kc, c * 64:(c + 1) * 64], wv[:, kc, :],
                               kc == 0, kc == 7, [tw[0], tnT], [pbt[pi_]])
                    CP("act", vtok[0:64, g0:g0 + 4, :], pb[pi_][0:64, :].rearrange("p (a b) -> p a b", b=128), [pbt[pi_]], [tvt])
                proj_fm(wq, tw[3], LT4, lambda pi_, off, n: ACT(qf[:, off:off + n], pb[pi_][:, 0:n], AF.Silu, [pbt[pi_]], [tq]))
                proj_fm(wgt, tw[4], LT4, lambda pi_, off, n: ACT(sgb[:, off:off + n], pb[pi_][:, 0:n], AF.Silu, [pbt[pi_]], [tsgb]))
                B3 = Bb.rearrange("p (c s) -> p c s", s=64)
                lf3 = lfb.rearrange("p (c s) -> p c s", s=64)
                for dr in range(2):
                    lbc = dr * 4 + h
                    wsb, wtk = (wff, tw[1]) if dr == 0 else (wfb, tw[2])
                    proj_fm(wsb, wtk, LT5, lambda pi_, off, n: ACT(kk[:, off:off + n], pb[pi_][:, 0:n], AF.Sigmoid,
                                                                    [pbt[pi_]], [tkk], scale=-1.0))
                    ACT(lfb, kk, AF.Ln, [tkk, tmod], [tlf], bias=ones_f[:, 0:1], scale=nomlT[:, lbc:lbc + 1])
                    S.op("dve", lambda e: e.tensor_tensor_scan(out=Bb, data0=onesr, data1=lfb, initial=0.0,
                                                                op0=ALU.mult, op1=ALU.add), [tone, tlf], [tB])
                    if dr == 0:
                        TT("dve", bref, B3[:, :, 0], lf3[:, :, 0], ALU.subtract, [tB, tlf], [tbref])
                        TT("dve", lf3, B3, bref.unsqueeze(2).broadcast_to([128, 36, 64]), ALU.subtract, [tB, tbref, tlf], [tlf])
                    else:
                        TT("dve", lf3, lf3, B3, ALU.subtract, [tB, tlf], [tlf])
                        TT("dve", lf3, lf3, B3[:, :, 63:64].broadcast_to([128, 36, 64]), ALU.add, [tB, tlf], [tlf])
                    ACT(Bb, lfb, AF.Exp, [tlf], [tB])
                    if dr == 0:
                        CP("dve", gcol[dr], B3[:, :, 63], [tB], [tg[dr]])
                    else:
                        CP("dve", gcol[dr], B3[:, :, 0], [tB], [tg[dr]])
                    TT("dve", qt_[dr], qf, Bb[:, 0:L], ALU.mult, [tq, tB], [tqt[dr]])
                    ACT(lfb, lfb, AF.Exp, [tlf], [tlf], scale=-1.0)
                    STT(kt_[dr], kk, omlT[:, lbc:lbc + 1], lfb, ALU.mult, ALU.mult, [tkk, tlf, tmod], [tkt[dr]])
                    for g0 in range(0, 36, 4):
                        pi_ = 6 + ((g0 // 4) % 2)
                        for ci in range(4):
                            c = g0 + ci
                            TR(pbb[pi_][0:64, ci * 128:(ci + 1) * 128], kt_[dr][:, c * 64:(c + 1) * 64], ident,
                               [tkt[dr], tconst], [pbt[pi_]])
                        CP("act", ktok[dr][0:64, g0:g0 + 4, :], pbb[pi_][0:64, 0:512].rearrange("p (a b) -> p a b", b=128),
                           [pbt[pi_]], [tktok[dr]])
                    MS("pool", tmpS[dr], 0.0, [ttS[dr]])
                    MS("pool", Sb2[dr][0], 0.0, [tSb2[dr][0]])
                orders = [[32, 33, 34, 35] + list(range(32)), [35, 34, 33, 32] + list(range(31, -1, -1))]
                for step in range(36):
                    par = step % 2
                    cc_ = [orders[dr][step] for dr in range(2)]
                    lat = cc_[0] < 32
                    if lat:
                        for dr in range(2):
                            c = cc_[dr]
                            MM(pb[dr][0:64, 0:64], kt_[dr][:, c * 64:(c + 1) * 64], qt_[dr][:, c * 64:(c + 1) * 64], True, True,
                               [tkt[dr], tqt[dr]], [pbt[dr]])
                    for dr in range(2):
                        c = cc_[dr]
                        MM(pb[4 + dr][:, 0:128], ktok[dr][0:64, c, :], vtok[0:64, c, :], True, True, [tktok[dr], tvt], [pbt[4 + dr]])
                    if lat:
                        for dr in range(2):
                            TT("dve", scm[dr][0:64], pb[dr][0:64, 0:64], masks_s[0:64, dr, :], ALU.mult,
                               [pbt[dr], tconst], [tscm[dr]])
                        for dr in range(2):
                            c = cc_[dr]
                            MM(pb[2 + dr][:, 0:64], vtok[0:64, c, :], scm[dr][0:64], True, False, [tvt, tscm[dr]], [pbt[2 + dr]])
                            MM(pb[2 + dr][:, 0:64], Sb2[dr][par], qt_[dr][:, c * 64:(c + 1) * 64], False, True,
                               [tSb2[dr][par], tqt[dr]], [pbt[2 + dr]])
                            CP("act", o_[dr][:, 0, c * 64:(c + 1) * 64], pb[2 + dr][:, 0:64], [pbt[2 + dr]], [to[dr]])
                    for dr in range(2):
                        c = cc_[dr]
                        cp_ = orders[dr][step - 1] if step > 0 else c
                        STT(tmpS[dr], tmpS[dr], gcol[dr][:, cp_:cp_ + 1], pb[4 + dr][:, 0:128], ALU.mult, ALU.add,
                            [ttS[dr], tg[dr], pbt[4 + dr]], [ttS[dr]])
                        TS("dve", Sb2[dr][1 - par], tmpS[dr], gcol[dr][:, c:c + 1], None, ALU.mult, None, [ttS[dr], tg[dr]],
                           [tSb2[dr][1 - par]])
                TT("pool", o_[0], o_[0], o_[1], ALU.add, [to[0], to[1]], [to[0]])
                for (off, n) in LT4:
                    rstd_of(o_[0], off, n, sq, tsq, rst, trst, to[0], ones_v, nchunks=1)
                    TT("dve", tmp[:, 0:n], o_[0][:, 0, off:off + n], rst[:, 0:n], ALU.mult, [to[0], trst], [ttmp])
                    STT(oab[:, off:off + n], tmp[:, 0:n], normw_s[:, 0:1], sgb[:, off:off + n], ALU.mult, ALU.mult,
                        [ttmp, tconst, tsgb], [toab])
                S.dma(q_sp, oas[:, h, :], oab, reads=[toab], evtok=toab)
                S.barrier()
                A.release()

        def P3(b, zT, tz):
            A.mark()
            gT = A.alloc([4, L], BF16)
            ztok = A.alloc([16, 512], BF16)
            pb_base = A.top
            Pbuf = A.alloc([32, 512], BF16)
            pb_end = A.top
            A.top = pb_base
            pT = A.alloc([L + 8], F32)
            uT = A.alloc([L], F32)
            wsl = [A.alloc([8, 128], BF16) for _ in range(2)]
            wstg3 = [A.alloc([8, 128], F32) for _ in range(2)]
            twstg3 = S.toks(2, "wstg3")
            assert A.top <= pb_end
            A.top = pb_end
            fib = [A.alloc([32, 128], BF16) for _ in range(2)]
            fmb = [A.alloc([2, 16, 128], BF16) for _ in range(2)]
            kb = [A.alloc([2, 512], F32) for _ in range(2)]
            tm = [A.alloc([512], F32) for _ in range(4)]
            tg_, tzt, tP, tpT, tuT = S.toks(5, "p3")
            twsl = S.toks(2, "wsl"); tfib = S.toks(2, "fib"); tfmb = S.toks(2, "fmb"); tkb = S.toks(2, "kb"); ttm = S.toks(4, "tm")

            def proj_conv(part, dst, tdst):
                S.barrier()
                MS("pool", pT[:, 0:1], 0.0, [tpT])
                MS("pool", pT[:, L + 1:L + 2], 0.0, [tpT])
                for cc in range(4):
                    sl = cc % 2
                    ci = part * 4 + cc
                    load_w(wsl[sl], 20 + ci, twsl[sl], wstg3[sl], twstg3[sl])
                    proj_fm(wsl[sl], twsl[sl], LT4,
                            lambda pi_, off, n: CP("act", pT[:, 1 + off:1 + off + n], pb[pi_][:, 0:n], [pbt[pi_]], [tpT]))
                    TS("dve", uT, pT[:, 1:L + 1], convw_s[:, 1, ci:ci + 1], convb_s[:, ci:ci + 1], ALU.mult, ALU.add,
                       [tpT, tconst], [tuT])
                    STT(uT, pT[:, 0:L], convw_s[:, 0, ci:ci + 1], uT, ALU.mult, ALU.add, [tpT, tconst, tuT], [tuT])
                    STT(dst[:, cc, :], pT[:, 2:L + 2], convw_s[:, 2, ci:ci + 1], uT, ALU.mult, ALU.add,
                        [tpT, tconst, tuT], [tdst])
                S.barrier()

            proj_conv(0, zT, tz)
            for o in range(2):
                proj_conv(1 + o, gT, tg_)
                for tt in range(16):
                    pi_ = 6 + (tt % 2)
                    for cc in range(4):
                        TR(pbb[pi_][:, cc * 128:(cc + 1) * 128], zT[:, cc, tt * 128:(tt + 1) * 128], ident,
                           [tz, tconst], [pbt[pi_]])
                    CP("act" if tt % 2 else "dve", ztok[:, tt, :], pbb[pi_][:, 0:512], [pbt[pi_]], [tzt])
                for j in range(16):
                    sl = j % 2
                    S.dma(q_sp, fmb[sl][:, 0], Fm[j], writes=[tfmb[sl]])
                    S.dma(q_sp, fmb[sl][:, 1], Fm[16 + j], writes=[tfmb[sl]])
                    S.dma(q_sp, kb[sl][:, 0, :], Ksp[o, 0, j], writes=[tkb[sl]])
                    S.dma(q_sp, kb[sl][:, 1, :], Ksp[o, 1, j], writes=[tkb[sl]])
                    pr, pim = 2 * sl, 2 * sl + 1
                    for lc in range(16):
                        MM(pb[pr], fmb[sl][:, 0, lc, :], ztok[:, lc, :], lc == 0, lc == 15, [tfmb[sl], tzt], [pbt[pr]])
                    for lc in range(16):
                        MM(pb[pim], fmb[sl][:, 1, lc, :], ztok[:, lc, :], lc == 0, lc == 15, [tfmb[sl], tzt], [pbt[pim]])
                    TT("dve", tm[0], pb[pr], kb[sl][:, 0, :], ALU.mult, [pbt[pr], tkb[sl]], [ttm[0]])
                    TT("dve", tm[1], pb[pim], kb[sl][:, 1, :], ALU.mult, [pbt[pim], tkb[sl]], [ttm[1]])
                    TT("pool", Pbuf[:, j, :], tm[0], tm[1], ALU.subtract, [ttm[0], ttm[1]], [tP])
                    TT("dve", tm[2], pb[pr], kb[sl][:, 1, :], ALU.mult, [pbt[pr], tkb[sl]], [ttm[2]])
                    TT("dve", tm[3], pb[pim], kb[sl][:, 0, :], ALU.mult, [pbt[pim], tkb[sl]], [ttm[3]])
                    TT("pool", Pbuf[:, 16 + j, :], tm[2], tm[3], ALU.add, [ttm[2], ttm[3]], [tP])
                k = 0
                for tt in range(16):
                    sl = tt % 2
                    S.dma(q_sp, fib[sl], Fi[tt], writes=[tfib[sl]])
                    for cc in range(4):
                        pi_ = 4 + (k % 2)
                        k += 1
                        for fc in range(32):
                            MM(pb[pi_][:, 0:128], Pbuf[:, fc, cc * 128:(cc + 1) * 128], fib[sl][:, fc, :], fc == 0, fc == 31,
                               [tP, tfib[sl]], [pbt[pi_]])
                        TT("dve", zT[:, cc, tt * 128:(tt + 1) * 128], gT[:, cc, tt * 128:(tt + 1) * 128], pb[pi_][:, 0:128],
                           ALU.mult, [tg_, pbt[pi_]], [tz])
            S.barrier()
            A.release()

        def P4(b, zT, tz, yT, ty):
            A.mark()
            oaT = A.alloc([4, L], BF16)
            toa = S.tok("oaT")
            S.dma(q_sp, oaT, oas, writes=[toa])
            wga = [A.alloc([8, 128], BF16) for _ in range(2)]
            wgb_ = [A.alloc([8, 128], BF16) for _ in range(2)]
            wa = [A.alloc([4, 128], BF16) for _ in range(2)]
            wb_ = [A.alloc([4, 128], BF16) for _ in range(2)]
            sga = [A.alloc([512], F32) for _ in range(2)]
            sgb2 = [A.alloc([512], F32) for _ in range(2)]
            t1 = [A.alloc([512], F32) for _ in range(2)]
            t2 = [A.alloc([512], F32) for _ in range(2)]
            tw4a = S.toks(2, "w4a"); tw4b = S.toks(2, "w4b"); tw4c = S.toks(2, "w4c"); tw4d = S.toks(2, "w4d")
            wstg4 = [A.alloc([8, 128], F32) for _ in range(2)]
            twstg4 = S.toks(2, "wstg4")
            tsa = S.toks(2, "sa"); tsb = S.toks(2, "sb"); tt1 = S.toks(2, "t1"); tt2 = S.toks(2, "t2")
            k = 0
            for dc in range(8):
                sl = dc % 2
                load_w(wga[sl], 32 + dc, tw4a[sl], wstg4[0], twstg4[0])
                load_w(wgb_[sl], 40 + dc, tw4b[sl], wstg4[1], twstg4[1])
                load_w(wa[sl], dc, tw4c[sl], wstg4[0], twstg4[0], src=wpa, nk=4)
                load_w(wb_[sl], dc, tw4d[sl], wstg4[1], twstg4[1], src=wpb, nk=4)
                for (off, n) in LT4:
                    ss = k % 2
                    k += 1
                    for kc in range(8):
                        MM(pb[4 * ss + 0], wga[sl][:, kc, :], nT[:, kc, off:off + n], kc == 0, kc == 7, [tw4a[sl], tnT], [pbt[4 * ss + 0]])
                    for kc in range(4):
                        MM(pb[4 * ss + 1], wa[sl][:, kc, :], oaT[:, kc, off:off + n], kc == 0, kc == 3, [tw4c[sl], toa], [pbt[4 * ss + 1]])
                    for kc in range(8):
                        MM(pb[4 * ss + 2], wgb_[sl][:, kc, :], nT[:, kc, off:off + n], kc == 0, kc == 7, [tw4b[sl], tnT], [pbt[4 * ss + 2]])
                    for kc in range(4):
                        MM(pb[4 * ss + 3], wb_[sl][:, kc, :], zT[:, kc, off:off + n], kc == 0, kc == 3, [tw4d[sl], tz], [pbt[4 * ss + 3]])
                    ACT(sga[ss], pb[4 * ss + 0], AF.Sigmoid, [pbt[4 * ss + 0]], [tsa[ss]])
                    ACT(sgb2[ss], pb[4 * ss + 2], AF.Sigmoid, [pbt[4 * ss + 2]], [tsb[ss]])
                    TT("dve", t1[ss], sga[ss], pb[4 * ss + 1], ALU.mult, [tsa[ss], pbt[4 * ss + 1]], [tt1[ss]])
                    TT("dve", t2[ss], sgb2[ss], pb[4 * ss + 3], ALU.mult, [tsb[ss], pbt[4 * ss + 3]], [tt2[ss]])
                    TT("pool", yT[:, dc, off:off + n], t1[ss], t2[ss], ALU.add, [tt1[ss], tt2[ss]], [ty])
            S.barrier()
            A.release()

        def P5(b, yT, ty):
            for blk in range(2):
                A.mark()
                W = 1024
                base = blk * W
                hb = A.alloc([8, W], F32)
                hbtok = S.tok("hb5")
                tld = S.toks(8, "hld")
                for dc in range(8):
                    S.dma(q_sp, hb[:, dc, :], hs[b, :, dc, base:base + W], writes=[tld[dc]])
                A.mark()
                wo = [A.alloc([8, 128], BF16) for _ in range(2)]
                two = S.toks(2, "wo")
                wstg5 = [A.alloc([8, 128], F32) for _ in range(2)]
                twstg5 = S.toks(2, "wstg5")
                tiles = [(0, 512, b), (512, 512, b)]
                k = 0
                for dc in range(8):
                    sl = dc % 2
                    load_w(wo[sl], dc, two[sl], wstg5[sl], twstg5[sl], src=wout)
                    for (off, n, col) in tiles:
                        pi_ = 6 + (k % 2)
                        k += 1
                        for kc in range(8):
                            MM(pb[pi_][:, 0:n], wo[sl][:, kc, :], yT[:, kc, base + off:base + off + n], kc == 0, kc == 7,
                               [two[sl], ty], [pbt[pi_]])
                        STT(hb[:, dc, off:off + n], pb[pi_][:, 0:n], mv(5, dc, col), hb[:, dc, off:off + n],
                            ALU.mult, ALU.add, [pbt[pi_], tmod, tld[dc]], [hbtok])
                S.barrier()
                A.release()
                ffn_block(1, hb, hbtok, tiles, 6, W)
                A.mark()
                sqs = [A.alloc([8, 512], BF16) for _ in range(2)]
                rsts = [A.alloc([512], F32) for _ in range(2)]
                tsqs = S.toks(2, "nrm5q"); trsts = S.toks(2, "nrm5r")
                rstd_multi(hb, [(off, n) for (off, n, col) in tiles], sqs, tsqs, rsts, trsts, hbtok, ones_d)
                for tix, (off, n, col) in enumerate(tiles):
                    rst, trst = rsts[tix], trsts[tix]
                    for dc in range(8):
                        STT(hb[:, dc, off:off + n], hb[:, dc, off:off + n], fnw_s[:, dc:dc + 1], rst[:, 0:n],
                            ALU.mult, ALU.mult, [hbtok, tconst, trst], [hbtok])
                S.dma(q_sp, out_t[b][:, base:base + W].rearrange("(dc p) t -> p dc t", p=128), hb, reads=[hbtok], evtok=hbtok)
                S.barrier()
                A.release()
                A.release()

        tnT = S.tok("nT")
        nT = None
        for b in range(nb):
            A.mark()
            nT = A.alloc([8, T], BF16)
            blocks = [
                [(0, 512, b), (512, 256, b)],
                [(768, 512, b), (1280, 256, b)],
                [(1536, 512, b), (2048, 256, 2)],
            ]
            for bi, tl in enumerate(blocks):
                A.mark()
                W = 768
                base = bi * 768
                hb = A.alloc([8, W], F32)
                hbtok = S.tok("hb")
                pst = [A.alloc([8, 512], F32)]
                tps = S.toks(1, "pos")
                k = 0
                for (off, n, col) in tl:
                    lo = off - base
                    if col == 2:
                        S.dma(q_sp, hb[:, :, lo:lo + n], ctx_t[b].rearrange("(dc p) t -> p dc t", p=128), writes=[hbtok])
                    else:
                        S.dma(q_sp, hb[:, :, lo:lo + n],
                              x_t[b][:, off:off + n].rearrange("(dc p) t -> p dc t", p=128), writes=[hbtok])
                        S.dma(q_sp, pst[0][:, :, 0:n], pos_t[:, off:off + n].rearrange("(dc p) t -> p dc t", p=128),
                              writes=[tps[0]])
                        for dc in range(8):
                            k += 1
                            TT("dve", hb[:, dc, lo:lo + n], hb[:, dc, lo:lo + n],
                               pst[0][:, dc, 0:n], ALU.add, [hbtok, tps[0]], [hbtok])
                ltiles = [(off - base, n, col) for (off, n, col) in tl]
                ffn_block(0, hb, hbtok, ltiles, 0, W)
                A.mark()
                sqs = [A.alloc([8, 512], BF16) for _ in range(2)]
                rsts = [A.alloc([512], F32) for _ in range(2)]
                tmp = [A.alloc([512], F32) for _ in range(2)]
                tsqs = S.toks(2, "nrmq"); trsts = S.toks(2, "nrmr")
                ttmp = S.toks(2, "tmp")
                k = 0
                rstd_multi(hb, [(off - base, n) for (off, n, col) in tl], sqs, tsqs, rsts, trsts, hbtok, ones_d)
                for tix, (off, n, col) in enumerate(tl):
                    lo = off - base
                    rst, trst = rsts[tix], trsts[tix]
                    if col != 2:
                        S.dma(q_sp, hs[b, :, :, off:off + n], hb[:, :, lo:lo + n], reads=[hbtok], evtok=hbtok)
                    for dc in range(8):
                        sl = k % 2
                        k += 1
                        TT("dve", tmp[sl][:, 0:n], hb[:, dc, lo:lo + n], rst[:, 0:n], ALU.mult, [hbtok, trst], [ttmp[sl]])
                        ACT(nT[:, dc, off:off + n], tmp[sl][:, 0:n], AF.Identity, [ttmp[sl], tmod], [tnT],
                            bias=mv(3, dc, col), scale=mv(4, dc, col))
                S.barrier()
                A.release()
                A.release()
            if b == 0:
                dump("nT0", nT, [128, 8, T], tnT, BF16)
                dump("hs0", hs[0], [128, 8, L], tnT)
            if stop_after == "p1":
                break
            tz = S.tok("zT")
            P2(b)
            zT = A.alloc([4, L], BF16)
            if b == 0:
                dump("oas", oas, [128, 4, L], tz, BF16)
            if stop_after == "p2":
                break
            P3(b, zT, tz)
            if b == 0:
                dump("zT", zT, [128, 4, L], tz, BF16)
            if stop_after == "p3":
                break
            yT = A.alloc_top([8, L], BF16)
            ty = S.tok("yT")
            P4(b, zT, tz, yT, ty)
            if b == 0:
                dump("yT", yT, [128, 8, L], ty, BF16)
            S.barrier()
            A.release()
            if stop_after == "p4":
                break
            P5(b, yT, ty)
            A.release_top()
        S.barrier()
        S.emit()
    return nc, din, dbg_out


def _bf(a):
    return np.ascontiguousarray(a).astype(ml_dtypes.bfloat16)


_CONST_CACHE = {}


def host_consts():
    if _CONST_CACHE:
        return _CONST_CACHE
    f32 = np.float32
    quarter = D // 4
    omega = (1.0 / (10000.0 ** (np.arange(quarter, dtype=f32) / quarter))).astype(f32)
    rows = L // 64
    ar = np.arange(rows, dtype=f32)[:, None] * omega
    ac = np.arange(64, dtype=f32)[:, None] * omega
    er = np.concatenate([np.sin(ar), np.cos(ar)], axis=-1)
    ec = np.concatenate([np.sin(ac), np.cos(ac)], axis=-1)
    emb = np.concatenate([np.broadcast_to(er[:, None, :], (rows, 64, D // 2)),
                          np.broadcast_to(ec[None, :, :], (rows, 64, D // 2))], axis=-1).reshape(L, D)
    pos_t = np.ascontiguousarray(emb.T.astype(f32))
    p = np.arange(L, dtype=f32)
    t = p / (L - 1)
    w = (2.0 * math.pi * p / L).astype(f32)
    fb = np.linspace(1e-4, 15, 16, dtype=f32)
    ang = w[:, None] * fb[None, :]
    z = np.concatenate([t[:, None], np.cos(ang), -np.sin(ang)], axis=-1).astype(f32)
    zfeat = np.ascontiguousarray(z.T)
    max_decay = math.log(1e-2) / 0.3
    min_decay = math.log(1e-2) / 1.5
    deltas = np.abs(np.linspace(min_decay, max_decay, 512, dtype=f32))
    window = (np.exp(-t[:, None] * deltas[None, :]) + 0.05).astype(f32)
    win = np.ascontiguousarray(window.reshape(16, 128, 512).transpose(1, 0, 2))
    wsh = np.zeros_like(window)
    wsh[1:] = window[:-1]
    wins = np.ascontiguousarray(wsh.reshape(16, 128, 512).transpose(1, 0, 2))
    winl = np.ascontiguousarray(window[L - 1:L])
    N = 2 * L
    tt = np.arange(L, dtype=np.float64)[:, None]
    ff = (np.arange(L, dtype=np.float64) + 0.5)[None, :]
    angm = 2.0 * np.pi * tt * ff / N
    Fc = np.cos(angm)
    Fs = -np.sin(angm)
    F = np.concatenate([Fc, Fs], axis=1)
    Fm = F.reshape(16, 128, 32, 128).transpose(2, 1, 0, 3)
    Fi = (2.0 / N) * F.T
    Fi = Fi.reshape(32, 128, 16, 128).transpose(2, 1, 0, 3)
    masks = np.zeros((64, 2, 64), f32)
    si = np.arange(64)[:, None]
    ti = np.arange(64)[None, :]
    masks[:, 0, :] = (si <= ti)
    masks[:, 1, :] = (si >= ti)
    _CONST_CACHE.update(dict(pos_t=pos_t, zfeat=zfeat, win=win, wins=wins, winl=winl, Fm=_bf(Fm), Fi=_bf(Fi),
                             ident=_bf(np.eye(128, dtype=f32)), masks=masks))
    return _CONST_CACHE


def prep_core(inp, bsel):
    f32 = np.float32
    c = host_consts()
    m = dict(c)
    nbl = len(bsel)
    m["x_t"] = np.ascontiguousarray(np.stack([inp["x"][b].T for b in bsel]))
    m["ctx_t"] = np.ascontiguousarray(np.stack([inp["ctx"][b].T for b in bsel]))
    ct = np.zeros((4, D), f32)
    for i, b in enumerate(bsel):
        ct[i] = inp["c"][b]
    ct[2] = inp["c_ctx"]
    m["c_t"] = np.ascontiguousarray(ct.reshape(4, 8, 128).transpose(2, 1, 0))
    m["mod_w"] = np.ascontiguousarray(inp["mod_w"][0])
    m["mod_b"] = np.ascontiguousarray(inp["mod_b"][0].reshape(72, 128).T)
    m["wg"] = np.ascontiguousarray(inp["ffn_w_gate"][0])
    m["wu"] = np.ascontiguousarray(inp["ffn_w_up"][0])
    m["wd"] = np.ascontiguousarray(inp["ffn_w_down"][0])
    m["w_in"] = np.ascontiguousarray(inp["w_in"][0])
    m["lbl"] = np.ascontiguousarray(inp["hgrn_lb_logits"].reshape(2, 2, 4, 128).transpose(3, 0, 1, 2).reshape(128, 2, 8))
    m["normw"] = np.ascontiguousarray(inp["hgrn_norm_w"][0].reshape(128, 1))
    m["convw"] = np.ascontiguousarray(inp["hyena_conv_w"][0].reshape(3, 12, 128).transpose(2, 0, 1))
    m["convb"] = np.ascontiguousarray(inp["hyena_conv_b"][0].reshape(12, 128).T)
    m["hw1"] = np.ascontiguousarray(inp["hyena_w1"][0])
    m["hb1"] = np.ascontiguousarray(inp["hyena_b1"][0].reshape(64, 1))
    m["hf1"] = np.ascontiguousarray(inp["hyena_freq1"][0].reshape(64, 1))
    m["hw2"] = np.ascontiguousarray(inp["hyena_w2"][0])
    m["hb2"] = np.ascontiguousarray(inp["hyena_b2"][0].reshape(64, 1))
    m["hf2"] = np.ascontiguousarray(inp["hyena_freq2"][0].reshape(64, 1))
    m["hw3"] = np.ascontiguousarray(inp["hyena_w3"][0])
    m["hbias"] = np.ascontiguousarray(np.broadcast_to(inp["hyena_bias"][0][None], (128, 2, 512)))
    m["wpa"] = np.ascontiguousarray(inp["w_proj_a"][0])
    m["wpb"] = np.ascontiguousarray(inp["w_proj_b"][0])
    m["wout"] = np.ascontiguousarray(inp["w_out"][0])
    m["fnw"] = np.ascontiguousarray(inp["final_norm_w"].reshape(8, 128).T)
    return {k: (v if v.dtype == ml_dtypes.bfloat16 else v.astype(f32)) for k, v in m.items()}


_PROG = {}


def kernel(**inputs):
    inputs = {k: np.asarray(v) for k, v in inputs.items()}
    if "full" not in _PROG:
        _PROG["full"] = build_program(nb=2)
    nc, din, _ = _PROG["full"]
    in_maps = []
    for core in range(NCORE):
        m = prep_core(inputs, [2 * core, 2 * core + 1])
        in_maps.append({k: m[k] for k in din})
    res = run_bass_kernel_spmd(nc, in_maps, core_ids=list(range(NCORE)))
    out = np.empty((16, L, D), np.float32)
    for core in range(NCORE):
        o = res.results[core]["out_t"]
        for i in range(2):
            out[2 * core + i] = o[i].T
    return out
```
